# Optimizing a Trainium2 kernel written in Bass

```python
import math
import jax
import jax.numpy as jnp
from jax import lax
import numpy as np

D_MODEL = 1024
BATCH = 4
SEQ = 4096
DEPTH = 2

GRID_W = 64

NA_HEADS = 8
NA_HEAD_DIM = 64
NA_WIN_ROWS = 8
NA_WIN_COLS = 16
NA_QBLOCK = 16
NA_KBLOCK = NA_QBLOCK + NA_WIN_COLS
WIDTH_A = NA_HEADS * NA_HEAD_DIM

GLA_HEADS = 4
GLA_DK = 64
GLA_DV = 128
GLA_RANK = 16
GLA_GATE_NORM = 16.0
WIDTH_B = GLA_HEADS * GLA_DV

HGRN_HEADS = 4
HGRN_DIM = 128
WIDTH_C = HGRN_HEADS * HGRN_DIM

SSD_HEADS = 8
SSD_HEAD_DIM = 64
SSD_GROUPS = 2
SSD_STATE = 128
SSD_CONV = 4
SSD_CHUNK = 64
WIDTH_D = SSD_HEADS * SSD_HEAD_DIM
SSD_CONV_CH = WIDTH_D + 2 * SSD_GROUPS * SSD_STATE
CONV_PAD = (SSD_CONV // 2, (SSD_CONV - 1) // 2)

REC_CHUNK = 16

N_EVEN = (DEPTH + 1) // 2
N_ODD = DEPTH // 2
EVEN_SPLITS = (WIDTH_A, WIDTH_A, WIDTH_A, WIDTH_A, GLA_HEADS * GLA_DK, GLA_HEADS * GLA_DK, WIDTH_B, WIDTH_B, GLA_RANK, GLA_RANK)
ODD_SPLITS = (WIDTH_C, WIDTH_C, WIDTH_C, WIDTH_C, WIDTH_C, WIDTH_D, SSD_CONV_CH, SSD_HEADS, SSD_HEADS)
EVEN_IN = 4 * WIDTH_A + 2 * GLA_HEADS * GLA_DK + 2 * WIDTH_B + 2 * GLA_RANK
ODD_IN = 5 * WIDTH_C + WIDTH_D + SSD_CONV_CH + 2 * SSD_HEADS

DEEPNORM_ALPHA = (2 * DEPTH) ** 0.25
DEEPNORM_BETA = (8 * DEPTH) ** -0.25
LN_EPS = 1e-5
RMS_EPS = 1e-6

kernel_name = 'hybrid_na_gla_hgrn2_ssd_encoder'


def split_cols(t, sizes):
    idx = np.cumsum(np.array(sizes))[:-1].tolist()
    return jnp.split(t, idx, axis=-1)


def layer_norm(x, g, b):
    xf = x.astype(jnp.float32)
    mu = jnp.mean(xf, -1, keepdims=True)
    var = jnp.mean(jnp.square(xf - mu), -1, keepdims=True)
    return ((xf - mu) * lax.rsqrt(var + LN_EPS) * g + b).astype(x.dtype)


def rms_norm(x, g):
    xf = x.astype(jnp.float32)
    return xf * lax.rsqrt(jnp.mean(jnp.square(xf), -1, keepdims=True) + RMS_EPS) * g.astype(jnp.float32)


def neighbourhood_attention(q, k, v, rpb):
    bsz, seq, heads, dh = q.shape
    rows = seq // GRID_W
    kr = min(NA_WIN_ROWS, rows)
    ncb = GRID_W // NA_QBLOCK
    r = jnp.arange(rows)
    row_start = jnp.clip(r - kr // 2, 0, rows - kr)
    key_rows = row_start[:, None] + jnp.arange(kr)[None, :]
    blk_start = jnp.clip(jnp.arange(ncb) * NA_QBLOCK - NA_WIN_COLS // 2, 0, GRID_W - NA_KBLOCK)
    key_cols = blk_start[:, None] + jnp.arange(NA_KBLOCK)[None, :]
    q_cols = (jnp.arange(ncb) * NA_QBLOCK)[:, None] + jnp.arange(NA_QBLOCK)[None, :]
    col_start = jnp.clip(q_cols - NA_WIN_COLS // 2, 0, GRID_W - NA_WIN_COLS)
    kc = key_cols[:, None, :]
    cs = col_start[:, :, None]
    col_ok = (kc >= cs) & (kc < cs + NA_WIN_COLS)
    nk = kr * NA_KBLOCK
    valid = jnp.broadcast_to(col_ok[:, :, None, :], (ncb, NA_QBLOCK, kr, NA_KBLOCK)).reshape(ncb, NA_QBLOCK, nk)
    dr_idx = key_rows - r[:, None] + NA_WIN_ROWS - 1
    dc_idx = jnp.clip(kc - q_cols[:, :, None], 1 - NA_WIN_COLS, NA_WIN_COLS - 1) + NA_WIN_COLS - 1
    bias = rpb[:, dr_idx[:, None, None, :, None], dc_idx[None, :, :, None, :]]
    bias = bias.reshape(heads, rows, ncb, NA_QBLOCK, nk).transpose(1, 2, 0, 3, 4).astype(jnp.float32)
    k_grid = k.reshape(bsz, rows, GRID_W, heads, dh)
    v_grid = v.reshape(bsz, rows, GRID_W, heads, dh)
    g_r = key_rows[:, None, :, None]
    g_c = key_cols[None, :, None, :]
    kg = k_grid[:, g_r, g_c].reshape(bsz, rows, ncb, nk, heads, dh)
    vg = v_grid[:, g_r, g_c].reshape(bsz, rows, ncb, nk, heads, dh)
    qb = q.reshape(bsz, rows, ncb, NA_QBLOCK, heads, dh)
    s = jnp.einsum('brnqhd,brnkhd->brnhqk', qb, kg).astype(jnp.float32) * (dh ** -0.5) + bias[None]
    s = jnp.where(valid[None, None, :, None], s, -jnp.inf)
    p = jax.nn.softmax(s, axis=-1).astype(v.dtype)
    o = jnp.einsum('brnhqk,brnkhd->brnqhd', p, vg)
    return o.reshape(bsz, seq, heads, dh)


def chunked_gated_recurrence(q, k, v, log_f):
    bsz, heads, seq, kd = q.shape
    vd = v.shape[-1]
    n = seq // REC_CHUNK
    q, k, log_f = [t.astype(jnp.float32).reshape(bsz, heads, n, REC_CHUNK, kd) for t in (q, k, log_f)]
    v = v.astype(jnp.float32).reshape(bsz, heads, n, REC_CHUNK, vd)
    b = jnp.cumsum(log_f, axis=3)
    mask = jnp.tril(jnp.ones((REC_CHUNK, REC_CHUNK), dtype=bool))
    diff = b[:, :, :, :, None, :] - b[:, :, :, None, :, :]
    decay_ij = jnp.exp(jnp.where(mask[:, :, None], diff, -jnp.inf))
    attn = jnp.einsum('bhnik,bhnjk,bhnijk->bhnij', q, k, decay_ij)
    o = jnp.einsum('bhnij,bhnjv->bhniv', attn, v)
    b_last = b[:, :, :, -1:, :]
    u = jnp.einsum('bhnjk,bhnjv->bhnkv', k * jnp.exp(b_last - b), v)
    chunk_decay = jnp.exp(b_last[:, :, :, 0, :])

    def step(s, inp):
        d, u_n = inp
        return d[..., None] * s + u_n, s

    s0 = jnp.zeros((bsz, heads, kd, vd), jnp.float32)
    _, s_prev = lax.scan(step, s0, (jnp.moveaxis(chunk_decay, 2, 0), jnp.moveaxis(u, 2, 0)))
    s_prev = jnp.moveaxis(s_prev, 0, 2)
    o = o + jnp.einsum('bhnik,bhnkv->bhniv', q * jnp.exp(b), s_prev)
    return o.reshape(bsz, heads, seq, vd)


def bidir_gated_recurrence(q, v, k_fwd, lf_fwd, k_bwd, lf_bwd):
    rev = lambda t: jnp.flip(t, axis=2)
    fwd = chunked_gated_recurrence(q, k_fwd, v, lf_fwd)
    bwd = rev(chunked_gated_recurrence(rev(q), rev(k_bwd), rev(v), rev(lf_bwd)))
    return fwd + bwd


def ssd_chunked(x, a, bm, cm):
    bsz, seq, heads, hp = x.shape
    groups, ns = bm.shape[2], bm.shape[3]
    rep = heads // groups
    n = seq // SSD_CHUNK
    x = x.astype(jnp.float32).reshape(bsz, n, SSD_CHUNK, groups, rep, hp)
    a = a.astype(jnp.float32).reshape(bsz, n, SSD_CHUNK, groups, rep).transpose(0, 3, 4, 1, 2)
    bm = bm.astype(jnp.float32).reshape(bsz, n, SSD_CHUNK, groups, ns)
    cm = cm.astype(jnp.float32).reshape(bsz, n, SSD_CHUNK, groups, ns)
    a_cum = jnp.cumsum(a, axis=-1)
    mask = jnp.tril(jnp.ones((SSD_CHUNK, SSD_CHUNK), dtype=bool))
    seg = jnp.exp(jnp.where(mask, a_cum[..., :, None] - a_cum[..., None, :], -jnp.inf))
    cb = jnp.einsum('bnigs,bnjgs->bgnij', cm, bm)
    y = jnp.einsum('bgnij,bgrnij,bnjgrp->bnigrp', cb, seg, x)
    decay_states = jnp.exp(a_cum[..., -1:] - a_cum)
    states = jnp.einsum('bnjgs,bgrnj,bnjgrp->bngrps', bm, decay_states, x)
    chunk_decay = jnp.exp(a_cum[..., -1])

    def step(s, inp):
        d, st = inp
        return d[..., None, None] * s + st, s

    s0 = jnp.zeros((bsz, groups, rep, hp, ns), jnp.float32)
    _, s_prev = lax.scan(step, s0, (jnp.moveaxis(chunk_decay, 3, 0), jnp.moveaxis(states, 1, 0)))
    s_prev = jnp.moveaxis(s_prev, 0, 1)
    y = y + jnp.einsum('bnigs,bgrni,bngrps->bnigrp', cm, jnp.exp(a_cum), s_prev)
    return y.reshape(bsz, seq, heads, hp)


def even_mixer(h, w_in, rpb, gla_w_up, gla_b, gla_norm_g, w_out):
    bsz, seq, _ = h.shape
    aq, ak, av, ag, bq, bk, bv, bg, lr_f, lr_b = split_cols(h @ w_in, EVEN_SPLITS)

    def heads(t, nh):
        return t.reshape(bsz, seq, nh, -1)

    def bhld(t, nh):
        return heads(t, nh).transpose(0, 2, 1, 3)

    ya = neighbourhood_attention(heads(aq, NA_HEADS), heads(ak, NA_HEADS), heads(av, NA_HEADS), rpb)
    ya = ya.reshape(bsz, seq, WIDTH_A) * jax.nn.silu(ag)

    def log_gate(lr, d):
        z = (lr @ gla_w_up[d] + gla_b[d]).astype(jnp.float32)
        return bhld(jax.nn.log_sigmoid(z) / GLA_GATE_NORM, GLA_HEADS)

    qb = bhld(bq, GLA_HEADS) * (GLA_DK ** -0.5)
    kb = bhld(bk, GLA_HEADS)
    ob = bidir_gated_recurrence(qb, bhld(bv, GLA_HEADS), kb, log_gate(lr_f, 0), kb, log_gate(lr_b, 1))
    ob = rms_norm(ob, gla_norm_g).transpose(0, 2, 1, 3).reshape(bsz, seq, WIDTH_B)
    yb = ob.astype(h.dtype) * jax.nn.silu(bg)
    return jnp.concatenate([ya, yb], axis=-1) @ w_out


def odd_mixer(h, w_in, lb, hgrn_norm_g, conv_w, conv_b, dt_bias, a_log, d_skip, ssm_norm_g, w_out):
    bsz, seq, _ = h.shape
    cq, cf_f, cf_b, ci, cg, dz, dxbc, dt_f, dt_b = split_cols(h @ w_in, ODD_SPLITS)

    def bhld(t, nh):
        return t.reshape(bsz, seq, nh, -1).transpose(0, 2, 1, 3)

    log_lb = jnp.log(lb)
    log_ub = jnp.log1p(-lb)

    def forget(z):
        z = z.astype(jnp.float32)
        log_f = jnp.logaddexp(log_lb, log_ub + jax.nn.log_sigmoid(z))
        k = (1.0 - lb) * jax.nn.sigmoid(-z)
        return bhld(k, HGRN_HEADS), bhld(log_f, HGRN_HEADS)

    k_f, lf_f = forget(cf_f)
    k_b, lf_b = forget(cf_b)
    qc = bhld(cq, HGRN_HEADS) * (HGRN_DIM ** -0.5)
    oc = bidir_gated_recurrence(qc, bhld(ci, HGRN_HEADS), k_f, lf_f, k_b, lf_b)
    oc = rms_norm(oc, hgrn_norm_g).transpose(0, 2, 1, 3).reshape(bsz, seq, WIDTH_C)
    yc = oc.astype(h.dtype) * jax.nn.silu(cg)

    xbc = lax.conv_general_dilated(dxbc, conv_w[:, None, :], (1,), [CONV_PAD],
                                   dimension_numbers=('NWC', 'WIO', 'NWC'),
                                   feature_group_count=SSD_CONV_CH)
    xbc = jax.nn.silu(xbc + conv_b)
    xs, bm, cm = split_cols(xbc, (WIDTH_D, SSD_GROUPS * SSD_STATE, SSD_GROUPS * SSD_STATE))
    xs = xs.reshape(bsz, seq, SSD_HEADS, SSD_HEAD_DIM).astype(jnp.float32)
    bm = bm.reshape(bsz, seq, SSD_GROUPS, SSD_STATE)
    cm = cm.reshape(bsz, seq, SSD_GROUPS, SSD_STATE)

    def dir_inputs(dt_raw, d):
        dt = jax.nn.softplus(dt_raw.astype(jnp.float32) + dt_bias[d].astype(jnp.float32))
        return xs * dt[..., None], dt * (-jnp.exp(a_log[d].astype(jnp.float32)))

    x_fw, a_fw = dir_inputs(dt_f, 0)
    x_bw, a_bw = dir_inputs(dt_b, 1)
    rev = lambda t: jnp.flip(t, axis=1)
    y = (ssd_chunked(x_fw, a_fw, bm, cm)
         + rev(ssd_chunked(rev(x_bw), rev(a_bw), rev(bm), rev(cm)))
         + d_skip.astype(jnp.float32)[:, None] * xs)
    y = rms_norm(y.reshape(bsz, seq, WIDTH_D) * jax.nn.silu(dz.astype(jnp.float32)), ssm_norm_g)
    yd = y.astype(h.dtype)
    return jnp.concatenate([yc, yd], axis=-1) @ w_out


def setup_inputs(seed: int = 0) -> dict:
    key = jax.random.key(seed)
    ks = jax.random.split(key, 24)

    def nrm(k, shape, s):
        return jax.random.normal(k, shape, jnp.float32) * s

    dt0 = jnp.exp(jax.random.uniform(ks[19], (N_ODD, 2, SSD_HEADS), jnp.float32)
                  * (math.log(0.1) - math.log(0.001)) + math.log(0.001))
    return {
        'x': nrm(ks[0], (BATCH, SEQ, D_MODEL), 1.0),
        'c': nrm(ks[1], (BATCH, D_MODEL), 1.0),
        'ada_w': nrm(ks[2], (DEPTH, D_MODEL, 3 * D_MODEL), D_MODEL ** -0.5),
        'ada_b': nrm(ks[3], (DEPTH, 3 * D_MODEL), 0.01),
        'ln_g': 1.0 + nrm(ks[4], (DEPTH, D_MODEL), 0.01),
        'ln_b': nrm(ks[5], (DEPTH, D_MODEL), 0.01),
        'e_w_in': nrm(ks[6], (N_EVEN, D_MODEL, EVEN_IN), D_MODEL ** -0.5),
        'e_rpb': nrm(ks[7], (N_EVEN, NA_HEADS, 2 * NA_WIN_ROWS - 1, 2 * NA_WIN_COLS - 1), 0.05),
        'e_gla_w_up': nrm(ks[8], (N_EVEN, 2, GLA_RANK, GLA_HEADS * GLA_DK), GLA_RANK ** -0.5),
        'e_gla_b': nrm(ks[9], (N_EVEN, 2, GLA_HEADS * GLA_DK), 0.01),
        'e_gla_norm_g': 1.0 + nrm(ks[10], (N_EVEN, GLA_DV), 0.01),
        'e_w_out': nrm(ks[11], (N_EVEN, WIDTH_A + WIDTH_B, D_MODEL), (WIDTH_A + WIDTH_B) ** -0.5 * DEEPNORM_BETA),
        'o_w_in': nrm(ks[12], (N_ODD, D_MODEL, ODD_IN), D_MODEL ** -0.5),
        'hgrn_lb': nrm(ks[13], (DEPTH, WIDTH_C), 1.0),
        'o_hgrn_norm_g': 1.0 + nrm(ks[14], (N_ODD, HGRN_DIM), 0.01),
        'o_conv_w': nrm(ks[15], (N_ODD, SSD_CONV, SSD_CONV_CH), SSD_CONV ** -0.5),
        'o_conv_b': nrm(ks[16], (N_ODD, SSD_CONV_CH), 0.01),
        'o_dt_bias': dt0 + jnp.log(-jnp.expm1(-dt0)),
        'o_a_log': jnp.log(jax.random.uniform(ks[17], (N_ODD, 2, SSD_HEADS), jnp.float32, 1.0, 16.0)),
        'o_d_skip': 1.0 + nrm(ks[18], (N_ODD, SSD_HEADS), 0.01),
        'o_ssm_norm_g': 1.0 + nrm(ks[20], (N_ODD, WIDTH_D), 0.01),
        'o_w_out': nrm(ks[21], (N_ODD, WIDTH_C + WIDTH_D, D_MODEL), (WIDTH_C + WIDTH_D) ** -0.5 * DEEPNORM_BETA),
    }


def reference(x, c, ada_w, ada_b, ln_g, ln_b, e_w_in, e_rpb, e_gla_w_up, e_gla_b, e_gla_norm_g, e_w_out,
              o_w_in, hgrn_lb, o_hgrn_norm_g, o_conv_w, o_conv_b, o_dt_bias, o_a_log, o_d_skip,
              o_ssm_norm_g, o_w_out):
    lb_cum = jnp.cumsum(jax.nn.softmax(hgrn_lb.astype(jnp.float32), axis=0), axis=0)
    cond = jax.nn.silu(c)
    for l in range(DEPTH):
        mod = cond @ ada_w[l] + ada_b[l]
        shift, scale, gate = jnp.split(mod[:, None, :], 3, axis=-1)
        h = x * (1.0 + scale) + shift
        i = l // 2
        if l % 2 == 0:
            y = even_mixer(h, e_w_in[i], e_rpb[i], e_gla_w_up[i], e_gla_b[i], e_gla_norm_g[i], e_w_out[i])
        else:
            y = odd_mixer(h, o_w_in[i], lb_cum[l] - lb_cum[0], o_hgrn_norm_g[i], o_conv_w[i], o_conv_b[i],
                          o_dt_bias[i], o_a_log[i], o_d_skip[i], o_ssm_norm_g[i], o_w_out[i])
        x = layer_norm(DEEPNORM_ALPHA * x + gate * y, ln_g[l], ln_b[l])
    return x
```

```python
import numpy as np
from contextlib import ExitStack
import concourse.bass as bass
import concourse.mybir as mybir
from concourse.bass_utils import run_bass_kernel_spmd

F32 = mybir.dt.float32
BF16 = mybir.dt.bfloat16
AF = mybir.ActivationFunctionType
ALU = mybir.AluOpType

D = 1024
NSLOT = 10
ALPHA = 4.0 ** 0.25
NEG = -30000.0


class Sched:
    ENGS = ('pe', 'act', 'dve', 'pool', 'sp')

    def __init__(self, nc, es):
        self.nc = nc
        self.streams = {e: [] for e in self.ENGS}
        self.cnt = {e: 0 for e in self.ENGS}
        self.sem = {e: es.enter_context(nc.semaphore('s_' + e)) for e in self.ENGS}
        self.waited = {e: {} for e in self.ENGS}
        self.lastw = {}
        self.readers = {}
        self.dslots = {}
        self.dnext = {}
        for q in ('sp', 'pool', 'act'):
            self.dslots[q] = [[es.enter_context(nc.semaphore('d_%s%d' % (q, i))), 0] for i in range(NSLOT)]
            self.dnext[q] = 0

    def _semh(self, key):
        if isinstance(key, str):
            return self.sem[key]
        return self.dslots[key[1]][key[2]][0]

    def _need(self, eng, dep):
        key, val = dep
        if key == eng and eng == 'pe':
            return
        if self.waited[eng].get(key, 0) >= val:
            return
        self.waited[eng][key] = val
        self.streams[eng].append(('w', key, val))

    def _deps(self, eng, r, w):
        for t in r:
            d = self.lastw.get(t)
            if d:
                self._need(eng, d)
        for t in w:
            d = self.lastw.get(t)
            if d:
                self._need(eng, d)
            rd = self.readers.get(t)
            if rd:
                for k, v in rd.items():
                    self._need(eng, (k, v))

    def _commit(self, dep, r, w):
        for t in r:
            rd = self.readers.setdefault(t, {})
            if rd.get(dep[0], 0) < dep[1]:
                rd[dep[0]] = dep[1]
        for t in w:
            self.lastw[t] = dep
            self.readers[t] = {}

    def op(self, eng, fn, r=(), w=()):
        self._deps(eng, r, w)
        self.cnt[eng] += 1
        self.streams[eng].append(('o', fn))
        self._commit((eng, self.cnt[eng]), r, w)

    def dma(self, q, out, in_, r=(), w=()):
        self._deps(q, r, w)
        i = self.dnext[q]
        self.dnext[q] = (i + 1) % NSLOT
        slot = self.dslots[q][i]
        key = ('d', q, i)
        if slot[1] > 0:
            self._need(q, (key, slot[1]))
        slot[1] += 16
        self.streams[q].append(('d', out, in_, key))
        self._commit((key, slot[1]), r, w)

    def barrier(self):
        deps = [(e, self.cnt[e]) for e in self.ENGS if self.cnt[e] > 0]
        for q in self.dslots:
            for i, sl in enumerate(self.dslots[q]):
                if sl[1] > 0:
                    deps.append((('d', q, i), sl[1]))
        for e in self.ENGS:
            for d in deps:
                self._need(e, d)

    def emit(self, block):
        decos = {'pe': block.tensor, 'act': block.scalar, 'dve': block.vector, 'pool': block.gpsimd, 'sp': block.sync}
        for e in self.ENGS:
            stream = self.streams[e]

            def body(eng, stream=stream, e=e):
                for it in stream:
                    if it[0] == 'w':
                        eng.wait_ge(self._semh(it[1]), it[2])
                    elif it[0] == 'o':
                        it[1](eng).then_inc(self.sem[e], 1)
                    else:
                        eng.dma_start(out=it[1], in_=it[2]).then_inc(self._semh(it[3]), 16)
            decos[e](body)


class Arena:
    def __init__(self, ap, ncols):
        self.ap = ap
        self.n = ncols
        self.pos = 0

    def reset(self, base=0):
        self.pos = base

    def f32(self, cols, shape=None):
        a = self.pos
        self.pos += cols
        assert self.pos <= self.n, ("arena overflow", self.pos, self.n)
        v = self.ap[:, a:a + cols]
        return v

    def bf16(self, cols):
        c32 = (cols + 1) // 2
        v = self.f32(c32).bitcast(BF16)
        return v[:, 0:cols]


def r3(ap, **kw):
    k = list(kw.keys())[0]
    return ap.rearrange("p (a %s) -> p a %s" % (k, k), **kw)


def MM(out, lhsT, rhs, start=True, stop=True, skip=False):
    if skip:
        return lambda e: e.matmul(out, lhsT=lhsT, rhs=rhs, start=start, stop=stop, skip_group_check=True)
    return lambda e: e.matmul(out, lhsT=lhsT, rhs=rhs, start=start, stop=stop)


def TR(out, in_, ident):
    return lambda e: e.transpose(out, in_, ident)


def ACT(out, in_, func, bias=None, scale=None, accum=None):
    kw = {}
    if bias is not None:
        kw['bias'] = bias
    if scale is not None:
        kw['scale'] = scale
    if accum is not None:
        kw['accum_out'] = accum
    return lambda e: e.activation(out=out, in_=in_, func=func, **kw)


def TT(out, in0, in1, op):
    return lambda e: e.tensor_tensor(out=out, in0=in0, in1=in1, op=op)


def TS(out, in0, s1, op0, s2=None, op1=None):
    if op1 is None:
        return lambda e: e.tensor_scalar(out=out, in0=in0, scalar1=s1, scalar2=None, op0=op0)
    return lambda e: e.tensor_scalar(out=out, in0=in0, scalar1=s1, scalar2=s2, op0=op0, op1=op1)


def STT(out, in0, scalar, in1, op0, op1):
    return lambda e: e.scalar_tensor_tensor(out=out, in0=in0, scalar=scalar, in1=in1, op0=op0, op1=op1)


def CP(out, in_):
    return lambda e: e.tensor_copy(out=out, in_=in_)


def MS(ap, c):
    return lambda e: e.memset(ap, c)


def build(L, nlayers=2, dbg=(), stop=99):
    nc = bass.Bass("TRN2", target_bir_lowering=False)
    NT = L // 128
    NS = L // 512
    ROWS = L // 64
    es = ExitStack()

    def din(name, shape, dt=F32):
        return nc.dram_tensor(name, list(shape), dt, kind="ExternalInput").ap()

    def dscr(name, shape, dt):
        kind = "ExternalOutput" if name in dbg else "Internal"
        return nc.dram_tensor(name, list(shape), dt, kind=kind).ap()

    x_in = din("x", [L, D])
    cT_in = din("cT", [128, 8])
    ada_w = din("ada_w", [2, D, 3 * D])
    ada_bT = din("ada_bT", [128, 2, 24])
    ada_bg = din("ada_bg", [2, D])
    ln_g = din("ln_g", [2, D])
    ln_b = din("ln_b", [2, D])
    e_w_in = din("e_w_in", [D, 3616])
    e_w_out = din("e_w_out", [D, D])
    btab = din("btab", [5, 128, 8 * 640])
    gla_w_up = din("gla_w_up", [2, 16, 256])
    gla_bT = din("gla_bT", [128, 2, 2])
    gla_ng = din("gla_ng", [1, 128])
    o_w_in = din("o_w_in", [D, 4112])
    o_w_out = din("o_w_out", [D, D])
    hgrn_lbT = din("hgrn_lbT", [128, 2, 4])
    hgrn_ng = din("hgrn_ng", [1, 128])
    conv_wT = din("conv_wT", [128, 8, 4])
    conv_bT = din("conv_bT", [128, 8])
    dt_bias = din("dt_bias", [1, 16])
    a_log = din("a_log", [1, 16])
    d_skip = din("d_skip", [1, 8])
    ssm_ng = din("ssm_ng", [1, 512])
    out = nc.dram_tensor("out", [L, D], F32, kind="ExternalOutput").ap()

    QTa = dscr("QTa", [512, L], BF16)
    KTa = dscr("KTa", [512, L], BF16)
    Va = dscr("Va", [L, 512], BF16)
    Va66 = dscr("Va66", [L, 528], BF16)
    Ga = dscr("Ga", [L, 512], BF16)
    qTb = dscr("qTb", [256, L], BF16)
    kTb = dscr("kTb", [256, L], BF16)
    Vb = dscr("Vb", [L, 512], BF16)
    Gb = dscr("Gb", [L, 512], BF16)
    lfT = dscr("lfT", [2, 256, L], F32)
    Of = dscr("Of", [L, 512], BF16)
    Ob = dscr("Ob", [L, 512], BF16)
    Y = dscr("Y", [L, D], BF16)
    X1 = dscr("X1", [L, D], F32)

    def sb(name, shape, dt):
        return es.enter_context(nc.sbuf_tensor(name, list(shape), dt))

    ident_f = sb("ident_f", [128, 128], F32)
    ident_b = sb("ident_b", [128, 128], BF16)
    ones_f = sb("ones_f", [128, 128], F32)
    mask_f128 = sb("mask_f128", [128, 128], BF16)
    mask_b128 = sb("mask_b128", [128, 128], BF16)
    mask_f32 = sb("mask_f32", [128, 128], BF16)
    mask_b32 = sb("mask_b32", [128, 128], BF16)
    seg128 = sb("seg128", [128, 512], F32)
    seg32 = sb("seg32", [128, 512], F32)
    modT = sb("modT", [128, 2, 16], F32)
    gate_bc = sb("gate_bc", [128, 2, D], F32)
    small = sb("small", [128, 64], F32)
    AW = 42000
    arena_t = sb("arena", [128, AW], F32)
    ar = Arena(arena_t, AW)
    banks = [es.enter_context(nc.psum_tensor("bank%d" % i, [128, 512], F32)) for i in range(8)]

    s = Sched(nc, es)
    uid = [0]

    def tok(prefix):
        uid[0] += 1
        return "%s#%d" % (prefix, uid[0])

    class Ring:
        def __init__(self, name, aps):
            self.aps = aps
            self.toks = [tok(name) for _ in aps]
            self.i = -1

        def next(self):
            self.i = (self.i + 1) % len(self.aps)
            return self.aps[self.i], self.toks[self.i]

    s.op('pool', MS(ident_f[:], 0.0), w=['ident_f'])
    s.op('pool', lambda e: e.affine_select(out=ident_f[:], in_=ident_f[:], pattern=[[-1, 128]], compare_op=ALU.not_equal,
                                           fill=1.0, base=0, channel_multiplier=1), r=['ident_f'], w=['ident_f'])
    s.op('pool', CP(ident_b[:], ident_f[:]), r=['ident_f'], w=['ident_b'])
    s.op('pool', MS(ones_f[:], 1.0), w=['ones_f'])
    s.op('pool', MS(mask_f128[:], 1.0), w=['mask_f128'])
    s.op('pool', lambda e: e.affine_select(out=mask_f128[:], in_=mask_f128[:], pattern=[[1, 128]], compare_op=ALU.is_ge,
                                           fill=0.0, base=0, channel_multiplier=-1), r=['mask_f128'], w=['mask_f128'])
    s.op('pool', MS(mask_b128[:], 1.0), w=['mask_b128'])
    s.op('pool', lambda e: e.affine_select(out=mask_b128[:], in_=mask_b128[:], pattern=[[-1, 128]], compare_op=ALU.is_ge,
                                           fill=0.0, base=0, channel_multiplier=1), r=['mask_b128'], w=['mask_b128'])
    s.op('pool', CP(mask_f32[:], mask_f128[:]), r=['mask_f128'], w=['mask_f32'])
    s.op('pool', CP(mask_b32[:], mask_b128[:]), r=['mask_b128'], w=['mask_b32'])
    for cb in range(4):
        s.op('pool', (lambda cb: lambda e: e.affine_select(out=mask_f32[:, 32 * cb:32 * cb + 32], in_=mask_f32[:, 32 * cb:32 * cb + 32],
                                                           pattern=[[0, 32]], compare_op=ALU.is_ge, fill=0.0, base=-32 * cb,
                                                           channel_multiplier=1))(cb), r=['mask_f32'], w=['mask_f32'])
        s.op('pool', (lambda cb: lambda e: e.affine_select(out=mask_b32[:, 32 * cb:32 * cb + 32], in_=mask_b32[:, 32 * cb:32 * cb + 32],
                                                           pattern=[[0, 32]], compare_op=ALU.is_ge, fill=0.0, base=32 * cb + 31,
                                                           channel_multiplier=-1))(cb), r=['mask_b32'], w=['mask_b32'])
    s.op('pool', MS(seg128[:], 1.0), w=['seg128'])
    s.op('pool', MS(seg32[:], 1.0), w=['seg32'])
    s.op('pool', MS(r3(seg128[:], b=128)[:, :, 0:1], 0.0), r=['seg128'], w=['seg128'])
    s.op('pool', MS(r3(seg32[:], b=32)[:, :, 0:1], 0.0), r=['seg32'], w=['seg32'])

    WIN = {}

    W_COLS = 8 * 4112 // 2

    def seg_order(c0s):
        order = []
        for c0 in c0s:
            if c0 // 512 not in order:
                order.append(c0 // 512)
        return order

    def prefetch_w_in(src, ncols, order, defer=False):
        w_in = r3(arena_t[:, 0:W_COLS].bitcast(BF16), b=4112)
        WIN['w'] = w_in
        v = src.rearrange("(k p) c -> p k c", p=128)
        thunks = []
        for pc in order:
            c0, c1 = pc * 512, min(ncols, (pc + 1) * 512)
            for k0 in (0, 4):
                def issue(dep=None, pc=pc, k0=k0, c0=c0, c1=c1):
                    s.dma('pool', w_in[:, k0:k0 + 4, c0:c1], v[:, k0:k0 + 4, c0:c1], r=([dep] if dep else []), w=['w_in%d' % pc])
                if defer:
                    thunks.append(issue)
                else:
                    issue()
        return thunks

    L0_C0S = [0, 128, 256, 384, 512, 640, 768, 896, 2048, 2176, 2304, 2432, 1024, 2560, 1536, 3072, 3584, 3600]
    L1_C0S = [0, 512, 1024, 1536, 2048, 2560, 3072, 3584, 4096]

    ar.reset(W_COLS)
    prefetch_w_in(e_w_in, 3616, seg_order(L0_C0S))
    cT = ar.f32(8)
    scT = ar.f32(16)
    sc_rep = ar.f32(8 * 128)
    abT = ar.f32(48)
    abg = ar.f32(2 * D)
    slabG = [ar.f32(8 * 512) for _ in range(3)]
    s.dma('sp', cT, cT_in[:, :], w=['cT'])
    s.dma('sp', abT, ada_bT.rearrange("p l c -> p (l c)"), w=['abT'])
    s.dma('sp', abg, ada_bg.rearrange("l d -> (l d)").rearrange("(o n) -> o n", o=1).partition_broadcast(128), w=['abg'])
    sc3 = r3(scT, b=2)
    s.op('act', ACT(sc3[:, :, 0], cT, AF.Silu), r=['cT'], w=['scT'])
    s.op('act', ACT(sc3[:, :, 1], cT, AF.Silu), r=['cT'], w=['scT'])
    scr3 = r3(sc_rep, b=128)
    for k in range(8):
        s.op('dve', TS(scr3[:, k, :], ones_f[:], sc3[:, k, 0:1], ALU.mult), r=['scT', 'ones_f'], w=['sc_rep'])
    rG = Ring('slabG', slabG)
    pmod = Ring('pmod', [banks[0], banks[1], banks[2], banks[3]])
    for l in range(nlayers):
        wv = ada_w[l].rearrange("(k p) c -> p k c", p=128)
        for sl_i in range(6):
            sl, st = rG.next()
            sl3 = r3(sl, b=512)
            s.dma('sp', sl3, wv[:, :, sl_i * 512:(sl_i + 1) * 512], w=[st])
            if sl_i < 4:
                for c4 in range(4):
                    cb = sl_i * 4 + c4
                    ps, pt = pmod.next()
                    for k in range(8):
                        s.op('pe', MM(ps[:, 0:2], sl3[:, k, c4 * 128:(c4 + 1) * 128], sc3[:, k, :], k == 0, k == 7), r=[st, 'scT'], w=[pt])
                    if cb < 8:
                        s.op('dve', TT(modT[:, l, cb:cb + 1], ps[:, 0:1], abT[:, l * 24 + cb:l * 24 + cb + 1], ALU.add),
                             r=[pt, 'abT'], w=['modT'])
                    else:
                        s.op('dve', STT(modT[:, l, cb:cb + 1], ps[:, 0:1], 1.0, abT[:, l * 24 + cb:l * 24 + cb + 1], ALU.add, ALU.add),
                             r=[pt, 'abT'], w=['modT'])
            else:
                hf = sl_i - 4
                ps, pt = pmod.next()
                for k in range(8):
                    s.op('pe', MM(ps[:, :], scr3[:, k, :], sl3[:, k, :], k == 0, k == 7), r=[st, 'sc_rep'], w=[pt])
                s.op('dve', TT(gate_bc[:, l, hf * 512:(hf + 1) * 512], ps[:, :], abg[:, l * D + hf * 512:l * D + (hf + 1) * 512], ALU.add),
                     r=[pt, 'abg'], w=['gate_bc'])

    def phase_a(l, x_src, segs):
        xts = [ar.f32(4 * D) for _ in range(2)]
        hTs = [ar.bf16(8 * 512) for _ in range(2)]
        rx = Ring('xt', xts)
        rh = Ring('hT', hTs)
        ptr = Ring('ptr', [banks[0], banks[1]])
        ppj = Ring('ppj', [banks[2], banks[3], banks[4], banks[5]])
        xv = x_src.rearrange("(n s p) d -> n p s d", p=128, s=4)
        state = {}

        xloaded = {}

        def load_x(i):
            xt, xtok = rx.next()
            xt3 = r3(xt, b=D)
            s.dma('sp', xt3, xv[i], w=[xtok])
            xloaded[i] = (xt3, xtok)

        def load_and_transpose(i):
            if i not in xloaded:
                load_x(i)
            xt3, xtok = xloaded.pop(i)
            hT, htok = rh.next()
            hT3 = r3(hT, b=512)
            for k in range(8):
                ps, pt = ptr.next()
                for sub in range(4):
                    s.op('pe', TR(ps[:, sub * 128:(sub + 1) * 128], xt3[:, sub, k * 128:(k + 1) * 128], ident_f[:]),
                         r=[xtok, 'ident_f'], w=[pt])
                s.op('act', ACT(hT3[:, k, :], ps[:, :], AF.Identity, bias=modT[:, l, k:k + 1], scale=modT[:, l, 8 + k:9 + k]),
                     r=[pt, 'modT'], w=[htok])
            state[i] = (hT3, htok)

        load_and_transpose(0)
        for i in range(NS):
            if i + 1 < NS:
                load_x(i + 1)
            hT3, htok = state.pop(i)
            for si, sg in enumerate(segs):
                if si == len(segs) // 2 and i + 1 < NS:
                    load_and_transpose(i + 1)
                if sg['kind'] == 'fm':
                    ps, pt = ppj.next()
                    n = sg['n']
                    for k in range(8):
                        s.op('pe', MM(ps[0:n, :], WIN['w'][:, k, sg['c0']:sg['c0'] + n], hT3[:, k, :], k == 0, k == 7),
                             r=['w_in%d' % (sg['c0'] // 512), htok], w=[pt])
                    sg['emit'](ps, pt, i, None)
                else:
                    n = sg['n']
                    for sub in range(4):
                        ps, pt = ppj.next()
                        for k in range(8):
                            s.op('pe', MM(ps[:, 0:n], hT3[:, k, sub * 128:(sub + 1) * 128], WIN['w'][:, k, sg['c0']:sg['c0'] + n], k == 0, k == 7),
                                 r=['w_in%d' % (sg['c0'] // 512), htok], w=[pt])
                        sg['emit'](ps, pt, i, sub)

    def phase_c(l, x_src, w_out_src, x_dst):
        s.barrier()
        if l == 0 and nlayers > 1:
            ar.reset(W_COLS)
        else:
            ar.reset()
        wo = r3(ar.bf16(8 * D), b=D)
        wov = w_out_src.rearrange("(k p) c -> p k c", p=128)
        wst = Ring('wst', [ar.f32(D) for _ in range(2)])
        for k in range(8):
            st, sttok = wst.next()
            s.dma('sp', st, wov[:, k, :], w=[sttok])
            s.op('dve', TT(wo[:, k, :], st, gate_bc[:, l, :], ALU.mult), r=[sttok, 'gate_bc'], w=['wo'])
        lng = ar.f32(D)
        lnb = ar.f32(D)
        epsc = ar.f32(2)
        s.op('pool', MS(epsc, 1e-5), w=['epsc'])
        s.dma('sp', lng, ln_g[l:l + 1, :].partition_broadcast(128), w=['lng'])
        s.dma('sp', lnb, ln_b[l:l + 1, :].partition_broadcast(128), w=['lnb'])
        ry = Ring('cy', [ar.bf16(D) for _ in range(4)])
        rxt = Ring('cx', [ar.f32(D) for _ in range(4)])
        ryt = Ring('cyT', [r3(ar.bf16(8 * 128), b=128) for _ in range(2)])
        rz = Ring('cz', [ar.f32(D) for _ in range(3)])
        ro = Ring('co', [ar.f32(D) for _ in range(3)])
        rst = Ring('cst', [ar.f32(24) for _ in range(3)])
        ptr = Ring('cptr', [banks[0], banks[1]])
        pmm = Ring('cpmm', [banks[2], banks[3], banks[4], banks[5]])
        loaded = {}

        def load(t):
            yt, ytok = ry.next()
            s.dma('sp', yt, Y[t * 128:(t + 1) * 128, :], w=[ytok])
            xt, xtok = rxt.next()
            s.dma('sp', xt, x_src[t * 128:(t + 1) * 128, :], w=[xtok])
            loaded[t] = (yt, ytok, xt, xtok)

        load(0)
        if NT > 1:
            load(1)
        stA = {}

        def stage_a(t):
            if t + 2 < NT:
                load(t + 2)
            yt, ytok, xt, xtok = loaded.pop(t)
            yT, yTtok = ryt.next()
            for half in range(2):
                ps, pt = ptr.next()
                psb = ps[:, :].bitcast(BF16)
                for kk in range(4):
                    k = half * 4 + kk
                    s.op('pe', TR(psb[:, kk * 128:(kk + 1) * 128], yt[:, k * 128:(k + 1) * 128], ident_b[:]), r=[ytok, 'ident_b'], w=[pt])
                s.op('act', ACT(yT[:, half * 4:half * 4 + 4, :], r3(psb[:, 0:512], b=128), AF.Copy), r=[pt], w=[yTtok])
            z, ztok = rz.next()
            for hf in range(2):
                ps, pt = pmm.next()
                for k in range(8):
                    s.op('pe', MM(ps[:, :], yT[:, k, :], wo[:, k, hf * 512:(hf + 1) * 512], k == 0, k == 7), r=[yTtok, 'wo'], w=[pt])
                s.op('dve', STT(z[:, hf * 512:(hf + 1) * 512], xt[:, hf * 512:(hf + 1) * 512], ALPHA, ps[:, :], ALU.mult, ALU.add),
                     r=[pt, xtok], w=[ztok])
            st, sttok = rst.next()
            s.op('dve', lambda e, st=st, z=z: e.bn_stats(out=st[:, 0:6], in_=z[:, 0:512]), r=[ztok], w=[sttok])
            s.op('dve', lambda e, st=st, z=z: e.bn_stats(out=st[:, 6:12], in_=z[:, 512:1024]), r=[ztok], w=[sttok])
            s.op('dve', lambda e, st=st: e.bn_aggr(out=st[:, 12:14], in_=st[:, 0:12]), r=[sttok], w=[sttok])
            s.op('act', ACT(st[:, 14:15], st[:, 13:14], AF.Ln, bias=epsc[:, 0:1]), r=[sttok, 'epsc'], w=[sttok])
            s.op('act', ACT(st[:, 15:16], st[:, 14:15], AF.Exp, scale=-0.5), r=[sttok], w=[sttok])
            stA[t] = (z, ztok, st, sttok)

        def stage_b(t):
            z, ztok, st, sttok = stA.pop(t)
            o, otok = ro.next()
            s.op('dve', TS(o, z, st[:, 12:13], ALU.subtract, st[:, 15:16], ALU.mult), r=[ztok, sttok], w=[otok])
            s.op('dve', TT(o, o, lng, ALU.mult), r=[otok, 'lng'], w=[otok])
            s.op('dve', TT(o, o, lnb, ALU.add), r=[otok, 'lnb'], w=[otok])
            s.dma('sp', x_dst[t * 128:(t + 1) * 128, :], o, r=[otok], w=[tok('xdst')])

        stage_a(0)
        for t in range(NT):
            if t + 1 < NT:
                stage_a(t + 1)
            stage_b(t)

    def recurrence(nunits, nh, dk, CS, sc, qT_src, kT_src, lf_src, V_src, dvw):
        s.barrier()
        if nh == 2 and nlayers > 1:
            ar.reset(W_COLS)
            wq = prefetch_w_in(o_w_in, 4112, seg_order(L1_C0S), defer=True)
        else:
            ar.reset()
            wq = []
        nsub = 128 // CS
        segm = seg128 if CS == 128 else seg32
        segk = 'seg128' if CS == 128 else 'seg32'
        vw = nh * 128
        chains = [(u, d) for u in range(nunits) for d in range(2)]
        tq = Ring('tq', [ar.bf16(512) for _ in range(2)])
        tk = Ring('tk', [ar.bf16(512) for _ in range(2)])
        tlf = Ring('tlf', [ar.f32(512) for _ in range(2)])
        tP = Ring('tP', [ar.f32(512) for _ in range(2)])
        tB = Ring('tB', [ar.f32(512) for _ in range(2)])
        tR = Ring('tR', [ar.f32(512) for _ in range(2)])
        tE = Ring('tE', [ar.bf16(512) for _ in range(4)])
        C = {}
        for ch in chains:
            C[ch] = dict(
                qs=Ring('qs', [ar.bf16(512) for _ in range(2)]),
                kh=Ring('kh', [ar.bf16(512) for _ in range(2)]),
                kb=Ring('kb', [ar.bf16(512) for _ in range(2)]),
                V=Ring('V', [r3(ar.bf16(4 * vw), b=vw) for _ in range(2)]),
                dch=Ring('dch', [ar.f32(16) for _ in range(2)]),
                S=ar.f32(128), Stok=tok('S'),
                Sbf=Ring('Sbf', [ar.bf16(128) for _ in range(nsub + 2)]),
                ATs=Ring('ATs', [r3(ar.bf16(nh * 128), b=128) for _ in range(2)]),
                kbt=Ring('kbt', [ar.bf16(128) for _ in range(2)]),
                ost=Ring('ost', [ar.bf16(vw) for _ in range(2)]),
            )
        if nsub > 1:
            cmask = r3(ar.bf16(nsub * 128), b=128)
            rmask = r3(ar.bf16(nsub * 128), b=128)
            s.op('pool', MS(cmask, 0.0), w=['cmask'])
            s.op('pool', MS(rmask, 1.0), w=['rmask'])
            for ci in range(nsub):
                s.op('pool', MS(cmask[:, ci, ci * CS:(ci + 1) * CS], 1.0), r=['cmask'], w=['cmask'])
                s.op('pool', (lambda ci: lambda e: e.affine_select(out=rmask[:, ci, :], in_=rmask[:, ci, :], pattern=[[0, 128]],
                                                                   compare_op=ALU.is_ge, fill=0.0, base=-CS * ci, channel_multiplier=1))(ci),
                     r=['rmask'], w=['rmask'])
                s.op('pool', (lambda ci: lambda e: e.affine_select(out=rmask[:, ci, :], in_=rmask[:, ci, :], pattern=[[0, 128]],
                                                                   compare_op=ALU.is_ge, fill=0.0, base=CS * ci + CS - 1, channel_multiplier=-1))(ci),
                     r=['rmask'], w=['rmask'])
            for ch in chains:
                C[ch]['qsm'] = Ring('qsm', [ar.bf16(128) for _ in range(2)])
                C[ch]['kbtm'] = Ring('kbtm', [ar.bf16(128) for _ in range(2)])
        if nh == 2:
            pAT = Ring('pAT', [[banks[0][:, 0:128], banks[1][:, 0:128]]])
        else:
            pAT = Ring('pAT', [[banks[0][:, 0:128]], [banks[1][:, 0:128]]])
        pU = Ring('pU', [banks[4], banks[5]])
        pT = Ring('pT', [banks[6], banks[7]])
        cur = {}
        for ch in chains:
            c = C[ch]
            s.op('pool', MS(c['S'], 0.0), w=[c['Stok']])
            sb0, sbt0 = c['Sbf'].next()
            s.op('pool', MS(sb0, 0.0), w=[sbt0])
            c['sprev'] = (sb0, sbt0)

        def prep(ch, st_i):
            u, d = ch
            c = C[ch]
            t0 = st_i * 512
            q, qtok = tq.next()
            k, ktok = tk.next()
            lf, lftok = tlf.next()
            s.dma('sp', q, qT_src[u][:, t0:t0 + 512], w=[qtok])
            s.dma('sp', k, kT_src[d][u][:, t0:t0 + 512], w=[ktok])
            s.dma('sp', lf, lf_src[d][u][:, t0:t0 + 512], w=[lftok])
            V, Vtok = c['V'].next()
            s.dma('sp', V, V_src[t0:t0 + 512, u * vw:(u + 1) * vw].rearrange("(s p) c -> p s c", p=128), w=[Vtok])
            P, Ptok = tP.next()
            s.op('dve', lambda e, P=P, lf=lf: e.tensor_tensor_scan(out=P, data0=segm[:], data1=lf, initial=0.0, op0=ALU.mult, op1=ALU.add),
                 r=[lftok, segk], w=[Ptok])
            P3 = r3(P, b=CS)
            nchk = 512 // CS
            totb = P3[:, :, CS - 1:CS].broadcast_to([128, nchk, CS])
            B, Btok = tB.next()
            R, Rtok = tR.next()
            if d == 0:
                s.op('dve', TT(r3(R, b=CS), totb, P3, ALU.subtract), r=[Ptok], w=[Rtok])
                Bd, Bdtok = P, Ptok
            else:
                s.op('dve', TT(R, P, lf, ALU.subtract), r=[Ptok, lftok], w=[Rtok])
                s.op('dve', TT(r3(B, b=CS), totb, r3(R, b=CS), ALU.subtract), r=[Ptok, Rtok], w=[Btok])
                Bd, Bdtok = B, Btok
            dch, dchtok = c['dch'].next()
            s.op('act', ACT(dch[:, 0:nchk], P3[:, :, CS - 1], AF.Exp, scale=sc), r=[Ptok], w=[dchtok])
            qs, qstok = c['qs'].next()
            kh, khtok = c['kh'].next()
            kb, kbtok = c['kb'].next()
            e1, e1tok = tE.next()
            s.op('act', ACT(e1, Bd, AF.Exp, scale=sc), r=[Bdtok], w=[e1tok])
            s.op('dve', TT(qs, q, e1, ALU.mult), r=[qtok, e1tok], w=[qstok])
            e2, e2tok = tE.next()
            s.op('act', ACT(e2, Bd, AF.Exp, scale=-sc), r=[Bdtok], w=[e2tok])
            s.op('dve', TT(kh, k, e2, ALU.mult), r=[ktok, e2tok], w=[khtok])
            e3, e3tok = tE.next()
            s.op('act', ACT(e3, R, AF.Exp, scale=sc), r=[Rtok], w=[e3tok])
            s.op('dve', TT(kb, k, e3, ALU.mult), r=[ktok, e3tok], w=[kbtok])
            cur[ch] = dict(qs=qs, qstok=qstok, kh=kh, khtok=khtok, kb=kb, kbtok=kbtok, V=V, Vtok=Vtok, dch=dch, dchtok=dchtok)

        nchain = len(chains)
        cpb = 4
        OT = {}
        for idx, ch in enumerate(chains):
            if nh == 1:
                bk = 2 + idx // cpb
                OT[ch] = dict(tiles=[banks[bk][:, (idx % cpb) * 128:(idx % cpb + 1) * 128]], toks=['pO%d' % bk], first=(idx % cpb == 0))
            else:
                OT[ch] = dict(tiles=[banks[2 + hh][:, idx * 128:(idx + 1) * 128] for hh in range(nh)],
                              toks=['pO%d' % (2 + hh) for hh in range(nh)], first=(idx == 0))
        ctx = {}

        def stage1(ch, st_i, sub):
            u, d = ch
            c = C[ch]
            cc = cur[ch]
            cols = slice(sub * 128, (sub + 1) * 128)
            mask = (mask_f128 if d == 0 else mask_b128) if CS == 128 else (mask_f32 if d == 0 else mask_b32)
            mtok = ('mask_f128' if d == 0 else 'mask_b128') if CS == 128 else ('mask_f32' if d == 0 else 'mask_b32')
            pa, patok = pAT.next()
            for hh in range(nh):
                pr = slice(hh * dk, (hh + 1) * dk)
                s.op('pe', MM(pa[hh], cc['kh'][pr, cols], cc['qs'][pr, cols]), r=[cc['khtok'], cc['qstok']], w=[patok])
            ATs, ATtok = c['ATs'].next()
            for hh in range(nh):
                s.op('dve', TT(ATs[:, hh, :], pa[hh], mask[:], ALU.mult), r=[patok, mtok], w=[ATtok])
            pt_, pttok = pT.next()
            ptb = pt_[:, :].bitcast(BF16)
            s.op('pe', TR(ptb[:, 0:128], cc['kb'][:, cols], ident_b[:]), r=[cc['kbtok'], 'ident_b'], w=[pttok])
            kbt, kbttok = c['kbt'].next()
            s.op('act', ACT(kbt, ptb[:, 0:128], AF.Copy), r=[pttok], w=[kbttok])
            x = dict(cc=cc, sub=sub, st_i=st_i, kbt=kbt, kbttok=kbttok)
            if nsub > 1:
                kb3, kb3tok = c['kbtm'].next()
                s.op('dve', TT(kb3, kbt, rmask[:, nsub - 1, :], ALU.mult), r=[kbttok, 'rmask'], w=[kb3tok])
                qs3, qs3tok = c['qsm'].next()
                s.op('dve', TT(qs3, cc['qs'][:, cols], cmask[:, nsub - 1, :], ALU.mult), r=[cc['qstok'], 'cmask'], w=[qs3tok])
                x.update(kb3=kb3, kb3tok=kb3tok, qs3=qs3, qs3tok=qs3tok)
            ot = OT[ch]
            for hh in range(nh):
                s.op('pe', MM(ot['tiles'][hh], ATs[:, hh, :], cc['V'][:, sub, hh * 128:(hh + 1) * 128], ot['first'], False, True),
                     r=[ATtok, cc['Vtok']], w=[ot['toks'][hh]])
            x['order'] = list(range(nsub)) if d == 0 else list(range(nsub - 1, -1, -1))
            ctx[ch] = x

        def stage2(ch, n_i):
            u, d = ch
            c = C[ch]
            x = ctx[ch]
            cc = x['cc']
            sub = x['sub']
            cidx = x['order'][n_i]
            last = (n_i == nsub - 1)
            rows = slice(cidx * CS, (cidx + 1) * CS)
            ccols = slice(sub * 128 + cidx * CS, sub * 128 + (cidx + 1) * CS)
            sprev, sprevtok = c['sprev']
            ot = OT[ch]
            masked = (nsub > 1 and cidx == nsub - 1)
            for hh in range(nh):
                pr = slice(hh * dk, (hh + 1) * dk)
                if masked:
                    s.op('pe', MM(ot['tiles'][hh], x['qs3'][pr, :], sprev[pr, :], False, last, True), r=[x['qs3tok'], sprevtok], w=[ot['toks'][hh]])
                else:
                    s.op('pe', MM(ot['tiles'][hh][rows, :], cc['qs'][pr, ccols], sprev[pr, :], False, last, True),
                         r=[cc['qstok'], sprevtok], w=[ot['toks'][hh]])
            pu, putok = pU.next()
            for hh in range(nh):
                pr = slice(hh * dk, (hh + 1) * dk)
                if masked:
                    s.op('pe', MM(pu[pr, 0:128], x['kb3'][:, pr], cc['V'][:, sub, hh * 128:(hh + 1) * 128]), r=[x['kb3tok'], cc['Vtok']], w=[putok])
                else:
                    s.op('pe', MM(pu[pr, 0:128], x['kbt'][rows, pr], cc['V'][rows, sub, hh * 128:(hh + 1) * 128]),
                         r=[x['kbttok'], cc['Vtok']], w=[putok])
            chk = sub * nsub + cidx
            s.op('dve', STT(c['S'], c['S'], cc['dch'][:, chk:chk + 1], pu[:, 0:128], ALU.mult, ALU.add),
                 r=[c['Stok'], cc['dchtok'], putok], w=[c['Stok']])
            sbn, sbntok = c['Sbf'].next()
            s.op('act', ACT(sbn, c['S'], AF.Copy), r=[c['Stok']], w=[sbntok])
            c['sprev'] = (sbn, sbntok)

        def stage3(ch):
            u, d = ch
            c = C[ch]
            x = ctx.pop(ch)
            ot = OT[ch]
            tglob = x['st_i'] * 4 + x['sub']
            ost, osttok = c['ost'].next()
            for hh in range(nh):
                s.op('act', ACT(ost[:, hh * 128:(hh + 1) * 128], ot['tiles'][hh], AF.Copy), r=[ot['toks'][hh]], w=[osttok])
            dst = Of if d == 0 else Ob
            s.dma('sp', dst[tglob * 128:(tglob + 1) * 128, u * vw:(u + 1) * vw], ost, r=[osttok], w=[tok('odst')])

        nxt = {}
        for ch in chains:
            prep(ch, 0 if ch[1] == 0 else NS - 1)
        for i in range(NS):
            now = {ch: cur[ch] for ch in chains}
            for sub_i in range(4):
                for ch in chains:
                    st_i = i if ch[1] == 0 else NS - 1 - i
                    sub = sub_i if ch[1] == 0 else 3 - sub_i
                    cur[ch] = now[ch]
                    stage1(ch, st_i, sub)
                for n_i in range(nsub):
                    for ch in chains:
                        stage2(ch, n_i)
                for ch in chains:
                    stage3(ch)
                if wq:
                    rr = C[chains[-1]]['ost']
                    wq.pop(0)(rr.toks[rr.i])
                if sub_i == 1 and i + 1 < NS:
                    for ch in chains:
                        prep(ch, (i + 1) if ch[1] == 0 else NS - 2 - i)
                        nxt[ch] = cur[ch]
            if i + 1 < NS:
                for ch in chains:
                    now[ch] = None
                    cur[ch] = nxt[ch]
        while wq:
            wq.pop(0)()

    def recurrence_b(qsrc, ksrc, lfsrc, V_src):
        s.barrier()
        ar.reset()
        CS, nsub, NU = 32, 4, 4
        segm = ar.f32(1024)
        s.op('pool', MS(segm, 1.0), w=['segm'])
        s.op('pool', MS(r3(segm, b=CS)[:, :, 0:1], 0.0), r=['segm'], w=['segm'])
        c3 = ar.bf16(128)
        r3m = ar.bf16(128)
        s.op('pool', MS(c3, 0.0), w=['c3'])
        s.op('pool', MS(c3[:, 96:128], 1.0), r=['c3'], w=['c3'])
        s.op('pool', MS(r3m, 1.0), w=['r3m'])
        s.op('pool', lambda e: e.affine_select(out=r3m, in_=r3m, pattern=[[0, 128]], compare_op=ALU.is_ge, fill=0.0, base=-96,
                                               channel_multiplier=1), r=['r3m'], w=['r3m'])
        qv = qsrc.rearrange("(u p) t -> p u t", p=128)
        kv_ = [ksrc[d].rearrange("(u p) t -> p u t", p=128) for d in range(2)]
        lv_ = [lfsrc[d].rearrange("(u p) t -> p u t", p=128) for d in range(2)]
        tq = Ring('bq', [r3(ar.bf16(1024), b=512) for _ in range(2)])
        tk = Ring('bk', [r3(ar.bf16(1024), b=512) for _ in range(2)])
        tlf = Ring('blf', [ar.f32(1024) for _ in range(4)])
        tP = Ring('bP', [ar.f32(1024) for _ in range(2)])
        tB = Ring('bB', [ar.f32(1024) for _ in range(1)])
        tR = Ring('bR', [ar.f32(1024) for _ in range(1)])
        tE = Ring('bE', [ar.bf16(1024) for _ in range(3)])
        G = {}
        for d in range(2):
            G[d] = dict(
                qs=Ring('bqs', [r3(ar.bf16(NU * 512), b=512) for _ in range(2)]),
                kh=Ring('bkh', [r3(ar.bf16(NU * 512), b=512) for _ in range(2)]),
                kb=Ring('bkb', [r3(ar.bf16(NU * 512), b=512) for _ in range(2)]),
                V=Ring('bV', [r3(ar.bf16(4 * 512), b=512) for _ in range(2)]),
                dch=Ring('bdch', [r3(ar.f32(NU * 16), b=16) for _ in range(2)]),
                S=r3(ar.f32(NU * 128), b=128), Stok=tok('bS'),
                Sbf=Ring('bSbf', [r3(ar.bf16(NU * 128), b=128) for _ in range(nsub + 2)]),
                ATs=Ring('bATs', [r3(ar.bf16(NU * 128), b=128) for _ in range(2)]),
                kbt=Ring('bkbt', [r3(ar.bf16(NU * 128), b=128) for _ in range(2)]),
                kb3=Ring('bkb3', [r3(ar.bf16(NU * 128), b=128) for _ in range(2)]),
                qs3=Ring('bqs3', [r3(ar.bf16(NU * 128), b=128) for _ in range(2)]),
                ost=Ring('bost', [ar.bf16(NU * 128) for _ in range(2)]),
                pa=(banks[0 + d], 'bpa%d' % d), po=(banks[2 + d], 'bpo%d' % d), pu=(banks[4 + d], 'bpu%d' % d), pt=(banks[6 + d], 'bpt%d' % d),
            )
            g = G[d]
            s.op('pool', MS(g['S'], 0.0), w=[g['Stok']])
            sb0, sbt0 = g['Sbf'].next()
            s.op('pool', MS(sb0, 0.0), w=[sbt0])
            g['sprev'] = (sb0, sbt0)
        cur = {}

        pre = {}

        def prep_load(d, st_i):
            g = G[d]
            t0 = st_i * 512
            V, Vtok = g['V'].next()
            s.dma('sp', V, V_src[t0:t0 + 512, :].rearrange("(s p) c -> p s c", p=128), w=[Vtok])
            lfs = []
            for pr_ in range(2):
                us = slice(2 * pr_, 2 * pr_ + 2)
                lf, lftok = tlf.next()
                s.dma('sp', r3(lf, b=512), lv_[d][:, us, t0:t0 + 512], w=[lftok])
                lfs.append((lf, lftok))
            pre[(d, st_i)] = (V, Vtok, lfs)

        def prep(d, st_i):
            g = G[d]
            t0 = st_i * 512
            if (d, st_i) not in pre:
                prep_load(d, st_i)
            V, Vtok, lfs = pre.pop((d, st_i))
            qs, qstok = g['qs'].next()
            kh, khtok = g['kh'].next()
            kb, kbtok = g['kb'].next()
            dch, dchtok = g['dch'].next()
            for pr_ in range(2):
                us = slice(2 * pr_, 2 * pr_ + 2)
                q, qtok = tq.next()
                k, ktok = tk.next()
                lf, lftok = lfs[pr_]
                s.dma('sp', q, qv[:, us, t0:t0 + 512], w=[qtok])
                s.dma('sp', k, kv_[d][:, us, t0:t0 + 512], w=[ktok])
                P, Ptok = tP.next()
                s.op('dve', lambda e, P=P, lf=lf: e.tensor_tensor_scan(out=P, data0=segm, data1=lf, initial=0.0, op0=ALU.mult, op1=ALU.add),
                     r=[lftok, 'segm'], w=[Ptok])
                P3 = r3(P, b=CS)
                nchk = 1024 // CS
                totb = P3[:, :, CS - 1:CS].broadcast_to([128, nchk, CS])
                if d == 0:
                    Bd, Bdtok = P, Ptok
                else:
                    R, Rtok = tR.next()
                    B, Btok = tB.next()
                    s.op('dve', TT(R, P, lf, ALU.subtract), r=[Ptok, lftok], w=[Rtok])
                    s.op('dve', TT(r3(B, b=CS), totb, r3(R, b=CS), ALU.subtract), r=[Ptok, Rtok], w=[Btok])
                    Bd, Bdtok = B, Btok
                dchv = dch[:, us, :].rearrange("p u c -> p (u c)")
                s.op('act', ACT(dchv, P3[:, :, CS - 1], AF.Exp), r=[Ptok], w=[dchtok])
                q2 = q.rearrange("p u t -> p (u t)")
                k2 = k.rearrange("p u t -> p (u t)")
                e1, e1tok = tE.next()
                s.op('act', ACT(e1, Bd, AF.Exp), r=[Bdtok], w=[e1tok])
                s.op('dve', TT(qs[:, us, :].rearrange("p u t -> p (u t)"), q2, e1, ALU.mult), r=[qtok, e1tok], w=[qstok])
                e2, e2tok = tE.next()
                s.op('act', ACT(e2, Bd, AF.Exp, scale=-1.0), r=[Bdtok], w=[e2tok])
                s.op('dve', TT(kh[:, us, :].rearrange("p u t -> p (u t)"), k2, e2, ALU.mult), r=[ktok, e2tok], w=[khtok])
                s.op('dve', TT(r3(kb[:, us, :].rearrange("p u t -> p (u t)"), b=CS), r3(kh[:, us, :].rearrange("p u t -> p (u t)"), b=CS),
                               dchv.unsqueeze(2).broadcast_to([128, nchk, CS]), ALU.mult), r=[khtok, dchtok], w=[kbtok])
            cur[d] = dict(qs=qs, qstok=qstok, kh=kh, khtok=khtok, kb=kb, kbtok=kbtok, V=V, Vtok=Vtok, dch=dch, dchtok=dchtok)

        ctx = {}

        def stage1(d, st_i, sub):
            g = G[d]
            cc = cur[d]
            cols = slice(sub * 128, (sub + 1) * 128)
            mask = mask_f32 if d == 0 else mask_b32
            mtok = 'mask_f32' if d == 0 else 'mask_b32'
            pa, patok = g['pa']
            for u in range(NU):
                s.op('pe', MM(pa[:, u * 128:(u + 1) * 128], cc['kh'][:, u, cols], cc['qs'][:, u, cols]), r=[cc['khtok'], cc['qstok']], w=[patok])
            ATs, ATtok = g['ATs'].next()
            s.op('dve', TT(ATs, r3(pa[:, :], b=128), mask[:].unsqueeze(1).broadcast_to([128, NU, 128]), ALU.mult), r=[patok, mtok], w=[ATtok])
            pt_, pttok = g['pt']
            ptb = pt_[:, :].bitcast(BF16)
            for u in range(NU):
                s.op('pe', TR(ptb[:, u * 128:(u + 1) * 128], cc['kb'][:, u, cols], ident_b[:]), r=[cc['kbtok'], 'ident_b'], w=[pttok])
            kbt, kbttok = g['kbt'].next()
            s.op('act', ACT(kbt, r3(ptb[:, 0:512], b=128), AF.Copy), r=[pttok], w=[kbttok])
            kb3, kb3tok = g['kb3'].next()
            s.op('dve', TT(kb3, kbt, r3m.unsqueeze(1).broadcast_to([128, NU, 128]), ALU.mult), r=[kbttok, 'r3m'], w=[kb3tok])
            qs3, qs3tok = g['qs3'].next()
            s.op('dve', TT(qs3, cc['qs'][:, :, cols], c3.unsqueeze(1).broadcast_to([128, NU, 128]), ALU.mult), r=[cc['qstok'], 'c3'], w=[qs3tok])
            po, potok = g['po']
            for u in range(NU):
                s.op('pe', MM(po[:, u * 128:(u + 1) * 128], ATs[:, u, :], cc['V'][:, sub, u * 128:(u + 1) * 128], u == 0, False, True),
                     r=[ATtok, cc['Vtok']], w=[potok])
            ctx[d] = dict(cc=cc, sub=sub, st_i=st_i, kbt=kbt, kbttok=kbttok, kb3=kb3, kb3tok=kb3tok, qs3=qs3, qs3tok=qs3tok,
                          order=list(range(nsub)) if d == 0 else list(range(nsub - 1, -1, -1)))

        def stage2(d, n_i):
            g = G[d]
            x = ctx[d]
            cc = x['cc']
            sub = x['sub']
            cidx = x['order'][n_i]
            last = (n_i == nsub - 1)
            rows = slice(cidx * CS, (cidx + 1) * CS)
            ccols = slice(sub * 128 + cidx * CS, sub * 128 + (cidx + 1) * CS)
            sprev, sprevtok = g['sprev']
            po, potok = g['po']
            pu, putok = g['pu']
            masked = (cidx == nsub - 1)
            for u in range(NU):
                us = slice(u * 128, (u + 1) * 128)
                if masked:
                    s.op('pe', MM(po[:, us], x['qs3'][:, u, :], sprev[:, u, :], False, last, True), r=[x['qs3tok'], sprevtok], w=[potok])
                else:
                    s.op('pe', MM(po[rows, us], cc['qs'][:, u, ccols], sprev[:, u, :], False, last, True), r=[cc['qstok'], sprevtok], w=[potok])
            for u in range(NU):
                us = slice(u * 128, (u + 1) * 128)
                if masked:
                    s.op('pe', MM(pu[:, us], x['kb3'][:, u, :], cc['V'][:, sub, us]), r=[x['kb3tok'], cc['Vtok']], w=[putok])
                else:
                    s.op('pe', MM(pu[:, us], x['kbt'][rows, u, :], cc['V'][rows, sub, us]), r=[x['kbttok'], cc['Vtok']], w=[putok])
            chk = sub * nsub + cidx
            dv_ = cc['dch'][:, :, chk:chk + 1].broadcast_to([128, NU, 128])
            s.op('dve', TT(g['S'], g['S'], dv_, ALU.mult), r=[g['Stok'], cc['dchtok']], w=[g['Stok']])
            s.op('dve', TT(g['S'], g['S'], r3(pu[:, :], b=128), ALU.add), r=[g['Stok'], putok], w=[g['Stok']])
            sbn, sbntok = g['Sbf'].next()
            s.op('act', ACT(sbn, g['S'], AF.Copy), r=[g['Stok']], w=[sbntok])
            g['sprev'] = (sbn, sbntok)

        def stage3(d):
            g = G[d]
            x = ctx.pop(d)
            po, potok = g['po']
            tglob = x['st_i'] * 4 + x['sub']
            ost, osttok = g['ost'].next()
            s.op('act', ACT(ost, po[:, :], AF.Copy), r=[potok], w=[osttok])
            dst = Of if d == 0 else Ob
            s.dma('sp', dst[tglob * 128:(tglob + 1) * 128, :], ost, r=[osttok], w=[tok('odst')])

        nxt = {}
        for d in range(2):
            prep(d, 0 if d == 0 else NS - 1)
        for i in range(NS):
            now = {d: cur[d] for d in range(2)}
            for sub_i in range(4):
                for d in range(2):
                    st_i = i if d == 0 else NS - 1 - i
                    sub = sub_i if d == 0 else 3 - sub_i
                    cur[d] = now[d]
                    stage1(d, st_i, sub)
                for n_i in range(nsub):
                    for d in range(2):
                        stage2(d, n_i)
                for d in range(2):
                    stage3(d)
                if sub_i == 0 and i + 1 < NS:
                    for d in range(2):
                        prep_load(d, (i + 1) if d == 0 else NS - 2 - i)
                if sub_i == 1 and i + 1 < NS:
                    for d in range(2):
                        prep(d, (i + 1) if d == 0 else NS - 2 - i)
                        nxt[d] = cur[d]
            if i + 1 < NS:
                for d in range(2):
                    cur[d] = nxt[d]

    def rec_final(G_src, ycol0):
        s.barrier()
        ar.reset(W_COLS if (ycol0 == 512 and nlayers > 1) else 0)
        epsr = ar.f32(2)
        s.op('pool', MS(epsr, 1e-6), w=['epsr'])
        rfl = Ring('rfl', [ar.bf16(512) for _ in range(4)])
        rf = Ring('rf', [ar.f32(512) for _ in range(3)])
        rb = Ring('rb', [ar.bf16(512) for _ in range(4)])
        rg = Ring('rg', [ar.bf16(512) for _ in range(5)])
        rsq = Ring('rsq', [ar.f32(512) for _ in range(2)])
        rss = Ring('rss', [ar.f32(8) for _ in range(3)])
        ry = Ring('ry', [ar.bf16(512) for _ in range(3)])
        loaded = {}

        def load(t):
            rows = slice(t * 128, (t + 1) * 128)
            fl, fltok = rfl.next()
            b, btok = rb.next()
            g, gtok = rg.next()
            s.dma('sp', fl, Of[rows, :], w=[fltok])
            s.dma('sp', b, Ob[rows, :], w=[btok])
            s.dma('sp', g, G_src[rows, :], w=[gtok])
            loaded[t] = (fl, fltok, b, btok, g, gtok)

        load(0)
        if NT > 1:
            load(1)
        stA = {}

        def stage_a(t):
            if t + 2 < NT:
                load(t + 2)
            fl, fltok, b, btok, g, gtok = loaded.pop(t)
            f, ftok = rf.next()
            s.op('dve', TT(f, fl, b, ALU.add), r=[fltok, btok], w=[ftok])
            sq, sqtok = rsq.next()
            ss, sstok = rss.next()
            for h in range(4):
                hs = slice(h * 128, (h + 1) * 128)
                s.op('act', ACT(sq[:, hs], f[:, hs], AF.Square, accum=ss[:, h:h + 1]), r=[ftok], w=[sqtok, sstok])
            s.op('act', ACT(ss[:, 0:4], ss[:, 0:4], AF.Ln, bias=epsr[:, 0:1], scale=1.0 / 128.0), r=[sstok, 'epsr'], w=[sstok])
            s.op('act', ACT(ss[:, 4:8], ss[:, 0:4], AF.Exp, scale=-0.5), r=[sstok], w=[sstok])
            stA[t] = (f, ftok, ss, sstok, g, gtok)

        def stage_b(t):
            rows = slice(t * 128, (t + 1) * 128)
            f, ftok, ss, sstok, g, gtok = stA.pop(t)
            y, ytok = ry.next()
            for h in range(4):
                hs = slice(h * 128, (h + 1) * 128)
                s.op('dve', STT(y[:, hs], f[:, hs], ss[:, 4 + h:5 + h], g[:, hs], ALU.mult, ALU.mult), r=[ftok, sstok, gtok], w=[ytok])
            s.dma('sp', Y[rows, ycol0:ycol0 + 512], y, r=[ytok], w=[tok('ydst')])

        stage_a(0)
        for t in range(NT):
            if t + 1 < NT:
                stage_a(t + 1)
            stage_b(t)

    holder = {}

    def fm_store(dst, row0, func=AF.Copy, scale=None, out_dt=BF16, col0=0):
        def emit(ps, pt, i, sub, n=128):
            st, sttok = holder['stg'].next()
            o = st.bitcast(BF16)[:, 0:512] if out_dt == BF16 else st
            s.op('act', ACT(o[0:n, :], ps[0:n, :], func, scale=scale), r=[pt], w=[sttok])
            s.dma('sp', dst[row0:row0 + n, col0 + i * 512:col0 + (i + 1) * 512], o[0:n, :], r=[sttok], w=[tok('fmdst')])
        return emit

    def tm_store(dst, func=AF.Copy, mulkey=None):
        def emit(ps, pt, i, sub):
            st, sttok = holder['stg'].next()
            o = st.bitcast(BF16)[:, 0:512]
            if mulkey is not None:
                s.op('act', ACT(st, ps[:, :], func), r=[pt], w=[sttok])
                st2, st2tok = holder['stg'].next()
                o = st2.bitcast(BF16)[:, 0:512]
                s.op('dve', TT(o, st, holder[mulkey], ALU.mult), r=[sttok, mulkey], w=[st2tok])
                sttok = st2tok
            elif func == AF.Copy:
                s.op('dve', CP(o, ps[:, :]), r=[pt], w=[sttok])
            else:
                s.op('act', ACT(o, ps[:, :], func), r=[pt], w=[sttok])
            t0 = i * 512 + sub * 128
            s.dma('sp', dst[t0:t0 + 128, :], o, r=[sttok], w=[tok('tmdst')])
        return emit

    def layer0():
        segs = []
        for blk in range(4):
            segs.append(dict(kind='fm', c0=blk * 128, n=128, emit=fm_store(QTa, blk * 128, scale=0.125)))
        for blk in range(4):
            segs.append(dict(kind='fm', c0=512 + blk * 128, n=128, emit=fm_store(KTa, blk * 128)))
        for blk in range(2):
            segs.append(dict(kind='fm', c0=2048 + blk * 128, n=128, emit=fm_store(qTb, blk * 128, scale=0.125)))
        for blk in range(2):
            segs.append(dict(kind='fm', c0=2304 + blk * 128, n=128, emit=fm_store(kTb, blk * 128)))

        def v66_emit(ps, pt, i, sub):
            st, sttok = holder['vstg'].next()
            s.op('dve', CP(st.rearrange("p (h d) -> p h d", d=66)[:, :, 0:64], r3(ps[:, :], b=64)), r=[pt], w=[sttok])
            t0 = i * 512 + sub * 128
            s.dma('sp', Va66[t0:t0 + 128, :], st, r=[sttok], w=[tok('v66')])
        segs.append(dict(kind='tm', c0=1024, n=512, emit=v66_emit))
        segs.append(dict(kind='tm', c0=2560, n=512, emit=tm_store(Vb)))
        segs.append(dict(kind='tm', c0=1536, n=512, emit=tm_store(Ga, AF.Silu)))
        segs.append(dict(kind='tm', c0=3072, n=512, emit=tm_store(Gb, AF.Silu, 'ngbc')))

        gpend = []

        def gate_emit(d):
            def emit(ps, pt, i, sub):
                lr, lrtok = holder['lr'].next()
                lrb = lr.bitcast(BF16)[:, 0:512]
                s.op('act', ACT(lrb[0:16, :], ps[0:16, :], AF.Copy), r=[pt], w=[lrtok])
                for blk in range(2):
                    pz, pztok = holder['pz'].next()
                    s.op('pe', MM(pz[:, :], holder['wupb'][0:16, d * 256 + blk * 128:d * 256 + (blk + 1) * 128], lrb[0:16, :]),
                         r=[lrtok, 'wupb'], w=[pztok])
                    st, sttok = holder['gst'].next()
                    s.op('act', ACT(st, pz[:, :], AF.Sigmoid, bias=holder['glab'][:, d * 2 + blk:d * 2 + blk + 1]), r=[pztok, 'glab'], w=[sttok])
                    gpend.append((st, sttok, d, blk, i))
                if d == 1:
                    while gpend:
                        st, sttok, d_, blk_, i_ = gpend.pop(0)
                        s.op('act', ACT(st, st, AF.Ln), r=[sttok], w=[sttok])
                        s.dma('sp', lfT[d_, blk_ * 128:(blk_ + 1) * 128, i_ * 512:(i_ + 1) * 512], st, r=[sttok], w=[tok('lfdst')])
            return emit
        segs.append(dict(kind='fm', c0=3584, n=16, emit=gate_emit(0)))
        segs.append(dict(kind='fm', c0=3600, n=16, emit=gate_emit(1)))

        def alloc_extras():
            holder['stg'] = Ring('stg', [ar.f32(512) for _ in range(8)])
            holder['lr'] = Ring('lr', [ar.f32(512) for _ in range(2)])
            holder['wup'] = ar.f32(512)
            holder['wupb'] = ar.bf16(512)
            holder['gst'] = Ring('gst', [ar.f32(512) for _ in range(6)])
            holder['glab'] = ar.f32(4)
            holder['pz'] = Ring('pz', [banks[6], banks[7]])
            vst = [ar.bf16(528) for _ in range(3)]
            holder['vstg'] = Ring('vstg', vst)
            for v_, vt_ in zip(vst, holder['vstg'].toks):
                s.op('pool', MS(v_, 1.0), w=[vt_])
            holder['ngbc'] = ar.f32(512)
            for h in range(4):
                s.dma('sp', holder['ngbc'][:, h * 128:(h + 1) * 128], gla_ng[0:1, :].partition_broadcast(128), w=['ngbc'])
            for d in range(2):
                s.dma('sp', holder['wup'][0:16, d * 256:(d + 1) * 256], gla_w_up[d], w=['wup'])
            s.op('dve', CP(holder['wupb'][0:16, :], holder['wup'][0:16, :]), r=['wup'], w=['wupb'])
            s.dma('sp', holder['glab'], gla_bT.rearrange("p d b -> p (d b)"), w=['glab'])
        phase_a_with(0, x_in, segs, alloc_extras, e_w_in, 3616)

    hkT = dscr("hkT", [2, 512, L], BF16)
    hlfT = dscr("hlfT", [2, 512, L], F32)
    xbcT = dscr("xbcT", [1024, L + 4], BF16)
    dtA = dscr("dtA", [L, 32], F32)
    Btm = dscr("Btm", [L, 256], BF16)

    def layer1_a():
        segs = []
        for blk in range(4):
            segs.append(dict(kind='fm', c0=blk * 128, n=128, emit=fm_store(QTa, blk * 128, scale=128.0 ** -0.5)))

        fpend = []

        def forget_emit(d, blk):
            def emit(ps, pt, i, sub):
                st, sttok = holder['fst'].next()
                s.op('act', ACT(st, ps[:, :], AF.Sigmoid), r=[pt], w=[sttok])
                s.op('dve', TS(st, st, holder['lbs'][:, 12 + blk:13 + blk], ALU.mult, holder['lbs'][:, 8 + blk:9 + blk], ALU.add),
                     r=[sttok, 'lbs'], w=[sttok])
                st2, st2tok = holder['stg'].next()
                kb_ = st2.bitcast(BF16)[:, 0:512]
                s.op('act', ACT(kb_, st, AF.Identity, bias=1.0, scale=-1.0), r=[sttok], w=[st2tok])
                s.dma('sp', hkT[d, blk * 128:(blk + 1) * 128, i * 512:(i + 1) * 512], kb_, r=[st2tok], w=[tok('hk')])
                fpend.append((st, sttok, d, blk, i))
                if d == 1 and blk == 3:
                    while fpend:
                        st_, sttok_, d_, blk_, i_ = fpend.pop(0)
                        s.op('act', ACT(st_, st_, AF.Ln), r=[sttok_], w=[sttok_])
                        s.dma('sp', hlfT[d_, blk_ * 128:(blk_ + 1) * 128, i_ * 512:(i_ + 1) * 512], st_, r=[sttok_], w=[tok('hlf')])
            return emit
        for d in range(2):
            for blk in range(4):
                segs.append(dict(kind='fm', c0=512 + d * 512 + blk * 128, n=128, emit=forget_emit(d, blk)))
        segs.append(dict(kind='tm', c0=1536, n=512, emit=tm_store(Vb)))
        segs.append(dict(kind='tm', c0=2048, n=512, emit=tm_store(Gb, AF.Silu, 'ngbc')))
        segs.append(dict(kind='tm', c0=2560, n=512, emit=tm_store(Ga, AF.Silu)))
        for blk in range(8):
            segs.append(dict(kind='fm', c0=3072 + blk * 128, n=128, emit=fm_store(xbcT, blk * 128, col0=2)))

        def dt_emit(ps, pt, i, sub):
            dtt, dttok = holder['dtt'].next()
            s.op('dve', TT(dtt[:, 0:16], ps[:, 0:16], holder['dtb'][:, 0:16], ALU.add), r=[pt, 'dtb'], w=[dttok])
            s.op('act', ACT(dtt[:, 0:16], dtt[:, 0:16], AF.Exp), r=[dttok], w=[dttok])
            s.op('act', ACT(dtt[:, 0:16], dtt[:, 0:16], AF.Ln, bias=1.0), r=[dttok], w=[dttok])
            s.op('dve', TT(dtt[:, 16:32], dtt[:, 0:16], holder['dtb'][:, 16:32], ALU.mult), r=[dttok, 'dtb'], w=[dttok])
            t0 = i * 512 + sub * 128
            s.dma('sp', dtA[t0:t0 + 128, :], dtt[:, 0:32], r=[dttok], w=[tok('dtA')])
        segs.append(dict(kind='tm', c0=4096, n=16, emit=dt_emit))

        def alloc_extras():
            holder['stg'] = Ring('stg', [ar.f32(512) for _ in range(8)])
            holder['dtt'] = Ring('dtt', [ar.f32(32) for _ in range(2)])
            holder['fst'] = Ring('fst', [ar.f32(512) for _ in range(10)])
            holder['dtb'] = ar.f32(32)
            holder['lbs'] = ar.f32(16)
            holder['ngbc'] = ar.f32(512)
            for h in range(4):
                s.dma('sp', holder['ngbc'][:, h * 128:(h + 1) * 128], hgrn_ng[0:1, :].partition_broadcast(128), w=['ngbc'])
            zz = ar.f32(16)
            lbs = holder['lbs']
            s.dma('sp', lbs[:, 0:8], hgrn_lbT.rearrange("p l b -> p (l b)"), w=['lbs'])
            s.op('dve', TT(lbs[:, 8:12], lbs[:, 4:8], lbs[:, 0:4], ALU.subtract), r=['lbs'], w=['lbs'])
            s.op('act', ACT(lbs[:, 8:12], lbs[:, 8:12], AF.Sigmoid), r=['lbs'], w=['lbs'])
            s.op('dve', TS(lbs[:, 12:16], lbs[:, 8:12], -1.0, ALU.mult, 1.0, ALU.add), r=['lbs'], w=['lbs'])
            dtb = holder['dtb']
            s.dma('sp', dtb[:, 0:16], dt_bias[0:1, :].partition_broadcast(128), w=['dtb'])
            s.dma('sp', dtb[:, 16:32], a_log[0:1, :].partition_broadcast(128), w=['dtb'])
            s.op('act', ACT(dtb[:, 16:32], dtb[:, 16:32], AF.Exp), r=['dtb'], w=['dtb'])
            s.op('dve', TS(dtb[:, 16:32], dtb[:, 16:32], -1.0, ALU.mult), r=['dtb'], w=['dtb'])
            s.op('pool', MS(zz, 0.0), w=['zz'])
            xv_ = xbcT.rearrange("(b p) t -> p b t", p=128)
            zzb = r3(zz.bitcast(BF16)[:, 0:16], b=2)
            s.dma('sp', xv_[:, :, 0:2], zzb, r=['zz'], w=[tok('xbcz')])
            s.dma('sp', xv_[:, :, L + 2:L + 4], zzb, r=['zz'], w=[tok('xbcz')])
        phase_a_with(1, X1, segs, alloc_extras, o_w_in, 4112)

    def phase_a_with(l, x_src, segs, extras, wsrc, wcols):
        s.barrier()
        ar.reset(W_COLS)
        extras()
        phase_a(l, x_src, segs)

    def na_phase():
        s.barrier()
        ar.reset()
        NK = 640
        E_int = r3(ar.bf16(8 * NK), b=NK)
        E_edge = r3(ar.bf16(8 * NK), b=NK)
        bst = ar.f32(8 * NK)
        KTs = Ring('KT', [r3(ar.bf16(4 * 1024), b=1024) for _ in range(4)])
        Vs_raw = [ar.bf16(8 * 8 * 66) for _ in range(4)]
        Vs = Ring('Vn', [v.rearrange("p (b h d) -> p b h d", b=8, h=8) for v in Vs_raw])
        QTs = Ring('QT', [r3(ar.bf16(4 * 128), b=128) for _ in range(5)])
        Gs = Ring('Gn', [ar.bf16(512) for _ in range(5)])
        eS = Ring('eS', [ar.bf16(NK) for _ in range(5)])
        PTs = Ring('PT', [ar.bf16(NK) for _ in range(5)])
        rec = Ring('rec', [ar.f32(8) for _ in range(2)])
        yst = Ring('yst', [ar.bf16(512) for _ in range(4)])
        pSA = Ring('pSA', [banks[0], banks[1], banks[2]])
        pSB = Ring('pSB', [banks[3], banks[4], banks[5]])
        pOn = Ring('pOn', [banks[6], banks[7]])

        def load_E(cls, dst, dtok):
            s.dma('sp', bst, btab[cls], w=['bst'])
            s.op('act', ACT(dst.rearrange("p h k -> p (h k)"), bst, AF.Exp), r=['bst'], w=[dtok])

        load_E(2, E_int, 'E_int')
        edge_loaded = [None]
        QTv = QTa.rearrange("(pr p) t -> p pr t", p=128)
        KTv = KTa.rearrange("(pr p) t -> p pr t", p=128)
        loaded = {}
        wloaded = {}
        NSB = NT // 4

        def wstart(sb):
            return min(max(8 * sb - 4, 0), ROWS - 16)

        def wload(sb):
            k0 = wstart(sb) * 64
            KT, KTtok = KTs.next()
            s.dma('sp', KT, KTv[:, :, k0:k0 + 1024], w=[KTtok])
            V, Vtok = Vs.next()
            s.dma('sp', V.rearrange("p b h d -> p b (h d)"), Va66[k0:k0 + 1024, :].rearrange("(b p) c -> p b c", p=128), w=[Vtok])
            wloaded[sb] = (KT, KTtok, V, Vtok)

        def load(t):
            QT, QTtok = QTs.next()
            s.dma('sp', QT, QTv[:, :, t * 128:(t + 1) * 128], w=[QTtok])
            G, Gtok = Gs.next()
            s.dma('sp', G, Ga[t * 128:(t + 1) * 128, :], w=[Gtok])
            loaded[t] = (QT, QTtok, G, Gtok)

        wload(0)
        if NSB > 1:
            wload(1)
        load(0)
        if NT > 1:
            load(1)
        tctx = {}

        def tile_ctx(t):
            if t + 2 < NT:
                load(t + 2)
            sb = t // 4
            if t % 4 == 0 and sb + 2 < NSB:
                wload(sb + 2)
            r = 2 * t
            ks = min(max(r - 4, 0), ROWS - 10)
            cls = (r - ks) // 2
            if cls == 2:
                E, Etok = E_int, 'E_int'
            else:
                if edge_loaded[0] != cls:
                    load_E(cls, E_edge, 'E_edge')
                    edge_loaded[0] = cls
                E, Etok = E_edge, 'E_edge'
            KT, KTtok, V, Vtok = wloaded[sb]
            if t % 4 == 3:
                wloaded.pop(sb)
            QT, QTtok, G, Gtok = loaded.pop(t)
            y, ytok = yst.next()
            boff = (ks - wstart(sb)) // 2
            tctx[t] = dict(E=E, Etok=Etok, KT=KT, KTtok=KTtok, V=V, Vtok=Vtok, QT=QT, QTtok=QTtok, G=G, Gtok=Gtok, y=y, ytok=ytok, po={}, boff=boff)

        def emit_S(t, h):
            c = tctx[t]
            pr = slice((h % 2) * 64, (h % 2) * 64 + 64)
            pa, patok = pSA.next()
            pb, pbtok = pSB.next()
            for blk in range(5):
                dst = pa[:, blk * 128:(blk + 1) * 128] if blk < 4 else pb[:, 0:128]
                s.op('pe', MM(dst, c['KT'][pr, h // 2, (c['boff'] + blk) * 128:(c['boff'] + blk + 1) * 128], c['QT'][pr, h // 2, :]),
                     r=[c['KTtok'], c['QTtok']], w=[patok if blk < 4 else pbtok])
            e_, etok = eS.next()
            s.op('act', ACT(e_[:, 0:512], pa[:, :], AF.Exp), r=[patok], w=[etok])
            s.op('act', ACT(e_[:, 512:640], pb[:, 0:128], AF.Exp), r=[pbtok], w=[etok])
            P, Ptok = PTs.next()
            s.op('dve', TT(P, e_, c['E'][:, h, :], ALU.mult), r=[etok, c['Etok']], w=[Ptok])
            return P, Ptok

        def emit_PV(t, h, P, Ptok):
            c = tctx[t]
            hg, hh = h // 4, h % 4
            if hg not in c['po']:
                c['po'][hg] = pOn.next()
            po, potok = c['po'][hg]
            for blk in range(5):
                s.op('pe', MM(po[:, hh * 65:hh * 65 + 65], P[:, blk * 128:(blk + 1) * 128], c['V'][:, c['boff'] + blk, h, 0:65], blk == 0, blk == 4),
                     r=[Ptok, c['Vtok']], w=[potok])
            if hh == 3:
                rc, rctok = rec.next()
                po3 = po[:, 0:260].rearrange("p (h d) -> p h d", d=65)
                s.op('dve', lambda e, rc=rc, po3=po3: e.reciprocal(out=rc[:, 0:4], in_=po3[:, :, 64]), r=[potok], w=[rctok])
                for j in range(4):
                    hj = hg * 4 + j
                    s.op('dve', STT(c['y'][:, hj * 64:(hj + 1) * 64], po3[:, j, 0:64], rc[:, j:j + 1], c['G'][:, hj * 64:(hj + 1) * 64], ALU.mult, ALU.mult),
                         r=[potok, rctok, c['Gtok']], w=[c['ytok']])
                if hg == 1:
                    s.dma('sp', Y[t * 128:(t + 1) * 128, 0:512], c['y'], r=[c['ytok']], w=[tok('ydst')])
                    tctx.pop(t)

        pend = []
        for t in range(NT):
            for h in range(8):
                if t not in tctx:
                    tile_ctx(t)
                P, Ptok = emit_S(t, h)
                pend.append((t, h, P, Ptok))
                if len(pend) > 2:
                    emit_PV(*pend.pop(0))
        while pend:
            emit_PV(*pend.pop(0))

    def ssd_conv():
        s.barrier()
        ar.reset()
        cw = ar.f32(32)
        cbias = ar.f32(8)
        s.dma('sp', cw, conv_wT.rearrange("p b k -> p (b k)"), w=['cw'])
        s.dma('sp', cbias, conv_bT[:, :], w=['cbias'])
        dg = ar.bf16(32 * 128)
        dg3 = r3(dg, b=128)
        for j in range(32):
            s.op('dve', TS(dg3[:, j, :], ident_f[:], cw[:, j:j + 1], ALU.mult), r=['ident_f', 'cw'], w=['dg'])
        rin = Ring('cin', [ar.bf16(516) for _ in range(6)])
        rfm = Ring('cfm', [ar.bf16(512) for _ in range(12)])
        rtm = Ring('ctm', [ar.bf16(512) for _ in range(3)])
        ptr = Ring('cvp', [banks[0], banks[1], banks[2]])
        pcv = Ring('pcv', [banks[3], banks[4], banks[5], banks[6]])
        items = [(i, blk) for i in range(NS) for blk in range(8)]
        cloaded = {}

        def cload(n):
            i, blk = items[n]
            xin, xtok = rin.next()
            s.dma('sp', xin[:, 0:515], xbcT[blk * 128:(blk + 1) * 128, i * 512:i * 512 + 515], w=[xtok])
            cloaded[n] = (xin, xtok)

        for n in range(min(4, len(items))):
            cload(n)
        for i in range(NS):
            t0 = i * 512
            fm = {}
            for blk in range(8):
                n = i * 8 + blk
                if n + 4 < len(items):
                    cload(n + 4)
                xin, xtok = cloaded.pop(n)
                pc, pctok = pcv.next()
                for k in range(4):
                    s.op('pe', MM(pc[:, :], dg3[:, blk * 4 + k, :], xin[:, k:k + 512], k == 0, k == 3), r=['dg', xtok], w=[pctok])
                o, otok = rfm.next()
                s.op('act', ACT(o, pc[:, :], AF.Silu, bias=cbias[:, blk:blk + 1]), r=[pctok, 'cbias'], w=[otok])
                fm[blk] = (o, otok)
                if blk in (4, 5):
                    s.dma('sp', qTb[(blk - 4) * 128:(blk - 3) * 128, t0:t0 + 512], o, r=[otok], w=[tok('BT')])
                if blk in (6, 7):
                    s.dma('sp', kTb[(blk - 6) * 128:(blk - 5) * 128, t0:t0 + 512], o, r=[otok], w=[tok('CT')])
            for sub in range(4):
                ps, pt = ptr.next()
                psb = ps[:, :].bitcast(BF16)
                for b4 in range(4):
                    o, otok = fm[b4]
                    s.op('pe', TR(psb[:, b4 * 128:(b4 + 1) * 128], o[:, sub * 128:(sub + 1) * 128], ident_b[:]), r=[otok, 'ident_b'], w=[pt])
                tm, tmtok = rtm.next()
                s.op('act', ACT(tm, psb[:, 0:512], AF.Copy), r=[pt], w=[tmtok])
                s.dma('sp', Va[t0 + sub * 128:t0 + (sub + 1) * 128, :], tm, r=[tmtok], w=[tok('xs')])
                ps, pt = ptr.next()
                psb = ps[:, :].bitcast(BF16)
                for b2 in range(2):
                    o, otok = fm[4 + b2]
                    s.op('pe', TR(psb[:, b2 * 128:(b2 + 1) * 128], o[:, sub * 128:(sub + 1) * 128], ident_b[:]), r=[otok, 'ident_b'], w=[pt])
                tm, tmtok = rtm.next()
                s.op('dve', CP(tm[:, 0:256], psb[:, 0:256]), r=[pt], w=[tmtok])
                s.dma('sp', Btm[t0 + sub * 128:t0 + (sub + 1) * 128, :], tm[:, 0:256], r=[tmtok], w=[tok('Btm')])

    def ssd_main():
        s.barrier()
        ar.reset()
        tri = [ar.f32(128), ar.f32(128)]
        mbf = [ar.f32(128), ar.f32(128)]
        s.op('pool', CP(tri[0], mask_f128[:]), r=['mask_f128'], w=['tri'])
        s.op('pool', CP(tri[1], mask_b128[:]), r=['mask_b128'], w=['tri'])
        mb4 = [ar.bf16(512), ar.bf16(512)]
        for d in range(2):
            s.op('pool', TS(mbf[d], tri[d], -1.0, ALU.add, -NEG, ALU.mult), r=['tri'], w=['mbf'])
            for hh in range(4):
                s.op('pool', CP(mb4[d][:, hh * 128:(hh + 1) * 128], mbf[d]), r=['mbf'], w=['mb4'])
        chains = [(g, d) for g in range(2) for d in range(2)]
        C = {}
        for ch in chains:
            C[ch] = dict(S=ar.f32(256), Stok=tok('sS'), Sbf=Ring('sSbf', [ar.bf16(256) for _ in range(3)]))
            s.op('pool', MS(C[ch]['S'], 0.0), w=[C[ch]['Stok']])
            sb0, sbt0 = C[ch]['Sbf'].next()
            s.op('pool', MS(sb0, 0.0), w=[sbt0])
            C[ch]['sprev'] = (sb0, sbt0)
        rdta = Ring('dta', [ar.f32(32) for _ in range(4)])
        rsm = Ring('sm', [ar.f32(64) for _ in range(4)])
        rR = Ring('sR', [ar.f32(1024) for _ in range(4)])
        rBT = Ring('sBT', [ar.bf16(128) for _ in range(9)])
        rCT = Ring('sCT', [ar.bf16(128) for _ in range(9)])
        rBm = Ring('sBm', [ar.bf16(128) for _ in range(9)])
        rxs = Ring('sxs', [ar.bf16(256) for _ in range(9)])
        rcb = Ring('scb', [ar.bf16(128) for _ in range(5)])
        rxdt = Ring('sxdt', [ar.bf16(256) for _ in range(5)])
        rxd = Ring('sxd', [ar.bf16(256) for _ in range(5)])
        rarg = Ring('sarg', [ar.f32(512) for _ in range(4)])
        rsg = Ring('ssg', [ar.bf16(512) for _ in range(4)])
        rat = Ring('sat', [ar.bf16(512) for _ in range(3)])
        ry1 = Ring('sy1', [ar.f32(256) for _ in range(3)])
        ry2 = Ring('sy2', [ar.f32(256) for _ in range(3)])
        rtS = Ring('stS', [ar.f32(256) for _ in range(2)])
        ryb = Ring('syb', [ar.bf16(256) for _ in range(3)])
        pq = Ring('spq', [banks[0], banks[1]])
        pCB = Ring('spCB', [banks[2]])
        pBC = Ring('spBC', [banks[3], banks[4]])
        pY = Ring('spY', [banks[5], banks[6]])
        pU = Ring('spU', [banks[7]])
        shared = {}
        sloaded = {}

        def sload(ch, t):
            g, d = ch
            rows = slice(t * 128, (t + 1) * 128)
            BT, BTtok = rBT.next()
            CT, CTtok = rCT.next()
            Bm, Bmtok = rBm.next()
            xs, xstok = rxs.next()
            s.dma('sp', BT, qTb[g * 128:(g + 1) * 128, rows], w=[BTtok])
            s.dma('sp', CT, kTb[g * 128:(g + 1) * 128, rows], w=[CTtok])
            s.dma('sp', Bm, Btm[rows, g * 128:(g + 1) * 128], w=[Bmtok])
            s.dma('sp', xs, Va[rows, g * 256:(g + 1) * 256], w=[xstok])
            sloaded[(ch, t)] = (BT, BTtok, CT, CTtok, Bm, Bmtok, xs, xstok)

        dloaded = {}

        def dload(d, t):
            dta, dtatok = rdta.next()
            s.dma('sp', dta, dtA[t * 128:(t + 1) * 128, :], w=[dtatok])
            dloaded[(d, t)] = (dta, dtatok)

        def dprep(d, t):
            dta, dtatok = dloaded.pop((d, t))
            a_d = dta[:, 16 + d * 8:24 + d * 8]
            q_, qtok = pq.next()
            s.op('pe', MM(q_[:, 0:8], tri[d], a_d), r=['tri', dtatok], w=[qtok])
            s.op('pe', MM(q_[:, 8:16], ones_f[:], a_d), r=['ones_f', dtatok], w=[qtok])
            sm, smtok = rsm.next()
            s.op('act', ACT(sm[:, 0:8], q_[:, 0:8], AF.Copy), r=[qtok], w=[smtok, qtok])
            s.op('act', ACT(sm[:, 8:16], q_[:, 0:8], AF.Exp), r=[qtok], w=[smtok, qtok])
            s.op('dve', TT(sm[:, 16:24], q_[:, 8:16], sm[:, 0:8], ALU.subtract), r=[qtok, smtok], w=[smtok, qtok])
            s.op('act', ACT(sm[:, 24:32], sm[:, 16:24], AF.Exp), r=[smtok], w=[smtok])
            s.op('act', ACT(sm[:, 32:40], q_[:, 8:16], AF.Exp), r=[qtok], w=[smtok, qtok])
            R, Rtok = rR.next()
            s.op('dve', TT(r3(R, b=128), a_d.unsqueeze(2).broadcast_to([128, 8, 128]), tri[d].unsqueeze(1).broadcast_to([128, 8, 128]), ALU.mult),
                 r=[dtatok, 'tri'], w=[Rtok])
            shared[(d, t)] = dict(dta=dta, dtatok=dtatok, sm=sm, smtok=smtok, R=R, Rtok=Rtok)

        X = {}

        def s2(ch, t):
            g, d = ch
            sh = shared[(d, t)]
            BT, BTtok, CT, CTtok, Bm, Bmtok, xs, xstok = sloaded.pop((ch, t))
            cbp, cbptok = pCB.next()
            s.op('pe', MM(cbp[:, 0:128], BT, CT), r=[BTtok, CTtok], w=[cbptok])
            cb, cbtok = rcb.next()
            s.op('act', ACT(cb, cbp[:, 0:128], AF.Copy), r=[cbptok], w=[cbtok])
            dta = sh['dta']
            sm = sh['sm']
            xdt, xdttok = rxdt.next()
            dtv = dta[:, d * 8 + g * 4:d * 8 + g * 4 + 4].unsqueeze(2).broadcast_to([128, 4, 64])
            s.op('dve', TT(r3(xdt, b=64), r3(xs, b=64), dtv, ALU.mult), r=[xstok, sh['dtatok']], w=[xdttok])
            xd, xdtok = rxd.next()
            dsv = sm[:, 24 + g * 4:28 + g * 4].unsqueeze(2).broadcast_to([128, 4, 64])
            s.op('dve', TT(r3(xd, b=64), r3(xdt, b=64), dsv, ALU.mult), r=[xdttok, sh['smtok']], w=[xdtok])
            X[ch] = dict(t=t, sh=sh, CT=CT, CTtok=CTtok, Bm=Bm, Bmtok=Bmtok, cb=cb, cbtok=cbtok, xdt=xdt, xdttok=xdttok, xd=xd, xdtok=xdtok)

        def bc(ch):
            g, d = ch
            x = X[ch]
            sh = x['sh']
            bcp, bcptok = pBC.next()
            s.op('pe', MM(bcp[:, :], ones_f[:], sh['R'][:, g * 512:(g + 1) * 512], True, False), r=['ones_f', sh['Rtok']], w=[bcptok])
            s.op('pe', MM(bcp[:, :], ident_b[:], mb4[d], False, True), r=['ident_b', 'mb4'], w=[bcptok])
            x['bcp'] = bcp
            x['bcptok'] = bcptok

        def s3a(ch):
            g, d = ch
            x = X[ch]
            sh = x['sh']
            sm = sh['sm']
            arg, argtok = rarg.next()
            acv = sm[:, g * 4:g * 4 + 4].unsqueeze(2).broadcast_to([128, 4, 128])
            s.op('dve', TT(r3(arg, b=128), r3(x['bcp'][:, :], b=128), acv, ALU.subtract), r=[x['bcptok'], sh['smtok']], w=[argtok])
            sg, sgtok = rsg.next()
            s.op('act', ACT(sg, arg, AF.Exp), r=[argtok], w=[sgtok])
            x['sg'] = sg
            x['sgtok'] = sgtok

        def s3b(ch):
            g, d = ch
            c = C[ch]
            x = X[ch]
            yp, yptok = pY.next()
            sprev, sprevtok = c['sprev']
            at, attok = rat.next()
            s.op('dve', TT(r3(at, b=128), r3(x['sg'], b=128), x['cb'].unsqueeze(1).broadcast_to([128, 4, 128]), ALU.mult), r=[x['cbtok'], x['sgtok']], w=[attok])
            for hh in range(4):
                s.op('pe', MM(yp[:, hh * 64:(hh + 1) * 64], at[:, hh * 128:(hh + 1) * 128], x['xdt'][:, hh * 64:(hh + 1) * 64]), r=[attok, x['xdttok']], w=[yptok])
                s.op('pe', MM(yp[:, 256 + hh * 64:256 + (hh + 1) * 64], x['CT'], sprev[:, hh * 64:(hh + 1) * 64]), r=[x['CTtok'], sprevtok], w=[yptok])
            x['yp'] = yp
            x['yptok'] = yptok

        def s3c(ch):
            g, d = ch
            x = X[ch]
            sh = x['sh']
            sm = sh['sm']
            yp, yptok = x['yp'], x['yptok']
            y1, y1tok = ry1.next()
            s.op('act', ACT(y1, yp[:, 0:256], AF.Copy), r=[yptok], w=[y1tok, yptok])
            y2, y2tok = ry2.next()
            eav = sm[:, 8 + g * 4:12 + g * 4].unsqueeze(2).broadcast_to([128, 4, 64])
            s.op('dve', TT(r3(y2, b=64), r3(yp[:, 256:512], b=64), eav, ALU.mult), r=[yptok, sh['smtok']], w=[y2tok, yptok])
            yb, ybtok = ryb.next()
            s.op('dve', TT(yb, y2, y1, ALU.add), r=[y2tok, y1tok], w=[ybtok])
            rows = slice(x['t'] * 128, (x['t'] + 1) * 128)
            dst = Of if d == 0 else Ob
            s.dma('sp', dst[rows, g * 256:(g + 1) * 256], yb, r=[ybtok], w=[tok('sodst')])

        def s4(ch):
            g, d = ch
            c = C[ch]
            x = X.pop(ch)
            sm = x['sh']['sm']
            up, uptok = pU.next()
            s.op('pe', MM(up[:, 0:256], x['Bm'], x['xd']), r=[x['Bmtok'], x['xdtok']], w=[uptok])
            tS, tStok = rtS.next()
            cdv = sm[:, 32 + g * 4:36 + g * 4].unsqueeze(2).broadcast_to([128, 4, 64])
            s.op('dve', TT(r3(tS, b=64), r3(c['S'], b=64), cdv, ALU.mult), r=[c['Stok'], x['sh']['smtok']], w=[tStok])
            s.op('dve', TT(c['S'], tS, up[:, 0:256], ALU.add), r=[tStok, uptok], w=[c['Stok']])
            sbn, sbntok = c['Sbf'].next()
            s.op('act', ACT(sbn, c['S'], AF.Copy), r=[c['Stok']], w=[sbntok])
            c['sprev'] = (sbn, sbntok)

        def tof(ch, i):
            return i if ch[1] == 0 else NT - 1 - i

        for ch in chains:
            sload(ch, tof(ch, 0))
        for d in range(2):
            dload(d, tof((0, d), 0))
        for i in range(NT):
            if i + 1 < NT:
                for ch in chains:
                    sload(ch, tof(ch, i + 1))
                for d in range(2):
                    dload(d, tof((0, d), i + 1))
            for d in range(2):
                dprep(d, tof((0, d), i))
            for ch in chains:
                s2(ch, tof(ch, i))
            nch = len(chains)
            bc(chains[0])
            bc(chains[1])
            s3a(chains[0])
            for k in range(nch + 1):
                if k + 1 < nch:
                    s3a(chains[k + 1])
                if k + 2 < nch:
                    bc(chains[k + 2])
                if k < nch:
                    s3b(chains[k])
                if k >= 1:
                    s3c(chains[k - 1])
            for ch in chains:
                s4(ch)
            for d in range(2):
                shared.pop((d, tof((0, d), i)))

    def ssd_final():
        s.barrier()
        ar.reset()
        dsk = ar.f32(8)
        ngb = ar.f32(512)
        s.dma('sp', dsk, d_skip[0:1, :].partition_broadcast(128), w=['dsk'])
        s.dma('sp', ngb, ssm_ng[0:1, :].partition_broadcast(128), w=['ngb'])
        epsf = ar.f32(2)
        s.op('pool', MS(epsf, 1e-6), w=['epsf'])
        rfl = Ring('ffl', [ar.bf16(512) for _ in range(4)])
        rf = Ring('ff', [ar.f32(512) for _ in range(3)])
        rb = Ring('fb', [ar.bf16(512) for _ in range(4)])
        rx = Ring('fx', [ar.bf16(512) for _ in range(4)])
        rg = Ring('fg', [ar.bf16(512) for _ in range(4)])
        rt = Ring('ft', [ar.f32(512) for _ in range(3)])
        rss = Ring('fss', [ar.f32(8) for _ in range(2)])
        ry = Ring('fy', [ar.bf16(512) for _ in range(3)])
        loaded = {}

        def load(t):
            rows = slice(t * 128, (t + 1) * 128)
            fl, fltok = rfl.next()
            b, btok = rb.next()
            xs, xstok = rx.next()
            g, gtok = rg.next()
            s.dma('sp', fl, Of[rows, :], w=[fltok])
            s.dma('sp', b, Ob[rows, :], w=[btok])
            s.dma('sp', xs, Va[rows, :], w=[xstok])
            s.dma('sp', g, Ga[rows, :], w=[gtok])
            loaded[t] = (fl, fltok, b, btok, xs, xstok, g, gtok)

        load(0)
        if NT > 1:
            load(1)
        stA = {}

        def stage_a(t):
            if t + 2 < NT:
                load(t + 2)
            fl, fltok, b, btok, xs, xstok, g, gtok = loaded.pop(t)
            f, ftok = rf.next()
            s.op('dve', TT(f, fl, b, ALU.add), r=[fltok, btok], w=[ftok])
            tmp, tmptok = rt.next()
            s.op('dve', TT(r3(tmp, b=64), r3(xs, b=64), dsk[:, 0:8].unsqueeze(2).broadcast_to([128, 8, 64]), ALU.mult), r=[xstok, 'dsk'], w=[tmptok])
            s.op('dve', TT(f, f, tmp, ALU.add), r=[ftok, tmptok], w=[ftok])
            s.op('dve', TT(f, f, g, ALU.mult), r=[ftok, gtok], w=[ftok])
            ss, sstok = rss.next()
            s.op('act', ACT(tmp, f, AF.Square, accum=ss[:, 0:1]), r=[ftok], w=[tmptok, sstok])
            s.op('act', ACT(ss[:, 1:2], ss[:, 0:1], AF.Ln, bias=epsf[:, 0:1], scale=1.0 / 512.0), r=[sstok, 'epsf'], w=[sstok])
            s.op('act', ACT(ss[:, 2:3], ss[:, 1:2], AF.Exp, scale=-0.5), r=[sstok], w=[sstok])
            stA[t] = (f, ftok, ss, sstok)

        def stage_b(t):
            rows = slice(t * 128, (t + 1) * 128)
            f, ftok, ss, sstok = stA.pop(t)
            y, ytok = ry.next()
            s.op('dve', STT(y, f, ss[:, 2:3], ngb, ALU.mult, ALU.mult), r=[ftok, sstok, 'ngb'], w=[ytok])
            s.dma('sp', Y[rows, 512:1024], y, r=[ytok], w=[tok('ydst')])

        stage_a(0)
        for t in range(NT):
            if t + 1 < NT:
                stage_a(t + 1)
            stage_b(t)

    qv = qTb.rearrange("(u p) t -> u p t", p=128)
    kv = kTb.rearrange("(u p) t -> u p t", p=128)
    lv = lfT.rearrange("d (u p) t -> d u p t", p=128)
    hq = QTa.rearrange("(u p) t -> u p t", p=128)
    hk = hkT.rearrange("d (u p) t -> d u p t", p=128)
    hl = hlfT.rearrange("d (u p) t -> d u p t", p=128)
    phases = [
        layer0,
        na_phase,
        lambda: recurrence(2, 2, 64, 128, 1.0 / 16.0, [qv[0], qv[1]], [[kv[0], kv[1]]] * 2,
                           [[lv[0, 0], lv[0, 1]], [lv[1, 0], lv[1, 1]]], Vb, 128),
        lambda: rec_final(Gb, 512),
        lambda: phase_c(0, x_in, e_w_out, X1 if nlayers > 1 else out),
    ]
    if nlayers > 1:
        phases += [
            layer1_a,
            lambda: recurrence_b(QTa, [hkT[0], hkT[1]], [hlfT[0], hlfT[1]], Vb),
            lambda: rec_final(Gb, 0),
            ssd_conv,
            ssd_main,
            ssd_final,
            lambda: phase_c(1, X1, o_w_out, out),
        ]
    for ph in phases[:stop]:
        ph()

    s.barrier()
    with nc.Block() as block:
        s.emit(block)
    return nc, es


def _na_btab(rpb, ROWS):
    H = rpb.shape[0]
    out = np.full((5, 128, H, 5, 128), NEG, np.float32)
    reps = {0: 0, 1: 2, 2: 4, 3: ROWS - 4, 4: ROWS - 2}
    p = np.arange(128)
    q = np.arange(128)
    for cls, r in reps.items():
        ks = min(max(r - 4, 0), ROWS - 10)
        for blk in range(5):
            KR = ks + (blk * 128 + p) // 64
            kc = p % 64
            R = r + q // 64
            qc = q % 64
            rs = np.clip(R - 4, 0, ROWS - 8)
            cs = np.clip(qc - 8, 0, 48)
            vr = (KR[:, None] >= rs[None, :]) & (KR[:, None] < rs[None, :] + 8)
            vc = (kc[:, None] >= cs[None, :]) & (kc[:, None] < cs[None, :] + 16)
            dr = np.clip(KR[:, None] - R[None, :] + 7, 0, 14)
            dc = np.clip(kc[:, None] - qc[None, :], -15, 15) + 15
            g = rpb[:, dr, dc]
            valid = (vr & vc)[None]
            out[cls, :, :, blk, :] = np.where(valid, g, NEG).transpose(1, 0, 2)
    return out.reshape(5, 128, H * 640)


def prep_inputs(b, L, x, c, ada_w, ada_b, ln_g, ln_b, e_w_in, e_rpb, e_gla_w_up, e_gla_b, e_gla_norm_g, e_w_out,
                o_w_in, hgrn_lb, o_hgrn_norm_g, o_conv_w, o_conv_b, o_dt_bias, o_a_log, o_d_skip, o_ssm_norm_g, o_w_out):
    f = lambda a: np.ascontiguousarray(np.asarray(a, dtype=np.float32))
    m = {}
    m["x"] = f(x[b])
    m["cT"] = f(c[b].reshape(8, 128).T)
    m["ada_w"] = f(ada_w)
    m["ada_bT"] = f(ada_b.reshape(2, 24, 128).transpose(2, 0, 1))
    m["ada_bg"] = f(ada_b[:, 2048:3072])
    m["ln_g"] = f(ln_g)
    m["ln_b"] = f(ln_b)
    m["e_w_in"] = f(e_w_in[0])
    m["e_w_out"] = f(e_w_out[0])
    m["btab"] = f(_na_btab(np.asarray(e_rpb[0]), L // 64))
    m["gla_w_up"] = f(e_gla_w_up[0])
    m["gla_bT"] = f(e_gla_b[0].reshape(2, 2, 128).transpose(2, 0, 1))
    m["gla_ng"] = f(e_gla_norm_g)
    m["o_w_in"] = f(o_w_in[0])
    m["o_w_out"] = f(o_w_out[0])
    m["hgrn_lbT"] = f(hgrn_lb.reshape(2, 4, 128).transpose(2, 0, 1))
    m["hgrn_ng"] = f(o_hgrn_norm_g)
    m["conv_wT"] = f(o_conv_w[0].reshape(4, 8, 128).transpose(2, 1, 0))
    m["conv_bT"] = f(o_conv_b[0].reshape(8, 128).T)
    m["dt_bias"] = f(o_dt_bias[0].reshape(1, 16))
    m["a_log"] = f(o_a_log[0].reshape(1, 16))
    m["d_skip"] = f(o_d_skip.reshape(1, 8))
    m["ssm_ng"] = f(o_ssm_norm_g.reshape(1, 512))
    return m


def kernel(**inputs):
    x = np.asarray(inputs["x"])
    B, L, _ = x.shape
    nc, es = build(L)
    in_maps = [prep_inputs(b, L, **inputs) for b in range(B)]
    res = run_bass_kernel_spmd(nc, in_maps, core_ids=list(range(B)))
    return np.stack([np.asarray(r["out"], dtype=np.float32) for r in res.results], axis=0)
```

```python
import numpy as np
from contextlib import ExitStack
import concourse.bass as bass
import concourse.mybir as mybir
from concourse.bass_utils import run_bass_kernel_spmd

F32 = mybir.dt.float32
BF16 = mybir.dt.bfloat16
AF = mybir.ActivationFunctionType
ALU = mybir.AluOpType

D = 1024
NSLOT = 10
ALPHA = 4.0 ** 0.25
NEG = -30000.0


class Sched:
    ENGS = ('pe', 'act', 'dve', 'pool', 'sp')

    def __init__(self, nc, es):
        self.nc = nc
        self.streams = {e: [] for e in self.ENGS}
        self.cnt = {e: 0 for e in self.ENGS}
        self.sem = {e: es.enter_context(nc.semaphore('s_' + e)) for e in self.ENGS}
        self.waited = {e: {} for e in self.ENGS}
        self.lastw = {}
        self.readers = {}
        self.dslots = {}
        self.dnext = {}
        for q in ('sp', 'pool', 'act'):
            self.dslots[q] = [[es.enter_context(nc.semaphore('d_%s%d' % (q, i))), 0] for i in range(NSLOT)]
            self.dnext[q] = 0

    def _semh(self, key):
        if isinstance(key, str):
            return self.sem[key]
        return self.dslots[key[1]][key[2]][0]

    def _need(self, eng, dep):
        key, val = dep
        if key == eng and eng == 'pe':
            return
        if self.waited[eng].get(key, 0) >= val:
            return
        self.waited[eng][key] = val
        self.streams[eng].append(('w', key, val))

    def _deps(self, eng, r, w):
        for t in r:
            d = self.lastw.get(t)
            if d:
                self._need(eng, d)
        for t in w:
            d = self.lastw.get(t)
            if d:
                self._need(eng, d)
            rd = self.readers.get(t)
            if rd:
                for k, v in rd.items():
                    self._need(eng, (k, v))

    def _commit(self, dep, r, w):
        for t in r:
            rd = self.readers.setdefault(t, {})
            if rd.get(dep[0], 0) < dep[1]:
                rd[dep[0]] = dep[1]
        for t in w:
            self.lastw[t] = dep
            self.readers[t] = {}

    def op(self, eng, fn, r=(), w=()):
        self._deps(eng, r, w)
        self.cnt[eng] += 1
        self.streams[eng].append(('o', fn))
        self._commit((eng, self.cnt[eng]), r, w)

    def dma(self, q, out, in_, r=(), w=()):
        self._deps(q, r, w)
        i = self.dnext[q]
        self.dnext[q] = (i + 1) % NSLOT
        slot = self.dslots[q][i]
        key = ('d', q, i)
        if slot[1] > 0:
            self._need(q, (key, slot[1]))
        slot[1] += 16
        self.streams[q].append(('d', out, in_, key))
        self._commit((key, slot[1]), r, w)

    def barrier(self):
        deps = [(e, self.cnt[e]) for e in self.ENGS if self.cnt[e] > 0]
        for q in self.dslots:
            for i, sl in enumerate(self.dslots[q]):
                if sl[1] > 0:
                    deps.append((('d', q, i), sl[1]))
        for e in self.ENGS:
            for d in deps:
                self._need(e, d)

    def emit(self, block):
        decos = {'pe': block.tensor, 'act': block.scalar, 'dve': block.vector, 'pool': block.gpsimd, 'sp': block.sync}
        for e in self.ENGS:
            stream = self.streams[e]

            def body(eng, stream=stream, e=e):
                for it in stream:
                    if it[0] == 'w':
                        eng.wait_ge(self._semh(it[1]), it[2])
                    elif it[0] == 'o':
                        it[1](eng).then_inc(self.sem[e], 1)
                    else:
                        eng.dma_start(out=it[1], in_=it[2]).then_inc(self._semh(it[3]), 16)
            decos[e](body)


class Arena:
    def __init__(self, ap, ncols):
        self.ap = ap
        self.n = ncols
        self.pos = 0

    def reset(self, base=0):
        self.pos = base

    def f32(self, cols, shape=None):
        a = self.pos
        self.pos += cols
        assert self.pos <= self.n, ("arena overflow", self.pos, self.n)
        v = self.ap[:, a:a + cols]
        return v

    def bf16(self, cols):
        c32 = (cols + 1) // 2
        v = self.f32(c32).bitcast(BF16)
        return v[:, 0:cols]


def r3(ap, **kw):
    k = list(kw.keys())[0]
    return ap.rearrange("p (a %s) -> p a %s" % (k, k), **kw)


def MM(out, lhsT, rhs, start=True, stop=True, skip=False):
    if skip:
        return lambda e: e.matmul(out, lhsT=lhsT, rhs=rhs, start=start, stop=stop, skip_group_check=True)
    return lambda e: e.matmul(out, lhsT=lhsT, rhs=rhs, start=start, stop=stop)


def TR(out, in_, ident):
    return lambda e: e.transpose(out, in_, ident)


def ACT(out, in_, func, bias=None, scale=None, accum=None):
    kw = {}
    if bias is not None:
        kw['bias'] = bias
    if scale is not None:
        kw['scale'] = scale
    if accum is not None:
        kw['accum_out'] = accum
    return lambda e: e.activation(out=out, in_=in_, func=func, **kw)


def TT(out, in0, in1, op):
    return lambda e: e.tensor_tensor(out=out, in0=in0, in1=in1, op=op)


def TS(out, in0, s1, op0, s2=None, op1=None):
    if op1 is None:
        return lambda e: e.tensor_scalar(out=out, in0=in0, scalar1=s1, scalar2=None, op0=op0)
    return lambda e: e.tensor_scalar(out=out, in0=in0, scalar1=s1, scalar2=s2, op0=op0, op1=op1)


def STT(out, in0, scalar, in1, op0, op1):
    return lambda e: e.scalar_tensor_tensor(out=out, in0=in0, scalar=scalar, in1=in1, op0=op0, op1=op1)


def CP(out, in_):
    return lambda e: e.tensor_copy(out=out, in_=in_)


def MS(ap, c):
    return lambda e: e.memset(ap, c)


def build(L, nlayers=2, dbg=(), stop=99):
    nc = bass.Bass("TRN2", target_bir_lowering=False)
    NT = L // 128
    NS = L // 512
    ROWS = L // 64
    es = ExitStack()

    def din(name, shape, dt=F32):
        return nc.dram_tensor(name, list(shape), dt, kind="ExternalInput").ap()

    def dscr(name, shape, dt):
        kind = "ExternalOutput" if name in dbg else "Internal"
        return nc.dram_tensor(name, list(shape), dt, kind=kind).ap()

    x_in = din("x", [L, D])
    cT_in = din("cT", [128, 8])
    ada_w = din("ada_w", [2, D, 3 * D])
    ada_bT = din("ada_bT", [128, 2, 24])
    ada_bg = din("ada_bg", [2, D])
    ln_g = din("ln_g", [2, D])
    ln_b = din("ln_b", [2, D])
    e_w_in = din("e_w_in", [D, 3616])
    e_w_out = din("e_w_out", [D, D])
    btab = din("btab", [5, 128, 8 * 640])
    gla_w_up = din("gla_w_up", [2, 16, 256])
    gla_bT = din("gla_bT", [128, 2, 2])
    gla_ng = din("gla_ng", [1, 128])
    o_w_in = din("o_w_in", [D, 4112])
    o_w_out = din("o_w_out", [D, D])
    hgrn_lbT = din("hgrn_lbT", [128, 2, 4])
    hgrn_ng = din("hgrn_ng", [1, 128])
    conv_wT = din("conv_wT", [128, 8, 4])
    conv_bT = din("conv_bT", [128, 8])
    dt_bias = din("dt_bias", [1, 16])
    a_log = din("a_log", [1, 16])
    d_skip = din("d_skip", [1, 8])
    ssm_ng = din("ssm_ng", [1, 512])
    out = nc.dram_tensor("out", [L, D], F32, kind="ExternalOutput").ap()

    QTa = dscr("QTa", [512, L], BF16)
    KTa = dscr("KTa", [512, L], BF16)
    Va = dscr("Va", [L, 512], BF16)
    Va66 = dscr("Va66", [L, 528], BF16)
    Ga = dscr("Ga", [L, 512], BF16)
    qTb = dscr("qTb", [256, L], BF16)
    kTb = dscr("kTb", [256, L], BF16)
    Vb = dscr("Vb", [L, 512], BF16)
    Gb = dscr("Gb", [L, 512], BF16)
    lfT = dscr("lfT", [2, 256, L], F32)
    Of = dscr("Of", [L, 512], BF16)
    Ob = dscr("Ob", [L, 512], BF16)
    Y = dscr("Y", [L, D], BF16)
    X1 = dscr("X1", [L, D], F32)

    def sb(name, shape, dt):
        return es.enter_context(nc.sbuf_tensor(name, list(shape), dt))

    ident_f = sb("ident_f", [128, 128], F32)
    ident_b = sb("ident_b", [128, 128], BF16)
    ones_f = sb("ones_f", [128, 128], F32)
    mask_f128 = sb("mask_f128", [128, 128], BF16)
    mask_b128 = sb("mask_b128", [128, 128], BF16)
    mask_f32 = sb("mask_f32", [128, 128], BF16)
    mask_b32 = sb("mask_b32", [128, 128], BF16)
    seg128 = sb("seg128", [128, 512], F32)
    seg32 = sb("seg32", [128, 512], F32)
    modT = sb("modT", [128, 2, 16], F32)
    gate_bc = sb("gate_bc", [128, 2, D], F32)
    small = sb("small", [128, 64], F32)
    AW = 42000
    arena_t = sb("arena", [128, AW], F32)
    ar = Arena(arena_t, AW)
    banks = [es.enter_context(nc.psum_tensor("bank%d" % i, [128, 512], F32)) for i in range(8)]

    s = Sched(nc, es)
    uid = [0]

    def tok(prefix):
        uid[0] += 1
        return "%s#%d" % (prefix, uid[0])

    class Ring:
        def __init__(self, name, aps):
            self.aps = aps
            self.toks = [tok(name) for _ in aps]
            self.i = -1

        def next(self):
            self.i = (self.i + 1) % len(self.aps)
            return self.aps[self.i], self.toks[self.i]

    s.op('pool', MS(ident_f[:], 0.0), w=['ident_f'])
    s.op('pool', lambda e: e.affine_select(out=ident_f[:], in_=ident_f[:], pattern=[[-1, 128]], compare_op=ALU.not_equal,
                                           fill=1.0, base=0, channel_multiplier=1), r=['ident_f'], w=['ident_f'])
    s.op('pool', CP(ident_b[:], ident_f[:]), r=['ident_f'], w=['ident_b'])
    s.op('pool', MS(ones_f[:], 1.0), w=['ones_f'])
    s.op('pool', MS(mask_f128[:], 1.0), w=['mask_f128'])
    s.op('pool', lambda e: e.affine_select(out=mask_f128[:], in_=mask_f128[:], pattern=[[1, 128]], compare_op=ALU.is_ge,
                                           fill=0.0, base=0, channel_multiplier=-1), r=['mask_f128'], w=['mask_f128'])
    s.op('pool', MS(mask_b128[:], 1.0), w=['mask_b128'])
    s.op('pool', lambda e: e.affine_select(out=mask_b128[:], in_=mask_b128[:], pattern=[[-1, 128]], compare_op=ALU.is_ge,
                                           fill=0.0, base=0, channel_multiplier=1), r=['mask_b128'], w=['mask_b128'])
    s.op('pool', CP(mask_f32[:], mask_f128[:]), r=['mask_f128'], w=['mask_f32'])
    s.op('pool', CP(mask_b32[:], mask_b128[:]), r=['mask_b128'], w=['mask_b32'])
    for cb in range(4):
        s.op('pool', (lambda cb: lambda e: e.affine_select(out=mask_f32[:, 32 * cb:32 * cb + 32], in_=mask_f32[:, 32 * cb:32 * cb + 32],
                                                           pattern=[[0, 32]], compare_op=ALU.is_ge, fill=0.0, base=-32 * cb,
                                                           channel_multiplier=1))(cb), r=['mask_f32'], w=['mask_f32'])
        s.op('pool', (lambda cb: lambda e: e.affine_select(out=mask_b32[:, 32 * cb:32 * cb + 32], in_=mask_b32[:, 32 * cb:32 * cb + 32],
                                                           pattern=[[0, 32]], compare_op=ALU.is_ge, fill=0.0, base=32 * cb + 31,
                                                           channel_multiplier=-1))(cb), r=['mask_b32'], w=['mask_b32'])
    s.op('pool', MS(seg128[:], 1.0), w=['seg128'])
    s.op('pool', MS(seg32[:], 1.0), w=['seg32'])
    s.op('pool', MS(r3(seg128[:], b=128)[:, :, 0:1], 0.0), r=['seg128'], w=['seg128'])
    s.op('pool', MS(r3(seg32[:], b=32)[:, :, 0:1], 0.0), r=['seg32'], w=['seg32'])

    WIN = {}

    W_COLS = 8 * 4112 // 2

    def seg_order(c0s):
        order = []
        for c0 in c0s:
            if c0 // 512 not in order:
                order.append(c0 // 512)
        return order

    def prefetch_w_in(src, ncols, order):
        w_in = r3(arena_t[:, 0:W_COLS].bitcast(BF16), b=4112)
        WIN['w'] = w_in
        v = src.rearrange("(k p) c -> p k c", p=128)
        for pc in order:
            c0, c1 = pc * 512, min(ncols, (pc + 1) * 512)
            for k0 in (0, 4):
                s.dma('pool', w_in[:, k0:k0 + 4, c0:c1], v[:, k0:k0 + 4, c0:c1], w=['w_in%d' % pc])

    L0_C0S = [0, 128, 256, 384, 512, 640, 768, 896, 2048, 2176, 2304, 2432, 1024, 2560, 1536, 3072, 3584, 3600]
    L1_C0S = [0, 512, 1024, 1536, 2048, 2560, 3072, 3584, 4096]

    ar.reset(W_COLS)
    prefetch_w_in(e_w_in, 3616, seg_order(L0_C0S))
    cT = ar.f32(8)
    scT = ar.f32(16)
    sc_rep = ar.f32(8 * 128)
    abT = ar.f32(48)
    abg = ar.f32(2 * D)
    slabG = [ar.f32(8 * 512) for _ in range(3)]
    s.dma('sp', cT, cT_in[:, :], w=['cT'])
    s.dma('sp', abT, ada_bT.rearrange("p l c -> p (l c)"), w=['abT'])
    s.dma('sp', abg, ada_bg.rearrange("l d -> (l d)").rearrange("(o n) -> o n", o=1).partition_broadcast(128), w=['abg'])
    sc3 = r3(scT, b=2)
    s.op('act', ACT(sc3[:, :, 0], cT, AF.Silu), r=['cT'], w=['scT'])
    s.op('act', ACT(sc3[:, :, 1], cT, AF.Silu), r=['cT'], w=['scT'])
    scr3 = r3(sc_rep, b=128)
    for k in range(8):
        s.op('dve', TS(scr3[:, k, :], ones_f[:], sc3[:, k, 0:1], ALU.mult), r=['scT', 'ones_f'], w=['sc_rep'])
    rG = Ring('slabG', slabG)
    pmod = Ring('pmod', [banks[0], banks[1], banks[2], banks[3]])
    for l in range(nlayers):
        wv = ada_w[l].rearrange("(k p) c -> p k c", p=128)
        for sl_i in range(6):
            sl, st = rG.next()
            sl3 = r3(sl, b=512)
            s.dma('sp', sl3, wv[:, :, sl_i * 512:(sl_i + 1) * 512], w=[st])
            if sl_i < 4:
                for c4 in range(4):
                    cb = sl_i * 4 + c4
                    ps, pt = pmod.next()
                    for k in range(8):
                        s.op('pe', MM(ps[:, 0:2], sl3[:, k, c4 * 128:(c4 + 1) * 128], sc3[:, k, :], k == 0, k == 7), r=[st, 'scT'], w=[pt])
                    if cb < 8:
                        s.op('dve', TT(modT[:, l, cb:cb + 1], ps[:, 0:1], abT[:, l * 24 + cb:l * 24 + cb + 1], ALU.add),
                             r=[pt, 'abT'], w=['modT'])
                    else:
                        s.op('dve', STT(modT[:, l, cb:cb + 1], ps[:, 0:1], 1.0, abT[:, l * 24 + cb:l * 24 + cb + 1], ALU.add, ALU.add),
                             r=[pt, 'abT'], w=['modT'])
            else:
                hf = sl_i - 4
                ps, pt = pmod.next()
                for k in range(8):
                    s.op('pe', MM(ps[:, :], scr3[:, k, :], sl3[:, k, :], k == 0, k == 7), r=[st, 'sc_rep'], w=[pt])
                s.op('dve', TT(gate_bc[:, l, hf * 512:(hf + 1) * 512], ps[:, :], abg[:, l * D + hf * 512:l * D + (hf + 1) * 512], ALU.add),
                     r=[pt, 'abg'], w=['gate_bc'])

    def phase_a(l, x_src, segs):
        xts = [ar.f32(4 * D) for _ in range(2)]
        hTs = [ar.bf16(8 * 512) for _ in range(2)]
        rx = Ring('xt', xts)
        rh = Ring('hT', hTs)
        ptr = Ring('ptr', [banks[0], banks[1]])
        ppj = Ring('ppj', [banks[2], banks[3], banks[4], banks[5]])
        xv = x_src.rearrange("(n s p) d -> n p s d", p=128, s=4)
        state = {}

        xloaded = {}

        def load_x(i):
            xt, xtok = rx.next()
            xt3 = r3(xt, b=D)
            s.dma('sp', xt3, xv[i], w=[xtok])
            xloaded[i] = (xt3, xtok)

        def load_and_transpose(i):
            if i not in xloaded:
                load_x(i)
            xt3, xtok = xloaded.pop(i)
            hT, htok = rh.next()
            hT3 = r3(hT, b=512)
            for k in range(8):
                ps, pt = ptr.next()
                for sub in range(4):
                    s.op('pe', TR(ps[:, sub * 128:(sub + 1) * 128], xt3[:, sub, k * 128:(k + 1) * 128], ident_f[:]),
                         r=[xtok, 'ident_f'], w=[pt])
                s.op('act', ACT(hT3[:, k, :], ps[:, :], AF.Identity, bias=modT[:, l, k:k + 1], scale=modT[:, l, 8 + k:9 + k]),
                     r=[pt, 'modT'], w=[htok])
            state[i] = (hT3, htok)

        load_and_transpose(0)
        for i in range(NS):
            if i + 1 < NS:
                load_x(i + 1)
            hT3, htok = state.pop(i)
            for si, sg in enumerate(segs):
                if si == len(segs) // 2 and i + 1 < NS:
                    load_and_transpose(i + 1)
                if sg['kind'] == 'fm':
                    ps, pt = ppj.next()
                    n = sg['n']
                    for k in range(8):
                        s.op('pe', MM(ps[0:n, :], WIN['w'][:, k, sg['c0']:sg['c0'] + n], hT3[:, k, :], k == 0, k == 7),
                             r=['w_in%d' % (sg['c0'] // 512), htok], w=[pt])
                    sg['emit'](ps, pt, i, None)
                else:
                    n = sg['n']
                    for sub in range(4):
                        ps, pt = ppj.next()
                        for k in range(8):
                            s.op('pe', MM(ps[:, 0:n], hT3[:, k, sub * 128:(sub + 1) * 128], WIN['w'][:, k, sg['c0']:sg['c0'] + n], k == 0, k == 7),
                                 r=['w_in%d' % (sg['c0'] // 512), htok], w=[pt])
                        sg['emit'](ps, pt, i, sub)

    def phase_c(l, x_src, w_out_src, x_dst):
        s.barrier()
        if l == 0 and nlayers > 1:
            ar.reset(W_COLS)
            prefetch_w_in(o_w_in, 4112, seg_order(L1_C0S))
        else:
            ar.reset()
        wo = r3(ar.bf16(8 * D), b=D)
        wov = w_out_src.rearrange("(k p) c -> p k c", p=128)
        wst = Ring('wst', [ar.f32(D) for _ in range(2)])
        for k in range(8):
            st, sttok = wst.next()
            s.dma('sp', st, wov[:, k, :], w=[sttok])
            s.op('dve', TT(wo[:, k, :], st, gate_bc[:, l, :], ALU.mult), r=[sttok, 'gate_bc'], w=['wo'])
        lng = ar.f32(D)
        lnb = ar.f32(D)
        epsc = ar.f32(2)
        s.op('pool', MS(epsc, 1e-5), w=['epsc'])
        s.dma('sp', lng, ln_g[l:l + 1, :].partition_broadcast(128), w=['lng'])
        s.dma('sp', lnb, ln_b[l:l + 1, :].partition_broadcast(128), w=['lnb'])
        ry = Ring('cy', [ar.bf16(D) for _ in range(4)])
        rxt = Ring('cx', [ar.f32(D) for _ in range(4)])
        ryt = Ring('cyT', [r3(ar.bf16(8 * 128), b=128) for _ in range(2)])
        rz = Ring('cz', [ar.f32(D) for _ in range(3)])
        ro = Ring('co', [ar.f32(D) for _ in range(3)])
        rst = Ring('cst', [ar.f32(24) for _ in range(3)])
        ptr = Ring('cptr', [banks[0], banks[1]])
        pmm = Ring('cpmm', [banks[2], banks[3], banks[4], banks[5]])
        loaded = {}

        def load(t):
            yt, ytok = ry.next()
            s.dma('sp', yt, Y[t * 128:(t + 1) * 128, :], w=[ytok])
            xt, xtok = rxt.next()
            s.dma('sp', xt, x_src[t * 128:(t + 1) * 128, :], w=[xtok])
            loaded[t] = (yt, ytok, xt, xtok)

        load(0)
        if NT > 1:
            load(1)
        stA = {}

        def stage_a(t):
            if t + 2 < NT:
                load(t + 2)
            yt, ytok, xt, xtok = loaded.pop(t)
            yT, yTtok = ryt.next()
            for half in range(2):
                ps, pt = ptr.next()
                psb = ps[:, :].bitcast(BF16)
                for kk in range(4):
                    k = half * 4 + kk
                    s.op('pe', TR(psb[:, kk * 128:(kk + 1) * 128], yt[:, k * 128:(k + 1) * 128], ident_b[:]), r=[ytok, 'ident_b'], w=[pt])
                s.op('act', ACT(yT[:, half * 4:half * 4 + 4, :], r3(psb[:, 0:512], b=128), AF.Copy), r=[pt], w=[yTtok])
            z, ztok = rz.next()
            for hf in range(2):
                ps, pt = pmm.next()
                for k in range(8):
                    s.op('pe', MM(ps[:, :], yT[:, k, :], wo[:, k, hf * 512:(hf + 1) * 512], k == 0, k == 7), r=[yTtok, 'wo'], w=[pt])
                s.op('dve', STT(z[:, hf * 512:(hf + 1) * 512], xt[:, hf * 512:(hf + 1) * 512], ALPHA, ps[:, :], ALU.mult, ALU.add),
                     r=[pt, xtok], w=[ztok])
            st, sttok = rst.next()
            s.op('dve', lambda e, st=st, z=z: e.bn_stats(out=st[:, 0:6], in_=z[:, 0:512]), r=[ztok], w=[sttok])
            s.op('dve', lambda e, st=st, z=z: e.bn_stats(out=st[:, 6:12], in_=z[:, 512:1024]), r=[ztok], w=[sttok])
            s.op('dve', lambda e, st=st: e.bn_aggr(out=st[:, 12:14], in_=st[:, 0:12]), r=[sttok], w=[sttok])
            s.op('act', ACT(st[:, 14:15], st[:, 13:14], AF.Ln, bias=epsc[:, 0:1]), r=[sttok, 'epsc'], w=[sttok])
            s.op('act', ACT(st[:, 15:16], st[:, 14:15], AF.Exp, scale=-0.5), r=[sttok], w=[sttok])
            stA[t] = (z, ztok, st, sttok)

        def stage_b(t):
            z, ztok, st, sttok = stA.pop(t)
            o, otok = ro.next()
            s.op('dve', TS(o, z, st[:, 12:13], ALU.subtract, st[:, 15:16], ALU.mult), r=[ztok, sttok], w=[otok])
            s.op('dve', TT(o, o, lng, ALU.mult), r=[otok, 'lng'], w=[otok])
            s.op('dve', TT(o, o, lnb, ALU.add), r=[otok, 'lnb'], w=[otok])
            s.dma('sp', x_dst[t * 128:(t + 1) * 128, :], o, r=[otok], w=[tok('xdst')])

        stage_a(0)
        for t in range(NT):
            if t + 1 < NT:
                stage_a(t + 1)
            stage_b(t)

    def recurrence(nunits, nh, dk, CS, sc, qT_src, kT_src, lf_src, V_src, dvw):
        s.barrier()
        ar.reset()
        nsub = 128 // CS
        segm = seg128 if CS == 128 else seg32
        segk = 'seg128' if CS == 128 else 'seg32'
        vw = nh * 128
        chains = [(u, d) for u in range(nunits) for d in range(2)]
        tq = Ring('tq', [ar.bf16(512) for _ in range(4)])
        tk = Ring('tk', [ar.bf16(512) for _ in range(4)])
        tlf = Ring('tlf', [ar.f32(512) for _ in range(4)])
        tP = Ring('tP', [ar.f32(512) for _ in range(2)])
        tB = Ring('tB', [ar.f32(512) for _ in range(2)])
        tR = Ring('tR', [ar.f32(512) for _ in range(2)])
        tE = Ring('tE', [ar.bf16(512) for _ in range(4)])
        C = {}
        for ch in chains:
            C[ch] = dict(
                qs=Ring('qs', [ar.bf16(512) for _ in range(2)]),
                kh=Ring('kh', [ar.bf16(512) for _ in range(2)]),
                kb=Ring('kb', [ar.bf16(512) for _ in range(2)]),
                V=Ring('V', [r3(ar.bf16(4 * vw), b=vw) for _ in range(2)]),
                dch=Ring('dch', [ar.f32(16) for _ in range(2)]),
                S=ar.f32(128), Stok=tok('S'),
                Sbf=Ring('Sbf', [ar.bf16(128) for _ in range(nsub + 2)]),
                ATs=Ring('ATs', [r3(ar.bf16(nh * 128), b=128) for _ in range(2)]),
                kbt=Ring('kbt', [ar.bf16(128) for _ in range(2)]),
                ost=Ring('ost', [ar.bf16(vw) for _ in range(2)]),
            )
        if nsub > 1:
            cmask = r3(ar.bf16(nsub * 128), b=128)
            rmask = r3(ar.bf16(nsub * 128), b=128)
            s.op('pool', MS(cmask, 0.0), w=['cmask'])
            s.op('pool', MS(rmask, 1.0), w=['rmask'])
            for ci in range(nsub):
                s.op('pool', MS(cmask[:, ci, ci * CS:(ci + 1) * CS], 1.0), r=['cmask'], w=['cmask'])
                s.op('pool', (lambda ci: lambda e: e.affine_select(out=rmask[:, ci, :], in_=rmask[:, ci, :], pattern=[[0, 128]],
                                                                   compare_op=ALU.is_ge, fill=0.0, base=-CS * ci, channel_multiplier=1))(ci),
                     r=['rmask'], w=['rmask'])
                s.op('pool', (lambda ci: lambda e: e.affine_select(out=rmask[:, ci, :], in_=rmask[:, ci, :], pattern=[[0, 128]],
                                                                   compare_op=ALU.is_ge, fill=0.0, base=CS * ci + CS - 1, channel_multiplier=-1))(ci),
                     r=['rmask'], w=['rmask'])
            for ch in chains:
                C[ch]['qsm'] = Ring('qsm', [ar.bf16(128) for _ in range(2)])
                C[ch]['kbtm'] = Ring('kbtm', [ar.bf16(128) for _ in range(2)])
        if nh == 2:
            pAT = Ring('pAT', [[banks[0][:, 0:128], banks[1][:, 0:128]]])
        else:
            pAT = Ring('pAT', [[banks[0][:, 0:128]], [banks[1][:, 0:128]]])
        pU = Ring('pU', [banks[4], banks[5]])
        pT = Ring('pT', [banks[6], banks[7]])
        cur = {}
        for ch in chains:
            c = C[ch]
            s.op('pool', MS(c['S'], 0.0), w=[c['Stok']])
            sb0, sbt0 = c['Sbf'].next()
            s.op('pool', MS(sb0, 0.0), w=[sbt0])
            c['sprev'] = (sb0, sbt0)

        pre = {}

        def prep_load(ch, st_i):
            u, d = ch
            c = C[ch]
            t0 = st_i * 512
            q, qtok = tq.next()
            k, ktok = tk.next()
            lf, lftok = tlf.next()
            s.dma('sp', lf, lf_src[d][u][:, t0:t0 + 512], w=[lftok])
            s.dma('sp', q, qT_src[u][:, t0:t0 + 512], w=[qtok])
            s.dma('sp', k, kT_src[d][u][:, t0:t0 + 512], w=[ktok])
            V, Vtok = c['V'].next()
            s.dma('sp', V, V_src[t0:t0 + 512, u * vw:(u + 1) * vw].rearrange("(s p) c -> p s c", p=128), w=[Vtok])
            pre[(ch, st_i)] = (q, qtok, k, ktok, lf, lftok, V, Vtok)

        def prep(ch, st_i):
            u, d = ch
            c = C[ch]
            t0 = st_i * 512
            if (ch, st_i) not in pre:
                prep_load(ch, st_i)
            q, qtok, k, ktok, lf, lftok, V, Vtok = pre.pop((ch, st_i))
            P, Ptok = tP.next()
            s.op('dve', lambda e, P=P, lf=lf: e.tensor_tensor_scan(out=P, data0=segm[:], data1=lf, initial=0.0, op0=ALU.mult, op1=ALU.add),
                 r=[lftok, segk], w=[Ptok])
            P3 = r3(P, b=CS)
            nchk = 512 // CS
            totb = P3[:, :, CS - 1:CS].broadcast_to([128, nchk, CS])
            B, Btok = tB.next()
            R, Rtok = tR.next()
            if d == 0:
                s.op('dve', TT(r3(R, b=CS), totb, P3, ALU.subtract), r=[Ptok], w=[Rtok])
                Bd, Bdtok = P, Ptok
            else:
                s.op('dve', TT(R, P, lf, ALU.subtract), r=[Ptok, lftok], w=[Rtok])
                s.op('dve', TT(r3(B, b=CS), totb, r3(R, b=CS), ALU.subtract), r=[Ptok, Rtok], w=[Btok])
                Bd, Bdtok = B, Btok
            dch, dchtok = c['dch'].next()
            s.op('act', ACT(dch[:, 0:nchk], P3[:, :, CS - 1], AF.Exp, scale=sc), r=[Ptok], w=[dchtok])
            qs, qstok = c['qs'].next()
            kh, khtok = c['kh'].next()
            kb, kbtok = c['kb'].next()
            e1, e1tok = tE.next()
            s.op('act', ACT(e1, Bd, AF.Exp, scale=sc), r=[Bdtok], w=[e1tok])
            s.op('dve', TT(qs, q, e1, ALU.mult), r=[qtok, e1tok], w=[qstok])
            e2, e2tok = tE.next()
            s.op('act', ACT(e2, Bd, AF.Exp, scale=-sc), r=[Bdtok], w=[e2tok])
            s.op('dve', TT(kh, k, e2, ALU.mult), r=[ktok, e2tok], w=[khtok])
            e3, e3tok = tE.next()
            s.op('act', ACT(e3, R, AF.Exp, scale=sc), r=[Rtok], w=[e3tok])
            s.op('dve', TT(kb, k, e3, ALU.mult), r=[ktok, e3tok], w=[kbtok])
            cur[ch] = dict(qs=qs, qstok=qstok, kh=kh, khtok=khtok, kb=kb, kbtok=kbtok, V=V, Vtok=Vtok, dch=dch, dchtok=dchtok)

        nchain = len(chains)
        cpb = 4
        OT = {}
        for idx, ch in enumerate(chains):
            if nh == 1:
                bk = 2 + idx // cpb
                OT[ch] = dict(tiles=[banks[bk][:, (idx % cpb) * 128:(idx % cpb + 1) * 128]], toks=['pO%d' % bk], first=(idx % cpb == 0))
            else:
                OT[ch] = dict(tiles=[banks[2 + hh][:, idx * 128:(idx + 1) * 128] for hh in range(nh)],
                              toks=['pO%d' % (2 + hh) for hh in range(nh)], first=(idx == 0))
        ctx = {}

        def stage1(ch, st_i, sub):
            u, d = ch
            c = C[ch]
            cc = cur[ch]
            cols = slice(sub * 128, (sub + 1) * 128)
            mask = (mask_f128 if d == 0 else mask_b128) if CS == 128 else (mask_f32 if d == 0 else mask_b32)
            mtok = ('mask_f128' if d == 0 else 'mask_b128') if CS == 128 else ('mask_f32' if d == 0 else 'mask_b32')
            pa, patok = pAT.next()
            for hh in range(nh):
                pr = slice(hh * dk, (hh + 1) * dk)
                s.op('pe', MM(pa[hh], cc['kh'][pr, cols], cc['qs'][pr, cols]), r=[cc['khtok'], cc['qstok']], w=[patok])
            ATs, ATtok = c['ATs'].next()
            for hh in range(nh):
                s.op('dve', TT(ATs[:, hh, :], pa[hh], mask[:], ALU.mult), r=[patok, mtok], w=[ATtok])
            pt_, pttok = pT.next()
            ptb = pt_[:, :].bitcast(BF16)
            s.op('pe', TR(ptb[:, 0:128], cc['kb'][:, cols], ident_b[:]), r=[cc['kbtok'], 'ident_b'], w=[pttok])
            kbt, kbttok = c['kbt'].next()
            s.op('act', ACT(kbt, ptb[:, 0:128], AF.Copy), r=[pttok], w=[kbttok])
            x = dict(cc=cc, sub=sub, st_i=st_i, kbt=kbt, kbttok=kbttok)
            if nsub > 1:
                kb3, kb3tok = c['kbtm'].next()
                s.op('dve', TT(kb3, kbt, rmask[:, nsub - 1, :], ALU.mult), r=[kbttok, 'rmask'], w=[kb3tok])
                qs3, qs3tok = c['qsm'].next()
                s.op('dve', TT(qs3, cc['qs'][:, cols], cmask[:, nsub - 1, :], ALU.mult), r=[cc['qstok'], 'cmask'], w=[qs3tok])
                x.update(kb3=kb3, kb3tok=kb3tok, qs3=qs3, qs3tok=qs3tok)
            ot = OT[ch]
            for hh in range(nh):
                s.op('pe', MM(ot['tiles'][hh], ATs[:, hh, :], cc['V'][:, sub, hh * 128:(hh + 1) * 128], ot['first'], False, True),
                     r=[ATtok, cc['Vtok']], w=[ot['toks'][hh]])
            x['order'] = list(range(nsub)) if d == 0 else list(range(nsub - 1, -1, -1))
            ctx[ch] = x

        def stage2(ch, n_i):
            u, d = ch
            c = C[ch]
            x = ctx[ch]
            cc = x['cc']
            sub = x['sub']
            cidx = x['order'][n_i]
            last = (n_i == nsub - 1)
            rows = slice(cidx * CS, (cidx + 1) * CS)
            ccols = slice(sub * 128 + cidx * CS, sub * 128 + (cidx + 1) * CS)
            sprev, sprevtok = c['sprev']
            ot = OT[ch]
            masked = (nsub > 1 and cidx == nsub - 1)
            for hh in range(nh):
                pr = slice(hh * dk, (hh + 1) * dk)
                if masked:
                    s.op('pe', MM(ot['tiles'][hh], x['qs3'][pr, :], sprev[pr, :], False, last, True), r=[x['qs3tok'], sprevtok], w=[ot['toks'][hh]])
                else:
                    s.op('pe', MM(ot['tiles'][hh][rows, :], cc['qs'][pr, ccols], sprev[pr, :], False, last, True),
                         r=[cc['qstok'], sprevtok], w=[ot['toks'][hh]])
            pu, putok = pU.next()
            for hh in range(nh):
                pr = slice(hh * dk, (hh + 1) * dk)
                if masked:
                    s.op('pe', MM(pu[pr, 0:128], x['kb3'][:, pr], cc['V'][:, sub, hh * 128:(hh + 1) * 128]), r=[x['kb3tok'], cc['Vtok']], w=[putok])
                else:
                    s.op('pe', MM(pu[pr, 0:128], x['kbt'][rows, pr], cc['V'][rows, sub, hh * 128:(hh + 1) * 128]),
                         r=[x['kbttok'], cc['Vtok']], w=[putok])
            chk = sub * nsub + cidx
            s.op('dve', STT(c['S'], c['S'], cc['dch'][:, chk:chk + 1], pu[:, 0:128], ALU.mult, ALU.add),
                 r=[c['Stok'], cc['dchtok'], putok], w=[c['Stok']])
            sbn, sbntok = c['Sbf'].next()
            s.op('act', ACT(sbn, c['S'], AF.Copy), r=[c['Stok']], w=[sbntok])
            c['sprev'] = (sbn, sbntok)

        def stage3(ch):
            u, d = ch
            c = C[ch]
            x = ctx.pop(ch)
            ot = OT[ch]
            tglob = x['st_i'] * 4 + x['sub']
            ost, osttok = c['ost'].next()
            for hh in range(nh):
                s.op('act', ACT(ost[:, hh * 128:(hh + 1) * 128], ot['tiles'][hh], AF.Copy), r=[ot['toks'][hh]], w=[osttok])
            dst = Of if d == 0 else Ob
            s.dma('sp', dst[tglob * 128:(tglob + 1) * 128, u * vw:(u + 1) * vw], ost, r=[osttok], w=[tok('odst')])

        nxt = {}
        for ch in chains:
            prep_load(ch, 0 if ch[1] == 0 else NS - 1)
        for ch in chains:
            prep(ch, 0 if ch[1] == 0 else NS - 1)
        for i in range(NS):
            now = {ch: cur[ch] for ch in chains}
            for sub_i in range(4):
                for ch in chains:
                    st_i = i if ch[1] == 0 else NS - 1 - i
                    sub = sub_i if ch[1] == 0 else 3 - sub_i
                    cur[ch] = now[ch]
                    stage1(ch, st_i, sub)
                for n_i in range(nsub):
                    for ch in chains:
                        stage2(ch, n_i)
                for ch in chains:
                    stage3(ch)
                if sub_i == 0 and i + 1 < NS:
                    for ch in chains:
                        prep_load(ch, (i + 1) if ch[1] == 0 else NS - 2 - i)
                if sub_i == 1 and i + 1 < NS:
                    for ch in chains:
                        prep(ch, (i + 1) if ch[1] == 0 else NS - 2 - i)
                        nxt[ch] = cur[ch]
            if i + 1 < NS:
                for ch in chains:
                    now[ch] = None
                    cur[ch] = nxt[ch]

    def recurrence_b(qsrc, ksrc, lfsrc, V_src):
        s.barrier()
        ar.reset()
        CS, nsub, NU = 32, 4, 4
        segm = ar.f32(1024)
        s.op('pool', MS(segm, 1.0), w=['segm'])
        s.op('pool', MS(r3(segm, b=CS)[:, :, 0:1], 0.0), r=['segm'], w=['segm'])
        c3 = ar.bf16(128)
        r3m = ar.bf16(128)
        s.op('pool', MS(c3, 0.0), w=['c3'])
        s.op('pool', MS(c3[:, 96:128], 1.0), r=['c3'], w=['c3'])
        s.op('pool', MS(r3m, 1.0), w=['r3m'])
        s.op('pool', lambda e: e.affine_select(out=r3m, in_=r3m, pattern=[[0, 128]], compare_op=ALU.is_ge, fill=0.0, base=-96,
                                               channel_multiplier=1), r=['r3m'], w=['r3m'])
        qv = qsrc.rearrange("(u p) t -> p u t", p=128)
        kv_ = [ksrc[d].rearrange("(u p) t -> p u t", p=128) for d in range(2)]
        lv_ = [lfsrc[d].rearrange("(u p) t -> p u t", p=128) for d in range(2)]
        tq = Ring('bq', [r3(ar.bf16(1024), b=512) for _ in range(2)])
        tk = Ring('bk', [r3(ar.bf16(1024), b=512) for _ in range(2)])
        tlf = Ring('blf', [ar.f32(1024) for _ in range(4)])
        tP = Ring('bP', [ar.f32(1024) for _ in range(2)])
        tB = Ring('bB', [ar.f32(1024) for _ in range(1)])
        tR = Ring('bR', [ar.f32(1024) for _ in range(1)])
        tE = Ring('bE', [ar.bf16(1024) for _ in range(3)])
        G = {}
        for d in range(2):
            G[d] = dict(
                qs=Ring('bqs', [r3(ar.bf16(NU * 512), b=512) for _ in range(2)]),
                kh=Ring('bkh', [r3(ar.bf16(NU * 512), b=512) for _ in range(2)]),
                kb=Ring('bkb', [r3(ar.bf16(NU * 512), b=512) for _ in range(2)]),
                V=Ring('bV', [r3(ar.bf16(4 * 512), b=512) for _ in range(2)]),
                dch=Ring('bdch', [r3(ar.f32(NU * 16), b=16) for _ in range(2)]),
                S=r3(ar.f32(NU * 128), b=128), Stok=tok('bS'),
                Sbf=Ring('bSbf', [r3(ar.bf16(NU * 128), b=128) for _ in range(nsub + 2)]),
                ATs=Ring('bATs', [r3(ar.bf16(NU * 128), b=128) for _ in range(2)]),
                kbt=Ring('bkbt', [r3(ar.bf16(NU * 128), b=128) for _ in range(2)]),
                kb3=Ring('bkb3', [r3(ar.bf16(NU * 128), b=128) for _ in range(2)]),
                qs3=Ring('bqs3', [r3(ar.bf16(NU * 128), b=128) for _ in range(2)]),
                ost=Ring('bost', [ar.bf16(NU * 128) for _ in range(2)]),
                pa=(banks[0 + d], 'bpa%d' % d), po=(banks[2 + d], 'bpo%d' % d), pu=(banks[4 + d], 'bpu%d' % d), pt=(banks[6 + d], 'bpt%d' % d),
            )
            g = G[d]
            s.op('pool', MS(g['S'], 0.0), w=[g['Stok']])
            sb0, sbt0 = g['Sbf'].next()
            s.op('pool', MS(sb0, 0.0), w=[sbt0])
            g['sprev'] = (sb0, sbt0)
        cur = {}

        pre = {}

        def prep_load(d, st_i):
            g = G[d]
            t0 = st_i * 512
            V, Vtok = g['V'].next()
            s.dma('sp', V, V_src[t0:t0 + 512, :].rearrange("(s p) c -> p s c", p=128), w=[Vtok])
            lfs = []
            for pr_ in range(2):
                us = slice(2 * pr_, 2 * pr_ + 2)
                lf, lftok = tlf.next()
                s.dma('sp', r3(lf, b=512), lv_[d][:, us, t0:t0 + 512], w=[lftok])
                lfs.append((lf, lftok))
            pre[(d, st_i)] = (V, Vtok, lfs)

        def prep(d, st_i):
            g = G[d]
            t0 = st_i * 512
            if (d, st_i) not in pre:
                prep_load(d, st_i)
            V, Vtok, lfs = pre.pop((d, st_i))
            qs, qstok = g['qs'].next()
            kh, khtok = g['kh'].next()
            kb, kbtok = g['kb'].next()
            dch, dchtok = g['dch'].next()
            for pr_ in range(2):
                us = slice(2 * pr_, 2 * pr_ + 2)
                q, qtok = tq.next()
                k, ktok = tk.next()
                lf, lftok = lfs[pr_]
                s.dma('sp', q, qv[:, us, t0:t0 + 512], w=[qtok])
                s.dma('sp', k, kv_[d][:, us, t0:t0 + 512], w=[ktok])
                P, Ptok = tP.next()
                s.op('dve', lambda e, P=P, lf=lf: e.tensor_tensor_scan(out=P, data0=segm, data1=lf, initial=0.0, op0=ALU.mult, op1=ALU.add),
                     r=[lftok, 'segm'], w=[Ptok])
                P3 = r3(P, b=CS)
                nchk = 1024 // CS
                totb = P3[:, :, CS - 1:CS].broadcast_to([128, nchk, CS])
                if d == 0:
                    Bd, Bdtok = P, Ptok
                else:
                    R, Rtok = tR.next()
                    B, Btok = tB.next()
                    s.op('dve', TT(R, P, lf, ALU.subtract), r=[Ptok, lftok], w=[Rtok])
                    s.op('dve', TT(r3(B, b=CS), totb, r3(R, b=CS), ALU.subtract), r=[Ptok, Rtok], w=[Btok])
                    Bd, Bdtok = B, Btok
                dchv = dch[:, us, :].rearrange("p u c -> p (u c)")
                s.op('act', ACT(dchv, P3[:, :, CS - 1], AF.Exp), r=[Ptok], w=[dchtok])
                q2 = q.rearrange("p u t -> p (u t)")
                k2 = k.rearrange("p u t -> p (u t)")
                e1, e1tok = tE.next()
                s.op('act', ACT(e1, Bd, AF.Exp), r=[Bdtok], w=[e1tok])
                s.op('dve', TT(qs[:, us, :].rearrange("p u t -> p (u t)"), q2, e1, ALU.mult), r=[qtok, e1tok], w=[qstok])
                e2, e2tok = tE.next()
                s.op('act', ACT(e2, Bd, AF.Exp, scale=-1.0), r=[Bdtok], w=[e2tok])
                s.op('dve', TT(kh[:, us, :].rearrange("p u t -> p (u t)"), k2, e2, ALU.mult), r=[ktok, e2tok], w=[khtok])
                s.op('dve', TT(r3(kb[:, us, :].rearrange("p u t -> p (u t)"), b=CS), r3(kh[:, us, :].rearrange("p u t -> p (u t)"), b=CS),
                               dchv.unsqueeze(2).broadcast_to([128, nchk, CS]), ALU.mult), r=[khtok, dchtok], w=[kbtok])
            cur[d] = dict(qs=qs, qstok=qstok, kh=kh, khtok=khtok, kb=kb, kbtok=kbtok, V=V, Vtok=Vtok, dch=dch, dchtok=dchtok)

        ctx = {}

        def stage1(d, st_i, sub):
            g = G[d]
            cc = cur[d]
            cols = slice(sub * 128, (sub + 1) * 128)
            mask = mask_f32 if d == 0 else mask_b32
            mtok = 'mask_f32' if d == 0 else 'mask_b32'
            pa, patok = g['pa']
            for u in range(NU):
                s.op('pe', MM(pa[:, u * 128:(u + 1) * 128], cc['kh'][:, u, cols], cc['qs'][:, u, cols]), r=[cc['khtok'], cc['qstok']], w=[patok])
            pt_, pttok = g['pt']
            ptb = pt_[:, :].bitcast(BF16)
            for u in range(NU):
                s.op('pe', TR(ptb[:, u * 128:(u + 1) * 128], cc['kb'][:, u, cols], ident_b[:]), r=[cc['kbtok'], 'ident_b'], w=[pttok])
            kbt, kbttok = g['kbt'].next()
            s.op('act', ACT(kbt, r3(ptb[:, 0:512], b=128), AF.Copy), r=[pttok], w=[kbttok])
            ATs, ATtok = g['ATs'].next()
            s.op('dve', TT(ATs, r3(pa[:, :], b=128), mask[:].unsqueeze(1).broadcast_to([128, NU, 128]), ALU.mult), r=[patok, mtok], w=[ATtok])
            qs3, qs3tok = g['qs3'].next()
            s.op('dve', TT(qs3, cc['qs'][:, :, cols], c3.unsqueeze(1).broadcast_to([128, NU, 128]), ALU.mult), r=[cc['qstok'], 'c3'], w=[qs3tok])
            kb3, kb3tok = g['kb3'].next()
            s.op('dve', TT(kb3, kbt, r3m.unsqueeze(1).broadcast_to([128, NU, 128]), ALU.mult), r=[kbttok, 'r3m'], w=[kb3tok])
            po, potok = g['po']
            for u in range(NU):
                s.op('pe', MM(po[:, u * 128:(u + 1) * 128], ATs[:, u, :], cc['V'][:, sub, u * 128:(u + 1) * 128], u == 0, False, True),
                     r=[ATtok, cc['Vtok']], w=[potok])
            ctx[d] = dict(cc=cc, sub=sub, st_i=st_i, kbt=kbt, kbttok=kbttok, kb3=kb3, kb3tok=kb3tok, qs3=qs3, qs3tok=qs3tok,
                          order=list(range(nsub)) if d == 0 else list(range(nsub - 1, -1, -1)))

        def stage2(d, n_i):
            g = G[d]
            x = ctx[d]
            cc = x['cc']
            sub = x['sub']
            cidx = x['order'][n_i]
            last = (n_i == nsub - 1)
            rows = slice(cidx * CS, (cidx + 1) * CS)
            ccols = slice(sub * 128 + cidx * CS, sub * 128 + (cidx + 1) * CS)
            sprev, sprevtok = g['sprev']
            po, potok = g['po']
            pu, putok = g['pu']
            masked = (cidx == nsub - 1)
            for u in range(NU):
                us = slice(u * 128, (u + 1) * 128)
                if masked:
                    s.op('pe', MM(po[:, us], x['qs3'][:, u, :], sprev[:, u, :], False, last, True), r=[x['qs3tok'], sprevtok], w=[potok])
                else:
                    s.op('pe', MM(po[rows, us], cc['qs'][:, u, ccols], sprev[:, u, :], False, last, True), r=[cc['qstok'], sprevtok], w=[potok])
            for u in range(NU):
                us = slice(u * 128, (u + 1) * 128)
                if masked:
                    s.op('pe', MM(pu[:, us], x['kb3'][:, u, :], cc['V'][:, sub, us]), r=[x['kb3tok'], cc['Vtok']], w=[putok])
                else:
                    s.op('pe', MM(pu[:, us], x['kbt'][rows, u, :], cc['V'][rows, sub, us]), r=[x['kbttok'], cc['Vtok']], w=[putok])
            chk = sub * nsub + cidx
            dv_ = cc['dch'][:, :, chk:chk + 1].broadcast_to([128, NU, 128])
            s.op('dve', TT(g['S'], g['S'], dv_, ALU.mult), r=[g['Stok'], cc['dchtok']], w=[g['Stok']])
            s.op('dve', TT(g['S'], g['S'], r3(pu[:, :], b=128), ALU.add), r=[g['Stok'], putok], w=[g['Stok']])
            sbn, sbntok = g['Sbf'].next()
            s.op('act', ACT(sbn, g['S'], AF.Copy), r=[g['Stok']], w=[sbntok])
            g['sprev'] = (sbn, sbntok)

        def stage3(d):
            g = G[d]
            x = ctx.pop(d)
            po, potok = g['po']
            tglob = x['st_i'] * 4 + x['sub']
            ost, osttok = g['ost'].next()
            s.op('act', ACT(ost, po[:, :], AF.Copy), r=[potok], w=[osttok])
            dst = Of if d == 0 else Ob
            s.dma('sp', dst[tglob * 128:(tglob + 1) * 128, :], ost, r=[osttok], w=[tok('odst')])

        nxt = {}
        for d in range(2):
            prep(d, 0 if d == 0 else NS - 1)
        for i in range(NS):
            now = {d: cur[d] for d in range(2)}
            for sub_i in range(4):
                for d in range(2):
                    st_i = i if d == 0 else NS - 1 - i
                    sub = sub_i if d == 0 else 3 - sub_i
                    cur[d] = now[d]
                    stage1(d, st_i, sub)
                for n_i in range(nsub):
                    for d in range(2):
                        stage2(d, n_i)
                for d in range(2):
                    stage3(d)
                if sub_i == 0 and i + 1 < NS:
                    for d in range(2):
                        prep_load(d, (i + 1) if d == 0 else NS - 2 - i)
                if sub_i == 1 and i + 1 < NS:
                    for d in range(2):
                        prep(d, (i + 1) if d == 0 else NS - 2 - i)
                        nxt[d] = cur[d]
            if i + 1 < NS:
                for d in range(2):
                    cur[d] = nxt[d]

    def rec_final(G_src, ycol0):
        s.barrier()
        ar.reset()
        epsr = ar.f32(2)
        s.op('pool', MS(epsr, 1e-6), w=['epsr'])
        rfl = Ring('rfl', [ar.bf16(512) for _ in range(4)])
        rf = Ring('rf', [ar.f32(512) for _ in range(3)])
        rb = Ring('rb', [ar.bf16(512) for _ in range(4)])
        rg = Ring('rg', [ar.bf16(512) for _ in range(5)])
        rsq = Ring('rsq', [ar.f32(512) for _ in range(2)])
        rss = Ring('rss', [ar.f32(8) for _ in range(3)])
        ry = Ring('ry', [ar.bf16(512) for _ in range(3)])
        loaded = {}

        def load(t):
            rows = slice(t * 128, (t + 1) * 128)
            fl, fltok = rfl.next()
            b, btok = rb.next()
            g, gtok = rg.next()
            s.dma('sp', fl, Of[rows, :], w=[fltok])
            s.dma('sp', b, Ob[rows, :], w=[btok])
            s.dma('sp', g, G_src[rows, :], w=[gtok])
            loaded[t] = (fl, fltok, b, btok, g, gtok)

        load(0)
        if NT > 1:
            load(1)
        stA = {}

        def stage_a(t):
            if t + 2 < NT:
                load(t + 2)
            fl, fltok, b, btok, g, gtok = loaded.pop(t)
            f, ftok = rf.next()
            s.op('dve', TT(f, fl, b, ALU.add), r=[fltok, btok], w=[ftok])
            sq, sqtok = rsq.next()
            ss, sstok = rss.next()
            for h in range(4):
                hs = slice(h * 128, (h + 1) * 128)
                s.op('act', ACT(sq[:, hs], f[:, hs], AF.Square, accum=ss[:, h:h + 1]), r=[ftok], w=[sqtok, sstok])
            s.op('act', ACT(ss[:, 0:4], ss[:, 0:4], AF.Ln, bias=epsr[:, 0:1], scale=1.0 / 128.0), r=[sstok, 'epsr'], w=[sstok])
            s.op('act', ACT(ss[:, 4:8], ss[:, 0:4], AF.Exp, scale=-0.5), r=[sstok], w=[sstok])
            stA[t] = (f, ftok, ss, sstok, g, gtok)

        def stage_b(t):
            rows = slice(t * 128, (t + 1) * 128)
            f, ftok, ss, sstok, g, gtok = stA.pop(t)
            y, ytok = ry.next()
            for h in range(4):
                hs = slice(h * 128, (h + 1) * 128)
                s.op('dve', STT(y[:, hs], f[:, hs], ss[:, 4 + h:5 + h], g[:, hs], ALU.mult, ALU.mult), r=[ftok, sstok, gtok], w=[ytok])
            s.dma('sp', Y[rows, ycol0:ycol0 + 512], y, r=[ytok], w=[tok('ydst')])

        stage_a(0)
        for t in range(NT):
            if t + 1 < NT:
                stage_a(t + 1)
            stage_b(t)

    holder = {}

    def fm_store(dst, row0, func=AF.Copy, scale=None, out_dt=BF16, col0=0):
        def emit(ps, pt, i, sub, n=128):
            st, sttok = holder['stg'].next()
            o = st.bitcast(BF16)[:, 0:512] if out_dt == BF16 else st
            s.op('act', ACT(o[0:n, :], ps[0:n, :], func, scale=scale), r=[pt], w=[sttok])
            s.dma('sp', dst[row0:row0 + n, col0 + i * 512:col0 + (i + 1) * 512], o[0:n, :], r=[sttok], w=[tok('fmdst')])
        return emit

    def tm_store(dst, func=AF.Copy, mulkey=None):
        def emit(ps, pt, i, sub):
            st, sttok = holder['stg'].next()
            o = st.bitcast(BF16)[:, 0:512]
            if mulkey is not None:
                s.op('act', ACT(st, ps[:, :], func), r=[pt], w=[sttok])
                st2, st2tok = holder['stg'].next()
                o = st2.bitcast(BF16)[:, 0:512]
                s.op('dve', TT(o, st, holder[mulkey], ALU.mult), r=[sttok, mulkey], w=[st2tok])
                sttok = st2tok
            elif func == AF.Copy:
                s.op('dve', CP(o, ps[:, :]), r=[pt], w=[sttok])
            else:
                s.op('act', ACT(o, ps[:, :], func), r=[pt], w=[sttok])
            t0 = i * 512 + sub * 128
            s.dma('sp', dst[t0:t0 + 128, :], o, r=[sttok], w=[tok('tmdst')])
        return emit

    def layer0():
        segs = []
        for blk in range(4):
            segs.append(dict(kind='fm', c0=blk * 128, n=128, emit=fm_store(QTa, blk * 128, scale=0.125)))
        for blk in range(4):
            segs.append(dict(kind='fm', c0=512 + blk * 128, n=128, emit=fm_store(KTa, blk * 128)))
        for blk in range(2):
            segs.append(dict(kind='fm', c0=2048 + blk * 128, n=128, emit=fm_store(qTb, blk * 128, scale=0.125)))
        for blk in range(2):
            segs.append(dict(kind='fm', c0=2304 + blk * 128, n=128, emit=fm_store(kTb, blk * 128)))

        def v66_emit(ps, pt, i, sub):
            st, sttok = holder['vstg'].next()
            s.op('dve', CP(st.rearrange("p (h d) -> p h d", d=66)[:, :, 0:64], r3(ps[:, :], b=64)), r=[pt], w=[sttok])
            t0 = i * 512 + sub * 128
            s.dma('sp', Va66[t0:t0 + 128, :], st, r=[sttok], w=[tok('v66')])
        segs.append(dict(kind='tm', c0=1024, n=512, emit=v66_emit))
        segs.append(dict(kind='tm', c0=2560, n=512, emit=tm_store(Vb)))
        segs.append(dict(kind='tm', c0=1536, n=512, emit=tm_store(Ga, AF.Silu)))
        segs.append(dict(kind='tm', c0=3072, n=512, emit=tm_store(Gb, AF.Silu, 'ngbc')))

        gpend = []

        def gate_emit(d):
            def emit(ps, pt, i, sub):
                lr, lrtok = holder['lr'].next()
                lrb = lr.bitcast(BF16)[:, 0:512]
                s.op('act', ACT(lrb[0:16, :], ps[0:16, :], AF.Copy), r=[pt], w=[lrtok])
                for blk in range(2):
                    pz, pztok = holder['pz'].next()
                    s.op('pe', MM(pz[:, :], holder['wupb'][0:16, d * 256 + blk * 128:d * 256 + (blk + 1) * 128], lrb[0:16, :]),
                         r=[lrtok, 'wupb'], w=[pztok])
                    st, sttok = holder['gst'].next()
                    s.op('act', ACT(st, pz[:, :], AF.Sigmoid, bias=holder['glab'][:, d * 2 + blk:d * 2 + blk + 1]), r=[pztok, 'glab'], w=[sttok])
                    gpend.append((st, sttok, d, blk, i))
                if d == 1:
                    while gpend:
                        st, sttok, d_, blk_, i_ = gpend.pop(0)
                        s.op('act', ACT(st, st, AF.Ln), r=[sttok], w=[sttok])
                        s.dma('sp', lfT[d_, blk_ * 128:(blk_ + 1) * 128, i_ * 512:(i_ + 1) * 512], st, r=[sttok], w=[tok('lfdst')])
            return emit
        segs.append(dict(kind='fm', c0=3584, n=16, emit=gate_emit(0)))
        segs.append(dict(kind='fm', c0=3600, n=16, emit=gate_emit(1)))

        def alloc_extras():
            holder['stg'] = Ring('stg', [ar.f32(512) for _ in range(8)])
            holder['lr'] = Ring('lr', [ar.f32(512) for _ in range(2)])
            holder['wup'] = ar.f32(512)
            holder['wupb'] = ar.bf16(512)
            holder['gst'] = Ring('gst', [ar.f32(512) for _ in range(6)])
            holder['glab'] = ar.f32(4)
            holder['pz'] = Ring('pz', [banks[6], banks[7]])
            vst = [ar.bf16(528) for _ in range(3)]
            holder['vstg'] = Ring('vstg', vst)
            for v_, vt_ in zip(vst, holder['vstg'].toks):
                s.op('pool', MS(v_, 1.0), w=[vt_])
            holder['ngbc'] = ar.f32(512)
            for h in range(4):
                s.dma('sp', holder['ngbc'][:, h * 128:(h + 1) * 128], gla_ng[0:1, :].partition_broadcast(128), w=['ngbc'])
            for d in range(2):
                s.dma('sp', holder['wup'][0:16, d * 256:(d + 1) * 256], gla_w_up[d], w=['wup'])
            s.op('dve', CP(holder['wupb'][0:16, :], holder['wup'][0:16, :]), r=['wup'], w=['wupb'])
            s.dma('sp', holder['glab'], gla_bT.rearrange("p d b -> p (d b)"), w=['glab'])
        phase_a_with(0, x_in, segs, alloc_extras, e_w_in, 3616)

    hkT = dscr("hkT", [2, 512, L], BF16)
    hlfT = dscr("hlfT", [2, 512, L], F32)
    xbcT = dscr("xbcT", [1024, L + 4], BF16)
    dtA = dscr("dtA", [L, 32], F32)
    Btm = dscr("Btm", [L, 256], BF16)

    def layer1_a():
        segs = []
        for blk in range(4):
            segs.append(dict(kind='fm', c0=blk * 128, n=128, emit=fm_store(QTa, blk * 128, scale=128.0 ** -0.5)))

        fpend = []

        def forget_emit(d, blk):
            def emit(ps, pt, i, sub):
                st, sttok = holder['fst'].next()
                s.op('act', ACT(st, ps[:, :], AF.Sigmoid), r=[pt], w=[sttok])
                s.op('dve', TS(st, st, holder['lbs'][:, 12 + blk:13 + blk], ALU.mult, holder['lbs'][:, 8 + blk:9 + blk], ALU.add),
                     r=[sttok, 'lbs'], w=[sttok])
                st2, st2tok = holder['stg'].next()
                kb_ = st2.bitcast(BF16)[:, 0:512]
                s.op('act', ACT(kb_, st, AF.Identity, bias=1.0, scale=-1.0), r=[sttok], w=[st2tok])
                s.dma('sp', hkT[d, blk * 128:(blk + 1) * 128, i * 512:(i + 1) * 512], kb_, r=[st2tok], w=[tok('hk')])
                fpend.append((st, sttok, d, blk, i))
                if d == 1 and blk == 3:
                    while fpend:
                        st_, sttok_, d_, blk_, i_ = fpend.pop(0)
                        s.op('act', ACT(st_, st_, AF.Ln), r=[sttok_], w=[sttok_])
                        s.dma('sp', hlfT[d_, blk_ * 128:(blk_ + 1) * 128, i_ * 512:(i_ + 1) * 512], st_, r=[sttok_], w=[tok('hlf')])
            return emit
        for d in range(2):
            for blk in range(4):
                segs.append(dict(kind='fm', c0=512 + d * 512 + blk * 128, n=128, emit=forget_emit(d, blk)))
        segs.append(dict(kind='tm', c0=1536, n=512, emit=tm_store(Vb)))
        segs.append(dict(kind='tm', c0=2048, n=512, emit=tm_store(Gb, AF.Silu, 'ngbc')))
        segs.append(dict(kind='tm', c0=2560, n=512, emit=tm_store(Ga, AF.Silu)))
        for blk in range(8):
            segs.append(dict(kind='fm', c0=3072 + blk * 128, n=128, emit=fm_store(xbcT, blk * 128, col0=2)))

        def dt_emit(ps, pt, i, sub):
            dtt, dttok = holder['dtt'].next()
            s.op('dve', TT(dtt[:, 0:16], ps[:, 0:16], holder['dtb'][:, 0:16], ALU.add), r=[pt, 'dtb'], w=[dttok])
            s.op('act', ACT(dtt[:, 0:16], dtt[:, 0:16], AF.Exp), r=[dttok], w=[dttok])
            s.op('act', ACT(dtt[:, 0:16], dtt[:, 0:16], AF.Ln, bias=1.0), r=[dttok], w=[dttok])
            s.op('dve', TT(dtt[:, 16:32], dtt[:, 0:16], holder['dtb'][:, 16:32], ALU.mult), r=[dttok, 'dtb'], w=[dttok])
            t0 = i * 512 + sub * 128
            s.dma('sp', dtA[t0:t0 + 128, :], dtt[:, 0:32], r=[dttok], w=[tok('dtA')])
        segs.append(dict(kind='tm', c0=4096, n=16, emit=dt_emit))

        def alloc_extras():
            holder['stg'] = Ring('stg', [ar.f32(512) for _ in range(8)])
            holder['dtt'] = Ring('dtt', [ar.f32(32) for _ in range(2)])
            holder['fst'] = Ring('fst', [ar.f32(512) for _ in range(10)])
            holder['dtb'] = ar.f32(32)
            holder['lbs'] = ar.f32(16)
            holder['ngbc'] = ar.f32(512)
            for h in range(4):
                s.dma('sp', holder['ngbc'][:, h * 128:(h + 1) * 128], hgrn_ng[0:1, :].partition_broadcast(128), w=['ngbc'])
            zz = ar.f32(16)
            lbs = holder['lbs']
            s.dma('sp', lbs[:, 0:8], hgrn_lbT.rearrange("p l b -> p (l b)"), w=['lbs'])
            s.op('dve', TT(lbs[:, 8:12], lbs[:, 4:8], lbs[:, 0:4], ALU.subtract), r=['lbs'], w=['lbs'])
            s.op('act', ACT(lbs[:, 8:12], lbs[:, 8:12], AF.Sigmoid), r=['lbs'], w=['lbs'])
            s.op('dve', TS(lbs[:, 12:16], lbs[:, 8:12], -1.0, ALU.mult, 1.0, ALU.add), r=['lbs'], w=['lbs'])
            dtb = holder['dtb']
            s.dma('sp', dtb[:, 0:16], dt_bias[0:1, :].partition_broadcast(128), w=['dtb'])
            s.dma('sp', dtb[:, 16:32], a_log[0:1, :].partition_broadcast(128), w=['dtb'])
            s.op('act', ACT(dtb[:, 16:32], dtb[:, 16:32], AF.Exp), r=['dtb'], w=['dtb'])
            s.op('dve', TS(dtb[:, 16:32], dtb[:, 16:32], -1.0, ALU.mult), r=['dtb'], w=['dtb'])
            s.op('pool', MS(zz, 0.0), w=['zz'])
            xv_ = xbcT.rearrange("(b p) t -> p b t", p=128)
            zzb = r3(zz.bitcast(BF16)[:, 0:16], b=2)
            s.dma('sp', xv_[:, :, 0:2], zzb, r=['zz'], w=[tok('xbcz')])
            s.dma('sp', xv_[:, :, L + 2:L + 4], zzb, r=['zz'], w=[tok('xbcz')])
        phase_a_with(1, X1, segs, alloc_extras, o_w_in, 4112)

    def phase_a_with(l, x_src, segs, extras, wsrc, wcols):
        s.barrier()
        ar.reset(W_COLS)
        extras()
        phase_a(l, x_src, segs)

    def na_phase():
        s.barrier()
        ar.reset()
        NK = 640
        E_int = r3(ar.bf16(8 * NK), b=NK)
        E_edge = r3(ar.bf16(8 * NK), b=NK)
        bst = ar.f32(8 * NK)
        KTs = Ring('KT', [r3(ar.bf16(4 * 1024), b=1024) for _ in range(4)])
        Vs_raw = [ar.bf16(8 * 8 * 66) for _ in range(4)]
        Vs = Ring('Vn', [v.rearrange("p (b h d) -> p b h d", b=8, h=8) for v in Vs_raw])
        QTs = Ring('QT', [r3(ar.bf16(4 * 128), b=128) for _ in range(5)])
        Gs = Ring('Gn', [ar.bf16(512) for _ in range(5)])
        eS = Ring('eS', [ar.bf16(NK) for _ in range(5)])
        PTs = Ring('PT', [ar.bf16(NK) for _ in range(5)])
        rec = Ring('rec', [ar.f32(8) for _ in range(2)])
        yst = Ring('yst', [ar.bf16(512) for _ in range(4)])
        pSA = Ring('pSA', [banks[0], banks[1], banks[2]])
        pSB = Ring('pSB', [banks[3], banks[4], banks[5]])
        pOn = Ring('pOn', [banks[6], banks[7]])

        def load_E(cls, dst, dtok):
            s.dma('sp', bst, btab[cls], w=['bst'])
            s.op('act', ACT(dst.rearrange("p h k -> p (h k)"), bst, AF.Exp), r=['bst'], w=[dtok])

        load_E(2, E_int, 'E_int')
        edge_loaded = [None]
        QTv = QTa.rearrange("(pr p) t -> p pr t", p=128)
        KTv = KTa.rearrange("(pr p) t -> p pr t", p=128)
        loaded = {}
        wloaded = {}
        NSB = NT // 4

        def wstart(sb):
            return min(max(8 * sb - 4, 0), ROWS - 16)

        def wload(sb):
            k0 = wstart(sb) * 64
            KT, KTtok = KTs.next()
            s.dma('sp', KT, KTv[:, :, k0:k0 + 1024], w=[KTtok])
            V, Vtok = Vs.next()
            s.dma('sp', V.rearrange("p b h d -> p b (h d)"), Va66[k0:k0 + 1024, :].rearrange("(b p) c -> p b c", p=128), w=[Vtok])
            wloaded[sb] = (KT, KTtok, V, Vtok)

        def load(t):
            QT, QTtok = QTs.next()
            s.dma('sp', QT, QTv[:, :, t * 128:(t + 1) * 128], w=[QTtok])
            G, Gtok = Gs.next()
            s.dma('sp', G, Ga[t * 128:(t + 1) * 128, :], w=[Gtok])
            loaded[t] = (QT, QTtok, G, Gtok)

        wload(0)
        if NSB > 1:
            wload(1)
        load(0)
        if NT > 1:
            load(1)
        tctx = {}

        def tile_ctx(t):
            if t + 2 < NT:
                load(t + 2)
            sb = t // 4
            if t % 4 == 0 and sb + 2 < NSB:
                wload(sb + 2)
            r = 2 * t
            ks = min(max(r - 4, 0), ROWS - 10)
            cls = (r - ks) // 2
            if cls == 2:
                E, Etok = E_int, 'E_int'
            else:
                if edge_loaded[0] != cls:
                    load_E(cls, E_edge, 'E_edge')
                    edge_loaded[0] = cls
                E, Etok = E_edge, 'E_edge'
            KT, KTtok, V, Vtok = wloaded[sb]
            if t % 4 == 3:
                wloaded.pop(sb)
            QT, QTtok, G, Gtok = loaded.pop(t)
            y, ytok = yst.next()
            boff = (ks - wstart(sb)) // 2
            tctx[t] = dict(E=E, Etok=Etok, KT=KT, KTtok=KTtok, V=V, Vtok=Vtok, QT=QT, QTtok=QTtok, G=G, Gtok=Gtok, y=y, ytok=ytok, po={}, boff=boff)

        def emit_S(t, h):
            c = tctx[t]
            pr = slice((h % 2) * 64, (h % 2) * 64 + 64)
            pa, patok = pSA.next()
            pb, pbtok = pSB.next()
            for blk in range(5):
                dst = pa[:, blk * 128:(blk + 1) * 128] if blk < 4 else pb[:, 0:128]
                s.op('pe', MM(dst, c['KT'][pr, h // 2, (c['boff'] + blk) * 128:(c['boff'] + blk + 1) * 128], c['QT'][pr, h // 2, :]),
                     r=[c['KTtok'], c['QTtok']], w=[patok if blk < 4 else pbtok])
            e_, etok = eS.next()
            s.op('act', ACT(e_[:, 0:512], pa[:, :], AF.Exp), r=[patok], w=[etok])
            s.op('act', ACT(e_[:, 512:640], pb[:, 0:128], AF.Exp), r=[pbtok], w=[etok])
            P, Ptok = PTs.next()
            s.op('dve', TT(P, e_, c['E'][:, h, :], ALU.mult), r=[etok, c['Etok']], w=[Ptok])
            return P, Ptok

        def emit_PV(t, h, P, Ptok):
            c = tctx[t]
            hg, hh = h // 4, h % 4
            if hg not in c['po']:
                c['po'][hg] = pOn.next()
            po, potok = c['po'][hg]
            for blk in range(5):
                s.op('pe', MM(po[:, hh * 65:hh * 65 + 65], P[:, blk * 128:(blk + 1) * 128], c['V'][:, c['boff'] + blk, h, 0:65], blk == 0, blk == 4),
                     r=[Ptok, c['Vtok']], w=[potok])
            if hh == 3:
                rc, rctok = rec.next()
                po3 = po[:, 0:260].rearrange("p (h d) -> p h d", d=65)
                s.op('dve', lambda e, rc=rc, po3=po3: e.reciprocal(out=rc[:, 0:4], in_=po3[:, :, 64]), r=[potok], w=[rctok])
                for j in range(4):
                    hj = hg * 4 + j
                    s.op('dve', STT(c['y'][:, hj * 64:(hj + 1) * 64], po3[:, j, 0:64], rc[:, j:j + 1], c['G'][:, hj * 64:(hj + 1) * 64], ALU.mult, ALU.mult),
                         r=[potok, rctok, c['Gtok']], w=[c['ytok']])
                if hg == 1:
                    s.dma('sp', Y[t * 128:(t + 1) * 128, 0:512], c['y'], r=[c['ytok']], w=[tok('ydst')])
                    tctx.pop(t)

        pend = []
        for t in range(NT):
            for h in range(8):
                if t not in tctx:
                    tile_ctx(t)
                P, Ptok = emit_S(t, h)
                pend.append((t, h, P, Ptok))
                if len(pend) > 2:
                    emit_PV(*pend.pop(0))
        while pend:
            emit_PV(*pend.pop(0))

    def ssd_conv():
        s.barrier()
        ar.reset()
        cw = ar.f32(32)
        cbias = ar.f32(8)
        s.dma('sp', cw, conv_wT.rearrange("p b k -> p (b k)"), w=['cw'])
        s.dma('sp', cbias, conv_bT[:, :], w=['cbias'])
        dg = ar.bf16(32 * 128)
        dg3 = r3(dg, b=128)
        for j in range(32):
            s.op('dve', TS(dg3[:, j, :], ident_f[:], cw[:, j:j + 1], ALU.mult), r=['ident_f', 'cw'], w=['dg'])
        rin = Ring('cin', [ar.bf16(516) for _ in range(6)])
        rfm = Ring('cfm', [ar.bf16(512) for _ in range(12)])
        rtm = Ring('ctm', [ar.bf16(512) for _ in range(3)])
        ptr = Ring('cvp', [banks[0], banks[1], banks[2]])
        pcv = Ring('pcv', [banks[3], banks[4], banks[5], banks[6]])
        items = [(i, blk) for i in range(NS) for blk in range(8)]
        cloaded = {}

        def cload(n):
            i, blk = items[n]
            xin, xtok = rin.next()
            s.dma('sp', xin[:, 0:515], xbcT[blk * 128:(blk + 1) * 128, i * 512:i * 512 + 515], w=[xtok])
            cloaded[n] = (xin, xtok)

        for n in range(min(4, len(items))):
            cload(n)
        for i in range(NS):
            t0 = i * 512
            fm = {}
            for blk in range(8):
                n = i * 8 + blk
                if n + 4 < len(items):
                    cload(n + 4)
                xin, xtok = cloaded.pop(n)
                pc, pctok = pcv.next()
                for k in range(4):
                    s.op('pe', MM(pc[:, :], dg3[:, blk * 4 + k, :], xin[:, k:k + 512], k == 0, k == 3), r=['dg', xtok], w=[pctok])
                o, otok = rfm.next()
                s.op('act', ACT(o, pc[:, :], AF.Silu, bias=cbias[:, blk:blk + 1]), r=[pctok, 'cbias'], w=[otok])
                fm[blk] = (o, otok)
                if blk in (4, 5):
                    s.dma('sp', qTb[(blk - 4) * 128:(blk - 3) * 128, t0:t0 + 512], o, r=[otok], w=[tok('BT')])
                if blk in (6, 7):
                    s.dma('sp', kTb[(blk - 6) * 128:(blk - 5) * 128, t0:t0 + 512], o, r=[otok], w=[tok('CT')])
            for sub in range(4):
                ps, pt = ptr.next()
                psb = ps[:, :].bitcast(BF16)
                for b4 in range(4):
                    o, otok = fm[b4]
                    s.op('pe', TR(psb[:, b4 * 128:(b4 + 1) * 128], o[:, sub * 128:(sub + 1) * 128], ident_b[:]), r=[otok, 'ident_b'], w=[pt])
                tm, tmtok = rtm.next()
                s.op('act', ACT(tm, psb[:, 0:512], AF.Copy), r=[pt], w=[tmtok])
                s.dma('sp', Va[t0 + sub * 128:t0 + (sub + 1) * 128, :], tm, r=[tmtok], w=[tok('xs')])
                ps, pt = ptr.next()
                psb = ps[:, :].bitcast(BF16)
                for b2 in range(2):
                    o, otok = fm[4 + b2]
                    s.op('pe', TR(psb[:, b2 * 128:(b2 + 1) * 128], o[:, sub * 128:(sub + 1) * 128], ident_b[:]), r=[otok, 'ident_b'], w=[pt])
                tm, tmtok = rtm.next()
                s.op('dve', CP(tm[:, 0:256], psb[:, 0:256]), r=[pt], w=[tmtok])
                s.dma('sp', Btm[t0 + sub * 128:t0 + (sub + 1) * 128, :], tm[:, 0:256], r=[tmtok], w=[tok('Btm')])

    def ssd_main():
        s.barrier()
        ar.reset()
        tri = [ar.f32(128), ar.f32(128)]
        mbf = [ar.f32(128), ar.f32(128)]
        s.op('pool', CP(tri[0], mask_f128[:]), r=['mask_f128'], w=['tri'])
        s.op('pool', CP(tri[1], mask_b128[:]), r=['mask_b128'], w=['tri'])
        mb4 = [ar.bf16(512), ar.bf16(512)]
        for d in range(2):
            s.op('pool', TS(mbf[d], tri[d], -1.0, ALU.add, -NEG, ALU.mult), r=['tri'], w=['mbf'])
            for hh in range(4):
                s.op('pool', CP(mb4[d][:, hh * 128:(hh + 1) * 128], mbf[d]), r=['mbf'], w=['mb4'])
        chains = [(g, d) for g in range(2) for d in range(2)]
        C = {}
        for ch in chains:
            C[ch] = dict(S=ar.f32(256), Stok=tok('sS'), Sbf=Ring('sSbf', [ar.bf16(256) for _ in range(3)]))
            s.op('pool', MS(C[ch]['S'], 0.0), w=[C[ch]['Stok']])
            sb0, sbt0 = C[ch]['Sbf'].next()
            s.op('pool', MS(sb0, 0.0), w=[sbt0])
            C[ch]['sprev'] = (sb0, sbt0)
        rdta = Ring('dta', [ar.f32(32) for _ in range(4)])
        rsm = Ring('sm', [ar.f32(64) for _ in range(4)])
        rR = Ring('sR', [ar.f32(1024) for _ in range(4)])
        rBT = Ring('sBT', [ar.bf16(128) for _ in range(9)])
        rCT = Ring('sCT', [ar.bf16(128) for _ in range(9)])
        rBm = Ring('sBm', [ar.bf16(128) for _ in range(9)])
        rxs = Ring('sxs', [ar.bf16(256) for _ in range(9)])
        rcb = Ring('scb', [ar.bf16(128) for _ in range(5)])
        rxdt = Ring('sxdt', [ar.bf16(256) for _ in range(5)])
        rxd = Ring('sxd', [ar.bf16(256) for _ in range(5)])
        rarg = Ring('sarg', [ar.f32(512) for _ in range(4)])
        rsg = Ring('ssg', [ar.bf16(512) for _ in range(4)])
        rat = Ring('sat', [ar.bf16(512) for _ in range(3)])
        ry1 = Ring('sy1', [ar.f32(256) for _ in range(3)])
        ry2 = Ring('sy2', [ar.f32(256) for _ in range(3)])
        rtS = Ring('stS', [ar.f32(256) for _ in range(2)])
        ryb = Ring('syb', [ar.bf16(256) for _ in range(3)])
        pq = Ring('spq', [banks[0], banks[1]])
        pCB = Ring('spCB', [banks[2]])
        pBC = Ring('spBC', [banks[3], banks[4]])
        pY = Ring('spY', [banks[5], banks[6]])
        pU = Ring('spU', [banks[7]])
        shared = {}
        sloaded = {}

        def sload(ch, t):
            g, d = ch
            rows = slice(t * 128, (t + 1) * 128)
            BT, BTtok = rBT.next()
            CT, CTtok = rCT.next()
            Bm, Bmtok = rBm.next()
            xs, xstok = rxs.next()
            s.dma('sp', BT, qTb[g * 128:(g + 1) * 128, rows], w=[BTtok])
            s.dma('sp', CT, kTb[g * 128:(g + 1) * 128, rows], w=[CTtok])
            s.dma('sp', Bm, Btm[rows, g * 128:(g + 1) * 128], w=[Bmtok])
            s.dma('sp', xs, Va[rows, g * 256:(g + 1) * 256], w=[xstok])
            sloaded[(ch, t)] = (BT, BTtok, CT, CTtok, Bm, Bmtok, xs, xstok)

        dloaded = {}

        def dload(d, t):
            dta, dtatok = rdta.next()
            s.dma('sp', dta, dtA[t * 128:(t + 1) * 128, :], w=[dtatok])
            dloaded[(d, t)] = (dta, dtatok)

        def dprep(d, t):
            dta, dtatok = dloaded.pop((d, t))
            a_d = dta[:, 16 + d * 8:24 + d * 8]
            q_, qtok = pq.next()
            s.op('pe', MM(q_[:, 0:8], tri[d], a_d), r=['tri', dtatok], w=[qtok])
            s.op('pe', MM(q_[:, 8:16], ones_f[:], a_d), r=['ones_f', dtatok], w=[qtok])
            sm, smtok = rsm.next()
            s.op('act', ACT(sm[:, 0:8], q_[:, 0:8], AF.Copy), r=[qtok], w=[smtok, qtok])
            s.op('act', ACT(sm[:, 8:16], q_[:, 0:8], AF.Exp), r=[qtok], w=[smtok, qtok])
            s.op('dve', TT(sm[:, 16:24], q_[:, 8:16], sm[:, 0:8], ALU.subtract), r=[qtok, smtok], w=[smtok, qtok])
            s.op('act', ACT(sm[:, 24:32], sm[:, 16:24], AF.Exp), r=[smtok], w=[smtok])
            s.op('act', ACT(sm[:, 32:40], q_[:, 8:16], AF.Exp), r=[qtok], w=[smtok, qtok])
            R, Rtok = rR.next()
            s.op('dve', TT(r3(R, b=128), a_d.unsqueeze(2).broadcast_to([128, 8, 128]), tri[d].unsqueeze(1).broadcast_to([128, 8, 128]), ALU.mult),
                 r=[dtatok, 'tri'], w=[Rtok])
            shared[(d, t)] = dict(dta=dta, dtatok=dtatok, sm=sm, smtok=smtok, R=R, Rtok=Rtok)

        X = {}

        def s2(ch, t):
            g, d = ch
            sh = shared[(d, t)]
            BT, BTtok, CT, CTtok, Bm, Bmtok, xs, xstok = sloaded.pop((ch, t))
            cbp, cbptok = pCB.next()
            s.op('pe', MM(cbp[:, 0:128], BT, CT), r=[BTtok, CTtok], w=[cbptok])
            cb, cbtok = rcb.next()
            s.op('act', ACT(cb, cbp[:, 0:128], AF.Copy), r=[cbptok], w=[cbtok])
            dta = sh['dta']
            sm = sh['sm']
            xdt, xdttok = rxdt.next()
            dtv = dta[:, d * 8 + g * 4:d * 8 + g * 4 + 4].unsqueeze(2).broadcast_to([128, 4, 64])
            s.op('dve', TT(r3(xdt, b=64), r3(xs, b=64), dtv, ALU.mult), r=[xstok, sh['dtatok']], w=[xdttok])
            xd, xdtok = rxd.next()
            dsv = sm[:, 24 + g * 4:28 + g * 4].unsqueeze(2).broadcast_to([128, 4, 64])
            s.op('dve', TT(r3(xd, b=64), r3(xdt, b=64), dsv, ALU.mult), r=[xdttok, sh['smtok']], w=[xdtok])
            X[ch] = dict(t=t, sh=sh, CT=CT, CTtok=CTtok, Bm=Bm, Bmtok=Bmtok, cb=cb, cbtok=cbtok, xdt=xdt, xdttok=xdttok, xd=xd, xdtok=xdtok)

        def bc(ch):
            g, d = ch
            x = X[ch]
            sh = x['sh']
            bcp, bcptok = pBC.next()
            s.op('pe', MM(bcp[:, :], ones_f[:], sh['R'][:, g * 512:(g + 1) * 512], True, False), r=['ones_f', sh['Rtok']], w=[bcptok])
            s.op('pe', MM(bcp[:, :], ident_b[:], mb4[d], False, True), r=['ident_b', 'mb4'], w=[bcptok])
            x['bcp'] = bcp
            x['bcptok'] = bcptok

        def s3a(ch):
            g, d = ch
            x = X[ch]
            sh = x['sh']
            sm = sh['sm']
            arg, argtok = rarg.next()
            acv = sm[:, g * 4:g * 4 + 4].unsqueeze(2).broadcast_to([128, 4, 128])
            s.op('dve', TT(r3(arg, b=128), r3(x['bcp'][:, :], b=128), acv, ALU.subtract), r=[x['bcptok'], sh['smtok']], w=[argtok])
            sg, sgtok = rsg.next()
            s.op('act', ACT(sg, arg, AF.Exp), r=[argtok], w=[sgtok])
            x['sg'] = sg
            x['sgtok'] = sgtok

        def s3b(ch):
            g, d = ch
            c = C[ch]
            x = X[ch]
            yp, yptok = pY.next()
            sprev, sprevtok = c['sprev']
            at, attok = rat.next()
            s.op('dve', TT(r3(at, b=128), r3(x['sg'], b=128), x['cb'].unsqueeze(1).broadcast_to([128, 4, 128]), ALU.mult), r=[x['cbtok'], x['sgtok']], w=[attok])
            for hh in range(4):
                s.op('pe', MM(yp[:, hh * 64:(hh + 1) * 64], at[:, hh * 128:(hh + 1) * 128], x['xdt'][:, hh * 64:(hh + 1) * 64]), r=[attok, x['xdttok']], w=[yptok])
                s.op('pe', MM(yp[:, 256 + hh * 64:256 + (hh + 1) * 64], x['CT'], sprev[:, hh * 64:(hh + 1) * 64]), r=[x['CTtok'], sprevtok], w=[yptok])
            x['yp'] = yp
            x['yptok'] = yptok

        def s3c(ch):
            g, d = ch
            x = X[ch]
            sh = x['sh']
            sm = sh['sm']
            yp, yptok = x['yp'], x['yptok']
            y1, y1tok = ry1.next()
            s.op('act', ACT(y1, yp[:, 0:256], AF.Copy), r=[yptok], w=[y1tok, yptok])
            y2, y2tok = ry2.next()
            eav = sm[:, 8 + g * 4:12 + g * 4].unsqueeze(2).broadcast_to([128, 4, 64])
            s.op('dve', TT(r3(y2, b=64), r3(yp[:, 256:512], b=64), eav, ALU.mult), r=[yptok, sh['smtok']], w=[y2tok, yptok])
            yb, ybtok = ryb.next()
            s.op('dve', TT(yb, y2, y1, ALU.add), r=[y2tok, y1tok], w=[ybtok])
            rows = slice(x['t'] * 128, (x['t'] + 1) * 128)
            dst = Of if d == 0 else Ob
            s.dma('sp', dst[rows, g * 256:(g + 1) * 256], yb, r=[ybtok], w=[tok('sodst')])

        def s4(ch):
            g, d = ch
            c = C[ch]
            x = X.pop(ch)
            sm = x['sh']['sm']
            up, uptok = pU.next()
            s.op('pe', MM(up[:, 0:256], x['Bm'], x['xd']), r=[x['Bmtok'], x['xdtok']], w=[uptok])
            tS, tStok = rtS.next()
            cdv = sm[:, 32 + g * 4:36 + g * 4].unsqueeze(2).broadcast_to([128, 4, 64])
            s.op('dve', TT(r3(tS, b=64), r3(c['S'], b=64), cdv, ALU.mult), r=[c['Stok'], x['sh']['smtok']], w=[tStok])
            s.op('dve', TT(c['S'], tS, up[:, 0:256], ALU.add), r=[tStok, uptok], w=[c['Stok']])
            sbn, sbntok = c['Sbf'].next()
            s.op('act', ACT(sbn, c['S'], AF.Copy), r=[c['Stok']], w=[sbntok])
            c['sprev'] = (sbn, sbntok)

        def tof(ch, i):
            return i if ch[1] == 0 else NT - 1 - i

        for ch in chains:
            sload(ch, tof(ch, 0))
        for d in range(2):
            dload(d, tof((0, d), 0))
        for i in range(NT):
            if i + 1 < NT:
                for ch in chains:
                    sload(ch, tof(ch, i + 1))
                for d in range(2):
                    dload(d, tof((0, d), i + 1))
            for d in range(2):
                dprep(d, tof((0, d), i))
            for ch in chains:
                s2(ch, tof(ch, i))
            nch = len(chains)
            bc(chains[0])
            bc(chains[1])
            s3a(chains[0])
            for k in range(nch + 1):
                if k + 1 < nch:
                    s3a(chains[k + 1])
                if k + 2 < nch:
                    bc(chains[k + 2])
                if k < nch:
                    s3b(chains[k])
                if k >= 1:
                    s3c(chains[k - 1])
            for ch in chains:
                s4(ch)
            for d in range(2):
                shared.pop((d, tof((0, d), i)))

    def ssd_final():
        s.barrier()
        ar.reset()
        dsk = ar.f32(8)
        ngb = ar.f32(512)
        s.dma('sp', dsk, d_skip[0:1, :].partition_broadcast(128), w=['dsk'])
        s.dma('sp', ngb, ssm_ng[0:1, :].partition_broadcast(128), w=['ngb'])
        epsf = ar.f32(2)
        s.op('pool', MS(epsf, 1e-6), w=['epsf'])
        rfl = Ring('ffl', [ar.bf16(512) for _ in range(4)])
        rf = Ring('ff', [ar.f32(512) for _ in range(3)])
        rb = Ring('fb', [ar.bf16(512) for _ in range(4)])
        rx = Ring('fx', [ar.bf16(512) for _ in range(4)])
        rg = Ring('fg', [ar.bf16(512) for _ in range(4)])
        rt = Ring('ft', [ar.f32(512) for _ in range(3)])
        rss = Ring('fss', [ar.f32(8) for _ in range(2)])
        ry = Ring('fy', [ar.bf16(512) for _ in range(3)])
        loaded = {}

        def load(t):
            rows = slice(t * 128, (t + 1) * 128)
            fl, fltok = rfl.next()
            b, btok = rb.next()
            xs, xstok = rx.next()
            g, gtok = rg.next()
            s.dma('sp', fl, Of[rows, :], w=[fltok])
            s.dma('sp', b, Ob[rows, :], w=[btok])
            s.dma('sp', xs, Va[rows, :], w=[xstok])
            s.dma('sp', g, Ga[rows, :], w=[gtok])
            loaded[t] = (fl, fltok, b, btok, xs, xstok, g, gtok)

        load(0)
        if NT > 1:
            load(1)
        stA = {}

        def stage_a(t):
            if t + 2 < NT:
                load(t + 2)
            fl, fltok, b, btok, xs, xstok, g, gtok = loaded.pop(t)
            f, ftok = rf.next()
            s.op('dve', TT(f, fl, b, ALU.add), r=[fltok, btok], w=[ftok])
            tmp, tmptok = rt.next()
            s.op('dve', TT(r3(tmp, b=64), r3(xs, b=64), dsk[:, 0:8].unsqueeze(2).broadcast_to([128, 8, 64]), ALU.mult), r=[xstok, 'dsk'], w=[tmptok])
            s.op('dve', TT(f, f, tmp, ALU.add), r=[ftok, tmptok], w=[ftok])
            s.op('dve', TT(f, f, g, ALU.mult), r=[ftok, gtok], w=[ftok])
            ss, sstok = rss.next()
            s.op('act', ACT(tmp, f, AF.Square, accum=ss[:, 0:1]), r=[ftok], w=[tmptok, sstok])
            s.op('act', ACT(ss[:, 1:2], ss[:, 0:1], AF.Ln, bias=epsf[:, 0:1], scale=1.0 / 512.0), r=[sstok, 'epsf'], w=[sstok])
            s.op('act', ACT(ss[:, 2:3], ss[:, 1:2], AF.Exp, scale=-0.5), r=[sstok], w=[sstok])
            stA[t] = (f, ftok, ss, sstok)

        def stage_b(t):
            rows = slice(t * 128, (t + 1) * 128)
            f, ftok, ss, sstok = stA.pop(t)
            y, ytok = ry.next()
            s.op('dve', STT(y, f, ss[:, 2:3], ngb, ALU.mult, ALU.mult), r=[ftok, sstok, 'ngb'], w=[ytok])
            s.dma('sp', Y[rows, 512:1024], y, r=[ytok], w=[tok('ydst')])

        stage_a(0)
        for t in range(NT):
            if t + 1 < NT:
                stage_a(t + 1)
            stage_b(t)

    qv = qTb.rearrange("(u p) t -> u p t", p=128)
    kv = kTb.rearrange("(u p) t -> u p t", p=128)
    lv = lfT.rearrange("d (u p) t -> d u p t", p=128)
    hq = QTa.rearrange("(u p) t -> u p t", p=128)
    hk = hkT.rearrange("d (u p) t -> d u p t", p=128)
    hl = hlfT.rearrange("d (u p) t -> d u p t", p=128)
    phases = [
        layer0,
        na_phase,
        lambda: recurrence(2, 2, 64, 128, 1.0 / 16.0, [qv[0], qv[1]], [[kv[0], kv[1]]] * 2,
                           [[lv[0, 0], lv[0, 1]], [lv[1, 0], lv[1, 1]]], Vb, 128),
        lambda: rec_final(Gb, 512),
        lambda: phase_c(0, x_in, e_w_out, X1 if nlayers > 1 else out),
    ]
    if nlayers > 1:
        phases += [
            layer1_a,
            lambda: recurrence_b(QTa, [hkT[0], hkT[1]], [hlfT[0], hlfT[1]], Vb),
            lambda: rec_final(Gb, 0),
            ssd_conv,
            ssd_main,
            ssd_final,
            lambda: phase_c(1, X1, o_w_out, out),
        ]
    for ph in phases[:stop]:
        ph()

    s.barrier()
    with nc.Block() as block:
        s.emit(block)
    return nc, es


def _na_btab(rpb, ROWS):
    H = rpb.shape[0]
    out = np.full((5, 128, H, 5, 128), NEG, np.float32)
    reps = {0: 0, 1: 2, 2: 4, 3: ROWS - 4, 4: ROWS - 2}
    p = np.arange(128)
    q = np.arange(128)
    for cls, r in reps.items():
        ks = min(max(r - 4, 0), ROWS - 10)
        for blk in range(5):
            KR = ks + (blk * 128 + p) // 64
            kc = p % 64
            R = r + q // 64
            qc = q % 64
            rs = np.clip(R - 4, 0, ROWS - 8)
            cs = np.clip(qc - 8, 0, 48)
            vr = (KR[:, None] >= rs[None, :]) & (KR[:, None] < rs[None, :] + 8)
            vc = (kc[:, None] >= cs[None, :]) & (kc[:, None] < cs[None, :] + 16)
            dr = np.clip(KR[:, None] - R[None, :] + 7, 0, 14)
            dc = np.clip(kc[:, None] - qc[None, :], -15, 15) + 15
            g = rpb[:, dr, dc]
            valid = (vr & vc)[None]
            out[cls, :, :, blk, :] = np.where(valid, g, NEG).transpose(1, 0, 2)
    return out.reshape(5, 128, H * 640)


def prep_inputs(b, L, x, c, ada_w, ada_b, ln_g, ln_b, e_w_in, e_rpb, e_gla_w_up, e_gla_b, e_gla_norm_g, e_w_out,
                o_w_in, hgrn_lb, o_hgrn_norm_g, o_conv_w, o_conv_b, o_dt_bias, o_a_log, o_d_skip, o_ssm_norm_g, o_w_out):
    f = lambda a: np.ascontiguousarray(np.asarray(a, dtype=np.float32))
    m = {}
    m["x"] = f(x[b])
    m["cT"] = f(c[b].reshape(8, 128).T)
    m["ada_w"] = f(ada_w)
    m["ada_bT"] = f(ada_b.reshape(2, 24, 128).transpose(2, 0, 1))
    m["ada_bg"] = f(ada_b[:, 2048:3072])
    m["ln_g"] = f(ln_g)
    m["ln_b"] = f(ln_b)
    m["e_w_in"] = f(e_w_in[0])
    m["e_w_out"] = f(e_w_out[0])
    m["btab"] = f(_na_btab(np.asarray(e_rpb[0]), L // 64))
    m["gla_w_up"] = f(e_gla_w_up[0])
    m["gla_bT"] = f(e_gla_b[0].reshape(2, 2, 128).transpose(2, 0, 1))
    m["gla_ng"] = f(e_gla_norm_g)
    m["o_w_in"] = f(o_w_in[0])
    m["o_w_out"] = f(o_w_out[0])
    m["hgrn_lbT"] = f(hgrn_lb.reshape(2, 4, 128).transpose(2, 0, 1))
    m["hgrn_ng"] = f(o_hgrn_norm_g)
    m["conv_wT"] = f(o_conv_w[0].reshape(4, 8, 128).transpose(2, 1, 0))
    m["conv_bT"] = f(o_conv_b[0].reshape(8, 128).T)
    m["dt_bias"] = f(o_dt_bias[0].reshape(1, 16))
    m["a_log"] = f(o_a_log[0].reshape(1, 16))
    m["d_skip"] = f(o_d_skip.reshape(1, 8))
    m["ssm_ng"] = f(o_ssm_norm_g.reshape(1, 512))
    return m


def kernel(**inputs):
    x = np.asarray(inputs["x"])
    B, L, _ = x.shape
    nc, es = build(L)
    in_maps = [prep_inputs(b, L, **inputs) for b in range(B)]
    res = run_bass_kernel_spmd(nc, in_maps, core_ids=list(range(B)))
    return np.stack([np.asarray(r["out"], dtype=np.float32) for r in res.results], axis=0)
```

```python
import numpy as np
from contextlib import ExitStack
import concourse.bass as bass
import concourse.mybir as mybir
from concourse.bass_utils import run_bass_kernel_spmd

F32 = mybir.dt.float32
BF16 = mybir.dt.bfloat16
AF = mybir.ActivationFunctionType
ALU = mybir.AluOpType

D = 1024
NSLOT = 10
ALPHA = 4.0 ** 0.25
NEG = -30000.0


class Sched:
    ENGS = ('pe', 'act', 'dve', 'pool', 'sp')

    def __init__(self, nc, es):
        self.nc = nc
        self.streams = {e: [] for e in self.ENGS}
        self.cnt = {e: 0 for e in self.ENGS}
        self.sem = {e: es.enter_context(nc.semaphore('s_' + e)) for e in self.ENGS}
        self.waited = {e: {} for e in self.ENGS}
        self.lastw = {}
        self.readers = {}
        self.dslots = {}
        self.dnext = {}
        for q in ('sp', 'pool', 'act'):
            self.dslots[q] = [[es.enter_context(nc.semaphore('d_%s%d' % (q, i))), 0] for i in range(NSLOT)]
            self.dnext[q] = 0

    def _semh(self, key):
        if isinstance(key, str):
            return self.sem[key]
        return self.dslots[key[1]][key[2]][0]

    def _need(self, eng, dep):
        key, val = dep
        if key == eng and eng == 'pe':
            return
        if self.waited[eng].get(key, 0) >= val:
            return
        self.waited[eng][key] = val
        self.streams[eng].append(('w', key, val))

    def _deps(self, eng, r, w):
        for t in r:
            d = self.lastw.get(t)
            if d:
                self._need(eng, d)
        for t in w:
            d = self.lastw.get(t)
            if d:
                self._need(eng, d)
            rd = self.readers.get(t)
            if rd:
                for k, v in rd.items():
                    self._need(eng, (k, v))

    def _commit(self, dep, r, w):
        for t in r:
            rd = self.readers.setdefault(t, {})
            if rd.get(dep[0], 0) < dep[1]:
                rd[dep[0]] = dep[1]
        for t in w:
            self.lastw[t] = dep
            self.readers[t] = {}

    def op(self, eng, fn, r=(), w=()):
        self._deps(eng, r, w)
        self.cnt[eng] += 1
        self.streams[eng].append(('o', fn))
        self._commit((eng, self.cnt[eng]), r, w)

    def dma(self, q, out, in_, r=(), w=()):
        self._deps(q, r, w)
        i = self.dnext[q]
        self.dnext[q] = (i + 1) % NSLOT
        slot = self.dslots[q][i]
        key = ('d', q, i)
        if slot[1] > 0:
            self._need(q, (key, slot[1]))
        slot[1] += 16
        self.streams[q].append(('d', out, in_, key))
        self._commit((key, slot[1]), r, w)

    def barrier(self):
        deps = [(e, self.cnt[e]) for e in self.ENGS if self.cnt[e] > 0]
        for q in self.dslots:
            for i, sl in enumerate(self.dslots[q]):
                if sl[1] > 0:
                    deps.append((('d', q, i), sl[1]))
        for e in self.ENGS:
            for d in deps:
                self._need(e, d)

    def emit(self, block):
        decos = {'pe': block.tensor, 'act': block.scalar, 'dve': block.vector, 'pool': block.gpsimd, 'sp': block.sync}
        for e in self.ENGS:
            stream = self.streams[e]

            def body(eng, stream=stream, e=e):
                for it in stream:
                    if it[0] == 'w':
                        eng.wait_ge(self._semh(it[1]), it[2])
                    elif it[0] == 'o':
                        it[1](eng).then_inc(self.sem[e], 1)
                    else:
                        eng.dma_start(out=it[1], in_=it[2]).then_inc(self._semh(it[3]), 16)
            decos[e](body)


class Arena:
    def __init__(self, ap, ncols):
        self.ap = ap
        self.n = ncols
        self.pos = 0

    def reset(self, base=0):
        self.pos = base

    def f32(self, cols, shape=None):
        a = self.pos
        self.pos += cols
        assert self.pos <= self.n, ("arena overflow", self.pos, self.n)
        v = self.ap[:, a:a + cols]
        return v

    def bf16(self, cols):
        c32 = (cols + 1) // 2
        v = self.f32(c32).bitcast(BF16)
        return v[:, 0:cols]


def r3(ap, **kw):
    k = list(kw.keys())[0]
    return ap.rearrange("p (a %s) -> p a %s" % (k, k), **kw)


def MM(out, lhsT, rhs, start=True, stop=True, skip=False):
    if skip:
        return lambda e: e.matmul(out, lhsT=lhsT, rhs=rhs, start=start, stop=stop, skip_group_check=True)
    return lambda e: e.matmul(out, lhsT=lhsT, rhs=rhs, start=start, stop=stop)


def TR(out, in_, ident):
    return lambda e: e.transpose(out, in_, ident)


def ACT(out, in_, func, bias=None, scale=None, accum=None):
    kw = {}
    if bias is not None:
        kw['bias'] = bias
    if scale is not None:
        kw['scale'] = scale
    if accum is not None:
        kw['accum_out'] = accum
    return lambda e: e.activation(out=out, in_=in_, func=func, **kw)


def TT(out, in0, in1, op):
    return lambda e: e.tensor_tensor(out=out, in0=in0, in1=in1, op=op)


def TS(out, in0, s1, op0, s2=None, op1=None):
    if op1 is None:
        return lambda e: e.tensor_scalar(out=out, in0=in0, scalar1=s1, scalar2=None, op0=op0)
    return lambda e: e.tensor_scalar(out=out, in0=in0, scalar1=s1, scalar2=s2, op0=op0, op1=op1)


def STT(out, in0, scalar, in1, op0, op1):
    return lambda e: e.scalar_tensor_tensor(out=out, in0=in0, scalar=scalar, in1=in1, op0=op0, op1=op1)


def CP(out, in_):
    return lambda e: e.tensor_copy(out=out, in_=in_)


def MS(ap, c):
    return lambda e: e.memset(ap, c)


def build(L, nlayers=2, dbg=(), stop=99):
    nc = bass.Bass("TRN2", target_bir_lowering=False)
    NT = L // 128
    NS = L // 512
    ROWS = L // 64
    es = ExitStack()

    def din(name, shape, dt=F32):
        return nc.dram_tensor(name, list(shape), dt, kind="ExternalInput").ap()

    def dscr(name, shape, dt):
        kind = "ExternalOutput" if name in dbg else "Internal"
        return nc.dram_tensor(name, list(shape), dt, kind=kind).ap()

    x_in = din("x", [L, D])
    cT_in = din("cT", [128, 8])
    ada_w = din("ada_w", [2, D, 3 * D])
    ada_bT = din("ada_bT", [128, 2, 24])
    ada_bg = din("ada_bg", [2, D])
    ln_g = din("ln_g", [2, D])
    ln_b = din("ln_b", [2, D])
    e_w_in = din("e_w_in", [D, 3616])
    e_w_out = din("e_w_out", [D, D])
    btab = din("btab", [5, 128, 8 * 640])
    gla_w_up = din("gla_w_up", [2, 16, 256])
    gla_bT = din("gla_bT", [128, 2, 2])
    gla_ng = din("gla_ng", [1, 128])
    o_w_in = din("o_w_in", [D, 4112])
    o_w_out = din("o_w_out", [D, D])
    hgrn_lbT = din("hgrn_lbT", [128, 2, 4])
    hgrn_ng = din("hgrn_ng", [1, 128])
    conv_wT = din("conv_wT", [128, 8, 4])
    conv_bT = din("conv_bT", [128, 8])
    dt_bias = din("dt_bias", [1, 16])
    a_log = din("a_log", [1, 16])
    d_skip = din("d_skip", [1, 8])
    ssm_ng = din("ssm_ng", [1, 512])
    out = nc.dram_tensor("out", [L, D], F32, kind="ExternalOutput").ap()

    QTa = dscr("QTa", [512, L], BF16)
    KTa = dscr("KTa", [512, L], BF16)
    Va = dscr("Va", [L, 512], BF16)
    Va66 = dscr("Va66", [L, 528], BF16)
    Ga = dscr("Ga", [L, 512], BF16)
    qTb = dscr("qTb", [256, L], BF16)
    kTb = dscr("kTb", [256, L], BF16)
    Vb = dscr("Vb", [L, 512], BF16)
    Gb = dscr("Gb", [L, 512], BF16)
    lfT = dscr("lfT", [2, 256, L], F32)
    Of = dscr("Of", [L, 512], BF16)
    Ob = dscr("Ob", [L, 512], BF16)
    Y = dscr("Y", [L, D], BF16)
    X1 = dscr("X1", [L, D], F32)

    def sb(name, shape, dt):
        return es.enter_context(nc.sbuf_tensor(name, list(shape), dt))

    ident_f = sb("ident_f", [128, 128], F32)
    ident_b = sb("ident_b", [128, 128], BF16)
    ones_f = sb("ones_f", [128, 128], F32)
    mask_f128 = sb("mask_f128", [128, 128], BF16)
    mask_b128 = sb("mask_b128", [128, 128], BF16)
    mask_f32 = sb("mask_f32", [128, 128], BF16)
    mask_b32 = sb("mask_b32", [128, 128], BF16)
    seg128 = sb("seg128", [128, 512], F32)
    seg32 = sb("seg32", [128, 512], F32)
    modT = sb("modT", [128, 2, 16], F32)
    gate_bc = sb("gate_bc", [128, 2, D], F32)
    small = sb("small", [128, 64], F32)
    AW = 42000
    arena_t = sb("arena", [128, AW], F32)
    ar = Arena(arena_t, AW)
    banks = [es.enter_context(nc.psum_tensor("bank%d" % i, [128, 512], F32)) for i in range(8)]

    s = Sched(nc, es)
    uid = [0]

    def tok(prefix):
        uid[0] += 1
        return "%s#%d" % (prefix, uid[0])

    class Ring:
        def __init__(self, name, aps):
            self.aps = aps
            self.toks = [tok(name) for _ in aps]
            self.i = -1

        def next(self):
            self.i = (self.i + 1) % len(self.aps)
            return self.aps[self.i], self.toks[self.i]

    s.op('pool', MS(ident_f[:], 0.0), w=['ident_f'])
    s.op('pool', lambda e: e.affine_select(out=ident_f[:], in_=ident_f[:], pattern=[[-1, 128]], compare_op=ALU.not_equal,
                                           fill=1.0, base=0, channel_multiplier=1), r=['ident_f'], w=['ident_f'])
    s.op('pool', CP(ident_b[:], ident_f[:]), r=['ident_f'], w=['ident_b'])
    s.op('pool', MS(ones_f[:], 1.0), w=['ones_f'])
    s.op('pool', MS(mask_f128[:], 1.0), w=['mask_f128'])
    s.op('pool', lambda e: e.affine_select(out=mask_f128[:], in_=mask_f128[:], pattern=[[1, 128]], compare_op=ALU.is_ge,
                                           fill=0.0, base=0, channel_multiplier=-1), r=['mask_f128'], w=['mask_f128'])
    s.op('pool', MS(mask_b128[:], 1.0), w=['mask_b128'])
    s.op('pool', lambda e: e.affine_select(out=mask_b128[:], in_=mask_b128[:], pattern=[[-1, 128]], compare_op=ALU.is_ge,
                                           fill=0.0, base=0, channel_multiplier=1), r=['mask_b128'], w=['mask_b128'])
    s.op('pool', CP(mask_f32[:], mask_f128[:]), r=['mask_f128'], w=['mask_f32'])
    s.op('pool', CP(mask_b32[:], mask_b128[:]), r=['mask_b128'], w=['mask_b32'])
    for cb in range(4):
        s.op('pool', (lambda cb: lambda e: e.affine_select(out=mask_f32[:, 32 * cb:32 * cb + 32], in_=mask_f32[:, 32 * cb:32 * cb + 32],
                                                           pattern=[[0, 32]], compare_op=ALU.is_ge, fill=0.0, base=-32 * cb,
                                                           channel_multiplier=1))(cb), r=['mask_f32'], w=['mask_f32'])
        s.op('pool', (lambda cb: lambda e: e.affine_select(out=mask_b32[:, 32 * cb:32 * cb + 32], in_=mask_b32[:, 32 * cb:32 * cb + 32],
                                                           pattern=[[0, 32]], compare_op=ALU.is_ge, fill=0.0, base=32 * cb + 31,
                                                           channel_multiplier=-1))(cb), r=['mask_b32'], w=['mask_b32'])
    s.op('pool', MS(seg128[:], 1.0), w=['seg128'])
    s.op('pool', MS(seg32[:], 1.0), w=['seg32'])
    s.op('pool', MS(r3(seg128[:], b=128)[:, :, 0:1], 0.0), r=['seg128'], w=['seg128'])
    s.op('pool', MS(r3(seg32[:], b=32)[:, :, 0:1], 0.0), r=['seg32'], w=['seg32'])

    WIN = {}

    W_COLS = 8 * 4112 // 2

    def seg_order(c0s):
        order = []
        for c0 in c0s:
            if c0 // 512 not in order:
                order.append(c0 // 512)
        return order

    def prefetch_w_in(src, ncols, order):
        w_in = r3(arena_t[:, 0:W_COLS].bitcast(BF16), b=4112)
        WIN['w'] = w_in
        v = src.rearrange("(k p) c -> p k c", p=128)
        for pc in order:
            c0, c1 = pc * 512, min(ncols, (pc + 1) * 512)
            for k0 in (0, 4):
                s.dma('pool', w_in[:, k0:k0 + 4, c0:c1], v[:, k0:k0 + 4, c0:c1], w=['w_in%d' % pc])

    L0_C0S = [0, 128, 256, 384, 512, 640, 768, 896, 2048, 2176, 2304, 2432, 1024, 2560, 1536, 3072, 3584, 3600]
    L1_C0S = [0, 512, 1024, 1536, 2048, 2560, 3072, 3584, 4096]

    ar.reset(W_COLS)
    prefetch_w_in(e_w_in, 3616, seg_order(L0_C0S))
    cT = ar.f32(8)
    scT = ar.f32(16)
    sc_rep = ar.f32(8 * 128)
    abT = ar.f32(48)
    abg = ar.f32(2 * D)
    slabG = [ar.f32(8 * 512) for _ in range(3)]
    s.dma('sp', cT, cT_in[:, :], w=['cT'])
    s.dma('sp', abT, ada_bT.rearrange("p l c -> p (l c)"), w=['abT'])
    s.dma('sp', abg, ada_bg.rearrange("l d -> (l d)").rearrange("(o n) -> o n", o=1).partition_broadcast(128), w=['abg'])
    sc3 = r3(scT, b=2)
    s.op('act', ACT(sc3[:, :, 0], cT, AF.Silu), r=['cT'], w=['scT'])
    s.op('act', ACT(sc3[:, :, 1], cT, AF.Silu), r=['cT'], w=['scT'])
    scr3 = r3(sc_rep, b=128)
    for k in range(8):
        s.op('dve', TS(scr3[:, k, :], ones_f[:], sc3[:, k, 0:1], ALU.mult), r=['scT', 'ones_f'], w=['sc_rep'])
    rG = Ring('slabG', slabG)
    pmod = Ring('pmod', [banks[0], banks[1], banks[2], banks[3]])
    for l in range(nlayers):
        wv = ada_w[l].rearrange("(k p) c -> p k c", p=128)
        for sl_i in range(6):
            sl, st = rG.next()
            sl3 = r3(sl, b=512)
            s.dma('sp', sl3, wv[:, :, sl_i * 512:(sl_i + 1) * 512], w=[st])
            if sl_i < 4:
                for c4 in range(4):
                    cb = sl_i * 4 + c4
                    ps, pt = pmod.next()
                    for k in range(8):
                        s.op('pe', MM(ps[:, 0:2], sl3[:, k, c4 * 128:(c4 + 1) * 128], sc3[:, k, :], k == 0, k == 7), r=[st, 'scT'], w=[pt])
                    if cb < 8:
                        s.op('dve', TT(modT[:, l, cb:cb + 1], ps[:, 0:1], abT[:, l * 24 + cb:l * 24 + cb + 1], ALU.add),
                             r=[pt, 'abT'], w=['modT'])
                    else:
                        s.op('dve', STT(modT[:, l, cb:cb + 1], ps[:, 0:1], 1.0, abT[:, l * 24 + cb:l * 24 + cb + 1], ALU.add, ALU.add),
                             r=[pt, 'abT'], w=['modT'])
            else:
                hf = sl_i - 4
                ps, pt = pmod.next()
                for k in range(8):
                    s.op('pe', MM(ps[:, :], scr3[:, k, :], sl3[:, k, :], k == 0, k == 7), r=[st, 'sc_rep'], w=[pt])
                s.op('dve', TT(gate_bc[:, l, hf * 512:(hf + 1) * 512], ps[:, :], abg[:, l * D + hf * 512:l * D + (hf + 1) * 512], ALU.add),
                     r=[pt, 'abg'], w=['gate_bc'])

    def phase_a(l, x_src, segs):
        xts = [ar.f32(4 * D) for _ in range(2)]
        hTs = [ar.bf16(8 * 512) for _ in range(2)]
        rx = Ring('xt', xts)
        rh = Ring('hT', hTs)
        ptr = Ring('ptr', [banks[0], banks[1]])
        ppj = Ring('ppj', [banks[2], banks[3], banks[4], banks[5]])
        xv = x_src.rearrange("(n s p) d -> n p s d", p=128, s=4)
        state = {}

        xloaded = {}

        def load_x(i):
            xt, xtok = rx.next()
            xt3 = r3(xt, b=D)
            s.dma('sp', xt3, xv[i], w=[xtok])
            xloaded[i] = (xt3, xtok)

        def load_and_transpose(i):
            if i not in xloaded:
                load_x(i)
            xt3, xtok = xloaded.pop(i)
            hT, htok = rh.next()
            hT3 = r3(hT, b=512)
            for k in range(8):
                ps, pt = ptr.next()
                for sub in range(4):
                    s.op('pe', TR(ps[:, sub * 128:(sub + 1) * 128], xt3[:, sub, k * 128:(k + 1) * 128], ident_f[:]),
                         r=[xtok, 'ident_f'], w=[pt])
                s.op('act', ACT(hT3[:, k, :], ps[:, :], AF.Identity, bias=modT[:, l, k:k + 1], scale=modT[:, l, 8 + k:9 + k]),
                     r=[pt, 'modT'], w=[htok])
            state[i] = (hT3, htok)

        load_and_transpose(0)
        for i in range(NS):
            if i + 1 < NS:
                load_x(i + 1)
            hT3, htok = state.pop(i)
            for si, sg in enumerate(segs):
                if si == len(segs) // 2 and i + 1 < NS:
                    load_and_transpose(i + 1)
                if sg['kind'] == 'fm':
                    ps, pt = ppj.next()
                    n = sg['n']
                    for k in range(8):
                        s.op('pe', MM(ps[0:n, :], WIN['w'][:, k, sg['c0']:sg['c0'] + n], hT3[:, k, :], k == 0, k == 7),
                             r=['w_in%d' % (sg['c0'] // 512), htok], w=[pt])
                    sg['emit'](ps, pt, i, None)
                else:
                    n = sg['n']
                    for sub in range(4):
                        ps, pt = ppj.next()
                        for k in range(8):
                            s.op('pe', MM(ps[:, 0:n], hT3[:, k, sub * 128:(sub + 1) * 128], WIN['w'][:, k, sg['c0']:sg['c0'] + n], k == 0, k == 7),
                                 r=['w_in%d' % (sg['c0'] // 512), htok], w=[pt])
                        sg['emit'](ps, pt, i, sub)

    def phase_c(l, x_src, w_out_src, x_dst):
        s.barrier()
        if l == 0 and nlayers > 1:
            ar.reset(W_COLS)
            prefetch_w_in(o_w_in, 4112, seg_order(L1_C0S))
        else:
            ar.reset()
        wo = r3(ar.bf16(8 * D), b=D)
        wov = w_out_src.rearrange("(k p) c -> p k c", p=128)
        wst = Ring('wst', [ar.f32(D) for _ in range(2)])
        for k in range(8):
            st, sttok = wst.next()
            s.dma('sp', st, wov[:, k, :], w=[sttok])
            s.op('dve', TT(wo[:, k, :], st, gate_bc[:, l, :], ALU.mult), r=[sttok, 'gate_bc'], w=['wo'])
        lng = ar.f32(D)
        lnb = ar.f32(D)
        epsc = ar.f32(2)
        s.op('pool', MS(epsc, 1e-5), w=['epsc'])
        s.dma('sp', lng, ln_g[l:l + 1, :].partition_broadcast(128), w=['lng'])
        s.dma('sp', lnb, ln_b[l:l + 1, :].partition_broadcast(128), w=['lnb'])
        ry = Ring('cy', [ar.bf16(D) for _ in range(4)])
        rxt = Ring('cx', [ar.f32(D) for _ in range(4)])
        ryt = Ring('cyT', [r3(ar.bf16(8 * 128), b=128) for _ in range(2)])
        rz = Ring('cz', [ar.f32(D) for _ in range(3)])
        ro = Ring('co', [ar.f32(D) for _ in range(3)])
        rst = Ring('cst', [ar.f32(24) for _ in range(3)])
        ptr = Ring('cptr', [banks[0], banks[1]])
        pmm = Ring('cpmm', [banks[2], banks[3], banks[4], banks[5]])
        loaded = {}

        def load(t):
            yt, ytok = ry.next()
            s.dma('sp', yt, Y[t * 128:(t + 1) * 128, :], w=[ytok])
            xt, xtok = rxt.next()
            s.dma('sp', xt, x_src[t * 128:(t + 1) * 128, :], w=[xtok])
            loaded[t] = (yt, ytok, xt, xtok)

        load(0)
        if NT > 1:
            load(1)
        stA = {}

        def stage_a(t):
            if t + 2 < NT:
                load(t + 2)
            yt, ytok, xt, xtok = loaded.pop(t)
            yT, yTtok = ryt.next()
            for half in range(2):
                ps, pt = ptr.next()
                psb = ps[:, :].bitcast(BF16)
                for kk in range(4):
                    k = half * 4 + kk
                    s.op('pe', TR(psb[:, kk * 128:(kk + 1) * 128], yt[:, k * 128:(k + 1) * 128], ident_b[:]), r=[ytok, 'ident_b'], w=[pt])
                s.op('act', ACT(yT[:, half * 4:half * 4 + 4, :], r3(psb[:, 0:512], b=128), AF.Copy), r=[pt], w=[yTtok])
            z, ztok = rz.next()
            for hf in range(2):
                ps, pt = pmm.next()
                for k in range(8):
                    s.op('pe', MM(ps[:, :], yT[:, k, :], wo[:, k, hf * 512:(hf + 1) * 512], k == 0, k == 7), r=[yTtok, 'wo'], w=[pt])
                s.op('dve', STT(z[:, hf * 512:(hf + 1) * 512], xt[:, hf * 512:(hf + 1) * 512], ALPHA, ps[:, :], ALU.mult, ALU.add),
                     r=[pt, xtok], w=[ztok])
            st, sttok = rst.next()
            s.op('dve', lambda e, st=st, z=z: e.bn_stats(out=st[:, 0:6], in_=z[:, 0:512]), r=[ztok], w=[sttok])
            s.op('dve', lambda e, st=st, z=z: e.bn_stats(out=st[:, 6:12], in_=z[:, 512:1024]), r=[ztok], w=[sttok])
            s.op('dve', lambda e, st=st: e.bn_aggr(out=st[:, 12:14], in_=st[:, 0:12]), r=[sttok], w=[sttok])
            s.op('act', ACT(st[:, 14:15], st[:, 13:14], AF.Ln, bias=epsc[:, 0:1]), r=[sttok, 'epsc'], w=[sttok])
            s.op('act', ACT(st[:, 15:16], st[:, 14:15], AF.Exp, scale=-0.5), r=[sttok], w=[sttok])
            stA[t] = (z, ztok, st, sttok)

        def stage_b(t):
            z, ztok, st, sttok = stA.pop(t)
            o, otok = ro.next()
            s.op('dve', TS(o, z, st[:, 12:13], ALU.subtract, st[:, 15:16], ALU.mult), r=[ztok, sttok], w=[otok])
            s.op('dve', TT(o, o, lng, ALU.mult), r=[otok, 'lng'], w=[otok])
            s.op('dve', TT(o, o, lnb, ALU.add), r=[otok, 'lnb'], w=[otok])
            s.dma('sp', x_dst[t * 128:(t + 1) * 128, :], o, r=[otok], w=[tok('xdst')])

        stage_a(0)
        for t in range(NT):
            if t + 1 < NT:
                stage_a(t + 1)
            stage_b(t)

    def recurrence(nunits, nh, dk, CS, sc, qT_src, kT_src, lf_src, V_src, dvw):
        s.barrier()
        ar.reset()
        nsub = 128 // CS
        segm = seg128 if CS == 128 else seg32
        segk = 'seg128' if CS == 128 else 'seg32'
        vw = nh * 128
        chains = [(u, d) for u in range(nunits) for d in range(2)]
        tq = Ring('tq', [ar.bf16(512) for _ in range(4)])
        tk = Ring('tk', [ar.bf16(512) for _ in range(4)])
        tlf = Ring('tlf', [ar.f32(512) for _ in range(4)])
        tP = Ring('tP', [ar.f32(512) for _ in range(2)])
        tB = Ring('tB', [ar.f32(512) for _ in range(2)])
        tR = Ring('tR', [ar.f32(512) for _ in range(2)])
        tE = Ring('tE', [ar.bf16(512) for _ in range(4)])
        C = {}
        for ch in chains:
            C[ch] = dict(
                qs=Ring('qs', [ar.bf16(512) for _ in range(2)]),
                kh=Ring('kh', [ar.bf16(512) for _ in range(2)]),
                kb=Ring('kb', [ar.bf16(512) for _ in range(2)]),
                V=Ring('V', [r3(ar.bf16(4 * vw), b=vw) for _ in range(2)]),
                dch=Ring('dch', [ar.f32(16) for _ in range(2)]),
                S=ar.f32(128), Stok=tok('S'),
                Sbf=Ring('Sbf', [ar.bf16(128) for _ in range(nsub + 2)]),
                ATs=Ring('ATs', [r3(ar.bf16(nh * 128), b=128) for _ in range(2)]),
                kbt=Ring('kbt', [ar.bf16(128) for _ in range(2)]),
                ost=Ring('ost', [ar.bf16(vw) for _ in range(2)]),
            )
        if nsub > 1:
            cmask = r3(ar.bf16(nsub * 128), b=128)
            rmask = r3(ar.bf16(nsub * 128), b=128)
            s.op('pool', MS(cmask, 0.0), w=['cmask'])
            s.op('pool', MS(rmask, 1.0), w=['rmask'])
            for ci in range(nsub):
                s.op('pool', MS(cmask[:, ci, ci * CS:(ci + 1) * CS], 1.0), r=['cmask'], w=['cmask'])
                s.op('pool', (lambda ci: lambda e: e.affine_select(out=rmask[:, ci, :], in_=rmask[:, ci, :], pattern=[[0, 128]],
                                                                   compare_op=ALU.is_ge, fill=0.0, base=-CS * ci, channel_multiplier=1))(ci),
                     r=['rmask'], w=['rmask'])
                s.op('pool', (lambda ci: lambda e: e.affine_select(out=rmask[:, ci, :], in_=rmask[:, ci, :], pattern=[[0, 128]],
                                                                   compare_op=ALU.is_ge, fill=0.0, base=CS * ci + CS - 1, channel_multiplier=-1))(ci),
                     r=['rmask'], w=['rmask'])
            for ch in chains:
                C[ch]['qsm'] = Ring('qsm', [ar.bf16(128) for _ in range(2)])
                C[ch]['kbtm'] = Ring('kbtm', [ar.bf16(128) for _ in range(2)])
        if nh == 2:
            pAT = Ring('pAT', [[banks[0][:, 0:128], banks[1][:, 0:128]]])
        else:
            pAT = Ring('pAT', [[banks[0][:, 0:128]], [banks[1][:, 0:128]]])
        pU = Ring('pU', [banks[4], banks[5]])
        pT = Ring('pT', [banks[6], banks[7]])
        cur = {}
        for ch in chains:
            c = C[ch]
            s.op('pool', MS(c['S'], 0.0), w=[c['Stok']])
            sb0, sbt0 = c['Sbf'].next()
            s.op('pool', MS(sb0, 0.0), w=[sbt0])
            c['sprev'] = (sb0, sbt0)

        pre = {}

        def prep_load(ch, st_i):
            u, d = ch
            c = C[ch]
            t0 = st_i * 512
            q, qtok = tq.next()
            k, ktok = tk.next()
            lf, lftok = tlf.next()
            s.dma('sp', lf, lf_src[d][u][:, t0:t0 + 512], w=[lftok])
            s.dma('sp', q, qT_src[u][:, t0:t0 + 512], w=[qtok])
            s.dma('sp', k, kT_src[d][u][:, t0:t0 + 512], w=[ktok])
            V, Vtok = c['V'].next()
            s.dma('sp', V, V_src[t0:t0 + 512, u * vw:(u + 1) * vw].rearrange("(s p) c -> p s c", p=128), w=[Vtok])
            pre[(ch, st_i)] = (q, qtok, k, ktok, lf, lftok, V, Vtok)

        def prep(ch, st_i):
            u, d = ch
            c = C[ch]
            t0 = st_i * 512
            if (ch, st_i) not in pre:
                prep_load(ch, st_i)
            q, qtok, k, ktok, lf, lftok, V, Vtok = pre.pop((ch, st_i))
            P, Ptok = tP.next()
            s.op('dve', lambda e, P=P, lf=lf: e.tensor_tensor_scan(out=P, data0=segm[:], data1=lf, initial=0.0, op0=ALU.mult, op1=ALU.add),
                 r=[lftok, segk], w=[Ptok])
            P3 = r3(P, b=CS)
            nchk = 512 // CS
            totb = P3[:, :, CS - 1:CS].broadcast_to([128, nchk, CS])
            B, Btok = tB.next()
            R, Rtok = tR.next()
            if d == 0:
                s.op('dve', TT(r3(R, b=CS), totb, P3, ALU.subtract), r=[Ptok], w=[Rtok])
                Bd, Bdtok = P, Ptok
            else:
                s.op('dve', TT(R, P, lf, ALU.subtract), r=[Ptok, lftok], w=[Rtok])
                s.op('dve', TT(r3(B, b=CS), totb, r3(R, b=CS), ALU.subtract), r=[Ptok, Rtok], w=[Btok])
                Bd, Bdtok = B, Btok
            dch, dchtok = c['dch'].next()
            s.op('act', ACT(dch[:, 0:nchk], P3[:, :, CS - 1], AF.Exp, scale=sc), r=[Ptok], w=[dchtok])
            qs, qstok = c['qs'].next()
            kh, khtok = c['kh'].next()
            kb, kbtok = c['kb'].next()
            e1, e1tok = tE.next()
            s.op('act', ACT(e1, Bd, AF.Exp, scale=sc), r=[Bdtok], w=[e1tok])
            s.op('dve', TT(qs, q, e1, ALU.mult), r=[qtok, e1tok], w=[qstok])
            e2, e2tok = tE.next()
            s.op('act', ACT(e2, Bd, AF.Exp, scale=-sc), r=[Bdtok], w=[e2tok])
            s.op('dve', TT(kh, k, e2, ALU.mult), r=[ktok, e2tok], w=[khtok])
            e3, e3tok = tE.next()
            s.op('act', ACT(e3, R, AF.Exp, scale=sc), r=[Rtok], w=[e3tok])
            s.op('dve', TT(kb, k, e3, ALU.mult), r=[ktok, e3tok], w=[kbtok])
            cur[ch] = dict(qs=qs, qstok=qstok, kh=kh, khtok=khtok, kb=kb, kbtok=kbtok, V=V, Vtok=Vtok, dch=dch, dchtok=dchtok)

        nchain = len(chains)
        cpb = 4
        OT = {}
        for idx, ch in enumerate(chains):
            if nh == 1:
                bk = 2 + idx // cpb
                OT[ch] = dict(tiles=[banks[bk][:, (idx % cpb) * 128:(idx % cpb + 1) * 128]], toks=['pO%d' % bk], first=(idx % cpb == 0))
            else:
                OT[ch] = dict(tiles=[banks[2 + hh][:, idx * 128:(idx + 1) * 128] for hh in range(nh)],
                              toks=['pO%d' % (2 + hh) for hh in range(nh)], first=(idx == 0))
        ctx = {}

        def stage1(ch, st_i, sub):
            u, d = ch
            c = C[ch]
            cc = cur[ch]
            cols = slice(sub * 128, (sub + 1) * 128)
            mask = (mask_f128 if d == 0 else mask_b128) if CS == 128 else (mask_f32 if d == 0 else mask_b32)
            mtok = ('mask_f128' if d == 0 else 'mask_b128') if CS == 128 else ('mask_f32' if d == 0 else 'mask_b32')
            pa, patok = pAT.next()
            for hh in range(nh):
                pr = slice(hh * dk, (hh + 1) * dk)
                s.op('pe', MM(pa[hh], cc['kh'][pr, cols], cc['qs'][pr, cols]), r=[cc['khtok'], cc['qstok']], w=[patok])
            ATs, ATtok = c['ATs'].next()
            for hh in range(nh):
                s.op('dve', TT(ATs[:, hh, :], pa[hh], mask[:], ALU.mult), r=[patok, mtok], w=[ATtok])
            pt_, pttok = pT.next()
            ptb = pt_[:, :].bitcast(BF16)
            s.op('pe', TR(ptb[:, 0:128], cc['kb'][:, cols], ident_b[:]), r=[cc['kbtok'], 'ident_b'], w=[pttok])
            kbt, kbttok = c['kbt'].next()
            s.op('act', ACT(kbt, ptb[:, 0:128], AF.Copy), r=[pttok], w=[kbttok])
            x = dict(cc=cc, sub=sub, st_i=st_i, kbt=kbt, kbttok=kbttok)
            if nsub > 1:
                kb3, kb3tok = c['kbtm'].next()
                s.op('dve', TT(kb3, kbt, rmask[:, nsub - 1, :], ALU.mult), r=[kbttok, 'rmask'], w=[kb3tok])
                qs3, qs3tok = c['qsm'].next()
                s.op('dve', TT(qs3, cc['qs'][:, cols], cmask[:, nsub - 1, :], ALU.mult), r=[cc['qstok'], 'cmask'], w=[qs3tok])
                x.update(kb3=kb3, kb3tok=kb3tok, qs3=qs3, qs3tok=qs3tok)
            ot = OT[ch]
            for hh in range(nh):
                s.op('pe', MM(ot['tiles'][hh], ATs[:, hh, :], cc['V'][:, sub, hh * 128:(hh + 1) * 128], ot['first'], False, True),
                     r=[ATtok, cc['Vtok']], w=[ot['toks'][hh]])
            x['order'] = list(range(nsub)) if d == 0 else list(range(nsub - 1, -1, -1))
            ctx[ch] = x

        def stage2(ch, n_i):
            u, d = ch
            c = C[ch]
            x = ctx[ch]
            cc = x['cc']
            sub = x['sub']
            cidx = x['order'][n_i]
            last = (n_i == nsub - 1)
            rows = slice(cidx * CS, (cidx + 1) * CS)
            ccols = slice(sub * 128 + cidx * CS, sub * 128 + (cidx + 1) * CS)
            sprev, sprevtok = c['sprev']
            ot = OT[ch]
            masked = (nsub > 1 and cidx == nsub - 1)
            for hh in range(nh):
                pr = slice(hh * dk, (hh + 1) * dk)
                if masked:
                    s.op('pe', MM(ot['tiles'][hh], x['qs3'][pr, :], sprev[pr, :], False, last, True), r=[x['qs3tok'], sprevtok], w=[ot['toks'][hh]])
                else:
                    s.op('pe', MM(ot['tiles'][hh][rows, :], cc['qs'][pr, ccols], sprev[pr, :], False, last, True),
                         r=[cc['qstok'], sprevtok], w=[ot['toks'][hh]])
            pu, putok = pU.next()
            for hh in range(nh):
                pr = slice(hh * dk, (hh + 1) * dk)
                if masked:
                    s.op('pe', MM(pu[pr, 0:128], x['kb3'][:, pr], cc['V'][:, sub, hh * 128:(hh + 1) * 128]), r=[x['kb3tok'], cc['Vtok']], w=[putok])
                else:
                    s.op('pe', MM(pu[pr, 0:128], x['kbt'][rows, pr], cc['V'][rows, sub, hh * 128:(hh + 1) * 128]),
                         r=[x['kbttok'], cc['Vtok']], w=[putok])
            chk = sub * nsub + cidx
            s.op('dve', STT(c['S'], c['S'], cc['dch'][:, chk:chk + 1], pu[:, 0:128], ALU.mult, ALU.add),
                 r=[c['Stok'], cc['dchtok'], putok], w=[c['Stok']])
            sbn, sbntok = c['Sbf'].next()
            s.op('act', ACT(sbn, c['S'], AF.Copy), r=[c['Stok']], w=[sbntok])
            c['sprev'] = (sbn, sbntok)

        def stage3(ch):
            u, d = ch
            c = C[ch]
            x = ctx.pop(ch)
            ot = OT[ch]
            tglob = x['st_i'] * 4 + x['sub']
            ost, osttok = c['ost'].next()
            for hh in range(nh):
                s.op('act', ACT(ost[:, hh * 128:(hh + 1) * 128], ot['tiles'][hh], AF.Copy), r=[ot['toks'][hh]], w=[osttok])
            dst = Of if d == 0 else Ob
            s.dma('sp', dst[tglob * 128:(tglob + 1) * 128, u * vw:(u + 1) * vw], ost, r=[osttok], w=[tok('odst')])

        nxt = {}
        for ch in chains:
            prep_load(ch, 0 if ch[1] == 0 else NS - 1)
        for ch in chains:
            prep(ch, 0 if ch[1] == 0 else NS - 1)
        for i in range(NS):
            now = {ch: cur[ch] for ch in chains}
            for sub_i in range(4):
                for ch in chains:
                    st_i = i if ch[1] == 0 else NS - 1 - i
                    sub = sub_i if ch[1] == 0 else 3 - sub_i
                    cur[ch] = now[ch]
                    stage1(ch, st_i, sub)
                for n_i in range(nsub):
                    for ch in chains:
                        stage2(ch, n_i)
                for ch in chains:
                    stage3(ch)
                if sub_i == 0 and i + 1 < NS:
                    for ch in chains:
                        prep_load(ch, (i + 1) if ch[1] == 0 else NS - 2 - i)
                if sub_i == 1 and i + 1 < NS:
                    for ch in chains:
                        prep(ch, (i + 1) if ch[1] == 0 else NS - 2 - i)
                        nxt[ch] = cur[ch]
            if i + 1 < NS:
                for ch in chains:
                    now[ch] = None
                    cur[ch] = nxt[ch]

    def recurrence_b(qsrc, ksrc, lfsrc, V_src):
        s.barrier()
        ar.reset()
        CS, nsub, NU = 32, 4, 4
        segm = ar.f32(1024)
        s.op('pool', MS(segm, 1.0), w=['segm'])
        s.op('pool', MS(r3(segm, b=CS)[:, :, 0:1], 0.0), r=['segm'], w=['segm'])
        c3 = ar.bf16(128)
        r3m = ar.bf16(128)
        s.op('pool', MS(c3, 0.0), w=['c3'])
        s.op('pool', MS(c3[:, 96:128], 1.0), r=['c3'], w=['c3'])
        s.op('pool', MS(r3m, 1.0), w=['r3m'])
        s.op('pool', lambda e: e.affine_select(out=r3m, in_=r3m, pattern=[[0, 128]], compare_op=ALU.is_ge, fill=0.0, base=-96,
                                               channel_multiplier=1), r=['r3m'], w=['r3m'])
        qv = qsrc.rearrange("(u p) t -> p u t", p=128)
        kv_ = [ksrc[d].rearrange("(u p) t -> p u t", p=128) for d in range(2)]
        lv_ = [lfsrc[d].rearrange("(u p) t -> p u t", p=128) for d in range(2)]
        tq = Ring('bq', [r3(ar.bf16(1024), b=512) for _ in range(4)])
        tk = Ring('bk', [r3(ar.bf16(1024), b=512) for _ in range(4)])
        tlf = Ring('blf', [ar.f32(1024) for _ in range(4)])
        tP = Ring('bP', [ar.f32(1024) for _ in range(2)])
        tB = Ring('bB', [ar.f32(1024) for _ in range(1)])
        tR = Ring('bR', [ar.f32(1024) for _ in range(1)])
        tE = Ring('bE', [ar.bf16(1024) for _ in range(3)])
        G = {}
        for d in range(2):
            G[d] = dict(
                qs=Ring('bqs', [r3(ar.bf16(NU * 512), b=512) for _ in range(2)]),
                kh=Ring('bkh', [r3(ar.bf16(NU * 512), b=512) for _ in range(2)]),
                kb=Ring('bkb', [r3(ar.bf16(NU * 512), b=512) for _ in range(2)]),
                V=Ring('bV', [r3(ar.bf16(4 * 512), b=512) for _ in range(2)]),
                dch=Ring('bdch', [r3(ar.f32(NU * 16), b=16) for _ in range(2)]),
                S=r3(ar.f32(NU * 128), b=128), Stok=tok('bS'),
                Sbf=Ring('bSbf', [r3(ar.bf16(NU * 128), b=128) for _ in range(nsub + 2)]),
                ATs=Ring('bATs', [r3(ar.bf16(NU * 128), b=128) for _ in range(2)]),
                kbt=Ring('bkbt', [r3(ar.bf16(NU * 128), b=128) for _ in range(2)]),
                kb3=Ring('bkb3', [r3(ar.bf16(NU * 128), b=128) for _ in range(2)]),
                qs3=Ring('bqs3', [r3(ar.bf16(NU * 128), b=128) for _ in range(2)]),
                ost=Ring('bost', [ar.bf16(NU * 128) for _ in range(2)]),
                pa=(banks[0 + d], 'bpa%d' % d), po=(banks[2 + d], 'bpo%d' % d), pu=(banks[4 + d], 'bpu%d' % d), pt=(banks[6 + d], 'bpt%d' % d),
            )
            g = G[d]
            s.op('pool', MS(g['S'], 0.0), w=[g['Stok']])
            sb0, sbt0 = g['Sbf'].next()
            s.op('pool', MS(sb0, 0.0), w=[sbt0])
            g['sprev'] = (sb0, sbt0)
        cur = {}

        pre = {}

        def prep_load(d, st_i):
            g = G[d]
            t0 = st_i * 512
            V, Vtok = g['V'].next()
            s.dma('sp', V, V_src[t0:t0 + 512, :].rearrange("(s p) c -> p s c", p=128), w=[Vtok])
            lfs = []
            for pr_ in range(2):
                us = slice(2 * pr_, 2 * pr_ + 2)
                lf, lftok = tlf.next()
                s.dma('sp', r3(lf, b=512), lv_[d][:, us, t0:t0 + 512], w=[lftok])
                q, qtok = tq.next()
                k, ktok = tk.next()
                s.dma('sp', q, qv[:, us, t0:t0 + 512], w=[qtok])
                s.dma('sp', k, kv_[d][:, us, t0:t0 + 512], w=[ktok])
                lfs.append((lf, lftok, q, qtok, k, ktok))
            pre[(d, st_i)] = (V, Vtok, lfs)

        def prep(d, st_i):
            g = G[d]
            t0 = st_i * 512
            if (d, st_i) not in pre:
                prep_load(d, st_i)
            V, Vtok, lfs = pre.pop((d, st_i))
            qs, qstok = g['qs'].next()
            kh, khtok = g['kh'].next()
            kb, kbtok = g['kb'].next()
            dch, dchtok = g['dch'].next()
            for pr_ in range(2):
                us = slice(2 * pr_, 2 * pr_ + 2)
                lf, lftok, q, qtok, k, ktok = lfs[pr_]
                P, Ptok = tP.next()
                s.op('dve', lambda e, P=P, lf=lf: e.tensor_tensor_scan(out=P, data0=segm, data1=lf, initial=0.0, op0=ALU.mult, op1=ALU.add),
                     r=[lftok, 'segm'], w=[Ptok])
                P3 = r3(P, b=CS)
                nchk = 1024 // CS
                totb = P3[:, :, CS - 1:CS].broadcast_to([128, nchk, CS])
                if d == 0:
                    Bd, Bdtok = P, Ptok
                else:
                    R, Rtok = tR.next()
                    B, Btok = tB.next()
                    s.op('dve', TT(R, P, lf, ALU.subtract), r=[Ptok, lftok], w=[Rtok])
                    s.op('dve', TT(r3(B, b=CS), totb, r3(R, b=CS), ALU.subtract), r=[Ptok, Rtok], w=[Btok])
                    Bd, Bdtok = B, Btok
                dchv = dch[:, us, :].rearrange("p u c -> p (u c)")
                s.op('act', ACT(dchv, P3[:, :, CS - 1], AF.Exp), r=[Ptok], w=[dchtok])
                q2 = q.rearrange("p u t -> p (u t)")
                k2 = k.rearrange("p u t -> p (u t)")
                e1, e1tok = tE.next()
                s.op('act', ACT(e1, Bd, AF.Exp), r=[Bdtok], w=[e1tok])
                s.op('dve', TT(qs[:, us, :].rearrange("p u t -> p (u t)"), q2, e1, ALU.mult), r=[qtok, e1tok], w=[qstok])
                e2, e2tok = tE.next()
                s.op('act', ACT(e2, Bd, AF.Exp, scale=-1.0), r=[Bdtok], w=[e2tok])
                s.op('dve', TT(kh[:, us, :].rearrange("p u t -> p (u t)"), k2, e2, ALU.mult), r=[ktok, e2tok], w=[khtok])
                s.op('dve', TT(r3(kb[:, us, :].rearrange("p u t -> p (u t)"), b=CS), r3(kh[:, us, :].rearrange("p u t -> p (u t)"), b=CS),
                               dchv.unsqueeze(2).broadcast_to([128, nchk, CS]), ALU.mult), r=[khtok, dchtok], w=[kbtok])
            cur[d] = dict(qs=qs, qstok=qstok, kh=kh, khtok=khtok, kb=kb, kbtok=kbtok, V=V, Vtok=Vtok, dch=dch, dchtok=dchtok)

        ctx = {}

        def stage1(d, st_i, sub):
            g = G[d]
            cc = cur[d]
            cols = slice(sub * 128, (sub + 1) * 128)
            mask = mask_f32 if d == 0 else mask_b32
            mtok = 'mask_f32' if d == 0 else 'mask_b32'
            pa, patok = g['pa']
            for u in range(NU):
                s.op('pe', MM(pa[:, u * 128:(u + 1) * 128], cc['kh'][:, u, cols], cc['qs'][:, u, cols]), r=[cc['khtok'], cc['qstok']], w=[patok])
            pt_, pttok = g['pt']
            ptb = pt_[:, :].bitcast(BF16)
            for u in range(NU):
                s.op('pe', TR(ptb[:, u * 128:(u + 1) * 128], cc['kb'][:, u, cols], ident_b[:]), r=[cc['kbtok'], 'ident_b'], w=[pttok])
            kbt, kbttok = g['kbt'].next()
            s.op('act', ACT(kbt, r3(ptb[:, 0:512], b=128), AF.Copy), r=[pttok], w=[kbttok])
            ATs, ATtok = g['ATs'].next()
            s.op('dve', TT(ATs, r3(pa[:, :], b=128), mask[:].unsqueeze(1).broadcast_to([128, NU, 128]), ALU.mult), r=[patok, mtok], w=[ATtok])
            qs3, qs3tok = g['qs3'].next()
            s.op('dve', TT(qs3, cc['qs'][:, :, cols], c3.unsqueeze(1).broadcast_to([128, NU, 128]), ALU.mult), r=[cc['qstok'], 'c3'], w=[qs3tok])
            kb3, kb3tok = g['kb3'].next()
            s.op('dve', TT(kb3, kbt, r3m.unsqueeze(1).broadcast_to([128, NU, 128]), ALU.mult), r=[kbttok, 'r3m'], w=[kb3tok])
            po, potok = g['po']
            for u in range(NU):
                s.op('pe', MM(po[:, u * 128:(u + 1) * 128], ATs[:, u, :], cc['V'][:, sub, u * 128:(u + 1) * 128], u == 0, False, True),
                     r=[ATtok, cc['Vtok']], w=[potok])
            ctx[d] = dict(cc=cc, sub=sub, st_i=st_i, kbt=kbt, kbttok=kbttok, kb3=kb3, kb3tok=kb3tok, qs3=qs3, qs3tok=qs3tok,
                          order=list(range(nsub)) if d == 0 else list(range(nsub - 1, -1, -1)))

        def stage2(d, n_i):
            g = G[d]
            x = ctx[d]
            cc = x['cc']
            sub = x['sub']
            cidx = x['order'][n_i]
            last = (n_i == nsub - 1)
            rows = slice(cidx * CS, (cidx + 1) * CS)
            ccols = slice(sub * 128 + cidx * CS, sub * 128 + (cidx + 1) * CS)
            sprev, sprevtok = g['sprev']
            po, potok = g['po']
            pu, putok = g['pu']
            masked = (cidx == nsub - 1)
            for u in range(NU):
                us = slice(u * 128, (u + 1) * 128)
                if masked:
                    s.op('pe', MM(po[:, us], x['qs3'][:, u, :], sprev[:, u, :], False, last, True), r=[x['qs3tok'], sprevtok], w=[potok])
                else:
                    s.op('pe', MM(po[rows, us], cc['qs'][:, u, ccols], sprev[:, u, :], False, last, True), r=[cc['qstok'], sprevtok], w=[potok])
            for u in range(NU):
                us = slice(u * 128, (u + 1) * 128)
                if masked:
                    s.op('pe', MM(pu[:, us], x['kb3'][:, u, :], cc['V'][:, sub, us]), r=[x['kb3tok'], cc['Vtok']], w=[putok])
                else:
                    s.op('pe', MM(pu[:, us], x['kbt'][rows, u, :], cc['V'][rows, sub, us]), r=[x['kbttok'], cc['Vtok']], w=[putok])
            chk = sub * nsub + cidx
            dv_ = cc['dch'][:, :, chk:chk + 1].broadcast_to([128, NU, 128])
            s.op('dve', TT(g['S'], g['S'], dv_, ALU.mult), r=[g['Stok'], cc['dchtok']], w=[g['Stok']])
            s.op('dve', TT(g['S'], g['S'], r3(pu[:, :], b=128), ALU.add), r=[g['Stok'], putok], w=[g['Stok']])
            sbn, sbntok = g['Sbf'].next()
            s.op('act', ACT(sbn, g['S'], AF.Copy), r=[g['Stok']], w=[sbntok])
            g['sprev'] = (sbn, sbntok)

        def stage3(d):
            g = G[d]
            x = ctx.pop(d)
            po, potok = g['po']
            tglob = x['st_i'] * 4 + x['sub']
            ost, osttok = g['ost'].next()
            s.op('act', ACT(ost, po[:, :], AF.Copy), r=[potok], w=[osttok])
            dst = Of if d == 0 else Ob
            s.dma('sp', dst[tglob * 128:(tglob + 1) * 128, :], ost, r=[osttok], w=[tok('odst')])

        nxt = {}
        for d in range(2):
            prep(d, 0 if d == 0 else NS - 1)
        for i in range(NS):
            now = {d: cur[d] for d in range(2)}
            for sub_i in range(4):
                for d in range(2):
                    st_i = i if d == 0 else NS - 1 - i
                    sub = sub_i if d == 0 else 3 - sub_i
                    cur[d] = now[d]
                    stage1(d, st_i, sub)
                for n_i in range(nsub):
                    for d in range(2):
                        stage2(d, n_i)
                for d in range(2):
                    stage3(d)
                if sub_i == 0 and i + 1 < NS:
                    for d in range(2):
                        prep_load(d, (i + 1) if d == 0 else NS - 2 - i)
                if sub_i == 1 and i + 1 < NS:
                    for d in range(2):
                        prep(d, (i + 1) if d == 0 else NS - 2 - i)
                        nxt[d] = cur[d]
            if i + 1 < NS:
                for d in range(2):
                    cur[d] = nxt[d]

    def rec_final(G_src, ycol0):
        s.barrier()
        ar.reset()
        epsr = ar.f32(2)
        s.op('pool', MS(epsr, 1e-6), w=['epsr'])
        rfl = Ring('rfl', [ar.bf16(512) for _ in range(4)])
        rf = Ring('rf', [ar.f32(512) for _ in range(3)])
        rb = Ring('rb', [ar.bf16(512) for _ in range(4)])
        rg = Ring('rg', [ar.bf16(512) for _ in range(5)])
        rsq = Ring('rsq', [ar.f32(512) for _ in range(2)])
        rss = Ring('rss', [ar.f32(8) for _ in range(3)])
        ry = Ring('ry', [ar.bf16(512) for _ in range(3)])
        loaded = {}

        def load(t):
            rows = slice(t * 128, (t + 1) * 128)
            fl, fltok = rfl.next()
            b, btok = rb.next()
            g, gtok = rg.next()
            s.dma('sp', fl, Of[rows, :], w=[fltok])
            s.dma('sp', b, Ob[rows, :], w=[btok])
            s.dma('sp', g, G_src[rows, :], w=[gtok])
            loaded[t] = (fl, fltok, b, btok, g, gtok)

        load(0)
        if NT > 1:
            load(1)
        stA = {}

        def stage_a(t):
            if t + 2 < NT:
                load(t + 2)
            fl, fltok, b, btok, g, gtok = loaded.pop(t)
            f, ftok = rf.next()
            s.op('dve', TT(f, fl, b, ALU.add), r=[fltok, btok], w=[ftok])
            sq, sqtok = rsq.next()
            ss, sstok = rss.next()
            for h in range(4):
                hs = slice(h * 128, (h + 1) * 128)
                s.op('act', ACT(sq[:, hs], f[:, hs], AF.Square, accum=ss[:, h:h + 1]), r=[ftok], w=[sqtok, sstok])
            s.op('act', ACT(ss[:, 0:4], ss[:, 0:4], AF.Ln, bias=epsr[:, 0:1], scale=1.0 / 128.0), r=[sstok, 'epsr'], w=[sstok])
            s.op('act', ACT(ss[:, 4:8], ss[:, 0:4], AF.Exp, scale=-0.5), r=[sstok], w=[sstok])
            stA[t] = (f, ftok, ss, sstok, g, gtok)

        def stage_b(t):
            rows = slice(t * 128, (t + 1) * 128)
            f, ftok, ss, sstok, g, gtok = stA.pop(t)
            y, ytok = ry.next()
            for h in range(4):
                hs = slice(h * 128, (h + 1) * 128)
                s.op('dve', STT(y[:, hs], f[:, hs], ss[:, 4 + h:5 + h], g[:, hs], ALU.mult, ALU.mult), r=[ftok, sstok, gtok], w=[ytok])
            s.dma('sp', Y[rows, ycol0:ycol0 + 512], y, r=[ytok], w=[tok('ydst')])

        stage_a(0)
        for t in range(NT):
            if t + 1 < NT:
                stage_a(t + 1)
            stage_b(t)

    holder = {}

    def fm_store(dst, row0, func=AF.Copy, scale=None, out_dt=BF16, col0=0):
        def emit(ps, pt, i, sub, n=128):
            st, sttok = holder['stg'].next()
            o = st.bitcast(BF16)[:, 0:512] if out_dt == BF16 else st
            s.op('act', ACT(o[0:n, :], ps[0:n, :], func, scale=scale), r=[pt], w=[sttok])
            s.dma('sp', dst[row0:row0 + n, col0 + i * 512:col0 + (i + 1) * 512], o[0:n, :], r=[sttok], w=[tok('fmdst')])
        return emit

    def tm_store(dst, func=AF.Copy, mulkey=None):
        def emit(ps, pt, i, sub):
            st, sttok = holder['stg'].next()
            o = st.bitcast(BF16)[:, 0:512]
            if mulkey is not None:
                s.op('act', ACT(st, ps[:, :], func), r=[pt], w=[sttok])
                st2, st2tok = holder['stg'].next()
                o = st2.bitcast(BF16)[:, 0:512]
                s.op('dve', TT(o, st, holder[mulkey], ALU.mult), r=[sttok, mulkey], w=[st2tok])
                sttok = st2tok
            elif func == AF.Copy:
                s.op('dve', CP(o, ps[:, :]), r=[pt], w=[sttok])
            else:
                s.op('act', ACT(o, ps[:, :], func), r=[pt], w=[sttok])
            t0 = i * 512 + sub * 128
            s.dma('sp', dst[t0:t0 + 128, :], o, r=[sttok], w=[tok('tmdst')])
        return emit

    def layer0():
        segs = []
        for blk in range(4):
            segs.append(dict(kind='fm', c0=blk * 128, n=128, emit=fm_store(QTa, blk * 128, scale=0.125)))
        for blk in range(4):
            segs.append(dict(kind='fm', c0=512 + blk * 128, n=128, emit=fm_store(KTa, blk * 128)))
        for blk in range(2):
            segs.append(dict(kind='fm', c0=2048 + blk * 128, n=128, emit=fm_store(qTb, blk * 128, scale=0.125)))
        for blk in range(2):
            segs.append(dict(kind='fm', c0=2304 + blk * 128, n=128, emit=fm_store(kTb, blk * 128)))

        def v66_emit(ps, pt, i, sub):
            st, sttok = holder['vstg'].next()
            s.op('dve', CP(st.rearrange("p (h d) -> p h d", d=66)[:, :, 0:64], r3(ps[:, :], b=64)), r=[pt], w=[sttok])
            t0 = i * 512 + sub * 128
            s.dma('sp', Va66[t0:t0 + 128, :], st, r=[sttok], w=[tok('v66')])
        segs.append(dict(kind='tm', c0=1024, n=512, emit=v66_emit))
        segs.append(dict(kind='tm', c0=2560, n=512, emit=tm_store(Vb)))
        segs.append(dict(kind='tm', c0=1536, n=512, emit=tm_store(Ga, AF.Silu)))
        segs.append(dict(kind='tm', c0=3072, n=512, emit=tm_store(Gb, AF.Silu, 'ngbc')))

        gpend = []

        def gate_emit(d):
            def emit(ps, pt, i, sub):
                lr, lrtok = holder['lr'].next()
                lrb = lr.bitcast(BF16)[:, 0:512]
                s.op('act', ACT(lrb[0:16, :], ps[0:16, :], AF.Copy), r=[pt], w=[lrtok])
                for blk in range(2):
                    pz, pztok = holder['pz'].next()
                    s.op('pe', MM(pz[:, :], holder['wupb'][0:16, d * 256 + blk * 128:d * 256 + (blk + 1) * 128], lrb[0:16, :]),
                         r=[lrtok, 'wupb'], w=[pztok])
                    st, sttok = holder['gst'].next()
                    s.op('act', ACT(st, pz[:, :], AF.Sigmoid, bias=holder['glab'][:, d * 2 + blk:d * 2 + blk + 1]), r=[pztok, 'glab'], w=[sttok])
                    gpend.append((st, sttok, d, blk, i))
                if d == 1:
                    while gpend:
                        st, sttok, d_, blk_, i_ = gpend.pop(0)
                        s.op('act', ACT(st, st, AF.Ln), r=[sttok], w=[sttok])
                        s.dma('sp', lfT[d_, blk_ * 128:(blk_ + 1) * 128, i_ * 512:(i_ + 1) * 512], st, r=[sttok], w=[tok('lfdst')])
            return emit
        segs.append(dict(kind='fm', c0=3584, n=16, emit=gate_emit(0)))
        segs.append(dict(kind='fm', c0=3600, n=16, emit=gate_emit(1)))

        def alloc_extras():
            holder['stg'] = Ring('stg', [ar.f32(512) for _ in range(8)])
            holder['lr'] = Ring('lr', [ar.f32(512) for _ in range(2)])
            holder['wup'] = ar.f32(512)
            holder['wupb'] = ar.bf16(512)
            holder['gst'] = Ring('gst', [ar.f32(512) for _ in range(6)])
            holder['glab'] = ar.f32(4)
            holder['pz'] = Ring('pz', [banks[6], banks[7]])
            vst = [ar.bf16(528) for _ in range(3)]
            holder['vstg'] = Ring('vstg', vst)
            for v_, vt_ in zip(vst, holder['vstg'].toks):
                s.op('pool', MS(v_, 1.0), w=[vt_])
            holder['ngbc'] = ar.f32(512)
            for h in range(4):
                s.dma('sp', holder['ngbc'][:, h * 128:(h + 1) * 128], gla_ng[0:1, :].partition_broadcast(128), w=['ngbc'])
            for d in range(2):
                s.dma('sp', holder['wup'][0:16, d * 256:(d + 1) * 256], gla_w_up[d], w=['wup'])
            s.op('dve', CP(holder['wupb'][0:16, :], holder['wup'][0:16, :]), r=['wup'], w=['wupb'])
            s.dma('sp', holder['glab'], gla_bT.rearrange("p d b -> p (d b)"), w=['glab'])
        phase_a_with(0, x_in, segs, alloc_extras, e_w_in, 3616)

    hkT = dscr("hkT", [2, 512, L], BF16)
    hlfT = dscr("hlfT", [2, 512, L], F32)
    xbcT = dscr("xbcT", [1024, L + 4], BF16)
    dtA = dscr("dtA", [L, 32], F32)
    Btm = dscr("Btm", [L, 256], BF16)

    def layer1_a():
        segs = []
        for blk in range(4):
            segs.append(dict(kind='fm', c0=blk * 128, n=128, emit=fm_store(QTa, blk * 128, scale=128.0 ** -0.5)))

        fpend = []

        def forget_emit(d, blk):
            def emit(ps, pt, i, sub):
                st, sttok = holder['fst'].next()
                s.op('act', ACT(st, ps[:, :], AF.Sigmoid), r=[pt], w=[sttok])
                s.op('dve', TS(st, st, holder['lbs'][:, 12 + blk:13 + blk], ALU.mult, holder['lbs'][:, 8 + blk:9 + blk], ALU.add),
                     r=[sttok, 'lbs'], w=[sttok])
                st2, st2tok = holder['stg'].next()
                kb_ = st2.bitcast(BF16)[:, 0:512]
                s.op('act', ACT(kb_, st, AF.Identity, bias=1.0, scale=-1.0), r=[sttok], w=[st2tok])
                s.dma('sp', hkT[d, blk * 128:(blk + 1) * 128, i * 512:(i + 1) * 512], kb_, r=[st2tok], w=[tok('hk')])
                fpend.append((st, sttok, d, blk, i))
                if d == 1 and blk == 3:
                    while fpend:
                        st_, sttok_, d_, blk_, i_ = fpend.pop(0)
                        s.op('act', ACT(st_, st_, AF.Ln), r=[sttok_], w=[sttok_])
                        s.dma('sp', hlfT[d_, blk_ * 128:(blk_ + 1) * 128, i_ * 512:(i_ + 1) * 512], st_, r=[sttok_], w=[tok('hlf')])
            return emit
        for d in range(2):
            for blk in range(4):
                segs.append(dict(kind='fm', c0=512 + d * 512 + blk * 128, n=128, emit=forget_emit(d, blk)))
        segs.append(dict(kind='tm', c0=1536, n=512, emit=tm_store(Vb)))
        segs.append(dict(kind='tm', c0=2048, n=512, emit=tm_store(Gb, AF.Silu, 'ngbc')))
        segs.append(dict(kind='tm', c0=2560, n=512, emit=tm_store(Ga, AF.Silu)))
        for blk in range(8):
            segs.append(dict(kind='fm', c0=3072 + blk * 128, n=128, emit=fm_store(xbcT, blk * 128, col0=2)))

        def dt_emit(ps, pt, i, sub):
            dtt, dttok = holder['dtt'].next()
            s.op('dve', TT(dtt[:, 0:16], ps[:, 0:16], holder['dtb'][:, 0:16], ALU.add), r=[pt, 'dtb'], w=[dttok])
            s.op('act', ACT(dtt[:, 0:16], dtt[:, 0:16], AF.Exp), r=[dttok], w=[dttok])
            s.op('act', ACT(dtt[:, 0:16], dtt[:, 0:16], AF.Ln, bias=1.0), r=[dttok], w=[dttok])
            s.op('dve', TT(dtt[:, 16:32], dtt[:, 0:16], holder['dtb'][:, 16:32], ALU.mult), r=[dttok, 'dtb'], w=[dttok])
            t0 = i * 512 + sub * 128
            s.dma('sp', dtA[t0:t0 + 128, :], dtt[:, 0:32], r=[dttok], w=[tok('dtA')])
        segs.append(dict(kind='tm', c0=4096, n=16, emit=dt_emit))

        def alloc_extras():
            holder['stg'] = Ring('stg', [ar.f32(512) for _ in range(8)])
            holder['dtt'] = Ring('dtt', [ar.f32(32) for _ in range(2)])
            holder['fst'] = Ring('fst', [ar.f32(512) for _ in range(10)])
            holder['dtb'] = ar.f32(32)
            holder['lbs'] = ar.f32(16)
            holder['ngbc'] = ar.f32(512)
            for h in range(4):
                s.dma('sp', holder['ngbc'][:, h * 128:(h + 1) * 128], hgrn_ng[0:1, :].partition_broadcast(128), w=['ngbc'])
            zz = ar.f32(16)
            lbs = holder['lbs']
            s.dma('sp', lbs[:, 0:8], hgrn_lbT.rearrange("p l b -> p (l b)"), w=['lbs'])
            s.op('dve', TT(lbs[:, 8:12], lbs[:, 4:8], lbs[:, 0:4], ALU.subtract), r=['lbs'], w=['lbs'])
            s.op('act', ACT(lbs[:, 8:12], lbs[:, 8:12], AF.Sigmoid), r=['lbs'], w=['lbs'])
            s.op('dve', TS(lbs[:, 12:16], lbs[:, 8:12], -1.0, ALU.mult, 1.0, ALU.add), r=['lbs'], w=['lbs'])
            dtb = holder['dtb']
            s.dma('sp', dtb[:, 0:16], dt_bias[0:1, :].partition_broadcast(128), w=['dtb'])
            s.dma('sp', dtb[:, 16:32], a_log[0:1, :].partition_broadcast(128), w=['dtb'])
            s.op('act', ACT(dtb[:, 16:32], dtb[:, 16:32], AF.Exp), r=['dtb'], w=['dtb'])
            s.op('dve', TS(dtb[:, 16:32], dtb[:, 16:32], -1.0, ALU.mult), r=['dtb'], w=['dtb'])
            s.op('pool', MS(zz, 0.0), w=['zz'])
            xv_ = xbcT.rearrange("(b p) t -> p b t", p=128)
            zzb = r3(zz.bitcast(BF16)[:, 0:16], b=2)
            s.dma('sp', xv_[:, :, 0:2], zzb, r=['zz'], w=[tok('xbcz')])
            s.dma('sp', xv_[:, :, L + 2:L + 4], zzb, r=['zz'], w=[tok('xbcz')])
        phase_a_with(1, X1, segs, alloc_extras, o_w_in, 4112)

    def phase_a_with(l, x_src, segs, extras, wsrc, wcols):
        s.barrier()
        ar.reset(W_COLS)
        extras()
        phase_a(l, x_src, segs)

    def na_phase():
        s.barrier()
        ar.reset()
        NK = 640
        E_int = r3(ar.bf16(8 * NK), b=NK)
        E_edge = r3(ar.bf16(8 * NK), b=NK)
        bst = ar.f32(8 * NK)
        KTs = Ring('KT', [r3(ar.bf16(4 * 1024), b=1024) for _ in range(4)])
        Vs_raw = [ar.bf16(8 * 8 * 66) for _ in range(4)]
        Vs = Ring('Vn', [v.rearrange("p (b h d) -> p b h d", b=8, h=8) for v in Vs_raw])
        QTs = Ring('QT', [r3(ar.bf16(4 * 128), b=128) for _ in range(5)])
        Gs = Ring('Gn', [ar.bf16(512) for _ in range(5)])
        eS = Ring('eS', [ar.bf16(NK) for _ in range(5)])
        PTs = Ring('PT', [ar.bf16(NK) for _ in range(5)])
        rec = Ring('rec', [ar.f32(8) for _ in range(2)])
        yst = Ring('yst', [ar.bf16(512) for _ in range(4)])
        pSA = Ring('pSA', [banks[0], banks[1], banks[2]])
        pSB = Ring('pSB', [banks[3], banks[4], banks[5]])
        pOn = Ring('pOn', [banks[6], banks[7]])

        def load_E(cls, dst, dtok):
            s.dma('sp', bst, btab[cls], w=['bst'])
            s.op('act', ACT(dst.rearrange("p h k -> p (h k)"), bst, AF.Exp), r=['bst'], w=[dtok])

        load_E(2, E_int, 'E_int')
        edge_loaded = [None]
        QTv = QTa.rearrange("(pr p) t -> p pr t", p=128)
        KTv = KTa.rearrange("(pr p) t -> p pr t", p=128)
        loaded = {}
        wloaded = {}
        NSB = NT // 4

        def wstart(sb):
            return min(max(8 * sb - 4, 0), ROWS - 16)

        def wload(sb):
            k0 = wstart(sb) * 64
            KT, KTtok = KTs.next()
            s.dma('sp', KT, KTv[:, :, k0:k0 + 1024], w=[KTtok])
            V, Vtok = Vs.next()
            s.dma('sp', V.rearrange("p b h d -> p b (h d)"), Va66[k0:k0 + 1024, :].rearrange("(b p) c -> p b c", p=128), w=[Vtok])
            wloaded[sb] = (KT, KTtok, V, Vtok)

        def load(t):
            QT, QTtok = QTs.next()
            s.dma('sp', QT, QTv[:, :, t * 128:(t + 1) * 128], w=[QTtok])
            G, Gtok = Gs.next()
            s.dma('sp', G, Ga[t * 128:(t + 1) * 128, :], w=[Gtok])
            loaded[t] = (QT, QTtok, G, Gtok)

        wload(0)
        if NSB > 1:
            wload(1)
        load(0)
        if NT > 1:
            load(1)
        tctx = {}

        def tile_ctx(t):
            if t + 2 < NT:
                load(t + 2)
            sb = t // 4
            if t % 4 == 0 and sb + 2 < NSB:
                wload(sb + 2)
            r = 2 * t
            ks = min(max(r - 4, 0), ROWS - 10)
            cls = (r - ks) // 2
            if cls == 2:
                E, Etok = E_int, 'E_int'
            else:
                if edge_loaded[0] != cls:
                    load_E(cls, E_edge, 'E_edge')
                    edge_loaded[0] = cls
                E, Etok = E_edge, 'E_edge'
            KT, KTtok, V, Vtok = wloaded[sb]
            if t % 4 == 3:
                wloaded.pop(sb)
            QT, QTtok, G, Gtok = loaded.pop(t)
            y, ytok = yst.next()
            boff = (ks - wstart(sb)) // 2
            tctx[t] = dict(E=E, Etok=Etok, KT=KT, KTtok=KTtok, V=V, Vtok=Vtok, QT=QT, QTtok=QTtok, G=G, Gtok=Gtok, y=y, ytok=ytok, po={}, boff=boff)

        def emit_S(t, h):
            c = tctx[t]
            pr = slice((h % 2) * 64, (h % 2) * 64 + 64)
            pa, patok = pSA.next()
            pb, pbtok = pSB.next()
            for blk in range(5):
                dst = pa[:, blk * 128:(blk + 1) * 128] if blk < 4 else pb[:, 0:128]
                s.op('pe', MM(dst, c['KT'][pr, h // 2, (c['boff'] + blk) * 128:(c['boff'] + blk + 1) * 128], c['QT'][pr, h // 2, :]),
                     r=[c['KTtok'], c['QTtok']], w=[patok if blk < 4 else pbtok])
            e_, etok = eS.next()
            s.op('act', ACT(e_[:, 0:512], pa[:, :], AF.Exp), r=[patok], w=[etok])
            s.op('act', ACT(e_[:, 512:640], pb[:, 0:128], AF.Exp), r=[pbtok], w=[etok])
            P, Ptok = PTs.next()
            s.op('dve', TT(P, e_, c['E'][:, h, :], ALU.mult), r=[etok, c['Etok']], w=[Ptok])
            return P, Ptok

        def emit_PV(t, h, P, Ptok):
            c = tctx[t]
            hg, hh = h // 4, h % 4
            if hg not in c['po']:
                c['po'][hg] = pOn.next()
            po, potok = c['po'][hg]
            for blk in range(5):
                s.op('pe', MM(po[:, hh * 65:hh * 65 + 65], P[:, blk * 128:(blk + 1) * 128], c['V'][:, c['boff'] + blk, h, 0:65], blk == 0, blk == 4),
                     r=[Ptok, c['Vtok']], w=[potok])
            if hh == 3:
                rc, rctok = rec.next()
                po3 = po[:, 0:260].rearrange("p (h d) -> p h d", d=65)
                s.op('dve', lambda e, rc=rc, po3=po3: e.reciprocal(out=rc[:, 0:4], in_=po3[:, :, 64]), r=[potok], w=[rctok])
                for j in range(4):
                    hj = hg * 4 + j
                    s.op('dve', STT(c['y'][:, hj * 64:(hj + 1) * 64], po3[:, j, 0:64], rc[:, j:j + 1], c['G'][:, hj * 64:(hj + 1) * 64], ALU.mult, ALU.mult),
                         r=[potok, rctok, c['Gtok']], w=[c['ytok']])
                if hg == 1:
                    s.dma('sp', Y[t * 128:(t + 1) * 128, 0:512], c['y'], r=[c['ytok']], w=[tok('ydst')])
                    tctx.pop(t)

        pend = []
        for t in range(NT):
            for h in range(8):
                if t not in tctx:
                    tile_ctx(t)
                P, Ptok = emit_S(t, h)
                pend.append((t, h, P, Ptok))
                if len(pend) > 2:
                    emit_PV(*pend.pop(0))
        while pend:
            emit_PV(*pend.pop(0))

    def ssd_conv():
        s.barrier()
        ar.reset()
        cw = ar.f32(32)
        cbias = ar.f32(8)
        s.dma('sp', cw, conv_wT.rearrange("p b k -> p (b k)"), w=['cw'])
        s.dma('sp', cbias, conv_bT[:, :], w=['cbias'])
        dg = ar.bf16(32 * 128)
        dg3 = r3(dg, b=128)
        for j in range(32):
            s.op('dve', TS(dg3[:, j, :], ident_f[:], cw[:, j:j + 1], ALU.mult), r=['ident_f', 'cw'], w=['dg'])
        rin = Ring('cin', [ar.bf16(516) for _ in range(6)])
        rfm = Ring('cfm', [ar.bf16(512) for _ in range(12)])
        rtm = Ring('ctm', [ar.bf16(512) for _ in range(3)])
        ptr = Ring('cvp', [banks[0], banks[1], banks[2]])
        pcv = Ring('pcv', [banks[3], banks[4], banks[5], banks[6]])
        items = [(i, blk) for i in range(NS) for blk in range(8)]
        cloaded = {}

        def cload(n):
            i, blk = items[n]
            xin, xtok = rin.next()
            s.dma('sp', xin[:, 0:515], xbcT[blk * 128:(blk + 1) * 128, i * 512:i * 512 + 515], w=[xtok])
            cloaded[n] = (xin, xtok)

        for n in range(min(4, len(items))):
            cload(n)
        for i in range(NS):
            t0 = i * 512
            fm = {}
            for blk in range(8):
                n = i * 8 + blk
                if n + 4 < len(items):
                    cload(n + 4)
                xin, xtok = cloaded.pop(n)
                pc, pctok = pcv.next()
                for k in range(4):
                    s.op('pe', MM(pc[:, :], dg3[:, blk * 4 + k, :], xin[:, k:k + 512], k == 0, k == 3), r=['dg', xtok], w=[pctok])
                o, otok = rfm.next()
                s.op('act', ACT(o, pc[:, :], AF.Silu, bias=cbias[:, blk:blk + 1]), r=[pctok, 'cbias'], w=[otok])
                fm[blk] = (o, otok)
                if blk in (4, 5):
                    s.dma('sp', qTb[(blk - 4) * 128:(blk - 3) * 128, t0:t0 + 512], o, r=[otok], w=[tok('BT')])
                if blk in (6, 7):
                    s.dma('sp', kTb[(blk - 6) * 128:(blk - 5) * 128, t0:t0 + 512], o, r=[otok], w=[tok('CT')])
            for sub in range(4):
                ps, pt = ptr.next()
                psb = ps[:, :].bitcast(BF16)
                for b4 in range(4):
                    o, otok = fm[b4]
                    s.op('pe', TR(psb[:, b4 * 128:(b4 + 1) * 128], o[:, sub * 128:(sub + 1) * 128], ident_b[:]), r=[otok, 'ident_b'], w=[pt])
                tm, tmtok = rtm.next()
                s.op('act', ACT(tm, psb[:, 0:512], AF.Copy), r=[pt], w=[tmtok])
                s.dma('sp', Va[t0 + sub * 128:t0 + (sub + 1) * 128, :], tm, r=[tmtok], w=[tok('xs')])
                ps, pt = ptr.next()
                psb = ps[:, :].bitcast(BF16)
                for b2 in range(2):
                    o, otok = fm[4 + b2]
                    s.op('pe', TR(psb[:, b2 * 128:(b2 + 1) * 128], o[:, sub * 128:(sub + 1) * 128], ident_b[:]), r=[otok, 'ident_b'], w=[pt])
                tm, tmtok = rtm.next()
                s.op('dve', CP(tm[:, 0:256], psb[:, 0:256]), r=[pt], w=[tmtok])
                s.dma('sp', Btm[t0 + sub * 128:t0 + (sub + 1) * 128, :], tm[:, 0:256], r=[tmtok], w=[tok('Btm')])

    def ssd_main():
        s.barrier()
        ar.reset()
        tri = [ar.f32(128), ar.f32(128)]
        mbf = [ar.f32(128), ar.f32(128)]
        s.op('pool', CP(tri[0], mask_f128[:]), r=['mask_f128'], w=['tri'])
        s.op('pool', CP(tri[1], mask_b128[:]), r=['mask_b128'], w=['tri'])
        mb4 = [ar.bf16(512), ar.bf16(512)]
        for d in range(2):
            s.op('pool', TS(mbf[d], tri[d], -1.0, ALU.add, -NEG, ALU.mult), r=['tri'], w=['mbf'])
            for hh in range(4):
                s.op('pool', CP(mb4[d][:, hh * 128:(hh + 1) * 128], mbf[d]), r=['mbf'], w=['mb4'])
        chains = [(g, d) for g in range(2) for d in range(2)]
        C = {}
        for ch in chains:
            C[ch] = dict(S=ar.f32(256), Stok=tok('sS'), Sbf=Ring('sSbf', [ar.bf16(256) for _ in range(3)]))
            s.op('pool', MS(C[ch]['S'], 0.0), w=[C[ch]['Stok']])
            sb0, sbt0 = C[ch]['Sbf'].next()
            s.op('pool', MS(sb0, 0.0), w=[sbt0])
            C[ch]['sprev'] = (sb0, sbt0)
        rdta = Ring('dta', [ar.f32(32) for _ in range(4)])
        rsm = Ring('sm', [ar.f32(64) for _ in range(4)])
        rR = Ring('sR', [ar.f32(1024) for _ in range(4)])
        rBT = Ring('sBT', [ar.bf16(128) for _ in range(9)])
        rCT = Ring('sCT', [ar.bf16(128) for _ in range(9)])
        rBm = Ring('sBm', [ar.bf16(128) for _ in range(9)])
        rxs = Ring('sxs', [ar.bf16(256) for _ in range(9)])
        rcb = Ring('scb', [ar.bf16(128) for _ in range(5)])
        rxdt = Ring('sxdt', [ar.bf16(256) for _ in range(5)])
        rxd = Ring('sxd', [ar.bf16(256) for _ in range(5)])
        rarg = Ring('sarg', [ar.f32(512) for _ in range(4)])
        rsg = Ring('ssg', [ar.bf16(512) for _ in range(4)])
        rat = Ring('sat', [ar.bf16(512) for _ in range(3)])
        ry1 = Ring('sy1', [ar.f32(256) for _ in range(3)])
        ry2 = Ring('sy2', [ar.f32(256) for _ in range(3)])
        rtS = Ring('stS', [ar.f32(256) for _ in range(2)])
        ryb = Ring('syb', [ar.bf16(256) for _ in range(3)])
        pq = Ring('spq', [banks[0], banks[1]])
        pCB = Ring('spCB', [banks[2]])
        pBC = Ring('spBC', [banks[3], banks[4]])
        pY = Ring('spY', [banks[5], banks[6]])
        pU = Ring('spU', [banks[7]])
        shared = {}
        sloaded = {}

        def sload(ch, t):
            g, d = ch
            rows = slice(t * 128, (t + 1) * 128)
            BT, BTtok = rBT.next()
            CT, CTtok = rCT.next()
            Bm, Bmtok = rBm.next()
            xs, xstok = rxs.next()
            s.dma('sp', BT, qTb[g * 128:(g + 1) * 128, rows], w=[BTtok])
            s.dma('sp', CT, kTb[g * 128:(g + 1) * 128, rows], w=[CTtok])
            s.dma('sp', Bm, Btm[rows, g * 128:(g + 1) * 128], w=[Bmtok])
            s.dma('sp', xs, Va[rows, g * 256:(g + 1) * 256], w=[xstok])
            sloaded[(ch, t)] = (BT, BTtok, CT, CTtok, Bm, Bmtok, xs, xstok)

        dloaded = {}

        def dload(d, t):
            dta, dtatok = rdta.next()
            s.dma('sp', dta, dtA[t * 128:(t + 1) * 128, :], w=[dtatok])
            dloaded[(d, t)] = (dta, dtatok)

        def dprep(d, t):
            dta, dtatok = dloaded.pop((d, t))
            a_d = dta[:, 16 + d * 8:24 + d * 8]
            q_, qtok = pq.next()
            s.op('pe', MM(q_[:, 0:8], tri[d], a_d), r=['tri', dtatok], w=[qtok])
            s.op('pe', MM(q_[:, 8:16], ones_f[:], a_d), r=['ones_f', dtatok], w=[qtok])
            sm, smtok = rsm.next()
            s.op('act', ACT(sm[:, 0:8], q_[:, 0:8], AF.Copy), r=[qtok], w=[smtok, qtok])
            s.op('act', ACT(sm[:, 8:16], q_[:, 0:8], AF.Exp), r=[qtok], w=[smtok, qtok])
            s.op('dve', TT(sm[:, 16:24], q_[:, 8:16], sm[:, 0:8], ALU.subtract), r=[qtok, smtok], w=[smtok, qtok])
            s.op('act', ACT(sm[:, 24:32], sm[:, 16:24], AF.Exp), r=[smtok], w=[smtok])
            s.op('act', ACT(sm[:, 32:40], q_[:, 8:16], AF.Exp), r=[qtok], w=[smtok, qtok])
            R, Rtok = rR.next()
            s.op('dve', TT(r3(R, b=128), a_d.unsqueeze(2).broadcast_to([128, 8, 128]), tri[d].unsqueeze(1).broadcast_to([128, 8, 128]), ALU.mult),
                 r=[dtatok, 'tri'], w=[Rtok])
            shared[(d, t)] = dict(dta=dta, dtatok=dtatok, sm=sm, smtok=smtok, R=R, Rtok=Rtok)

        X = {}

        def s2(ch, t):
            g, d = ch
            sh = shared[(d, t)]
            BT, BTtok, CT, CTtok, Bm, Bmtok, xs, xstok = sloaded.pop((ch, t))
            cbp, cbptok = pCB.next()
            s.op('pe', MM(cbp[:, 0:128], BT, CT), r=[BTtok, CTtok], w=[cbptok])
            cb, cbtok = rcb.next()
            s.op('act', ACT(cb, cbp[:, 0:128], AF.Copy), r=[cbptok], w=[cbtok])
            dta = sh['dta']
            sm = sh['sm']
            xdt, xdttok = rxdt.next()
            dtv = dta[:, d * 8 + g * 4:d * 8 + g * 4 + 4].unsqueeze(2).broadcast_to([128, 4, 64])
            s.op('dve', TT(r3(xdt, b=64), r3(xs, b=64), dtv, ALU.mult), r=[xstok, sh['dtatok']], w=[xdttok])
            xd, xdtok = rxd.next()
            dsv = sm[:, 24 + g * 4:28 + g * 4].unsqueeze(2).broadcast_to([128, 4, 64])
            s.op('dve', TT(r3(xd, b=64), r3(xdt, b=64), dsv, ALU.mult), r=[xdttok, sh['smtok']], w=[xdtok])
            X[ch] = dict(t=t, sh=sh, CT=CT, CTtok=CTtok, Bm=Bm, Bmtok=Bmtok, cb=cb, cbtok=cbtok, xdt=xdt, xdttok=xdttok, xd=xd, xdtok=xdtok)

        def bc(ch):
            g, d = ch
            x = X[ch]
            sh = x['sh']
            bcp, bcptok = pBC.next()
            s.op('pe', MM(bcp[:, :], ones_f[:], sh['R'][:, g * 512:(g + 1) * 512], True, False), r=['ones_f', sh['Rtok']], w=[bcptok])
            s.op('pe', MM(bcp[:, :], ident_b[:], mb4[d], False, True), r=['ident_b', 'mb4'], w=[bcptok])
            x['bcp'] = bcp
            x['bcptok'] = bcptok

        def s3a(ch):
            g, d = ch
            x = X[ch]
            sh = x['sh']
            sm = sh['sm']
            arg, argtok = rarg.next()
            acv = sm[:, g * 4:g * 4 + 4].unsqueeze(2).broadcast_to([128, 4, 128])
            s.op('dve', TT(r3(arg, b=128), r3(x['bcp'][:, :], b=128), acv, ALU.subtract), r=[x['bcptok'], sh['smtok']], w=[argtok])
            sg, sgtok = rsg.next()
            s.op('act', ACT(sg, arg, AF.Exp), r=[argtok], w=[sgtok])
            x['sg'] = sg
            x['sgtok'] = sgtok

        def s3b(ch):
            g, d = ch
            c = C[ch]
            x = X[ch]
            yp, yptok = pY.next()
            sprev, sprevtok = c['sprev']
            at, attok = rat.next()
            s.op('dve', TT(r3(at, b=128), r3(x['sg'], b=128), x['cb'].unsqueeze(1).broadcast_to([128, 4, 128]), ALU.mult), r=[x['cbtok'], x['sgtok']], w=[attok])
            for hh in range(4):
                s.op('pe', MM(yp[:, hh * 64:(hh + 1) * 64], at[:, hh * 128:(hh + 1) * 128], x['xdt'][:, hh * 64:(hh + 1) * 64]), r=[attok, x['xdttok']], w=[yptok])
                s.op('pe', MM(yp[:, 256 + hh * 64:256 + (hh + 1) * 64], x['CT'], sprev[:, hh * 64:(hh + 1) * 64]), r=[x['CTtok'], sprevtok], w=[yptok])
            x['yp'] = yp
            x['yptok'] = yptok

        def s3c(ch):
            g, d = ch
            x = X[ch]
            sh = x['sh']
            sm = sh['sm']
            yp, yptok = x['yp'], x['yptok']
            y1, y1tok = ry1.next()
            s.op('act', ACT(y1, yp[:, 0:256], AF.Copy), r=[yptok], w=[y1tok, yptok])
            y2, y2tok = ry2.next()
            eav = sm[:, 8 + g * 4:12 + g * 4].unsqueeze(2).broadcast_to([128, 4, 64])
            s.op('dve', TT(r3(y2, b=64), r3(yp[:, 256:512], b=64), eav, ALU.mult), r=[yptok, sh['smtok']], w=[y2tok, yptok])
            yb, ybtok = ryb.next()
            s.op('dve', TT(yb, y2, y1, ALU.add), r=[y2tok, y1tok], w=[ybtok])
            rows = slice(x['t'] * 128, (x['t'] + 1) * 128)
            dst = Of if d == 0 else Ob
            s.dma('sp', dst[rows, g * 256:(g + 1) * 256], yb, r=[ybtok], w=[tok('sodst')])

        def s4(ch):
            g, d = ch
            c = C[ch]
            x = X.pop(ch)
            sm = x['sh']['sm']
            up, uptok = pU.next()
            s.op('pe', MM(up[:, 0:256], x['Bm'], x['xd']), r=[x['Bmtok'], x['xdtok']], w=[uptok])
            tS, tStok = rtS.next()
            cdv = sm[:, 32 + g * 4:36 + g * 4].unsqueeze(2).broadcast_to([128, 4, 64])
            s.op('dve', TT(r3(tS, b=64), r3(c['S'], b=64), cdv, ALU.mult), r=[c['Stok'], x['sh']['smtok']], w=[tStok])
            s.op('dve', TT(c['S'], tS, up[:, 0:256], ALU.add), r=[tStok, uptok], w=[c['Stok']])
            sbn, sbntok = c['Sbf'].next()
            s.op('act', ACT(sbn, c['S'], AF.Copy), r=[c['Stok']], w=[sbntok])
            c['sprev'] = (sbn, sbntok)

        def tof(ch, i):
            return i if ch[1] == 0 else NT - 1 - i

        for ch in chains:
            sload(ch, tof(ch, 0))
        for d in range(2):
            dload(d, tof((0, d), 0))
        for i in range(NT):
            if i + 1 < NT:
                for ch in chains:
                    sload(ch, tof(ch, i + 1))
                for d in range(2):
                    dload(d, tof((0, d), i + 1))
            for d in range(2):
                dprep(d, tof((0, d), i))
            for ch in chains:
                s2(ch, tof(ch, i))
            nch = len(chains)
            bc(chains[0])
            bc(chains[1])
            s3a(chains[0])
            for k in range(nch + 1):
                if k + 1 < nch:
                    s3a(chains[k + 1])
                if k + 2 < nch:
                    bc(chains[k + 2])
                if k < nch:
                    s3b(chains[k])
                if k >= 1:
                    s3c(chains[k - 1])
            for ch in chains:
                s4(ch)
            for d in range(2):
                shared.pop((d, tof((0, d), i)))

    def ssd_final():
        s.barrier()
        ar.reset()
        dsk = ar.f32(8)
        ngb = ar.f32(512)
        s.dma('sp', dsk, d_skip[0:1, :].partition_broadcast(128), w=['dsk'])
        s.dma('sp', ngb, ssm_ng[0:1, :].partition_broadcast(128), w=['ngb'])
        epsf = ar.f32(2)
        s.op('pool', MS(epsf, 1e-6), w=['epsf'])
        rfl = Ring('ffl', [ar.bf16(512) for _ in range(4)])
        rf = Ring('ff', [ar.f32(512) for _ in range(3)])
        rb = Ring('fb', [ar.bf16(512) for _ in range(4)])
        rx = Ring('fx', [ar.bf16(512) for _ in range(4)])
        rg = Ring('fg', [ar.bf16(512) for _ in range(4)])
        rt = Ring('ft', [ar.f32(512) for _ in range(3)])
        rss = Ring('fss', [ar.f32(8) for _ in range(2)])
        ry = Ring('fy', [ar.bf16(512) for _ in range(3)])
        loaded = {}

        def load(t):
            rows = slice(t * 128, (t + 1) * 128)
            fl, fltok = rfl.next()
            b, btok = rb.next()
            xs, xstok = rx.next()
            g, gtok = rg.next()
            s.dma('sp', fl, Of[rows, :], w=[fltok])
            s.dma('sp', b, Ob[rows, :], w=[btok])
            s.dma('sp', xs, Va[rows, :], w=[xstok])
            s.dma('sp', g, Ga[rows, :], w=[gtok])
            loaded[t] = (fl, fltok, b, btok, xs, xstok, g, gtok)

        load(0)
        if NT > 1:
            load(1)
        stA = {}

        def stage_a(t):
            if t + 2 < NT:
                load(t + 2)
            fl, fltok, b, btok, xs, xstok, g, gtok = loaded.pop(t)
            f, ftok = rf.next()
            s.op('dve', TT(f, fl, b, ALU.add), r=[fltok, btok], w=[ftok])
            tmp, tmptok = rt.next()
            s.op('dve', TT(r3(tmp, b=64), r3(xs, b=64), dsk[:, 0:8].unsqueeze(2).broadcast_to([128, 8, 64]), ALU.mult), r=[xstok, 'dsk'], w=[tmptok])
            s.op('dve', TT(f, f, tmp, ALU.add), r=[ftok, tmptok], w=[ftok])
            s.op('dve', TT(f, f, g, ALU.mult), r=[ftok, gtok], w=[ftok])
            ss, sstok = rss.next()
            s.op('act', ACT(tmp, f, AF.Square, accum=ss[:, 0:1]), r=[ftok], w=[tmptok, sstok])
            s.op('act', ACT(ss[:, 1:2], ss[:, 0:1], AF.Ln, bias=epsf[:, 0:1], scale=1.0 / 512.0), r=[sstok, 'epsf'], w=[sstok])
            s.op('act', ACT(ss[:, 2:3], ss[:, 1:2], AF.Exp, scale=-0.5), r=[sstok], w=[sstok])
            stA[t] = (f, ftok, ss, sstok)

        def stage_b(t):
            rows = slice(t * 128, (t + 1) * 128)
            f, ftok, ss, sstok = stA.pop(t)
            y, ytok = ry.next()
            s.op('dve', STT(y, f, ss[:, 2:3], ngb, ALU.mult, ALU.mult), r=[ftok, sstok, 'ngb'], w=[ytok])
            s.dma('sp', Y[rows, 512:1024], y, r=[ytok], w=[tok('ydst')])

        stage_a(0)
        for t in range(NT):
            if t + 1 < NT:
                stage_a(t + 1)
            stage_b(t)

    qv = qTb.rearrange("(u p) t -> u p t", p=128)
    kv = kTb.rearrange("(u p) t -> u p t", p=128)
    lv = lfT.rearrange("d (u p) t -> d u p t", p=128)
    hq = QTa.rearrange("(u p) t -> u p t", p=128)
    hk = hkT.rearrange("d (u p) t -> d u p t", p=128)
    hl = hlfT.rearrange("d (u p) t -> d u p t", p=128)
    phases = [
        layer0,
        na_phase,
        lambda: recurrence(2, 2, 64, 128, 1.0 / 16.0, [qv[0], qv[1]], [[kv[0], kv[1]]] * 2,
                           [[lv[0, 0], lv[0, 1]], [lv[1, 0], lv[1, 1]]], Vb, 128),
        lambda: rec_final(Gb, 512),
        lambda: phase_c(0, x_in, e_w_out, X1 if nlayers > 1 else out),
    ]
    if nlayers > 1:
        phases += [
            layer1_a,
            lambda: recurrence_b(QTa, [hkT[0], hkT[1]], [hlfT[0], hlfT[1]], Vb),
            lambda: rec_final(Gb, 0),
            ssd_conv,
            ssd_main,
            ssd_final,
            lambda: phase_c(1, X1, o_w_out, out),
        ]
    for ph in phases[:stop]:
        ph()

    s.barrier()
    with nc.Block() as block:
        s.emit(block)
    return nc, es


def _na_btab(rpb, ROWS):
    H = rpb.shape[0]
    out = np.full((5, 128, H, 5, 128), NEG, np.float32)
    reps = {0: 0, 1: 2, 2: 4, 3: ROWS - 4, 4: ROWS - 2}
    p = np.arange(128)
    q = np.arange(128)
    for cls, r in reps.items():
        ks = min(max(r - 4, 0), ROWS - 10)
        for blk in range(5):
            KR = ks + (blk * 128 + p) // 64
            kc = p % 64
            R = r + q // 64
            qc = q % 64
            rs = np.clip(R - 4, 0, ROWS - 8)
            cs = np.clip(qc - 8, 0, 48)
            vr = (KR[:, None] >= rs[None, :]) & (KR[:, None] < rs[None, :] + 8)
            vc = (kc[:, None] >= cs[None, :]) & (kc[:, None] < cs[None, :] + 16)
            dr = np.clip(KR[:, None] - R[None, :] + 7, 0, 14)
            dc = np.clip(kc[:, None] - qc[None, :], -15, 15) + 15
            g = rpb[:, dr, dc]
            valid = (vr & vc)[None]
            out[cls, :, :, blk, :] = np.where(valid, g, NEG).transpose(1, 0, 2)
    return out.reshape(5, 128, H * 640)


def prep_inputs(b, L, x, c, ada_w, ada_b, ln_g, ln_b, e_w_in, e_rpb, e_gla_w_up, e_gla_b, e_gla_norm_g, e_w_out,
                o_w_in, hgrn_lb, o_hgrn_norm_g, o_conv_w, o_conv_b, o_dt_bias, o_a_log, o_d_skip, o_ssm_norm_g, o_w_out):
    f = lambda a: np.ascontiguousarray(np.asarray(a, dtype=np.float32))
    m = {}
    m["x"] = f(x[b])
    m["cT"] = f(c[b].reshape(8, 128).T)
    m["ada_w"] = f(ada_w)
    m["ada_bT"] = f(ada_b.reshape(2, 24, 128).transpose(2, 0, 1))
    m["ada_bg"] = f(ada_b[:, 2048:3072])
    m["ln_g"] = f(ln_g)
    m["ln_b"] = f(ln_b)
    m["e_w_in"] = f(e_w_in[0])
    m["e_w_out"] = f(e_w_out[0])
    m["btab"] = f(_na_btab(np.asarray(e_rpb[0]), L // 64))
    m["gla_w_up"] = f(e_gla_w_up[0])
    m["gla_bT"] = f(e_gla_b[0].reshape(2, 2, 128).transpose(2, 0, 1))
    m["gla_ng"] = f(e_gla_norm_g)
    m["o_w_in"] = f(o_w_in[0])
    m["o_w_out"] = f(o_w_out[0])
    m["hgrn_lbT"] = f(hgrn_lb.reshape(2, 4, 128).transpose(2, 0, 1))
    m["hgrn_ng"] = f(o_hgrn_norm_g)
    m["conv_wT"] = f(o_conv_w[0].reshape(4, 8, 128).transpose(2, 1, 0))
    m["conv_bT"] = f(o_conv_b[0].reshape(8, 128).T)
    m["dt_bias"] = f(o_dt_bias[0].reshape(1, 16))
    m["a_log"] = f(o_a_log[0].reshape(1, 16))
    m["d_skip"] = f(o_d_skip.reshape(1, 8))
    m["ssm_ng"] = f(o_ssm_norm_g.reshape(1, 512))
    return m


def kernel(**inputs):
    x = np.asarray(inputs["x"])
    B, L, _ = x.shape
    nc, es = build(L)
    in_maps = [prep_inputs(b, L, **inputs) for b in range(B)]
    res = run_bass_kernel_spmd(nc, in_maps, core_ids=list(range(B)))
    return np.stack([np.asarray(r["out"], dtype=np.float32) for r in res.results], axis=0)
```

```python
import numpy as np
from contextlib import ExitStack
import concourse.bass as bass
import concourse.mybir as mybir
from concourse.bass_utils import run_bass_kernel_spmd

F32 = mybir.dt.float32
BF16 = mybir.dt.bfloat16
AF = mybir.ActivationFunctionType
ALU = mybir.AluOpType

D = 1024
NSLOT = 10
ALPHA = 4.0 ** 0.25
NEG = -30000.0


class Sched:
    ENGS = ('pe', 'act', 'dve', 'pool', 'sp')

    def __init__(self, nc, es):
        self.nc = nc
        self.streams = {e: [] for e in self.ENGS}
        self.cnt = {e: 0 for e in self.ENGS}
        self.sem = {e: es.enter_context(nc.semaphore('s_' + e)) for e in self.ENGS}
        self.waited = {e: {} for e in self.ENGS}
        self.lastw = {}
        self.readers = {}
        self.dslots = {}
        self.dnext = {}
        for q in ('sp', 'pool', 'act'):
            self.dslots[q] = [[es.enter_context(nc.semaphore('d_%s%d' % (q, i))), 0] for i in range(NSLOT)]
            self.dnext[q] = 0

    def _semh(self, key):
        if isinstance(key, str):
            return self.sem[key]
        return self.dslots[key[1]][key[2]][0]

    def _need(self, eng, dep):
        key, val = dep
        if key == eng and eng == 'pe':
            return
        if self.waited[eng].get(key, 0) >= val:
            return
        self.waited[eng][key] = val
        self.streams[eng].append(('w', key, val))

    def _deps(self, eng, r, w):
        for t in r:
            d = self.lastw.get(t)
            if d:
                self._need(eng, d)
        for t in w:
            d = self.lastw.get(t)
            if d:
                self._need(eng, d)
            rd = self.readers.get(t)
            if rd:
                for k, v in rd.items():
                    self._need(eng, (k, v))

    def _commit(self, dep, r, w):
        for t in r:
            rd = self.readers.setdefault(t, {})
            if rd.get(dep[0], 0) < dep[1]:
                rd[dep[0]] = dep[1]
        for t in w:
            self.lastw[t] = dep
            self.readers[t] = {}

    def op(self, eng, fn, r=(), w=()):
        self._deps(eng, r, w)
        self.cnt[eng] += 1
        self.streams[eng].append(('o', fn))
        self._commit((eng, self.cnt[eng]), r, w)

    def dma(self, q, out, in_, r=(), w=()):
        self._deps(q, r, w)
        i = self.dnext[q]
        self.dnext[q] = (i + 1) % NSLOT
        slot = self.dslots[q][i]
        key = ('d', q, i)
        if slot[1] > 0:
            self._need(q, (key, slot[1]))
        slot[1] += 16
        self.streams[q].append(('d', out, in_, key))
        self._commit((key, slot[1]), r, w)

    def barrier(self):
        deps = [(e, self.cnt[e]) for e in self.ENGS if self.cnt[e] > 0]
        for q in self.dslots:
            for i, sl in enumerate(self.dslots[q]):
                if sl[1] > 0:
                    deps.append((('d', q, i), sl[1]))
        for e in self.ENGS:
            for d in deps:
                self._need(e, d)

    def emit(self, block):
        decos = {'pe': block.tensor, 'act': block.scalar, 'dve': block.vector, 'pool': block.gpsimd, 'sp': block.sync}
        for e in self.ENGS:
            stream = self.streams[e]

            def body(eng, stream=stream, e=e):
                for it in stream:
                    if it[0] == 'w':
                        eng.wait_ge(self._semh(it[1]), it[2])
                    elif it[0] == 'o':
                        it[1](eng).then_inc(self.sem[e], 1)
                    else:
                        eng.dma_start(out=it[1], in_=it[2]).then_inc(self._semh(it[3]), 16)
            decos[e](body)


class Arena:
    def __init__(self, ap, ncols):
        self.ap = ap
        self.n = ncols
        self.pos = 0

    def reset(self, base=0):
        self.pos = base

    def f32(self, cols, shape=None):
        a = self.pos
        self.pos += cols
        assert self.pos <= self.n, ("arena overflow", self.pos, self.n)
        v = self.ap[:, a:a + cols]
        return v

    def bf16(self, cols):
        c32 = (cols + 1) // 2
        v = self.f32(c32).bitcast(BF16)
        return v[:, 0:cols]


def r3(ap, **kw):
    k = list(kw.keys())[0]
    return ap.rearrange("p (a %s) -> p a %s" % (k, k), **kw)


def MM(out, lhsT, rhs, start=True, stop=True, skip=False):
    if skip:
        return lambda e: e.matmul(out, lhsT=lhsT, rhs=rhs, start=start, stop=stop, skip_group_check=True)
    return lambda e: e.matmul(out, lhsT=lhsT, rhs=rhs, start=start, stop=stop)


def TR(out, in_, ident):
    return lambda e: e.transpose(out, in_, ident)


def ACT(out, in_, func, bias=None, scale=None, accum=None):
    kw = {}
    if bias is not None:
        kw['bias'] = bias
    if scale is not None:
        kw['scale'] = scale
    if accum is not None:
        kw['accum_out'] = accum
    return lambda e: e.activation(out=out, in_=in_, func=func, **kw)


def TT(out, in0, in1, op):
    return lambda e: e.tensor_tensor(out=out, in0=in0, in1=in1, op=op)


def TS(out, in0, s1, op0, s2=None, op1=None):
    if op1 is None:
        return lambda e: e.tensor_scalar(out=out, in0=in0, scalar1=s1, scalar2=None, op0=op0)
    return lambda e: e.tensor_scalar(out=out, in0=in0, scalar1=s1, scalar2=s2, op0=op0, op1=op1)


def STT(out, in0, scalar, in1, op0, op1):
    return lambda e: e.scalar_tensor_tensor(out=out, in0=in0, scalar=scalar, in1=in1, op0=op0, op1=op1)


def CP(out, in_):
    return lambda e: e.tensor_copy(out=out, in_=in_)


def MS(ap, c):
    return lambda e: e.memset(ap, c)


def build(L, nlayers=2, dbg=(), stop=99):
    nc = bass.Bass("TRN2", target_bir_lowering=False)
    NT = L // 128
    NS = L // 512
    ROWS = L // 64
    es = ExitStack()

    def din(name, shape, dt=F32):
        return nc.dram_tensor(name, list(shape), dt, kind="ExternalInput").ap()

    def dscr(name, shape, dt):
        kind = "ExternalOutput" if name in dbg else "Internal"
        return nc.dram_tensor(name, list(shape), dt, kind=kind).ap()

    x_in = din("x", [L, D])
    cT_in = din("cT", [128, 8])
    ada_w = din("ada_w", [2, D, 3 * D])
    ada_bT = din("ada_bT", [128, 2, 24])
    ada_bg = din("ada_bg", [2, D])
    ln_g = din("ln_g", [2, D])
    ln_b = din("ln_b", [2, D])
    e_w_in = din("e_w_in", [D, 3616])
    e_w_out = din("e_w_out", [D, D])
    btab = din("btab", [5, 128, 8 * 640])
    gla_w_up = din("gla_w_up", [2, 16, 256])
    gla_bT = din("gla_bT", [128, 2, 2])
    gla_ng = din("gla_ng", [1, 128])
    o_w_in = din("o_w_in", [D, 4112])
    o_w_out = din("o_w_out", [D, D])
    hgrn_lbT = din("hgrn_lbT", [128, 2, 4])
    hgrn_ng = din("hgrn_ng", [1, 128])
    conv_wT = din("conv_wT", [128, 8, 4])
    conv_bT = din("conv_bT", [128, 8])
    dt_bias = din("dt_bias", [1, 16])
    a_log = din("a_log", [1, 16])
    d_skip = din("d_skip", [1, 8])
    ssm_ng = din("ssm_ng", [1, 512])
    out = nc.dram_tensor("out", [L, D], F32, kind="ExternalOutput").ap()

    QTa = dscr("QTa", [512, L], BF16)
    KTa = dscr("KTa", [512, L], BF16)
    Va = dscr("Va", [L, 512], BF16)
    Va66 = dscr("Va66", [L, 528], BF16)
    Ga = dscr("Ga", [L, 512], BF16)
    qTb = dscr("qTb", [256, L], BF16)
    kTb = dscr("kTb", [256, L], BF16)
    Vb = dscr("Vb", [L, 512], BF16)
    Gb = dscr("Gb", [L, 512], BF16)
    lfT = dscr("lfT", [2, 256, L], F32)
    Of = dscr("Of", [L, 512], BF16)
    Ob = dscr("Ob", [L, 512], BF16)
    Y = dscr("Y", [L, D], BF16)
    X1 = dscr("X1", [L, D], F32)

    def sb(name, shape, dt):
        return es.enter_context(nc.sbuf_tensor(name, list(shape), dt))

    ident_f = sb("ident_f", [128, 128], F32)
    ident_b = sb("ident_b", [128, 128], BF16)
    ones_f = sb("ones_f", [128, 128], F32)
    mask_f128 = sb("mask_f128", [128, 128], BF16)
    mask_b128 = sb("mask_b128", [128, 128], BF16)
    mask_f32 = sb("mask_f32", [128, 128], BF16)
    mask_b32 = sb("mask_b32", [128, 128], BF16)
    seg128 = sb("seg128", [128, 512], F32)
    seg32 = sb("seg32", [128, 512], F32)
    modT = sb("modT", [128, 2, 16], F32)
    gate_bc = sb("gate_bc", [128, 2, D], F32)
    small = sb("small", [128, 64], F32)
    AW = 42000
    arena_t = sb("arena", [128, AW], F32)
    ar = Arena(arena_t, AW)
    banks = [es.enter_context(nc.psum_tensor("bank%d" % i, [128, 512], F32)) for i in range(8)]

    s = Sched(nc, es)
    uid = [0]

    def tok(prefix):
        uid[0] += 1
        return "%s#%d" % (prefix, uid[0])

    class Ring:
        def __init__(self, name, aps):
            self.aps = aps
            self.toks = [tok(name) for _ in aps]
            self.i = -1

        def next(self):
            self.i = (self.i + 1) % len(self.aps)
            return self.aps[self.i], self.toks[self.i]

    s.op('pool', MS(ident_f[:], 0.0), w=['ident_f'])
    s.op('pool', lambda e: e.affine_select(out=ident_f[:], in_=ident_f[:], pattern=[[-1, 128]], compare_op=ALU.not_equal,
                                           fill=1.0, base=0, channel_multiplier=1), r=['ident_f'], w=['ident_f'])
    s.op('pool', CP(ident_b[:], ident_f[:]), r=['ident_f'], w=['ident_b'])
    s.op('pool', MS(ones_f[:], 1.0), w=['ones_f'])
    s.op('pool', MS(mask_f128[:], 1.0), w=['mask_f128'])
    s.op('pool', lambda e: e.affine_select(out=mask_f128[:], in_=mask_f128[:], pattern=[[1, 128]], compare_op=ALU.is_ge,
                                           fill=0.0, base=0, channel_multiplier=-1), r=['mask_f128'], w=['mask_f128'])
    s.op('pool', MS(mask_b128[:], 1.0), w=['mask_b128'])
    s.op('pool', lambda e: e.affine_select(out=mask_b128[:], in_=mask_b128[:], pattern=[[-1, 128]], compare_op=ALU.is_ge,
                                           fill=0.0, base=0, channel_multiplier=1), r=['mask_b128'], w=['mask_b128'])
    s.op('pool', CP(mask_f32[:], mask_f128[:]), r=['mask_f128'], w=['mask_f32'])
    s.op('pool', CP(mask_b32[:], mask_b128[:]), r=['mask_b128'], w=['mask_b32'])
    for cb in range(4):
        s.op('pool', (lambda cb: lambda e: e.affine_select(out=mask_f32[:, 32 * cb:32 * cb + 32], in_=mask_f32[:, 32 * cb:32 * cb + 32],
                                                           pattern=[[0, 32]], compare_op=ALU.is_ge, fill=0.0, base=-32 * cb,
                                                           channel_multiplier=1))(cb), r=['mask_f32'], w=['mask_f32'])
        s.op('pool', (lambda cb: lambda e: e.affine_select(out=mask_b32[:, 32 * cb:32 * cb + 32], in_=mask_b32[:, 32 * cb:32 * cb + 32],
                                                           pattern=[[0, 32]], compare_op=ALU.is_ge, fill=0.0, base=32 * cb + 31,
                                                           channel_multiplier=-1))(cb), r=['mask_b32'], w=['mask_b32'])
    s.op('pool', MS(seg128[:], 1.0), w=['seg128'])
    s.op('pool', MS(seg32[:], 1.0), w=['seg32'])
    s.op('pool', MS(r3(seg128[:], b=128)[:, :, 0:1], 0.0), r=['seg128'], w=['seg128'])
    s.op('pool', MS(r3(seg32[:], b=32)[:, :, 0:1], 0.0), r=['seg32'], w=['seg32'])

    WIN = {}

    W_COLS = 8 * 4112 // 2

    def seg_order(c0s):
        order = []
        for c0 in c0s:
            if c0 // 512 not in order:
                order.append(c0 // 512)
        return order

    def prefetch_w_in(src, ncols, order):
        w_in = r3(arena_t[:, 0:W_COLS].bitcast(BF16), b=4112)
        WIN['w'] = w_in
        v = src.rearrange("(k p) c -> p k c", p=128)
        for pc in order:
            c0, c1 = pc * 512, min(ncols, (pc + 1) * 512)
            for k0 in (0, 4):
                s.dma('pool', w_in[:, k0:k0 + 4, c0:c1], v[:, k0:k0 + 4, c0:c1], w=['w_in%d' % pc])

    L0_C0S = [0, 128, 256, 384, 512, 640, 768, 896, 2048, 2176, 2304, 2432, 1024, 2560, 1536, 3072, 3584, 3600]
    L1_C0S = [0, 512, 1024, 1536, 2048, 2560, 3072, 3584, 4096]

    ar.reset(W_COLS)
    prefetch_w_in(e_w_in, 3616, seg_order(L0_C0S))
    cT = ar.f32(8)
    scT = ar.f32(16)
    sc_rep = ar.f32(8 * 128)
    abT = ar.f32(48)
    abg = ar.f32(2 * D)
    slabG = [ar.f32(8 * 512) for _ in range(3)]
    s.dma('sp', cT, cT_in[:, :], w=['cT'])
    s.dma('sp', abT, ada_bT.rearrange("p l c -> p (l c)"), w=['abT'])
    s.dma('sp', abg, ada_bg.rearrange("l d -> (l d)").rearrange("(o n) -> o n", o=1).partition_broadcast(128), w=['abg'])
    sc3 = r3(scT, b=2)
    s.op('act', ACT(sc3[:, :, 0], cT, AF.Silu), r=['cT'], w=['scT'])
    s.op('act', ACT(sc3[:, :, 1], cT, AF.Silu), r=['cT'], w=['scT'])
    scr3 = r3(sc_rep, b=128)
    for k in range(8):
        s.op('dve', TS(scr3[:, k, :], ones_f[:], sc3[:, k, 0:1], ALU.mult), r=['scT', 'ones_f'], w=['sc_rep'])
    rG = Ring('slabG', slabG)
    pmod = Ring('pmod', [banks[0], banks[1], banks[2], banks[3]])
    for l in range(nlayers):
        wv = ada_w[l].rearrange("(k p) c -> p k c", p=128)
        for sl_i in range(6):
            sl, st = rG.next()
            sl3 = r3(sl, b=512)
            s.dma('sp', sl3, wv[:, :, sl_i * 512:(sl_i + 1) * 512], w=[st])
            if sl_i < 4:
                for c4 in range(4):
                    cb = sl_i * 4 + c4
                    ps, pt = pmod.next()
                    for k in range(8):
                        s.op('pe', MM(ps[:, 0:2], sl3[:, k, c4 * 128:(c4 + 1) * 128], sc3[:, k, :], k == 0, k == 7), r=[st, 'scT'], w=[pt])
                    if cb < 8:
                        s.op('dve', TT(modT[:, l, cb:cb + 1], ps[:, 0:1], abT[:, l * 24 + cb:l * 24 + cb + 1], ALU.add),
                             r=[pt, 'abT'], w=['modT'])
                    else:
                        s.op('dve', STT(modT[:, l, cb:cb + 1], ps[:, 0:1], 1.0, abT[:, l * 24 + cb:l * 24 + cb + 1], ALU.add, ALU.add),
                             r=[pt, 'abT'], w=['modT'])
            else:
                hf = sl_i - 4
                ps, pt = pmod.next()
                for k in range(8):
                    s.op('pe', MM(ps[:, :], scr3[:, k, :], sl3[:, k, :], k == 0, k == 7), r=[st, 'sc_rep'], w=[pt])
                s.op('dve', TT(gate_bc[:, l, hf * 512:(hf + 1) * 512], ps[:, :], abg[:, l * D + hf * 512:l * D + (hf + 1) * 512], ALU.add),
                     r=[pt, 'abg'], w=['gate_bc'])

    def phase_a(l, x_src, segs):
        xts = [ar.f32(4 * D) for _ in range(2)]
        hTs = [ar.bf16(8 * 512) for _ in range(2)]
        rx = Ring('xt', xts)
        rh = Ring('hT', hTs)
        ptr = Ring('ptr', [banks[0], banks[1]])
        ppj = Ring('ppj', [banks[2], banks[3], banks[4], banks[5]])
        xv = x_src.rearrange("(n s p) d -> n p s d", p=128, s=4)
        state = {}

        xloaded = {}

        def load_x(i):
            xt, xtok = rx.next()
            xt3 = r3(xt, b=D)
            s.dma('sp', xt3, xv[i], w=[xtok])
            xloaded[i] = (xt3, xtok)

        def load_and_transpose(i):
            if i not in xloaded:
                load_x(i)
            xt3, xtok = xloaded.pop(i)
            hT, htok = rh.next()
            hT3 = r3(hT, b=512)
            for k in range(8):
                ps, pt = ptr.next()
                for sub in range(4):
                    s.op('pe', TR(ps[:, sub * 128:(sub + 1) * 128], xt3[:, sub, k * 128:(k + 1) * 128], ident_f[:]),
                         r=[xtok, 'ident_f'], w=[pt])
                s.op('act', ACT(hT3[:, k, :], ps[:, :], AF.Identity, bias=modT[:, l, k:k + 1], scale=modT[:, l, 8 + k:9 + k]),
                     r=[pt, 'modT'], w=[htok])
            state[i] = (hT3, htok)

        load_and_transpose(0)
        for i in range(NS):
            if i + 1 < NS:
                load_x(i + 1)
            hT3, htok = state.pop(i)
            for si, sg in enumerate(segs):
                if si == len(segs) // 2 and i + 1 < NS:
                    load_and_transpose(i + 1)
                if sg['kind'] == 'fm':
                    ps, pt = ppj.next()
                    n = sg['n']
                    for k in range(8):
                        s.op('pe', MM(ps[0:n, :], WIN['w'][:, k, sg['c0']:sg['c0'] + n], hT3[:, k, :], k == 0, k == 7),
                             r=['w_in%d' % (sg['c0'] // 512), htok], w=[pt])
                    sg['emit'](ps, pt, i, None)
                else:
                    n = sg['n']
                    for sub in range(4):
                        ps, pt = ppj.next()
                        for k in range(8):
                            s.op('pe', MM(ps[:, 0:n], hT3[:, k, sub * 128:(sub + 1) * 128], WIN['w'][:, k, sg['c0']:sg['c0'] + n], k == 0, k == 7),
                                 r=['w_in%d' % (sg['c0'] // 512), htok], w=[pt])
                        sg['emit'](ps, pt, i, sub)

    def phase_c(l, x_src, w_out_src, x_dst):
        s.barrier()
        if l == 0 and nlayers > 1:
            ar.reset(W_COLS)
            prefetch_w_in(o_w_in, 4112, seg_order(L1_C0S))
        else:
            ar.reset()
        wo = r3(ar.bf16(8 * D), b=D)
        wov = w_out_src.rearrange("(k p) c -> p k c", p=128)
        wst = Ring('wst', [ar.f32(D) for _ in range(4)])
        for k in range(8):
            st, sttok = wst.next()
            s.dma('sp', st, wov[:, k, :], w=[sttok])
            s.op('dve', TT(wo[:, k, :], st, gate_bc[:, l, :], ALU.mult), r=[sttok, 'gate_bc'], w=['wo'])
        lng = ar.f32(D)
        lnb = ar.f32(D)
        epsc = ar.f32(2)
        s.op('pool', MS(epsc, 1e-5), w=['epsc'])
        s.dma('sp', lng, ln_g[l:l + 1, :].partition_broadcast(128), w=['lng'])
        s.dma('sp', lnb, ln_b[l:l + 1, :].partition_broadcast(128), w=['lnb'])
        ry = Ring('cy', [ar.bf16(D) for _ in range(4)])
        rxt = Ring('cx', [ar.f32(D) for _ in range(4)])
        ryt = Ring('cyT', [r3(ar.bf16(8 * 128), b=128) for _ in range(2)])
        rz = Ring('cz', [ar.f32(D) for _ in range(3)])
        ro = Ring('co', [ar.f32(D) for _ in range(3)])
        rst = Ring('cst', [ar.f32(24) for _ in range(3)])
        ptr = Ring('cptr', [banks[0], banks[1]])
        pmm = Ring('cpmm', [banks[2], banks[3], banks[4], banks[5]])
        loaded = {}

        def load(t):
            yt, ytok = ry.next()
            s.dma('sp', yt, Y[t * 128:(t + 1) * 128, :], w=[ytok])
            xt, xtok = rxt.next()
            s.dma('sp', xt, x_src[t * 128:(t + 1) * 128, :], w=[xtok])
            loaded[t] = (yt, ytok, xt, xtok)

        load(0)
        if NT > 1:
            load(1)
        stA = {}

        def stage_a(t):
            if t + 2 < NT:
                load(t + 2)
            yt, ytok, xt, xtok = loaded.pop(t)
            yT, yTtok = ryt.next()
            for half in range(2):
                ps, pt = ptr.next()
                psb = ps[:, :].bitcast(BF16)
                for kk in range(4):
                    k = half * 4 + kk
                    s.op('pe', TR(psb[:, kk * 128:(kk + 1) * 128], yt[:, k * 128:(k + 1) * 128], ident_b[:]), r=[ytok, 'ident_b'], w=[pt])
                s.op('act', ACT(yT[:, half * 4:half * 4 + 4, :], r3(psb[:, 0:512], b=128), AF.Copy), r=[pt], w=[yTtok])
            z, ztok = rz.next()
            for hf in range(2):
                ps, pt = pmm.next()
                for k in range(8):
                    s.op('pe', MM(ps[:, :], yT[:, k, :], wo[:, k, hf * 512:(hf + 1) * 512], k == 0, k == 7), r=[yTtok, 'wo'], w=[pt])
                s.op('dve', STT(z[:, hf * 512:(hf + 1) * 512], xt[:, hf * 512:(hf + 1) * 512], ALPHA, ps[:, :], ALU.mult, ALU.add),
                     r=[pt, xtok], w=[ztok])
            st, sttok = rst.next()
            s.op('dve', lambda e, st=st, z=z: e.bn_stats(out=st[:, 0:6], in_=z[:, 0:512]), r=[ztok], w=[sttok])
            s.op('dve', lambda e, st=st, z=z: e.bn_stats(out=st[:, 6:12], in_=z[:, 512:1024]), r=[ztok], w=[sttok])
            s.op('dve', lambda e, st=st: e.bn_aggr(out=st[:, 12:14], in_=st[:, 0:12]), r=[sttok], w=[sttok])
            s.op('act', ACT(st[:, 14:15], st[:, 13:14], AF.Ln, bias=epsc[:, 0:1]), r=[sttok, 'epsc'], w=[sttok])
            s.op('act', ACT(st[:, 15:16], st[:, 14:15], AF.Exp, scale=-0.5), r=[sttok], w=[sttok])
            stA[t] = (z, ztok, st, sttok)

        def stage_b(t):
            z, ztok, st, sttok = stA.pop(t)
            o, otok = ro.next()
            s.op('dve', TS(o, z, st[:, 12:13], ALU.subtract, st[:, 15:16], ALU.mult), r=[ztok, sttok], w=[otok])
            s.op('dve', TT(o, o, lng, ALU.mult), r=[otok, 'lng'], w=[otok])
            s.op('dve', TT(o, o, lnb, ALU.add), r=[otok, 'lnb'], w=[otok])
            s.dma('sp', x_dst[t * 128:(t + 1) * 128, :], o, r=[otok], w=[tok('xdst')])

        stage_a(0)
        for t in range(NT):
            if t + 1 < NT:
                stage_a(t + 1)
            stage_b(t)

    def recurrence(nunits, nh, dk, CS, sc, qT_src, kT_src, lf_src, V_src, dvw):
        s.barrier()
        ar.reset()
        nsub = 128 // CS
        segm = seg128 if CS == 128 else seg32
        segk = 'seg128' if CS == 128 else 'seg32'
        vw = nh * 128
        chains = [(u, d) for u in range(nunits) for d in range(2)]
        tq = Ring('tq', [ar.bf16(512) for _ in range(4)])
        tk = Ring('tk', [ar.bf16(512) for _ in range(4)])
        tlf = Ring('tlf', [ar.f32(512) for _ in range(4)])
        tP = Ring('tP', [ar.f32(512) for _ in range(2)])
        tB = Ring('tB', [ar.f32(512) for _ in range(2)])
        tR = Ring('tR', [ar.f32(512) for _ in range(2)])
        tE = Ring('tE', [ar.bf16(512) for _ in range(4)])
        C = {}
        for ch in chains:
            C[ch] = dict(
                qs=Ring('qs', [ar.bf16(512) for _ in range(2)]),
                kh=Ring('kh', [ar.bf16(512) for _ in range(2)]),
                kb=Ring('kb', [ar.bf16(512) for _ in range(2)]),
                V=Ring('V', [r3(ar.bf16(4 * vw), b=vw) for _ in range(2)]),
                dch=Ring('dch', [ar.f32(16) for _ in range(2)]),
                S=ar.f32(128), Stok=tok('S'),
                Sbf=Ring('Sbf', [ar.bf16(128) for _ in range(nsub + 2)]),
                ATs=Ring('ATs', [r3(ar.bf16(nh * 128), b=128) for _ in range(2)]),
                kbt=Ring('kbt', [ar.bf16(128) for _ in range(2)]),
                ost=Ring('ost', [ar.bf16(vw) for _ in range(2)]),
            )
        if nsub > 1:
            cmask = r3(ar.bf16(nsub * 128), b=128)
            rmask = r3(ar.bf16(nsub * 128), b=128)
            s.op('pool', MS(cmask, 0.0), w=['cmask'])
            s.op('pool', MS(rmask, 1.0), w=['rmask'])
            for ci in range(nsub):
                s.op('pool', MS(cmask[:, ci, ci * CS:(ci + 1) * CS], 1.0), r=['cmask'], w=['cmask'])
                s.op('pool', (lambda ci: lambda e: e.affine_select(out=rmask[:, ci, :], in_=rmask[:, ci, :], pattern=[[0, 128]],
                                                                   compare_op=ALU.is_ge, fill=0.0, base=-CS * ci, channel_multiplier=1))(ci),
                     r=['rmask'], w=['rmask'])
                s.op('pool', (lambda ci: lambda e: e.affine_select(out=rmask[:, ci, :], in_=rmask[:, ci, :], pattern=[[0, 128]],
                                                                   compare_op=ALU.is_ge, fill=0.0, base=CS * ci + CS - 1, channel_multiplier=-1))(ci),
                     r=['rmask'], w=['rmask'])
            for ch in chains:
                C[ch]['qsm'] = Ring('qsm', [ar.bf16(128) for _ in range(2)])
                C[ch]['kbtm'] = Ring('kbtm', [ar.bf16(128) for _ in range(2)])
        if nh == 2:
            pAT = Ring('pAT', [[banks[0][:, 0:128], banks[1][:, 0:128]]])
        else:
            pAT = Ring('pAT', [[banks[0][:, 0:128]], [banks[1][:, 0:128]]])
        pU = Ring('pU', [banks[4], banks[5]])
        pT = Ring('pT', [banks[6], banks[7]])
        cur = {}
        for ch in chains:
            c = C[ch]
            s.op('pool', MS(c['S'], 0.0), w=[c['Stok']])
            sb0, sbt0 = c['Sbf'].next()
            s.op('pool', MS(sb0, 0.0), w=[sbt0])
            c['sprev'] = (sb0, sbt0)

        pre = {}

        def prep_load(ch, st_i):
            u, d = ch
            c = C[ch]
            t0 = st_i * 512
            q, qtok = tq.next()
            k, ktok = tk.next()
            lf, lftok = tlf.next()
            s.dma('sp', lf, lf_src[d][u][:, t0:t0 + 512], w=[lftok])
            s.dma('sp', q, qT_src[u][:, t0:t0 + 512], w=[qtok])
            s.dma('sp', k, kT_src[d][u][:, t0:t0 + 512], w=[ktok])
            V, Vtok = c['V'].next()
            s.dma('sp', V, V_src[t0:t0 + 512, u * vw:(u + 1) * vw].rearrange("(s p) c -> p s c", p=128), w=[Vtok])
            pre[(ch, st_i)] = (q, qtok, k, ktok, lf, lftok, V, Vtok)

        def prep(ch, st_i):
            u, d = ch
            c = C[ch]
            t0 = st_i * 512
            if (ch, st_i) not in pre:
                prep_load(ch, st_i)
            q, qtok, k, ktok, lf, lftok, V, Vtok = pre.pop((ch, st_i))
            P, Ptok = tP.next()
            s.op('dve', lambda e, P=P, lf=lf: e.tensor_tensor_scan(out=P, data0=segm[:], data1=lf, initial=0.0, op0=ALU.mult, op1=ALU.add),
                 r=[lftok, segk], w=[Ptok])
            P3 = r3(P, b=CS)
            nchk = 512 // CS
            totb = P3[:, :, CS - 1:CS].broadcast_to([128, nchk, CS])
            B, Btok = tB.next()
            R, Rtok = tR.next()
            if d == 0:
                s.op('dve', TT(r3(R, b=CS), totb, P3, ALU.subtract), r=[Ptok], w=[Rtok])
                Bd, Bdtok = P, Ptok
            else:
                s.op('dve', TT(R, P, lf, ALU.subtract), r=[Ptok, lftok], w=[Rtok])
                s.op('dve', TT(r3(B, b=CS), totb, r3(R, b=CS), ALU.subtract), r=[Ptok, Rtok], w=[Btok])
                Bd, Bdtok = B, Btok
            dch, dchtok = c['dch'].next()
            s.op('act', ACT(dch[:, 0:nchk], P3[:, :, CS - 1], AF.Exp, scale=sc), r=[Ptok], w=[dchtok])
            qs, qstok = c['qs'].next()
            kh, khtok = c['kh'].next()
            kb, kbtok = c['kb'].next()
            e1, e1tok = tE.next()
            s.op('act', ACT(e1, Bd, AF.Exp, scale=sc), r=[Bdtok], w=[e1tok])
            s.op('dve', TT(qs, q, e1, ALU.mult), r=[qtok, e1tok], w=[qstok])
            e2, e2tok = tE.next()
            s.op('act', ACT(e2, Bd, AF.Exp, scale=-sc), r=[Bdtok], w=[e2tok])
            s.op('dve', TT(kh, k, e2, ALU.mult), r=[ktok, e2tok], w=[khtok])
            e3, e3tok = tE.next()
            s.op('act', ACT(e3, R, AF.Exp, scale=sc), r=[Rtok], w=[e3tok])
            s.op('dve', TT(kb, k, e3, ALU.mult), r=[ktok, e3tok], w=[kbtok])
            cur[ch] = dict(qs=qs, qstok=qstok, kh=kh, khtok=khtok, kb=kb, kbtok=kbtok, V=V, Vtok=Vtok, dch=dch, dchtok=dchtok)

        nchain = len(chains)
        cpb = 4
        OT = {}
        for idx, ch in enumerate(chains):
            if nh == 1:
                bk = 2 + idx // cpb
                OT[ch] = dict(tiles=[banks[bk][:, (idx % cpb) * 128:(idx % cpb + 1) * 128]], toks=['pO%d' % bk], first=(idx % cpb == 0))
            else:
                OT[ch] = dict(tiles=[banks[2 + hh][:, idx * 128:(idx + 1) * 128] for hh in range(nh)],
                              toks=['pO%d' % (2 + hh) for hh in range(nh)], first=(idx == 0))
        ctx = {}

        def stage1(ch, st_i, sub):
            u, d = ch
            c = C[ch]
            cc = cur[ch]
            cols = slice(sub * 128, (sub + 1) * 128)
            mask = (mask_f128 if d == 0 else mask_b128) if CS == 128 else (mask_f32 if d == 0 else mask_b32)
            mtok = ('mask_f128' if d == 0 else 'mask_b128') if CS == 128 else ('mask_f32' if d == 0 else 'mask_b32')
            pa, patok = pAT.next()
            for hh in range(nh):
                pr = slice(hh * dk, (hh + 1) * dk)
                s.op('pe', MM(pa[hh], cc['kh'][pr, cols], cc['qs'][pr, cols]), r=[cc['khtok'], cc['qstok']], w=[patok])
            ATs, ATtok = c['ATs'].next()
            for hh in range(nh):
                s.op('dve', TT(ATs[:, hh, :], pa[hh], mask[:], ALU.mult), r=[patok, mtok], w=[ATtok])
            pt_, pttok = pT.next()
            ptb = pt_[:, :].bitcast(BF16)
            s.op('pe', TR(ptb[:, 0:128], cc['kb'][:, cols], ident_b[:]), r=[cc['kbtok'], 'ident_b'], w=[pttok])
            kbt, kbttok = c['kbt'].next()
            s.op('act', ACT(kbt, ptb[:, 0:128], AF.Copy), r=[pttok], w=[kbttok])
            x = dict(cc=cc, sub=sub, st_i=st_i, kbt=kbt, kbttok=kbttok)
            if nsub > 1:
                kb3, kb3tok = c['kbtm'].next()
                s.op('dve', TT(kb3, kbt, rmask[:, nsub - 1, :], ALU.mult), r=[kbttok, 'rmask'], w=[kb3tok])
                qs3, qs3tok = c['qsm'].next()
                s.op('dve', TT(qs3, cc['qs'][:, cols], cmask[:, nsub - 1, :], ALU.mult), r=[cc['qstok'], 'cmask'], w=[qs3tok])
                x.update(kb3=kb3, kb3tok=kb3tok, qs3=qs3, qs3tok=qs3tok)
            ot = OT[ch]
            for hh in range(nh):
                s.op('pe', MM(ot['tiles'][hh], ATs[:, hh, :], cc['V'][:, sub, hh * 128:(hh + 1) * 128], ot['first'], False, True),
                     r=[ATtok, cc['Vtok']], w=[ot['toks'][hh]])
            x['order'] = list(range(nsub)) if d == 0 else list(range(nsub - 1, -1, -1))
            ctx[ch] = x

        def stage2(ch, n_i):
            u, d = ch
            c = C[ch]
            x = ctx[ch]
            cc = x['cc']
            sub = x['sub']
            cidx = x['order'][n_i]
            last = (n_i == nsub - 1)
            rows = slice(cidx * CS, (cidx + 1) * CS)
            ccols = slice(sub * 128 + cidx * CS, sub * 128 + (cidx + 1) * CS)
            sprev, sprevtok = c['sprev']
            ot = OT[ch]
            masked = (nsub > 1 and cidx == nsub - 1)
            for hh in range(nh):
                pr = slice(hh * dk, (hh + 1) * dk)
                if masked:
                    s.op('pe', MM(ot['tiles'][hh], x['qs3'][pr, :], sprev[pr, :], False, last, True), r=[x['qs3tok'], sprevtok], w=[ot['toks'][hh]])
                else:
                    s.op('pe', MM(ot['tiles'][hh][rows, :], cc['qs'][pr, ccols], sprev[pr, :], False, last, True),
                         r=[cc['qstok'], sprevtok], w=[ot['toks'][hh]])
            pu, putok = pU.next()
            for hh in range(nh):
                pr = slice(hh * dk, (hh + 1) * dk)
                if masked:
                    s.op('pe', MM(pu[pr, 0:128], x['kb3'][:, pr], cc['V'][:, sub, hh * 128:(hh + 1) * 128]), r=[x['kb3tok'], cc['Vtok']], w=[putok])
                else:
                    s.op('pe', MM(pu[pr, 0:128], x['kbt'][rows, pr], cc['V'][rows, sub, hh * 128:(hh + 1) * 128]),
                         r=[x['kbttok'], cc['Vtok']], w=[putok])
            chk = sub * nsub + cidx
            s.op('dve', STT(c['S'], c['S'], cc['dch'][:, chk:chk + 1], pu[:, 0:128], ALU.mult, ALU.add),
                 r=[c['Stok'], cc['dchtok'], putok], w=[c['Stok']])
            sbn, sbntok = c['Sbf'].next()
            s.op('act', ACT(sbn, c['S'], AF.Copy), r=[c['Stok']], w=[sbntok])
            c['sprev'] = (sbn, sbntok)

        def stage3(ch):
            u, d = ch
            c = C[ch]
            x = ctx.pop(ch)
            ot = OT[ch]
            tglob = x['st_i'] * 4 + x['sub']
            ost, osttok = c['ost'].next()
            for hh in range(nh):
                s.op('act', ACT(ost[:, hh * 128:(hh + 1) * 128], ot['tiles'][hh], AF.Copy), r=[ot['toks'][hh]], w=[osttok])
            dst = Of if d == 0 else Ob
            s.dma('sp', dst[tglob * 128:(tglob + 1) * 128, u * vw:(u + 1) * vw], ost, r=[osttok], w=[tok('odst')])

        nxt = {}
        for ch in chains:
            prep_load(ch, 0 if ch[1] == 0 else NS - 1)
        for ch in chains:
            prep(ch, 0 if ch[1] == 0 else NS - 1)
        for i in range(NS):
            now = {ch: cur[ch] for ch in chains}
            for sub_i in range(4):
                for ch in chains:
                    st_i = i if ch[1] == 0 else NS - 1 - i
                    sub = sub_i if ch[1] == 0 else 3 - sub_i
                    cur[ch] = now[ch]
                    stage1(ch, st_i, sub)
                for n_i in range(nsub):
                    for ch in chains:
                        stage2(ch, n_i)
                for ch in chains:
                    stage3(ch)
                if sub_i == 0 and i + 1 < NS:
                    for ch in chains:
                        prep_load(ch, (i + 1) if ch[1] == 0 else NS - 2 - i)
                if sub_i == 1 and i + 1 < NS:
                    for ch in chains:
                        prep(ch, (i + 1) if ch[1] == 0 else NS - 2 - i)
                        nxt[ch] = cur[ch]
            if i + 1 < NS:
                for ch in chains:
                    now[ch] = None
                    cur[ch] = nxt[ch]

    def recurrence_b(qsrc, ksrc, lfsrc, V_src):
        s.barrier()
        ar.reset()
        CS, nsub, NU = 32, 4, 4
        segm = ar.f32(1024)
        s.op('pool', MS(segm, 1.0), w=['segm'])
        s.op('pool', MS(r3(segm, b=CS)[:, :, 0:1], 0.0), r=['segm'], w=['segm'])
        c3 = ar.bf16(128)
        r3m = ar.bf16(128)
        s.op('pool', MS(c3, 0.0), w=['c3'])
        s.op('pool', MS(c3[:, 96:128], 1.0), r=['c3'], w=['c3'])
        s.op('pool', MS(r3m, 1.0), w=['r3m'])
        s.op('pool', lambda e: e.affine_select(out=r3m, in_=r3m, pattern=[[0, 128]], compare_op=ALU.is_ge, fill=0.0, base=-96,
                                               channel_multiplier=1), r=['r3m'], w=['r3m'])
        qv = qsrc.rearrange("(u p) t -> p u t", p=128)
        kv_ = [ksrc[d].rearrange("(u p) t -> p u t", p=128) for d in range(2)]
        lv_ = [lfsrc[d].rearrange("(u p) t -> p u t", p=128) for d in range(2)]
        tq = Ring('bq', [r3(ar.bf16(1024), b=512) for _ in range(4)])
        tk = Ring('bk', [r3(ar.bf16(1024), b=512) for _ in range(4)])
        tlf = Ring('blf', [ar.f32(1024) for _ in range(4)])
        tP = Ring('bP', [ar.f32(1024) for _ in range(2)])
        tB = Ring('bB', [ar.f32(1024) for _ in range(1)])
        tR = Ring('bR', [ar.f32(1024) for _ in range(1)])
        tE = Ring('bE', [ar.bf16(1024) for _ in range(3)])
        G = {}
        for d in range(2):
            G[d] = dict(
                qs=Ring('bqs', [r3(ar.bf16(NU * 512), b=512) for _ in range(2)]),
                kh=Ring('bkh', [r3(ar.bf16(NU * 512), b=512) for _ in range(2)]),
                kb=Ring('bkb', [r3(ar.bf16(NU * 512), b=512) for _ in range(2)]),
                V=Ring('bV', [r3(ar.bf16(4 * 512), b=512) for _ in range(2)]),
                dch=Ring('bdch', [r3(ar.f32(NU * 16), b=16) for _ in range(2)]),
                S=r3(ar.f32(NU * 128), b=128), Stok=tok('bS'),
                Sbf=Ring('bSbf', [r3(ar.bf16(NU * 128), b=128) for _ in range(nsub + 2)]),
                ATs=Ring('bATs', [r3(ar.bf16(NU * 128), b=128) for _ in range(2)]),
                kbt=Ring('bkbt', [r3(ar.bf16(NU * 128), b=128) for _ in range(2)]),
                kb3=Ring('bkb3', [r3(ar.bf16(NU * 128), b=128) for _ in range(2)]),
                qs3=Ring('bqs3', [r3(ar.bf16(NU * 128), b=128) for _ in range(2)]),
                ost=Ring('bost', [ar.bf16(NU * 128) for _ in range(2)]),
                pa=(banks[0 + d], 'bpa%d' % d), po=(banks[2 + d], 'bpo%d' % d), pu=(banks[4 + d], 'bpu%d' % d), pt=(banks[6 + d], 'bpt%d' % d),
            )
            g = G[d]
            s.op('pool', MS(g['S'], 0.0), w=[g['Stok']])
            sb0, sbt0 = g['Sbf'].next()
            s.op('pool', MS(sb0, 0.0), w=[sbt0])
            g['sprev'] = (sb0, sbt0)
        cur = {}

        pre = {}

        def prep_load(d, st_i):
            g = G[d]
            t0 = st_i * 512
            V, Vtok = g['V'].next()
            s.dma('sp', V, V_src[t0:t0 + 512, :].rearrange("(s p) c -> p s c", p=128), w=[Vtok])
            lfs = []
            for pr_ in range(2):
                us = slice(2 * pr_, 2 * pr_ + 2)
                lf, lftok = tlf.next()
                s.dma('sp', r3(lf, b=512), lv_[d][:, us, t0:t0 + 512], w=[lftok])
                q, qtok = tq.next()
                k, ktok = tk.next()
                s.dma('sp', q, qv[:, us, t0:t0 + 512], w=[qtok])
                s.dma('sp', k, kv_[d][:, us, t0:t0 + 512], w=[ktok])
                lfs.append((lf, lftok, q, qtok, k, ktok))
            pre[(d, st_i)] = (V, Vtok, lfs)

        def prep(d, st_i):
            g = G[d]
            t0 = st_i * 512
            if (d, st_i) not in pre:
                prep_load(d, st_i)
            V, Vtok, lfs = pre.pop((d, st_i))
            qs, qstok = g['qs'].next()
            kh, khtok = g['kh'].next()
            kb, kbtok = g['kb'].next()
            dch, dchtok = g['dch'].next()
            for pr_ in range(2):
                us = slice(2 * pr_, 2 * pr_ + 2)
                lf, lftok, q, qtok, k, ktok = lfs[pr_]
                P, Ptok = tP.next()
                s.op('dve', lambda e, P=P, lf=lf: e.tensor_tensor_scan(out=P, data0=segm, data1=lf, initial=0.0, op0=ALU.mult, op1=ALU.add),
                     r=[lftok, 'segm'], w=[Ptok])
                P3 = r3(P, b=CS)
                nchk = 1024 // CS
                totb = P3[:, :, CS - 1:CS].broadcast_to([128, nchk, CS])
                if d == 0:
                    Bd, Bdtok = P, Ptok
                else:
                    R, Rtok = tR.next()
                    B, Btok = tB.next()
                    s.op('dve', TT(R, P, lf, ALU.subtract), r=[Ptok, lftok], w=[Rtok])
                    s.op('dve', TT(r3(B, b=CS), totb, r3(R, b=CS), ALU.subtract), r=[Ptok, Rtok], w=[Btok])
                    Bd, Bdtok = B, Btok
                dchv = dch[:, us, :].rearrange("p u c -> p (u c)")
                s.op('act', ACT(dchv, P3[:, :, CS - 1], AF.Exp), r=[Ptok], w=[dchtok])
                q2 = q.rearrange("p u t -> p (u t)")
                k2 = k.rearrange("p u t -> p (u t)")
                e1, e1tok = tE.next()
                s.op('act', ACT(e1, Bd, AF.Exp), r=[Bdtok], w=[e1tok])
                s.op('dve', TT(qs[:, us, :].rearrange("p u t -> p (u t)"), q2, e1, ALU.mult), r=[qtok, e1tok], w=[qstok])
                e2, e2tok = tE.next()
                s.op('act', ACT(e2, Bd, AF.Exp, scale=-1.0), r=[Bdtok], w=[e2tok])
                s.op('dve', TT(kh[:, us, :].rearrange("p u t -> p (u t)"), k2, e2, ALU.mult), r=[ktok, e2tok], w=[khtok])
                s.op('dve', TT(r3(kb[:, us, :].rearrange("p u t -> p (u t)"), b=CS), r3(kh[:, us, :].rearrange("p u t -> p (u t)"), b=CS),
                               dchv.unsqueeze(2).broadcast_to([128, nchk, CS]), ALU.mult), r=[khtok, dchtok], w=[kbtok])
            cur[d] = dict(qs=qs, qstok=qstok, kh=kh, khtok=khtok, kb=kb, kbtok=kbtok, V=V, Vtok=Vtok, dch=dch, dchtok=dchtok)

        ctx = {}

        def stage1(d, st_i, sub):
            g = G[d]
            cc = cur[d]
            cols = slice(sub * 128, (sub + 1) * 128)
            mask = mask_f32 if d == 0 else mask_b32
            mtok = 'mask_f32' if d == 0 else 'mask_b32'
            pa, patok = g['pa']
            for u in range(NU):
                s.op('pe', MM(pa[:, u * 128:(u + 1) * 128], cc['kh'][:, u, cols], cc['qs'][:, u, cols]), r=[cc['khtok'], cc['qstok']], w=[patok])
            pt_, pttok = g['pt']
            ptb = pt_[:, :].bitcast(BF16)
            for u in range(NU):
                s.op('pe', TR(ptb[:, u * 128:(u + 1) * 128], cc['kb'][:, u, cols], ident_b[:]), r=[cc['kbtok'], 'ident_b'], w=[pttok])
            kbt, kbttok = g['kbt'].next()
            s.op('act', ACT(kbt, r3(ptb[:, 0:512], b=128), AF.Copy), r=[pttok], w=[kbttok])
            ATs, ATtok = g['ATs'].next()
            s.op('dve', TT(ATs, r3(pa[:, :], b=128), mask[:].unsqueeze(1).broadcast_to([128, NU, 128]), ALU.mult), r=[patok, mtok], w=[ATtok])
            qs3, qs3tok = g['qs3'].next()
            s.op('dve', TT(qs3, cc['qs'][:, :, cols], c3.unsqueeze(1).broadcast_to([128, NU, 128]), ALU.mult), r=[cc['qstok'], 'c3'], w=[qs3tok])
            kb3, kb3tok = g['kb3'].next()
            s.op('dve', TT(kb3, kbt, r3m.unsqueeze(1).broadcast_to([128, NU, 128]), ALU.mult), r=[kbttok, 'r3m'], w=[kb3tok])
            po, potok = g['po']
            for u in range(NU):
                s.op('pe', MM(po[:, u * 128:(u + 1) * 128], ATs[:, u, :], cc['V'][:, sub, u * 128:(u + 1) * 128], u == 0, False, True),
                     r=[ATtok, cc['Vtok']], w=[potok])
            ctx[d] = dict(cc=cc, sub=sub, st_i=st_i, kbt=kbt, kbttok=kbttok, kb3=kb3, kb3tok=kb3tok, qs3=qs3, qs3tok=qs3tok,
                          order=list(range(nsub)) if d == 0 else list(range(nsub - 1, -1, -1)))

        def stage2(d, n_i):
            g = G[d]
            x = ctx[d]
            cc = x['cc']
            sub = x['sub']
            cidx = x['order'][n_i]
            last = (n_i == nsub - 1)
            rows = slice(cidx * CS, (cidx + 1) * CS)
            ccols = slice(sub * 128 + cidx * CS, sub * 128 + (cidx + 1) * CS)
            sprev, sprevtok = g['sprev']
            po, potok = g['po']
            pu, putok = g['pu']
            masked = (cidx == nsub - 1)
            for u in range(NU):
                us = slice(u * 128, (u + 1) * 128)
                if masked:
                    s.op('pe', MM(po[:, us], x['qs3'][:, u, :], sprev[:, u, :], False, last, True), r=[x['qs3tok'], sprevtok], w=[potok])
                else:
                    s.op('pe', MM(po[rows, us], cc['qs'][:, u, ccols], sprev[:, u, :], False, last, True), r=[cc['qstok'], sprevtok], w=[potok])
            for u in range(NU):
                us = slice(u * 128, (u + 1) * 128)
                if masked:
                    s.op('pe', MM(pu[:, us], x['kb3'][:, u, :], cc['V'][:, sub, us]), r=[x['kb3tok'], cc['Vtok']], w=[putok])
                else:
                    s.op('pe', MM(pu[:, us], x['kbt'][rows, u, :], cc['V'][rows, sub, us]), r=[x['kbttok'], cc['Vtok']], w=[putok])
            chk = sub * nsub + cidx
            dv_ = cc['dch'][:, :, chk:chk + 1].broadcast_to([128, NU, 128])
            s.op('dve', TT(g['S'], g['S'], dv_, ALU.mult), r=[g['Stok'], cc['dchtok']], w=[g['Stok']])
            s.op('dve', TT(g['S'], g['S'], r3(pu[:, :], b=128), ALU.add), r=[g['Stok'], putok], w=[g['Stok']])
            sbn, sbntok = g['Sbf'].next()
            s.op('act', ACT(sbn, g['S'], AF.Copy), r=[g['Stok']], w=[sbntok])
            g['sprev'] = (sbn, sbntok)

        def stage3(d):
            g = G[d]
            x = ctx.pop(d)
            po, potok = g['po']
            tglob = x['st_i'] * 4 + x['sub']
            ost, osttok = g['ost'].next()
            s.op('act', ACT(ost, po[:, :], AF.Copy), r=[potok], w=[osttok])
            dst = Of if d == 0 else Ob
            s.dma('sp', dst[tglob * 128:(tglob + 1) * 128, :], ost, r=[osttok], w=[tok('odst')])

        nxt = {}
        for d in range(2):
            prep(d, 0 if d == 0 else NS - 1)
        for i in range(NS):
            now = {d: cur[d] for d in range(2)}
            for sub_i in range(4):
                for d in range(2):
                    st_i = i if d == 0 else NS - 1 - i
                    sub = sub_i if d == 0 else 3 - sub_i
                    cur[d] = now[d]
                    stage1(d, st_i, sub)
                for n_i in range(nsub):
                    for d in range(2):
                        stage2(d, n_i)
                for d in range(2):
                    stage3(d)
                if sub_i == 0 and i + 1 < NS:
                    for d in range(2):
                        prep_load(d, (i + 1) if d == 0 else NS - 2 - i)
                if sub_i == 1 and i + 1 < NS:
                    for d in range(2):
                        prep(d, (i + 1) if d == 0 else NS - 2 - i)
                        nxt[d] = cur[d]
            if i + 1 < NS:
                for d in range(2):
                    cur[d] = nxt[d]

    def rec_final(G_src, ycol0):
        s.barrier()
        ar.reset()
        epsr = ar.f32(2)
        s.op('pool', MS(epsr, 1e-6), w=['epsr'])
        rfl = Ring('rfl', [ar.bf16(512) for _ in range(4)])
        rf = Ring('rf', [ar.f32(512) for _ in range(3)])
        rb = Ring('rb', [ar.bf16(512) for _ in range(4)])
        rg = Ring('rg', [ar.bf16(512) for _ in range(5)])
        rsq = Ring('rsq', [ar.f32(512) for _ in range(2)])
        rss = Ring('rss', [ar.f32(8) for _ in range(3)])
        ry = Ring('ry', [ar.bf16(512) for _ in range(3)])
        loaded = {}

        def load(t):
            rows = slice(t * 128, (t + 1) * 128)
            fl, fltok = rfl.next()
            b, btok = rb.next()
            g, gtok = rg.next()
            s.dma('sp', fl, Of[rows, :], w=[fltok])
            s.dma('sp', b, Ob[rows, :], w=[btok])
            s.dma('sp', g, G_src[rows, :], w=[gtok])
            loaded[t] = (fl, fltok, b, btok, g, gtok)

        load(0)
        if NT > 1:
            load(1)
        stA = {}

        def stage_a(t):
            if t + 2 < NT:
                load(t + 2)
            fl, fltok, b, btok, g, gtok = loaded.pop(t)
            f, ftok = rf.next()
            s.op('dve', TT(f, fl, b, ALU.add), r=[fltok, btok], w=[ftok])
            sq, sqtok = rsq.next()
            ss, sstok = rss.next()
            for h in range(4):
                hs = slice(h * 128, (h + 1) * 128)
                s.op('act', ACT(sq[:, hs], f[:, hs], AF.Square, accum=ss[:, h:h + 1]), r=[ftok], w=[sqtok, sstok])
            s.op('act', ACT(ss[:, 0:4], ss[:, 0:4], AF.Ln, bias=epsr[:, 0:1], scale=1.0 / 128.0), r=[sstok, 'epsr'], w=[sstok])
            s.op('act', ACT(ss[:, 4:8], ss[:, 0:4], AF.Exp, scale=-0.5), r=[sstok], w=[sstok])
            stA[t] = (f, ftok, ss, sstok, g, gtok)

        def stage_b(t):
            rows = slice(t * 128, (t + 1) * 128)
            f, ftok, ss, sstok, g, gtok = stA.pop(t)
            y, ytok = ry.next()
            for h in range(4):
                hs = slice(h * 128, (h + 1) * 128)
                s.op('dve', STT(y[:, hs], f[:, hs], ss[:, 4 + h:5 + h], g[:, hs], ALU.mult, ALU.mult), r=[ftok, sstok, gtok], w=[ytok])
            s.dma('sp', Y[rows, ycol0:ycol0 + 512], y, r=[ytok], w=[tok('ydst')])

        stage_a(0)
        for t in range(NT):
            if t + 1 < NT:
                stage_a(t + 1)
            stage_b(t)

    holder = {}

    def fm_store(dst, row0, func=AF.Copy, scale=None, out_dt=BF16, col0=0):
        def emit(ps, pt, i, sub, n=128):
            st, sttok = holder['stg'].next()
            o = st.bitcast(BF16)[:, 0:512] if out_dt == BF16 else st
            s.op('act', ACT(o[0:n, :], ps[0:n, :], func, scale=scale), r=[pt], w=[sttok])
            s.dma('sp', dst[row0:row0 + n, col0 + i * 512:col0 + (i + 1) * 512], o[0:n, :], r=[sttok], w=[tok('fmdst')])
        return emit

    def tm_store(dst, func=AF.Copy, mulkey=None):
        def emit(ps, pt, i, sub):
            st, sttok = holder['stg'].next()
            o = st.bitcast(BF16)[:, 0:512]
            if mulkey is not None:
                s.op('act', ACT(st, ps[:, :], func), r=[pt], w=[sttok])
                st2, st2tok = holder['stg'].next()
                o = st2.bitcast(BF16)[:, 0:512]
                s.op('dve', TT(o, st, holder[mulkey], ALU.mult), r=[sttok, mulkey], w=[st2tok])
                sttok = st2tok
            elif func == AF.Copy:
                s.op('dve', CP(o, ps[:, :]), r=[pt], w=[sttok])
            else:
                s.op('act', ACT(o, ps[:, :], func), r=[pt], w=[sttok])
            t0 = i * 512 + sub * 128
            s.dma('sp', dst[t0:t0 + 128, :], o, r=[sttok], w=[tok('tmdst')])
        return emit

    def layer0():
        segs = []
        for blk in range(4):
            segs.append(dict(kind='fm', c0=blk * 128, n=128, emit=fm_store(QTa, blk * 128, scale=0.125)))
        for blk in range(4):
            segs.append(dict(kind='fm', c0=512 + blk * 128, n=128, emit=fm_store(KTa, blk * 128)))
        for blk in range(2):
            segs.append(dict(kind='fm', c0=2048 + blk * 128, n=128, emit=fm_store(qTb, blk * 128, scale=0.125)))
        for blk in range(2):
            segs.append(dict(kind='fm', c0=2304 + blk * 128, n=128, emit=fm_store(kTb, blk * 128)))

        def v66_emit(ps, pt, i, sub):
            st, sttok = holder['vstg'].next()
            s.op('dve', CP(st.rearrange("p (h d) -> p h d", d=66)[:, :, 0:64], r3(ps[:, :], b=64)), r=[pt], w=[sttok])
            t0 = i * 512 + sub * 128
            s.dma('sp', Va66[t0:t0 + 128, :], st, r=[sttok], w=[tok('v66')])
        segs.append(dict(kind='tm', c0=1024, n=512, emit=v66_emit))
        segs.append(dict(kind='tm', c0=2560, n=512, emit=tm_store(Vb)))
        segs.append(dict(kind='tm', c0=1536, n=512, emit=tm_store(Ga, AF.Silu)))
        segs.append(dict(kind='tm', c0=3072, n=512, emit=tm_store(Gb, AF.Silu, 'ngbc')))

        gpend = []

        def gate_emit(d):
            def emit(ps, pt, i, sub):
                lr, lrtok = holder['lr'].next()
                lrb = lr.bitcast(BF16)[:, 0:512]
                s.op('act', ACT(lrb[0:16, :], ps[0:16, :], AF.Copy), r=[pt], w=[lrtok])
                for blk in range(2):
                    pz, pztok = holder['pz'].next()
                    s.op('pe', MM(pz[:, :], holder['wupb'][0:16, d * 256 + blk * 128:d * 256 + (blk + 1) * 128], lrb[0:16, :]),
                         r=[lrtok, 'wupb'], w=[pztok])
                    st, sttok = holder['gst'].next()
                    s.op('act', ACT(st, pz[:, :], AF.Sigmoid, bias=holder['glab'][:, d * 2 + blk:d * 2 + blk + 1]), r=[pztok, 'glab'], w=[sttok])
                    gpend.append((st, sttok, d, blk, i))
                if d == 1:
                    while gpend:
                        st, sttok, d_, blk_, i_ = gpend.pop(0)
                        s.op('act', ACT(st, st, AF.Ln), r=[sttok], w=[sttok])
                        s.dma('sp', lfT[d_, blk_ * 128:(blk_ + 1) * 128, i_ * 512:(i_ + 1) * 512], st, r=[sttok], w=[tok('lfdst')])
            return emit
        segs.append(dict(kind='fm', c0=3584, n=16, emit=gate_emit(0)))
        segs.append(dict(kind='fm', c0=3600, n=16, emit=gate_emit(1)))

        def alloc_extras():
            holder['stg'] = Ring('stg', [ar.f32(512) for _ in range(8)])
            holder['lr'] = Ring('lr', [ar.f32(512) for _ in range(2)])
            holder['wup'] = ar.f32(512)
            holder['wupb'] = ar.bf16(512)
            holder['gst'] = Ring('gst', [ar.f32(512) for _ in range(6)])
            holder['glab'] = ar.f32(4)
            holder['pz'] = Ring('pz', [banks[6], banks[7]])
            vst = [ar.bf16(528) for _ in range(3)]
            holder['vstg'] = Ring('vstg', vst)
            for v_, vt_ in zip(vst, holder['vstg'].toks):
                s.op('pool', MS(v_, 1.0), w=[vt_])
            holder['ngbc'] = ar.f32(512)
            for h in range(4):
                s.dma('sp', holder['ngbc'][:, h * 128:(h + 1) * 128], gla_ng[0:1, :].partition_broadcast(128), w=['ngbc'])
            for d in range(2):
                s.dma('sp', holder['wup'][0:16, d * 256:(d + 1) * 256], gla_w_up[d], w=['wup'])
            s.op('dve', CP(holder['wupb'][0:16, :], holder['wup'][0:16, :]), r=['wup'], w=['wupb'])
            s.dma('sp', holder['glab'], gla_bT.rearrange("p d b -> p (d b)"), w=['glab'])
        phase_a_with(0, x_in, segs, alloc_extras, e_w_in, 3616)

    hkT = dscr("hkT", [2, 512, L], BF16)
    hlfT = dscr("hlfT", [2, 512, L], F32)
    xbcT = dscr("xbcT", [1024, L + 4], BF16)
    dtA = dscr("dtA", [L, 32], F32)
    Btm = dscr("Btm", [L, 256], BF16)

    def layer1_a():
        segs = []
        for blk in range(4):
            segs.append(dict(kind='fm', c0=blk * 128, n=128, emit=fm_store(QTa, blk * 128, scale=128.0 ** -0.5)))

        fpend = []

        def forget_emit(d, blk):
            def emit(ps, pt, i, sub):
                st, sttok = holder['fst'].next()
                s.op('act', ACT(st, ps[:, :], AF.Sigmoid), r=[pt], w=[sttok])
                s.op('dve', TS(st, st, holder['lbs'][:, 12 + blk:13 + blk], ALU.mult, holder['lbs'][:, 8 + blk:9 + blk], ALU.add),
                     r=[sttok, 'lbs'], w=[sttok])
                st2, st2tok = holder['stg'].next()
                kb_ = st2.bitcast(BF16)[:, 0:512]
                s.op('act', ACT(kb_, st, AF.Identity, bias=1.0, scale=-1.0), r=[sttok], w=[st2tok])
                s.dma('sp', hkT[d, blk * 128:(blk + 1) * 128, i * 512:(i + 1) * 512], kb_, r=[st2tok], w=[tok('hk')])
                fpend.append((st, sttok, d, blk, i))
                if d == 1 and blk == 3:
                    while fpend:
                        st_, sttok_, d_, blk_, i_ = fpend.pop(0)
                        s.op('act', ACT(st_, st_, AF.Ln), r=[sttok_], w=[sttok_])
                        s.dma('sp', hlfT[d_, blk_ * 128:(blk_ + 1) * 128, i_ * 512:(i_ + 1) * 512], st_, r=[sttok_], w=[tok('hlf')])
            return emit
        for d in range(2):
            for blk in range(4):
                segs.append(dict(kind='fm', c0=512 + d * 512 + blk * 128, n=128, emit=forget_emit(d, blk)))
        segs.append(dict(kind='tm', c0=1536, n=512, emit=tm_store(Vb)))
        segs.append(dict(kind='tm', c0=2048, n=512, emit=tm_store(Gb, AF.Silu, 'ngbc')))
        segs.append(dict(kind='tm', c0=2560, n=512, emit=tm_store(Ga, AF.Silu)))
        for blk in range(8):
            segs.append(dict(kind='fm', c0=3072 + blk * 128, n=128, emit=fm_store(xbcT, blk * 128, col0=2)))

        def dt_emit(ps, pt, i, sub):
            dtt, dttok = holder['dtt'].next()
            s.op('dve', TT(dtt[:, 0:16], ps[:, 0:16], holder['dtb'][:, 0:16], ALU.add), r=[pt, 'dtb'], w=[dttok])
            s.op('act', ACT(dtt[:, 0:16], dtt[:, 0:16], AF.Exp), r=[dttok], w=[dttok])
            s.op('act', ACT(dtt[:, 0:16], dtt[:, 0:16], AF.Ln, bias=1.0), r=[dttok], w=[dttok])
            s.op('dve', TT(dtt[:, 16:32], dtt[:, 0:16], holder['dtb'][:, 16:32], ALU.mult), r=[dttok, 'dtb'], w=[dttok])
            t0 = i * 512 + sub * 128
            s.dma('sp', dtA[t0:t0 + 128, :], dtt[:, 0:32], r=[dttok], w=[tok('dtA')])
        segs.append(dict(kind='tm', c0=4096, n=16, emit=dt_emit))

        def alloc_extras():
            holder['stg'] = Ring('stg', [ar.f32(512) for _ in range(8)])
            holder['dtt'] = Ring('dtt', [ar.f32(32) for _ in range(2)])
            holder['fst'] = Ring('fst', [ar.f32(512) for _ in range(10)])
            holder['dtb'] = ar.f32(32)
            holder['lbs'] = ar.f32(16)
            holder['ngbc'] = ar.f32(512)
            for h in range(4):
                s.dma('sp', holder['ngbc'][:, h * 128:(h + 1) * 128], hgrn_ng[0:1, :].partition_broadcast(128), w=['ngbc'])
            zz = ar.f32(16)
            lbs = holder['lbs']
            s.dma('sp', lbs[:, 0:8], hgrn_lbT.rearrange("p l b -> p (l b)"), w=['lbs'])
            s.op('dve', TT(lbs[:, 8:12], lbs[:, 4:8], lbs[:, 0:4], ALU.subtract), r=['lbs'], w=['lbs'])
            s.op('act', ACT(lbs[:, 8:12], lbs[:, 8:12], AF.Sigmoid), r=['lbs'], w=['lbs'])
            s.op('dve', TS(lbs[:, 12:16], lbs[:, 8:12], -1.0, ALU.mult, 1.0, ALU.add), r=['lbs'], w=['lbs'])
            dtb = holder['dtb']
            s.dma('sp', dtb[:, 0:16], dt_bias[0:1, :].partition_broadcast(128), w=['dtb'])
            s.dma('sp', dtb[:, 16:32], a_log[0:1, :].partition_broadcast(128), w=['dtb'])
            s.op('act', ACT(dtb[:, 16:32], dtb[:, 16:32], AF.Exp), r=['dtb'], w=['dtb'])
            s.op('dve', TS(dtb[:, 16:32], dtb[:, 16:32], -1.0, ALU.mult), r=['dtb'], w=['dtb'])
            s.op('pool', MS(zz, 0.0), w=['zz'])
            xv_ = xbcT.rearrange("(b p) t -> p b t", p=128)
            zzb = r3(zz.bitcast(BF16)[:, 0:16], b=2)
            s.dma('sp', xv_[:, :, 0:2], zzb, r=['zz'], w=[tok('xbcz')])
            s.dma('sp', xv_[:, :, L + 2:L + 4], zzb, r=['zz'], w=[tok('xbcz')])
        phase_a_with(1, X1, segs, alloc_extras, o_w_in, 4112)

    def phase_a_with(l, x_src, segs, extras, wsrc, wcols):
        s.barrier()
        ar.reset(W_COLS)
        extras()
        phase_a(l, x_src, segs)

    def na_phase():
        s.barrier()
        ar.reset()
        NK = 640
        E_int = r3(ar.bf16(8 * NK), b=NK)
        E_edge = r3(ar.bf16(8 * NK), b=NK)
        bst = ar.f32(8 * NK)
        KTs = Ring('KT', [r3(ar.bf16(4 * 1024), b=1024) for _ in range(4)])
        Vs_raw = [ar.bf16(8 * 8 * 66) for _ in range(4)]
        Vs = Ring('Vn', [v.rearrange("p (b h d) -> p b h d", b=8, h=8) for v in Vs_raw])
        QTs = Ring('QT', [r3(ar.bf16(4 * 128), b=128) for _ in range(5)])
        Gs = Ring('Gn', [ar.bf16(512) for _ in range(5)])
        eS = Ring('eS', [ar.bf16(NK) for _ in range(5)])
        PTs = Ring('PT', [ar.bf16(NK) for _ in range(5)])
        rec = Ring('rec', [ar.f32(8) for _ in range(2)])
        yst = Ring('yst', [ar.bf16(512) for _ in range(4)])
        pSA = Ring('pSA', [banks[0], banks[1], banks[2]])
        pSB = Ring('pSB', [banks[3], banks[4], banks[5]])
        pOn = Ring('pOn', [banks[6], banks[7]])

        def load_E(cls, dst, dtok):
            s.dma('sp', bst, btab[cls], w=['bst'])
            s.op('act', ACT(dst.rearrange("p h k -> p (h k)"), bst, AF.Exp), r=['bst'], w=[dtok])

        load_E(2, E_int, 'E_int')
        edge_loaded = [None]
        QTv = QTa.rearrange("(pr p) t -> p pr t", p=128)
        KTv = KTa.rearrange("(pr p) t -> p pr t", p=128)
        loaded = {}
        wloaded = {}
        NSB = NT // 4

        def wstart(sb):
            return min(max(8 * sb - 4, 0), ROWS - 16)

        def wload(sb):
            k0 = wstart(sb) * 64
            KT, KTtok = KTs.next()
            s.dma('sp', KT, KTv[:, :, k0:k0 + 1024], w=[KTtok])
            V, Vtok = Vs.next()
            s.dma('sp', V.rearrange("p b h d -> p b (h d)"), Va66[k0:k0 + 1024, :].rearrange("(b p) c -> p b c", p=128), w=[Vtok])
            wloaded[sb] = (KT, KTtok, V, Vtok)

        def load(t):
            QT, QTtok = QTs.next()
            s.dma('sp', QT, QTv[:, :, t * 128:(t + 1) * 128], w=[QTtok])
            G, Gtok = Gs.next()
            s.dma('sp', G, Ga[t * 128:(t + 1) * 128, :], w=[Gtok])
            loaded[t] = (QT, QTtok, G, Gtok)

        wload(0)
        if NSB > 1:
            wload(1)
        load(0)
        if NT > 1:
            load(1)
        tctx = {}

        def tile_ctx(t):
            if t + 2 < NT:
                load(t + 2)
            sb = t // 4
            if t % 4 == 0 and sb + 2 < NSB:
                wload(sb + 2)
            r = 2 * t
            ks = min(max(r - 4, 0), ROWS - 10)
            cls = (r - ks) // 2
            if cls == 2:
                E, Etok = E_int, 'E_int'
            else:
                if edge_loaded[0] != cls:
                    load_E(cls, E_edge, 'E_edge')
                    edge_loaded[0] = cls
                E, Etok = E_edge, 'E_edge'
            KT, KTtok, V, Vtok = wloaded[sb]
            if t % 4 == 3:
                wloaded.pop(sb)
            QT, QTtok, G, Gtok = loaded.pop(t)
            y, ytok = yst.next()
            boff = (ks - wstart(sb)) // 2
            tctx[t] = dict(E=E, Etok=Etok, KT=KT, KTtok=KTtok, V=V, Vtok=Vtok, QT=QT, QTtok=QTtok, G=G, Gtok=Gtok, y=y, ytok=ytok, po={}, boff=boff)

        def emit_S(t, h):
            c = tctx[t]
            pr = slice((h % 2) * 64, (h % 2) * 64 + 64)
            pa, patok = pSA.next()
            pb, pbtok = pSB.next()
            for blk in range(5):
                dst = pa[:, blk * 128:(blk + 1) * 128] if blk < 4 else pb[:, 0:128]
                s.op('pe', MM(dst, c['KT'][pr, h // 2, (c['boff'] + blk) * 128:(c['boff'] + blk + 1) * 128], c['QT'][pr, h // 2, :]),
                     r=[c['KTtok'], c['QTtok']], w=[patok if blk < 4 else pbtok])
            e_, etok = eS.next()
            s.op('act', ACT(e_[:, 0:512], pa[:, :], AF.Exp), r=[patok], w=[etok])
            s.op('act', ACT(e_[:, 512:640], pb[:, 0:128], AF.Exp), r=[pbtok], w=[etok])
            P, Ptok = PTs.next()
            s.op('dve', TT(P, e_, c['E'][:, h, :], ALU.mult), r=[etok, c['Etok']], w=[Ptok])
            return P, Ptok

        def emit_PV(t, h, P, Ptok):
            c = tctx[t]
            hg, hh = h // 4, h % 4
            if hg not in c['po']:
                c['po'][hg] = pOn.next()
            po, potok = c['po'][hg]
            for blk in range(5):
                s.op('pe', MM(po[:, hh * 65:hh * 65 + 65], P[:, blk * 128:(blk + 1) * 128], c['V'][:, c['boff'] + blk, h, 0:65], blk == 0, blk == 4),
                     r=[Ptok, c['Vtok']], w=[potok])
            if hh == 3:
                rc, rctok = rec.next()
                po3 = po[:, 0:260].rearrange("p (h d) -> p h d", d=65)
                s.op('dve', lambda e, rc=rc, po3=po3: e.reciprocal(out=rc[:, 0:4], in_=po3[:, :, 64]), r=[potok], w=[rctok])
                for j in range(4):
                    hj = hg * 4 + j
                    s.op('dve', STT(c['y'][:, hj * 64:(hj + 1) * 64], po3[:, j, 0:64], rc[:, j:j + 1], c['G'][:, hj * 64:(hj + 1) * 64], ALU.mult, ALU.mult),
                         r=[potok, rctok, c['Gtok']], w=[c['ytok']])
                if hg == 1:
                    s.dma('sp', Y[t * 128:(t + 1) * 128, 0:512], c['y'], r=[c['ytok']], w=[tok('ydst')])
                    tctx.pop(t)

        pend = []
        for t in range(NT):
            for h in range(8):
                if t not in tctx:
                    tile_ctx(t)
                P, Ptok = emit_S(t, h)
                pend.append((t, h, P, Ptok))
                if len(pend) > 2:
                    emit_PV(*pend.pop(0))
        while pend:
            emit_PV(*pend.pop(0))

    def ssd_conv():
        s.barrier()
        ar.reset()
        cw = ar.f32(32)
        cbias = ar.f32(8)
        s.dma('sp', cw, conv_wT.rearrange("p b k -> p (b k)"), w=['cw'])
        s.dma('sp', cbias, conv_bT[:, :], w=['cbias'])
        dg = ar.bf16(32 * 128)
        dg3 = r3(dg, b=128)
        for j in range(32):
            s.op('dve', TS(dg3[:, j, :], ident_f[:], cw[:, j:j + 1], ALU.mult), r=['ident_f', 'cw'], w=['dg'])
        rin = Ring('cin', [ar.bf16(516) for _ in range(6)])
        rfm = Ring('cfm', [ar.bf16(512) for _ in range(12)])
        rtm = Ring('ctm', [ar.bf16(512) for _ in range(3)])
        ptr = Ring('cvp', [banks[0], banks[1], banks[2]])
        pcv = Ring('pcv', [banks[3], banks[4], banks[5], banks[6]])
        items = [(i, blk) for i in range(NS) for blk in range(8)]
        cloaded = {}

        def cload(n):
            i, blk = items[n]
            xin, xtok = rin.next()
            s.dma('sp', xin[:, 0:515], xbcT[blk * 128:(blk + 1) * 128, i * 512:i * 512 + 515], w=[xtok])
            cloaded[n] = (xin, xtok)

        for n in range(min(4, len(items))):
            cload(n)
        for i in range(NS):
            t0 = i * 512
            fm = {}
            for blk in range(8):
                n = i * 8 + blk
                if n + 4 < len(items):
                    cload(n + 4)
                xin, xtok = cloaded.pop(n)
                pc, pctok = pcv.next()
                for k in range(4):
                    s.op('pe', MM(pc[:, :], dg3[:, blk * 4 + k, :], xin[:, k:k + 512], k == 0, k == 3), r=['dg', xtok], w=[pctok])
                o, otok = rfm.next()
                s.op('act', ACT(o, pc[:, :], AF.Silu, bias=cbias[:, blk:blk + 1]), r=[pctok, 'cbias'], w=[otok])
                fm[blk] = (o, otok)
                if blk in (4, 5):
                    s.dma('sp', qTb[(blk - 4) * 128:(blk - 3) * 128, t0:t0 + 512], o, r=[otok], w=[tok('BT')])
                if blk in (6, 7):
                    s.dma('sp', kTb[(blk - 6) * 128:(blk - 5) * 128, t0:t0 + 512], o, r=[otok], w=[tok('CT')])
            for sub in range(4):
                ps, pt = ptr.next()
                psb = ps[:, :].bitcast(BF16)
                for b4 in range(4):
                    o, otok = fm[b4]
                    s.op('pe', TR(psb[:, b4 * 128:(b4 + 1) * 128], o[:, sub * 128:(sub + 1) * 128], ident_b[:]), r=[otok, 'ident_b'], w=[pt])
                tm, tmtok = rtm.next()
                s.op('act', ACT(tm, psb[:, 0:512], AF.Copy), r=[pt], w=[tmtok])
                s.dma('sp', Va[t0 + sub * 128:t0 + (sub + 1) * 128, :], tm, r=[tmtok], w=[tok('xs')])
                ps, pt = ptr.next()
                psb = ps[:, :].bitcast(BF16)
                for b2 in range(2):
                    o, otok = fm[4 + b2]
                    s.op('pe', TR(psb[:, b2 * 128:(b2 + 1) * 128], o[:, sub * 128:(sub + 1) * 128], ident_b[:]), r=[otok, 'ident_b'], w=[pt])
                tm, tmtok = rtm.next()
                s.op('dve', CP(tm[:, 0:256], psb[:, 0:256]), r=[pt], w=[tmtok])
                s.dma('sp', Btm[t0 + sub * 128:t0 + (sub + 1) * 128, :], tm[:, 0:256], r=[tmtok], w=[tok('Btm')])

    def ssd_main():
        s.barrier()
        ar.reset()
        tri = [ar.f32(128), ar.f32(128)]
        mbf = [ar.f32(128), ar.f32(128)]
        s.op('pool', CP(tri[0], mask_f128[:]), r=['mask_f128'], w=['tri'])
        s.op('pool', CP(tri[1], mask_b128[:]), r=['mask_b128'], w=['tri'])
        mb4 = [ar.bf16(512), ar.bf16(512)]
        for d in range(2):
            s.op('pool', TS(mbf[d], tri[d], -1.0, ALU.add, -NEG, ALU.mult), r=['tri'], w=['mbf'])
            for hh in range(4):
                s.op('pool', CP(mb4[d][:, hh * 128:(hh + 1) * 128], mbf[d]), r=['mbf'], w=['mb4'])
        chains = [(g, d) for g in range(2) for d in range(2)]
        C = {}
        for ch in chains:
            C[ch] = dict(S=ar.f32(256), Stok=tok('sS'), Sbf=Ring('sSbf', [ar.bf16(256) for _ in range(3)]))
            s.op('pool', MS(C[ch]['S'], 0.0), w=[C[ch]['Stok']])
            sb0, sbt0 = C[ch]['Sbf'].next()
            s.op('pool', MS(sb0, 0.0), w=[sbt0])
            C[ch]['sprev'] = (sb0, sbt0)
        rdta = Ring('dta', [ar.f32(32) for _ in range(4)])
        rsm = Ring('sm', [ar.f32(64) for _ in range(4)])
        rR = Ring('sR', [ar.f32(1024) for _ in range(4)])
        rBT = Ring('sBT', [ar.bf16(128) for _ in range(9)])
        rCT = Ring('sCT', [ar.bf16(128) for _ in range(9)])
        rBm = Ring('sBm', [ar.bf16(128) for _ in range(9)])
        rxs = Ring('sxs', [ar.bf16(256) for _ in range(9)])
        rcb = Ring('scb', [ar.bf16(128) for _ in range(5)])
        rxdt = Ring('sxdt', [ar.bf16(256) for _ in range(5)])
        rxd = Ring('sxd', [ar.bf16(256) for _ in range(5)])
        rarg = Ring('sarg', [ar.f32(512) for _ in range(4)])
        rsg = Ring('ssg', [ar.bf16(512) for _ in range(4)])
        rat = Ring('sat', [ar.bf16(512) for _ in range(3)])
        ry1 = Ring('sy1', [ar.f32(256) for _ in range(3)])
        ry2 = Ring('sy2', [ar.f32(256) for _ in range(3)])
        rtS = Ring('stS', [ar.f32(256) for _ in range(2)])
        ryb = Ring('syb', [ar.bf16(256) for _ in range(3)])
        pq = Ring('spq', [banks[0], banks[1]])
        pCB = Ring('spCB', [banks[2]])
        pBC = Ring('spBC', [banks[3], banks[4]])
        pY = Ring('spY', [banks[5], banks[6]])
        pU = Ring('spU', [banks[7]])
        shared = {}
        sloaded = {}

        def sload(ch, t):
            g, d = ch
            rows = slice(t * 128, (t + 1) * 128)
            BT, BTtok = rBT.next()
            CT, CTtok = rCT.next()
            Bm, Bmtok = rBm.next()
            xs, xstok = rxs.next()
            s.dma('sp', BT, qTb[g * 128:(g + 1) * 128, rows], w=[BTtok])
            s.dma('sp', CT, kTb[g * 128:(g + 1) * 128, rows], w=[CTtok])
            s.dma('sp', Bm, Btm[rows, g * 128:(g + 1) * 128], w=[Bmtok])
            s.dma('sp', xs, Va[rows, g * 256:(g + 1) * 256], w=[xstok])
            sloaded[(ch, t)] = (BT, BTtok, CT, CTtok, Bm, Bmtok, xs, xstok)

        dloaded = {}

        def dload(d, t):
            dta, dtatok = rdta.next()
            s.dma('sp', dta, dtA[t * 128:(t + 1) * 128, :], w=[dtatok])
            dloaded[(d, t)] = (dta, dtatok)

        def dprep(d, t):
            dta, dtatok = dloaded.pop((d, t))
            a_d = dta[:, 16 + d * 8:24 + d * 8]
            q_, qtok = pq.next()
            s.op('pe', MM(q_[:, 0:8], tri[d], a_d), r=['tri', dtatok], w=[qtok])
            s.op('pe', MM(q_[:, 8:16], ones_f[:], a_d), r=['ones_f', dtatok], w=[qtok])
            sm, smtok = rsm.next()
            s.op('act', ACT(sm[:, 0:8], q_[:, 0:8], AF.Copy), r=[qtok], w=[smtok, qtok])
            s.op('act', ACT(sm[:, 8:16], q_[:, 0:8], AF.Exp), r=[qtok], w=[smtok, qtok])
            s.op('dve', TT(sm[:, 16:24], q_[:, 8:16], sm[:, 0:8], ALU.subtract), r=[qtok, smtok], w=[smtok, qtok])
            s.op('act', ACT(sm[:, 24:32], sm[:, 16:24], AF.Exp), r=[smtok], w=[smtok])
            s.op('act', ACT(sm[:, 32:40], q_[:, 8:16], AF.Exp), r=[qtok], w=[smtok, qtok])
            R, Rtok = rR.next()
            s.op('dve', TT(r3(R, b=128), a_d.unsqueeze(2).broadcast_to([128, 8, 128]), tri[d].unsqueeze(1).broadcast_to([128, 8, 128]), ALU.mult),
                 r=[dtatok, 'tri'], w=[Rtok])
            shared[(d, t)] = dict(dta=dta, dtatok=dtatok, sm=sm, smtok=smtok, R=R, Rtok=Rtok)

        X = {}

        def s2(ch, t):
            g, d = ch
            sh = shared[(d, t)]
            BT, BTtok, CT, CTtok, Bm, Bmtok, xs, xstok = sloaded.pop((ch, t))
            cbp, cbptok = pCB.next()
            s.op('pe', MM(cbp[:, 0:128], BT, CT), r=[BTtok, CTtok], w=[cbptok])
            cb, cbtok = rcb.next()
            s.op('act', ACT(cb, cbp[:, 0:128], AF.Copy), r=[cbptok], w=[cbtok])
            dta = sh['dta']
            sm = sh['sm']
            xdt, xdttok = rxdt.next()
            dtv = dta[:, d * 8 + g * 4:d * 8 + g * 4 + 4].unsqueeze(2).broadcast_to([128, 4, 64])
            s.op('dve', TT(r3(xdt, b=64), r3(xs, b=64), dtv, ALU.mult), r=[xstok, sh['dtatok']], w=[xdttok])
            xd, xdtok = rxd.next()
            dsv = sm[:, 24 + g * 4:28 + g * 4].unsqueeze(2).broadcast_to([128, 4, 64])
            s.op('dve', TT(r3(xd, b=64), r3(xdt, b=64), dsv, ALU.mult), r=[xdttok, sh['smtok']], w=[xdtok])
            X[ch] = dict(t=t, sh=sh, CT=CT, CTtok=CTtok, Bm=Bm, Bmtok=Bmtok, cb=cb, cbtok=cbtok, xdt=xdt, xdttok=xdttok, xd=xd, xdtok=xdtok)

        def bc(ch):
            g, d = ch
            x = X[ch]
            sh = x['sh']
            bcp, bcptok = pBC.next()
            s.op('pe', MM(bcp[:, :], ones_f[:], sh['R'][:, g * 512:(g + 1) * 512], True, False), r=['ones_f', sh['Rtok']], w=[bcptok])
            s.op('pe', MM(bcp[:, :], ident_b[:], mb4[d], False, True), r=['ident_b', 'mb4'], w=[bcptok])
            x['bcp'] = bcp
            x['bcptok'] = bcptok

        def s3a(ch):
            g, d = ch
            x = X[ch]
            sh = x['sh']
            sm = sh['sm']
            arg, argtok = rarg.next()
            acv = sm[:, g * 4:g * 4 + 4].unsqueeze(2).broadcast_to([128, 4, 128])
            s.op('dve', TT(r3(arg, b=128), r3(x['bcp'][:, :], b=128), acv, ALU.subtract), r=[x['bcptok'], sh['smtok']], w=[argtok])
            sg, sgtok = rsg.next()
            s.op('act', ACT(sg, arg, AF.Exp), r=[argtok], w=[sgtok])
            x['sg'] = sg
            x['sgtok'] = sgtok

        def s3b(ch):
            g, d = ch
            c = C[ch]
            x = X[ch]
            yp, yptok = pY.next()
            sprev, sprevtok = c['sprev']
            at, attok = rat.next()
            s.op('dve', TT(r3(at, b=128), r3(x['sg'], b=128), x['cb'].unsqueeze(1).broadcast_to([128, 4, 128]), ALU.mult), r=[x['cbtok'], x['sgtok']], w=[attok])
            for hh in range(4):
                s.op('pe', MM(yp[:, hh * 64:(hh + 1) * 64], at[:, hh * 128:(hh + 1) * 128], x['xdt'][:, hh * 64:(hh + 1) * 64]), r=[attok, x['xdttok']], w=[yptok])
                s.op('pe', MM(yp[:, 256 + hh * 64:256 + (hh + 1) * 64], x['CT'], sprev[:, hh * 64:(hh + 1) * 64]), r=[x['CTtok'], sprevtok], w=[yptok])
            x['yp'] = yp
            x['yptok'] = yptok

        def s3c(ch):
            g, d = ch
            x = X[ch]
            sh = x['sh']
            sm = sh['sm']
            yp, yptok = x['yp'], x['yptok']
            y1, y1tok = ry1.next()
            s.op('act', ACT(y1, yp[:, 0:256], AF.Copy), r=[yptok], w=[y1tok, yptok])
            y2, y2tok = ry2.next()
            eav = sm[:, 8 + g * 4:12 + g * 4].unsqueeze(2).broadcast_to([128, 4, 64])
            s.op('dve', TT(r3(y2, b=64), r3(yp[:, 256:512], b=64), eav, ALU.mult), r=[yptok, sh['smtok']], w=[y2tok, yptok])
            yb, ybtok = ryb.next()
            s.op('dve', TT(yb, y2, y1, ALU.add), r=[y2tok, y1tok], w=[ybtok])
            rows = slice(x['t'] * 128, (x['t'] + 1) * 128)
            dst = Of if d == 0 else Ob
            s.dma('sp', dst[rows, g * 256:(g + 1) * 256], yb, r=[ybtok], w=[tok('sodst')])

        def s4(ch):
            g, d = ch
            c = C[ch]
            x = X.pop(ch)
            sm = x['sh']['sm']
            up, uptok = pU.next()
            s.op('pe', MM(up[:, 0:256], x['Bm'], x['xd']), r=[x['Bmtok'], x['xdtok']], w=[uptok])
            tS, tStok = rtS.next()
            cdv = sm[:, 32 + g * 4:36 + g * 4].unsqueeze(2).broadcast_to([128, 4, 64])
            s.op('dve', TT(r3(tS, b=64), r3(c['S'], b=64), cdv, ALU.mult), r=[c['Stok'], x['sh']['smtok']], w=[tStok])
            s.op('dve', TT(c['S'], tS, up[:, 0:256], ALU.add), r=[tStok, uptok], w=[c['Stok']])
            sbn, sbntok = c['Sbf'].next()
            s.op('act', ACT(sbn, c['S'], AF.Copy), r=[c['Stok']], w=[sbntok])
            c['sprev'] = (sbn, sbntok)

        def tof(ch, i):
            return i if ch[1] == 0 else NT - 1 - i

        for ch in chains:
            sload(ch, tof(ch, 0))
        for d in range(2):
            dload(d, tof((0, d), 0))
        for i in range(NT):
            if i + 1 < NT:
                for ch in chains:
                    sload(ch, tof(ch, i + 1))
                for d in range(2):
                    dload(d, tof((0, d), i + 1))
            for d in range(2):
                dprep(d, tof((0, d), i))
            for ch in chains:
                s2(ch, tof(ch, i))
            nch = len(chains)
            bc(chains[0])
            bc(chains[1])
            s3a(chains[0])
            for k in range(nch + 1):
                if k + 1 < nch:
                    s3a(chains[k + 1])
                if k + 2 < nch:
                    bc(chains[k + 2])
                if k < nch:
                    s3b(chains[k])
                if k >= 1:
                    s3c(chains[k - 1])
            for ch in chains:
                s4(ch)
            for d in range(2):
                shared.pop((d, tof((0, d), i)))

    def ssd_final():
        s.barrier()
        ar.reset()
        dsk = ar.f32(8)
        ngb = ar.f32(512)
        s.dma('sp', dsk, d_skip[0:1, :].partition_broadcast(128), w=['dsk'])
        s.dma('sp', ngb, ssm_ng[0:1, :].partition_broadcast(128), w=['ngb'])
        epsf = ar.f32(2)
        s.op('pool', MS(epsf, 1e-6), w=['epsf'])
        rfl = Ring('ffl', [ar.bf16(512) for _ in range(4)])
        rf = Ring('ff', [ar.f32(512) for _ in range(3)])
        rb = Ring('fb', [ar.bf16(512) for _ in range(4)])
        rx = Ring('fx', [ar.bf16(512) for _ in range(4)])
        rg = Ring('fg', [ar.bf16(512) for _ in range(4)])
        rt = Ring('ft', [ar.f32(512) for _ in range(3)])
        rss = Ring('fss', [ar.f32(8) for _ in range(2)])
        ry = Ring('fy', [ar.bf16(512) for _ in range(3)])
        loaded = {}

        def load(t):
            rows = slice(t * 128, (t + 1) * 128)
            fl, fltok = rfl.next()
            b, btok = rb.next()
            xs, xstok = rx.next()
            g, gtok = rg.next()
            s.dma('sp', fl, Of[rows, :], w=[fltok])
            s.dma('sp', b, Ob[rows, :], w=[btok])
            s.dma('sp', xs, Va[rows, :], w=[xstok])
            s.dma('sp', g, Ga[rows, :], w=[gtok])
            loaded[t] = (fl, fltok, b, btok, xs, xstok, g, gtok)

        load(0)
        if NT > 1:
            load(1)
        stA = {}

        def stage_a(t):
            if t + 2 < NT:
                load(t + 2)
            fl, fltok, b, btok, xs, xstok, g, gtok = loaded.pop(t)
            f, ftok = rf.next()
            s.op('dve', TT(f, fl, b, ALU.add), r=[fltok, btok], w=[ftok])
            tmp, tmptok = rt.next()
            s.op('dve', TT(r3(tmp, b=64), r3(xs, b=64), dsk[:, 0:8].unsqueeze(2).broadcast_to([128, 8, 64]), ALU.mult), r=[xstok, 'dsk'], w=[tmptok])
            s.op('dve', TT(f, f, tmp, ALU.add), r=[ftok, tmptok], w=[ftok])
            s.op('dve', TT(f, f, g, ALU.mult), r=[ftok, gtok], w=[ftok])
            ss, sstok = rss.next()
            s.op('act', ACT(tmp, f, AF.Square, accum=ss[:, 0:1]), r=[ftok], w=[tmptok, sstok])
            s.op('act', ACT(ss[:, 1:2], ss[:, 0:1], AF.Ln, bias=epsf[:, 0:1], scale=1.0 / 512.0), r=[sstok, 'epsf'], w=[sstok])
            s.op('act', ACT(ss[:, 2:3], ss[:, 1:2], AF.Exp, scale=-0.5), r=[sstok], w=[sstok])
            stA[t] = (f, ftok, ss, sstok)

        def stage_b(t):
            rows = slice(t * 128, (t + 1) * 128)
            f, ftok, ss, sstok = stA.pop(t)
            y, ytok = ry.next()
            s.op('dve', STT(y, f, ss[:, 2:3], ngb, ALU.mult, ALU.mult), r=[ftok, sstok, 'ngb'], w=[ytok])
            s.dma('sp', Y[rows, 512:1024], y, r=[ytok], w=[tok('ydst')])

        stage_a(0)
        for t in range(NT):
            if t + 1 < NT:
                stage_a(t + 1)
            stage_b(t)

    qv = qTb.rearrange("(u p) t -> u p t", p=128)
    kv = kTb.rearrange("(u p) t -> u p t", p=128)
    lv = lfT.rearrange("d (u p) t -> d u p t", p=128)
    hq = QTa.rearrange("(u p) t -> u p t", p=128)
    hk = hkT.rearrange("d (u p) t -> d u p t", p=128)
    hl = hlfT.rearrange("d (u p) t -> d u p t", p=128)
    phases = [
        layer0,
        na_phase,
        lambda: recurrence(2, 2, 64, 128, 1.0 / 16.0, [qv[0], qv[1]], [[kv[0], kv[1]]] * 2,
                           [[lv[0, 0], lv[0, 1]], [lv[1, 0], lv[1, 1]]], Vb, 128),
        lambda: rec_final(Gb, 512),
        lambda: phase_c(0, x_in, e_w_out, X1 if nlayers > 1 else out),
    ]
    if nlayers > 1:
        phases += [
            layer1_a,
            lambda: recurrence_b(QTa, [hkT[0], hkT[1]], [hlfT[0], hlfT[1]], Vb),
            lambda: rec_final(Gb, 0),
            ssd_conv,
            ssd_main,
            ssd_final,
            lambda: phase_c(1, X1, o_w_out, out),
        ]
    for ph in phases[:stop]:
        ph()

    s.barrier()
    with nc.Block() as block:
        s.emit(block)
    return nc, es


def _na_btab(rpb, ROWS):
    H = rpb.shape[0]
    out = np.full((5, 128, H, 5, 128), NEG, np.float32)
    reps = {0: 0, 1: 2, 2: 4, 3: ROWS - 4, 4: ROWS - 2}
    p = np.arange(128)
    q = np.arange(128)
    for cls, r in reps.items():
        ks = min(max(r - 4, 0), ROWS - 10)
        for blk in range(5):
            KR = ks + (blk * 128 + p) // 64
            kc = p % 64
            R = r + q // 64
            qc = q % 64
            rs = np.clip(R - 4, 0, ROWS - 8)
            cs = np.clip(qc - 8, 0, 48)
            vr = (KR[:, None] >= rs[None, :]) & (KR[:, None] < rs[None, :] + 8)
            vc = (kc[:, None] >= cs[None, :]) & (kc[:, None] < cs[None, :] + 16)
            dr = np.clip(KR[:, None] - R[None, :] + 7, 0, 14)
            dc = np.clip(kc[:, None] - qc[None, :], -15, 15) + 15
            g = rpb[:, dr, dc]
            valid = (vr & vc)[None]
            out[cls, :, :, blk, :] = np.where(valid, g, NEG).transpose(1, 0, 2)
    return out.reshape(5, 128, H * 640)


def prep_inputs(b, L, x, c, ada_w, ada_b, ln_g, ln_b, e_w_in, e_rpb, e_gla_w_up, e_gla_b, e_gla_norm_g, e_w_out,
                o_w_in, hgrn_lb, o_hgrn_norm_g, o_conv_w, o_conv_b, o_dt_bias, o_a_log, o_d_skip, o_ssm_norm_g, o_w_out):
    f = lambda a: np.ascontiguousarray(np.asarray(a, dtype=np.float32))
    m = {}
    m["x"] = f(x[b])
    m["cT"] = f(c[b].reshape(8, 128).T)
    m["ada_w"] = f(ada_w)
    m["ada_bT"] = f(ada_b.reshape(2, 24, 128).transpose(2, 0, 1))
    m["ada_bg"] = f(ada_b[:, 2048:3072])
    m["ln_g"] = f(ln_g)
    m["ln_b"] = f(ln_b)
    m["e_w_in"] = f(e_w_in[0])
    m["e_w_out"] = f(e_w_out[0])
    m["btab"] = f(_na_btab(np.asarray(e_rpb[0]), L // 64))
    m["gla_w_up"] = f(e_gla_w_up[0])
    m["gla_bT"] = f(e_gla_b[0].reshape(2, 2, 128).transpose(2, 0, 1))
    m["gla_ng"] = f(e_gla_norm_g)
    m["o_w_in"] = f(o_w_in[0])
    m["o_w_out"] = f(o_w_out[0])
    m["hgrn_lbT"] = f(hgrn_lb.reshape(2, 4, 128).transpose(2, 0, 1))
    m["hgrn_ng"] = f(o_hgrn_norm_g)
    m["conv_wT"] = f(o_conv_w[0].reshape(4, 8, 128).transpose(2, 1, 0))
    m["conv_bT"] = f(o_conv_b[0].reshape(8, 128).T)
    m["dt_bias"] = f(o_dt_bias[0].reshape(1, 16))
    m["a_log"] = f(o_a_log[0].reshape(1, 16))
    m["d_skip"] = f(o_d_skip.reshape(1, 8))
    m["ssm_ng"] = f(o_ssm_norm_g.reshape(1, 512))
    return m


def kernel(**inputs):
    x = np.asarray(inputs["x"])
    B, L, _ = x.shape
    nc, es = build(L)
    in_maps = [prep_inputs(b, L, **inputs) for b in range(B)]
    res = run_bass_kernel_spmd(nc, in_maps, core_ids=list(range(B)))
    return np.stack([np.asarray(r["out"], dtype=np.float32) for r in res.results], axis=0)
```
